# Optimizing a Trainium2 kernel written in Bass

```python
import math
import jax, jax.numpy as jnp
from jax import lax
import numpy as np

D_MODEL = 1024
BATCH = 8
SEQ = 4096
DEPTH = 2

N_MEM = 256
EPS = 1e-6

RET_WIDTH = D_MODEL // 2
RET_HEADS = 4
RET_HEAD_DIM = RET_WIDTH // RET_HEADS
RET_CHUNK = 128
ROPE_BASE = 10000.0
S5_WIDTH = D_MODEL - RET_WIDTH
S5_GROUP = 16
S5_GROUPS = S5_WIDTH // S5_GROUP
S5_STATE = 64
EVEN_IN = 4 * RET_WIDTH + S5_WIDTH

GDN_HEADS = 8
GDN_HEAD_DIM = D_MODEL // GDN_HEADS
GDN_WIDTH = GDN_HEADS * GDN_HEAD_DIM
GDN_CONV = 4
GDN_CHUNK = 64
ODD_IN = 4 * GDN_WIDTH + 2 * GDN_HEADS

XA_HEADS = 4
XA_HEAD_DIM = D_MODEL // XA_HEADS

FFN_DIM = ((8 * D_MODEL) // 3 + 255) // 256 * 256
FFN_CONV = 3

kernel_name = "hybrid_retention_s5_gdn_convffn"

F32 = jnp.float32


def rmsnorm(x, g):
    xf = x.astype(F32)
    y = xf * lax.rsqrt(jnp.mean(xf * xf, axis=-1, keepdims=True) + EPS)
    return (y * g.astype(F32)).astype(x.dtype)


def causal_dwconv(x, w):
    k_w, ch = w.shape
    return lax.conv_general_dilated(x, w[:, None, :].astype(x.dtype), window_strides=(1,),
                                    padding=[(k_w - 1, 0)],
                                    dimension_numbers=('NWC', 'WIO', 'NWC'),
                                    feature_group_count=ch)


def rotary(x, positions):
    half = x.shape[-1] // 2
    inv = jnp.exp(-math.log(ROPE_BASE) * jnp.arange(half, dtype=F32) / half)
    ang = positions.astype(F32)[:, None] * inv[None, :]
    cos = jnp.cos(ang)[None, :, None, :]
    sin = jnp.sin(ang)[None, :, None, :]
    x1, x2 = x[..., :half], x[..., half:]
    return jnp.concatenate([x1 * cos - x2 * sin, x1 * sin + x2 * cos], axis=-1)


def retention_chunkwise(q, k, v):
    b, h, s, dh = q.shape
    c = RET_CHUNK
    n = s // c
    log_gamma = jnp.log1p(-jnp.exp2(-5.0 - jnp.arange(h, dtype=F32)))
    idx = jnp.arange(c, dtype=F32)
    diff = idx[:, None] - idx[None, :]
    causal = diff >= 0
    intra = jnp.where(causal, jnp.exp(log_gamma[:, None, None] * jnp.where(causal, diff, 0.0)), 0.0)
    q = q.reshape(b, h, n, c, dh)
    k = k.reshape(b, h, n, c, dh)
    v = v.reshape(b, h, n, c, dh)
    scores = jnp.einsum('bhnid,bhnjd->bhnij', q, k) * intra[None, :, None]
    inner = jnp.einsum('bhnij,bhnjd->bhnid', scores, v)
    k_dec = k * jnp.exp(log_gamma[:, None] * (c - 1 - idx))[None, :, None, :, None]
    kv = jnp.einsum('bhnjd,bhnje->nbhde', k_dec, v)
    chunk_decay = jnp.exp(log_gamma * c)[None, :, None, None]

    def step(state, kv_n):
        return state * chunk_decay + kv_n, state

    _, prev = lax.scan(step, jnp.zeros((b, h, dh, dh), F32), kv)
    q_dec = q * jnp.exp(log_gamma[:, None] * (idx + 1))[None, :, None, :, None]
    cross = jnp.einsum('bhnid,nbhde->bhnie', q_dec, prev)
    return (inner + cross).reshape(b, h, s, dh)


def complex_affine_combine(e1, e2):
    a1r, a1i, b1r, b1i = e1
    a2r, a2i, b2r, b2i = e2
    return (a2r * a1r - a2i * a1i,
            a2r * a1i + a2i * a1r,
            a2r * b1r - a2i * b1i + b2r,
            a2r * b1i + a2i * b1r + b2i)


def s5_ssm(u, lam_re, lam_im, b_re, b_im, c_re, c_im, d, log_dt):
    bsz, s, _ = u.shape
    uf = u.astype(F32).reshape(bsz, s, S5_GROUPS, S5_GROUP)
    lr, li = lam_re.astype(F32), lam_im.astype(F32)
    dt = jnp.exp(log_dt.astype(F32))[:, None]
    mag = jnp.exp(lr * dt)
    a_re = mag * jnp.cos(li * dt)
    a_im = mag * jnp.sin(li * dt)
    den = lr * lr + li * li
    z_re = ((a_re - 1.0) * lr + a_im * li) / den
    z_im = (a_im * lr - (a_re - 1.0) * li) / den
    br, bi = b_re.astype(F32), b_im.astype(F32)
    bb_re = z_re[:, None, :] * br - z_im[:, None, :] * bi
    bb_im = z_re[:, None, :] * bi + z_im[:, None, :] * br
    bu_re = jnp.einsum('bsgh,ghp->bsgp', uf, bb_re)
    bu_im = jnp.einsum('bsgh,ghp->bsgp', uf, bb_im)
    elems = (jnp.broadcast_to(a_re, bu_re.shape), jnp.broadcast_to(a_im, bu_re.shape), bu_re, bu_im)
    _, _, st_re, st_im = lax.associative_scan(complex_affine_combine, elems, axis=1)
    y = (jnp.einsum('bsgp,gph->bsgh', st_re, c_re.astype(F32))
         - jnp.einsum('bsgp,gph->bsgh', st_im, c_im.astype(F32))
         + d.astype(F32) * uf)
    return y.reshape(bsz, s, S5_WIDTH)


def even_mixer(h, w_in, ret_norm, lam_re, lam_im, b_re, b_im, c_re, c_im, s5_d, s5_log_dt,
               w_glu, b_glu, w_out):
    bsz, s, _ = h.shape
    proj = h @ w_in
    q, k, v, gate, u = jnp.split(proj, [RET_WIDTH, 2 * RET_WIDTH, 3 * RET_WIDTH, 4 * RET_WIDTH], axis=-1)
    pos = jnp.arange(s)

    def heads(t):
        return t.astype(F32).reshape(bsz, s, RET_HEADS, RET_HEAD_DIM)

    qh = rotary(heads(q), pos)
    kh = rotary(heads(k), pos) * (RET_HEAD_DIM ** -0.5)
    o = retention_chunkwise(qh.transpose(0, 2, 1, 3), kh.transpose(0, 2, 1, 3),
                            heads(v).transpose(0, 2, 1, 3)).transpose(0, 2, 1, 3)
    o = o * lax.rsqrt(jnp.mean(o * o, axis=-1, keepdims=True) + EPS)
    o = o.reshape(bsz, s, RET_WIDTH) * ret_norm.astype(F32) * jax.nn.silu(gate.astype(F32))
    y = s5_ssm(u, lam_re, lam_im, b_re, b_im, c_re, c_im, s5_d, s5_log_dt)
    y = jax.nn.gelu(y)
    y = y * jax.nn.sigmoid(y @ w_glu.astype(F32) + b_glu.astype(F32))
    merged = jnp.concatenate([o, y], axis=-1).astype(h.dtype)
    return merged @ w_out


def gated_delta_chunkwise(q, k, v, g, beta):
    b, h, s, dk = q.shape
    dv = v.shape[-1]
    c = GDN_CHUNK
    n = s // c
    q = q.reshape(b, h, n, c, dk)
    k = k.reshape(b, h, n, c, dk)
    v = v.reshape(b, h, n, c, dv)
    gc = jnp.cumsum(g.reshape(b, h, n, c), axis=-1)
    beta = beta.reshape(b, h, n, c)
    kb = k * beta[..., None]
    vb = v * beta[..., None]
    incl = jnp.tril(jnp.ones((c, c), bool))
    strict = jnp.tril(jnp.ones((c, c), bool), -1)
    gdiff = gc[..., :, None] - gc[..., None, :]
    decay = jnp.where(incl, jnp.exp(jnp.where(incl, gdiff, 0.0)), 0.0)
    a_mat = jnp.where(strict, jnp.einsum('bhnid,bhnjd->bhnij', kb, k) * decay, 0.0)
    eye = jnp.eye(c, dtype=F32)
    t_mat = lax.linalg.triangular_solve(a_mat + eye, jnp.broadcast_to(eye, a_mat.shape),
                                        left_side=True, lower=True)
    w = jnp.einsum('bhnij,bhnjd->bhnid', t_mat, kb * jnp.exp(gc)[..., None])
    u = jnp.einsum('bhnij,bhnjd->bhnid', t_mat, vb)
    qk = jnp.where(incl, jnp.einsum('bhnid,bhnjd->bhnij', q, k) * decay, 0.0)
    q_dec = q * jnp.exp(gc)[..., None]
    k_dec = k * jnp.exp(gc[..., -1:] - gc)[..., None]
    g_last = jnp.exp(gc[..., -1])
    xs = tuple(jnp.moveaxis(t, 2, 0) for t in (q_dec, k_dec, u, w, qk, g_last))

    def step(state, inp):
        qd, kd, un, wn, qkn, gl = inp
        v_new = un - jnp.einsum('bhcd,bhde->bhce', wn, state)
        o = jnp.einsum('bhcd,bhde->bhce', qd, state) + jnp.einsum('bhij,bhje->bhie', qkn, v_new)
        state = state * gl[..., None, None] + jnp.einsum('bhcd,bhce->bhde', kd, v_new)
        return state, o

    _, o = lax.scan(step, jnp.zeros((b, h, dk, dv), F32), xs)
    return jnp.moveaxis(o, 0, 2).reshape(b, h, s, dv)


def odd_mixer(h, w_in, conv_w, a_log, dt_bias, o_norm, w_out):
    bsz, s, _ = h.shape
    proj = h @ w_in
    qkv, z, b_in, a_in = jnp.split(proj, [3 * GDN_WIDTH, 4 * GDN_WIDTH, 4 * GDN_WIDTH + GDN_HEADS], axis=-1)
    qkv = jax.nn.silu(causal_dwconv(qkv, conv_w)).astype(F32)
    q, k, v = jnp.split(qkv, 3, axis=-1)

    def heads(t):
        return t.reshape(bsz, s, GDN_HEADS, GDN_HEAD_DIM).transpose(0, 2, 1, 3)

    def l2n(t):
        return t * lax.rsqrt(jnp.sum(t * t, axis=-1, keepdims=True) + EPS)

    q = l2n(heads(q)) * (GDN_HEAD_DIM ** -0.5)
    k = l2n(heads(k))
    v = heads(v)
    beta = jax.nn.sigmoid(b_in.astype(F32)).transpose(0, 2, 1)
    g = -(jnp.exp(a_log.astype(F32)) * jax.nn.softplus(a_in.astype(F32) + dt_bias.astype(F32)))
    g = g.transpose(0, 2, 1)
    o = gated_delta_chunkwise(q, k, v, g, beta).transpose(0, 2, 1, 3)
    o = o * lax.rsqrt(jnp.mean(o * o, axis=-1, keepdims=True) + EPS) * o_norm.astype(F32)
    o = o * jax.nn.silu(z.astype(F32).reshape(bsz, s, GDN_HEADS, GDN_HEAD_DIM))
    return o.reshape(bsz, s, GDN_WIDTH).astype(h.dtype) @ w_out


def memory_cross_attention(h, mem_n, wq, wkv, wo):
    bsz, s, _ = h.shape
    m = mem_n.shape[1]
    q = (h @ wq).reshape(bsz, s, XA_HEADS, XA_HEAD_DIM)
    k, v = jnp.split(mem_n @ wkv, 2, axis=-1)
    k = k.reshape(bsz, m, XA_HEADS, XA_HEAD_DIM)
    v = v.reshape(bsz, m, XA_HEADS, XA_HEAD_DIM)
    scores = jnp.einsum('bshd,bmhd->bhsm', q, k).astype(F32) * (XA_HEAD_DIM ** -0.5)
    p = jax.nn.softmax(scores, axis=-1).astype(h.dtype)
    o = jnp.einsum('bhsm,bmhd->bshd', p, v).reshape(bsz, s, D_MODEL)
    return o @ wo


def conv_ffn(h, w_up, conv_w, w_down):
    hu = causal_dwconv(h @ w_up, conv_w)
    up, gate = jnp.split(hu, 2, axis=-1)
    return (jax.nn.silu(gate) * up) @ w_down


def setup_inputs(seed: int = 0) -> dict:
    key = jax.random.key(seed)
    it = iter(jax.random.split(key, 64))

    def nrm(shape, scale):
        return jax.random.normal(next(it), shape, F32) * scale

    def dense(fan_in, fan_out):
        return nrm((fan_in, fan_out), fan_in ** -0.5)

    def gain(n):
        return 1.0 + nrm((n,), 0.02)

    def log_uniform(shape, lo, hi):
        return jax.random.uniform(next(it), shape, F32, math.log(lo), math.log(hi))

    def common(p):
        return {
            p + "xa_norm": gain(D_MODEL),
            p + "mem_norm": gain(D_MODEL),
            p + "xa_wq": dense(D_MODEL, D_MODEL),
            p + "xa_wkv": dense(D_MODEL, 2 * D_MODEL),
            p + "xa_wo": dense(D_MODEL, D_MODEL),
            p + "ffn_norm": gain(D_MODEL),
            p + "ffn_w_up": dense(D_MODEL, 2 * FFN_DIM),
            p + "ffn_conv": nrm((FFN_CONV, 2 * FFN_DIM), FFN_CONV ** -0.5),
            p + "ffn_w_down": dense(FFN_DIM, D_MODEL),
        }

    out = {
        "x": nrm((BATCH, SEQ, D_MODEL), 1.0),
        "mem": nrm((BATCH, N_MEM, D_MODEL), 1.0),
        "l0_mix_norm": gain(D_MODEL),
        "l0_w_in": dense(D_MODEL, EVEN_IN),
        "l0_ret_norm": gain(RET_WIDTH),
        "l0_s5_lambda_re": -0.5 + nrm((S5_GROUPS, S5_STATE), 0.01),
        "l0_s5_lambda_im": math.pi * jnp.broadcast_to(jnp.arange(S5_STATE, dtype=F32), (S5_GROUPS, S5_STATE))
                           + nrm((S5_GROUPS, S5_STATE), 0.01),
        "l0_s5_b_re": nrm((S5_GROUPS, S5_GROUP, S5_STATE), (2 * S5_GROUP) ** -0.5),
        "l0_s5_b_im": nrm((S5_GROUPS, S5_GROUP, S5_STATE), (2 * S5_GROUP) ** -0.5),
        "l0_s5_c_re": nrm((S5_GROUPS, S5_STATE, S5_GROUP), (2 * S5_STATE) ** -0.5),
        "l0_s5_c_im": nrm((S5_GROUPS, S5_STATE, S5_GROUP), (2 * S5_STATE) ** -0.5),
        "l0_s5_d": nrm((S5_GROUPS, S5_GROUP), 1.0),
        "l0_s5_log_dt": log_uniform((S5_GROUPS,), 1e-3, 1e-1),
        "l0_s5_w_glu": dense(S5_WIDTH, S5_WIDTH),
        "l0_s5_b_glu": nrm((S5_WIDTH,), 0.01),
        "l0_w_out": dense(D_MODEL, D_MODEL),
    }
    out.update(common("l0_"))
    dt = jnp.exp(log_uniform((GDN_HEADS,), 1e-3, 1e-1))
    out.update({
        "l1_mix_norm": gain(D_MODEL),
        "l1_w_in": dense(D_MODEL, ODD_IN),
        "l1_conv": nrm((GDN_CONV, 3 * GDN_WIDTH), GDN_CONV ** -0.5),
        "l1_a_log": jnp.log(jax.random.uniform(next(it), (GDN_HEADS,), F32, 1.0, 16.0)),
        "l1_dt_bias": dt + jnp.log(-jnp.expm1(-dt)),
        "l1_o_norm": gain(GDN_HEAD_DIM),
        "l1_w_out": dense(GDN_WIDTH, D_MODEL),
    })
    out.update(common("l1_"))
    out["final_norm"] = gain(D_MODEL)
    return out


def reference(x, mem,
              l0_mix_norm, l0_w_in, l0_ret_norm, l0_s5_lambda_re, l0_s5_lambda_im, l0_s5_b_re, l0_s5_b_im,
              l0_s5_c_re, l0_s5_c_im, l0_s5_d, l0_s5_log_dt, l0_s5_w_glu, l0_s5_b_glu, l0_w_out,
              l0_xa_norm, l0_mem_norm, l0_xa_wq, l0_xa_wkv, l0_xa_wo,
              l0_ffn_norm, l0_ffn_w_up, l0_ffn_conv, l0_ffn_w_down,
              l1_mix_norm, l1_w_in, l1_conv, l1_a_log, l1_dt_bias, l1_o_norm, l1_w_out,
              l1_xa_norm, l1_mem_norm, l1_xa_wq, l1_xa_wkv, l1_xa_wo,
              l1_ffn_norm, l1_ffn_w_up, l1_ffn_conv, l1_ffn_w_down,
              final_norm):
    mixers = (
        lambda h: even_mixer(h, l0_w_in, l0_ret_norm, l0_s5_lambda_re, l0_s5_lambda_im, l0_s5_b_re,
                             l0_s5_b_im, l0_s5_c_re, l0_s5_c_im, l0_s5_d, l0_s5_log_dt,
                             l0_s5_w_glu, l0_s5_b_glu, l0_w_out),
        lambda h: odd_mixer(h, l1_w_in, l1_conv, l1_a_log, l1_dt_bias, l1_o_norm, l1_w_out),
    )
    commons = (
        (l0_mix_norm, l0_xa_norm, l0_mem_norm, l0_xa_wq, l0_xa_wkv, l0_xa_wo,
         l0_ffn_norm, l0_ffn_w_up, l0_ffn_conv, l0_ffn_w_down),
        (l1_mix_norm, l1_xa_norm, l1_mem_norm, l1_xa_wq, l1_xa_wkv, l1_xa_wo,
         l1_ffn_norm, l1_ffn_w_up, l1_ffn_conv, l1_ffn_w_down),
    )
    for i in range(DEPTH):
        (mix_norm, xa_norm, mem_norm, xa_wq, xa_wkv, xa_wo,
         ffn_norm, ffn_w_up, ffn_conv, ffn_w_down) = commons[i]
        x = x + mixers[i](rmsnorm(x, mix_norm))
        x = x + memory_cross_attention(rmsnorm(x, xa_norm), rmsnorm(mem, mem_norm), xa_wq, xa_wkv, xa_wo)
        x = x + conv_ffn(rmsnorm(x, ffn_norm), ffn_w_up, ffn_conv, ffn_w_down)
    return rmsnorm(x, final_norm)
```

```python
import math
from contextlib import ExitStack

import numpy as np
import ml_dtypes
import concourse.bass as bass
import concourse.mybir as mybir
from concourse.bass_utils import run_bass_kernel_spmd

F32 = mybir.dt.float32
BF16 = mybir.dt.bfloat16
I32 = mybir.dt.int32
ALU = mybir.AluOpType
AF = mybir.ActivationFunctionType
AX = mybir.AxisListType
PE, ACT, DVE, POOL, SP = "tensor", "scalar", "vector", "gpsimd", "sync"

D = 1024
S = 4096
TT = 512
NT = S // TT
NMEM = 256
EPS = 1e-6
FFN = 2816
NPAIR = FFN // 128


class Prog:
    def __init__(self, nc):
        self.nc = nc
        self.ops = []
        self.track = {}
        self.stack = ExitStack()
        self.fence = frozenset()
        self.last_eng = {}
        self.dma_since = []
        self.ps_rr = 0
        self.slot_map = {}
        self.wslot_map = {}

    def sb(self, name, shape, dt=F32):
        return self.stack.enter_context(self.nc.sbuf_tensor(name, list(shape), dt))

    def ps(self, name, shape, dt=F32):
        return self.stack.enter_context(self.nc.psum_tensor(name, list(shape), dt))

    def op(self, eng, fn, reads=(), writes=(), dkey=None):
        oid = len(self.ops)
        deps = set(self.fence)
        writes = list(writes) + [k for k in reads if k[0] in ("ps", "psb")]
        for k in reads:
            t = self.track.get(k)
            if t and t[0] is not None:
                deps.add(t[0])
        for k in writes:
            t = self.track.get(k)
            if t:
                if t[0] is not None:
                    deps.add(t[0])
                deps.update(t[1])
        for k in reads:
            t = self.track.setdefault(k, [None, []])
            t[1].append(oid)
        for k in writes:
            self.track[k] = [oid, []]
        deps.discard(oid)
        isw = dkey is not None and dkey[0] == "W"
        if dkey is not None:
            m = self.wslot_map if isw else self.slot_map
            if dkey not in m:
                m[dkey] = ("w" if isw else "d", len(m))
            dkey = m[dkey]
        self.ops.append(dict(eng=eng, fn=fn, deps=deps, dkey=dkey, sig=False))
        if not isw:
            self.last_eng[eng] = oid
            if dkey is not None:
                self.dma_since.append(oid)
        return oid

    def barrier(self, final=False):
        self.fence = frozenset(list(self.last_eng.values()) + self.dma_since)
        self.dma_since = []
        self.track = {k: v for k, v in self.track.items() if k[0] == "W"}
        self.slot_map = {}
        if final:
            self.wslot_map = {}

    def emit(self):
        nc = self.nc
        ops = self.ops
        for o in ops:
            for d in o["deps"]:
                p = ops[d]
                if p["dkey"] is None and p["eng"] == PE and o["eng"] == PE and o["dkey"] is None:
                    continue
                p["sig"] = True
        engs = [PE, ACT, DVE, POOL, SP]
        cnt = {e: 0 for e in engs}
        dcnt = {}
        for o in ops:
            if o["dkey"] is not None:
                dcnt[o["dkey"]] = dcnt.get(o["dkey"], 0) + 16
                o["semk"] = ("d", o["dkey"])
                o["seq"] = dcnt[o["dkey"]]
            elif o["sig"]:
                cnt[o["eng"]] += 1
                o["semk"] = ("e", o["eng"])
                o["seq"] = cnt[o["eng"]]
        semkeys = [("e", e) for e in engs if cnt[e] > 0] + [("d", k) for k in dcnt]
        sems = {}
        for sk in semkeys:
            sems[sk] = self.stack.enter_context(nc.semaphore("s_" + "_".join(str(x) for x in sk)))
        per_eng = {e: [] for e in engs}
        for i, o in enumerate(ops):
            per_eng[o["eng"]].append(i)
        self.stats = {e: len(per_eng[e]) for e in engs}
        self.stats["sems"] = len(sems)
        self.stats["maxcnt"] = dict(cnt)
        self.stats["dcnt"] = max(dcnt.values()) if dcnt else 0

        def run_engine(ename, eobj):
            waited = {}
            for i in per_eng[ename]:
                o = ops[i]
                need = {}
                for d in o["deps"]:
                    p = ops[d]
                    if "semk" not in p:
                        continue
                    if p["dkey"] is None and p["eng"] == PE and ename == PE and o["dkey"] is None:
                        continue
                    sk = p["semk"]
                    if p["seq"] > need.get(sk, 0):
                        need[sk] = p["seq"]
                for sk, v in need.items():
                    if waited.get(sk, 0) >= v:
                        continue
                    eobj.wait_ge(sems[sk], v)
                    waited[sk] = v
                ins = o["fn"](eobj)
                if o["dkey"] is not None:
                    ins.then_inc(sems[o["semk"]], 16)
                elif o["sig"]:
                    ins.then_inc(sems[o["semk"]], 1)
            last = {}
            for i in per_eng[ename]:
                o = ops[i]
                if o["dkey"] is not None:
                    last[o["semk"]] = max(last.get(o["semk"], 0), o["seq"])
            for sk, v in last.items():
                if waited.get(sk, 0) < v:
                    eobj.wait_ge(sems[sk], v)

        block = self.stack.enter_context(nc.Block())
        if per_eng[SP]:
            @block.sync
            def _(e):
                run_engine(SP, e)
        if per_eng[PE]:
            @block.tensor
            def _(e):
                run_engine(PE, e)
        if per_eng[ACT]:
            @block.scalar
            def _(e):
                run_engine(ACT, e)
        if per_eng[DVE]:
            @block.vector
            def _(e):
                run_engine(DVE, e)
        if per_eng[POOL]:
            @block.gpsimd
            def _(e):
                run_engine(POOL, e)
        self.stack.close()


class Ctx:
    pass


def mm_chain(P, ps_ap, pairs, reads, writes):
    pairs = list(pairs)

    def fn(e):
        n = len(pairs)
        ins = None
        for i, (l, r) in enumerate(pairs):
            ins = e.matmul(ps_ap, lhsT=l, rhs=r, start=(i == 0), stop=(i == n - 1))
        return ins
    return P.op(PE, fn, reads, writes)


def next_psb(C):
    b = C.psb_rr % 2
    C.psb_rr += 1
    return C.psbs[b], ("psb", b)


def next_ps(C, lo=0, hi=None):
    hi = len(C.psum) if hi is None else hi
    k = (lo, hi)
    r = C.ps_cnt.get(k, 0)
    C.ps_cnt[k] = r + 1
    b = lo + r % (hi - lo)
    return C.psum[b], ("ps", b)


def wload(P, C, dst, w_dram, kc, F, gain=None, name="w", f_dst0=0):
    assert gain is None
    FW = 1408
    pcs = []
    for f0 in range(0, F, FW):
        fw = min(FW, F - f0)
        pcs.append((f0, fw))
        for c in range(kc):
            P.op(POOL, lambda e, c=c, f0=f0, fw=fw: e.dma_start(out=dst[:, c, f0:f0 + fw], in_=w_dram[c * 128:(c + 1) * 128, f0:f0 + fw]),
                 writes=[("W", name, f0, c)], dkey=("W", name, f0))
    C.wreg[name] = (pcs, kc)


def wkeys(C, name, col=None, n=128):
    pcs, kc = C.wreg[name]
    out = []
    for (f0, fw) in pcs:
        if col is None or (f0 < col + n and col < f0 + fw):
            out += [("W", name, f0, c) for c in range(kc)]
    return out


def gain_key(g):
    return ("gains",)


def load_xt(P, C, src, j, xt, xt_key, ncols=TT, col0=None, dkey="xt"):
    c0 = j * ncols if col0 is None else col0
    srcv = src.rearrange("(c p) t -> p c t", p=128)
    P.op(SP, lambda e: e.dma_start(out=xt[:, :, :], in_=srcv[:, :, c0:c0 + ncols]), writes=[xt_key], dkey=(dkey, xt_key))


def norm_chunked(P, C, xt, xt_key, hn, hn_key, sqs, rstd, gain, kc=8):
    pst, psk = next_ps(C)
    for c in range(kc):
        q = c % len(sqs)
        P.op(ACT, lambda e, c=c, q=q: e.activation(out=sqs[q][:, :], in_=xt[:, c, :], func=AF.Square), reads=[xt_key], writes=[("sqs", q)])
        P.op(PE, lambda e, c=c, q=q: e.matmul(pst[:, :], lhsT=C.ones_bf[:, :], rhs=sqs[q][:, :], start=(c == 0), stop=(c == kc - 1)),
             reads=[("sqs", q)], writes=[psk])
    P.op(ACT, lambda e: e.activation(out=rstd[:, :], in_=pst[:, :], func=AF.Sqrt, scale=1.0 / D, bias=C.eps_col[:, 0:1]), reads=[psk], writes=[("rstd",)])
    P.op(DVE, lambda e: e.reciprocal(out=rstd[:, :], in_=rstd[:, :]), reads=[("rstd",)], writes=[("rstd",)])
    for c in range(kc):
        P.op(DVE, lambda e, c=c: e.scalar_tensor_tensor(out=hn[:, c, :], in0=xt[:, c, :], scalar=gain[:, c:c + 1], in1=rstd[:, :], op0=ALU.mult, op1=ALU.mult),
             reads=[xt_key, ("rstd",)], writes=[(hn_key[0], "a" if c < 4 else "b")])


def norm_tile(P, C, src, j, xt, xt_key, hn, hn_key, sq, rstd, ncols=TT, kc=8, col0=None, dkey="xt", sq_keys=(("sq",),), pshi=None, gain=None, load=True):
    if load:
        load_xt(P, C, src, j, xt, xt_key, ncols=ncols, col0=col0, dkey=dkey)
    P.op(ACT, lambda e: e.activation(out=sq[:, :, :ncols], in_=xt[:, :, :], func=AF.Square), reads=[xt_key], writes=list(sq_keys))
    pst, psk = next_ps(C, 0, pshi)
    mm_chain(P, pst[:, :ncols], [(C.ones_bf[:, :], sq[:, c, :ncols]) for c in range(kc)], reads=list(sq_keys), writes=[psk])
    P.op(ACT, lambda e: e.activation(out=rstd[:, :ncols], in_=pst[:, :ncols], func=AF.Sqrt, scale=1.0 / D, bias=C.eps_col[:, 0:1]),
         reads=[psk], writes=[("rstd",)])
    P.op(DVE, lambda e: e.reciprocal(out=rstd[:, :ncols], in_=rstd[:, :ncols]), reads=[("rstd",)], writes=[("rstd",)])
    for c in range(kc):
        P.op(DVE, lambda e, c=c: e.scalar_tensor_tensor(out=hn[:, c, :], in0=xt[:, c, :], scalar=gain[:, c:c + 1], in1=rstd[:, :ncols], op0=ALU.mult, op1=ALU.mult),
             reads=[xt_key, ("rstd",)], writes=[(hn_key[0], "a" if c < 4 else "b")])


def stage_xattn(P, C, lp):
    nc = P.nc
    W = C.w
    xres = C.xres
    scale = 256 ** -0.5
    with ExitStack() as es:
        sb = lambda name, shape, dt=F32: es.enter_context(nc.sbuf_tensor("sb_" + lp + name, list(shape), dt))
        wq = sb("xa_wq", [128, 8, 1024], BF16)
        wo = sb("xa_wo", [128, 8, 1024], BF16)
        kT = sb("xa_kT", [128, 8, NMEM], BF16)
        vtok = sb("xa_vtok", [128, 2, 1024], BF16)
        g_xa = C.gains[lp + "xa_norm"]
        g_mem = C.gains[lp + "mem_norm"]
        wload(P, C, wq, W[lp + "xa_wq"], 8, 1024, name="xa_wq")
        wload(P, C, wo, W[lp + "xa_wo"], 8, 1024, gain=None, name="xa_wo")
        P.barrier()
        xts = [sb(f"xa_xt{i}", [128, 8, TT]) for i in range(2)]
        hns = [sb(f"xa_hn{i}", [128, 8, TT], BF16) for i in range(2)]
        xt = xts[0]
        hn = hns[0]
        sq = sb("xa_sq", [128, 8, TT], BF16)
        rstd = sb("xa_rstd", [128, TT])
        with ExitStack() as es2:
            wkv = es2.enter_context(nc.sbuf_tensor("sb_" + lp + "xa_wkv", [128, 8, 2048], BF16))
            memx = es2.enter_context(nc.sbuf_tensor("sb_" + lp + "xa_memx", [128, 8, NMEM], F32))
            memn = es2.enter_context(nc.sbuf_tensor("sb_" + lp + "xa_memn", [128, 8, NMEM], BF16))
            wload(P, C, wkv, W[lp + "xa_wkv"], 8, 2048, name="xa_wkv")
            P.barrier()
            norm_tile(P, C, C.memT, 0, memx, ("memx",), memn, ("memn",), sq, rstd, ncols=NMEM, dkey="memx", gain=g_mem)
            for fb in range(8):
                pst, psk = next_ps(C)
                mm_chain(P, pst[:, :NMEM], [(wkv[:, c, fb * 128:(fb + 1) * 128], memn[:, c, :]) for c in range(8)],
                         reads=[*wkeys(C, "xa_wkv", fb * 128), ("memn", "a"), ("memn", "b")], writes=[psk])
                P.op(ACT, lambda e, pst=pst, fb=fb: e.copy(out=kT[:, fb, :], in_=pst[:, :NMEM]), reads=[psk], writes=[("kT",)])
            for mb in range(2):
                for hf in range(2):
                    pst, psk = next_ps(C)
                    mm_chain(P, pst[:, :], [(memn[:, c, mb * 128:(mb + 1) * 128], wkv[:, c, 1024 + hf * 512:1024 + (hf + 1) * 512]) for c in range(8)],
                             reads=[*wkeys(C, "xa_wkv", 1024 + hf * 512, 512), ("memn", "a"), ("memn", "b")], writes=[psk])
                    P.op(ACT, lambda e, pst=pst, mb=mb, hf=hf: e.copy(out=vtok[:, mb, hf * 512:(hf + 1) * 512], in_=pst[:, :]),
                         reads=[psk], writes=[("vtok",)])
            P.barrier()
        qT = sb("xa_qT", [128, 8, TT], BF16)
        oT = sb("xa_oT", [128, 8, TT], BF16)
        pexp = [sb(f"xa_pexp{i}", [128, NMEM]) for i in range(4)]
        pn = [sb(f"xa_pn{i}", [128, NMEM], BF16) for i in range(4)]
        pT = [sb(f"xa_pT{i}", [128, 2, 128], BF16) for i in range(8)]
        st = [sb(f"xa_st{i}", [128, 4]) for i in range(4)]
        norm_tile(P, C, xres, 0, xts[0], ("xt", 0), hns[0], ("hn0",), sq, rstd, gain=g_xa)
        for j in range(NT):
            bb_ = j % 2
            xt = xts[bb_]
            hn = hns[bb_]
            XK = ("xt", bb_)
            HN = [("hn%d" % bb_, "a"), ("hn%d" % bb_, "b")]
            for fb in range(8):
                pst, psk = next_ps(C)
                mm_chain(P, pst[:, :], [(wq[:, c, fb * 128:(fb + 1) * 128], hn[:, c, :]) for c in range(8)],
                         reads=[*wkeys(C, "xa_wq", fb * 128), *HN], writes=[psk])
                P.op(ACT, lambda e, pst=pst, fb=fb: e.copy(out=qT[:, fb, :], in_=pst[:, :]), reads=[psk], writes=[("qT", fb)])
            units = [(hh, tb) for hh in range(4) for tb in range(4)]

            def s1(i):
                hh, tb = units[i]
                u = i % 4
                pst, psk = next_ps(C)
                mm_chain(P, pst[:, :NMEM], [(qT[:, 2 * hh + dc, tb * 128:(tb + 1) * 128], kT[:, 2 * hh + dc, :]) for dc in range(2)],
                         reads=[("qT", 2 * hh), ("qT", 2 * hh + 1), ("kT",)], writes=[psk])
                s_ = st[u]
                P.op(DVE, lambda e: e.reduce_max(out=s_[:, 0:1], in_=pst[:, :NMEM], axis=AX.X), reads=[psk], writes=[("st", u, 0)])
                P.op(DVE, lambda e: e.tensor_scalar(out=s_[:, 1:2], in0=s_[:, 0:1], scalar1=-scale, scalar2=None, op0=ALU.mult),
                     reads=[("st", u, 0)], writes=[("st", u, 1)])
                P.op(ACT, lambda e: e.activation(out=pexp[u][:, :], in_=pst[:, :NMEM], func=AF.Exp, scale=scale, bias=s_[:, 1:2], accum_out=s_[:, 2:3]),
                     reads=[psk, ("st", u, 1)], writes=[("pexp", u), ("st", u, 2)])

            def s1b(i):
                u = i % 4
                s_ = st[u]
                P.op(DVE, lambda e: e.reciprocal(out=s_[:, 3:4], in_=s_[:, 2:3]), reads=[("st", u, 2)], writes=[("st", u, 3)])
                P.op(DVE, lambda e: e.tensor_scalar(out=pn[u][:, :], in0=pexp[u][:, :], scalar1=s_[:, 3:4], scalar2=None, op0=ALU.mult),
                     reads=[("pexp", u), ("st", u, 3)], writes=[("pn", u)])

            def s2(i):
                hh, tb = units[i]
                u = i % 4
                pbt, pbk_ = next_psb(C)

                def trf(e):
                    ins = None
                    for mb_ in range(2):
                        ins = e.transpose(pbt[:, mb_ * 128:(mb_ + 1) * 128], pn[u][:, mb_ * 128:(mb_ + 1) * 128], C.ident_bf[:, :])
                    return ins
                P.op(PE, trf, reads=[("pn", u)], writes=[pbk_])
                pti = (hh % 2) * 4 + tb
                P.op(ACT, lambda e: e.copy(out=pT[pti][:, :, :], in_=pbt[:, 0:256].rearrange("p (a b) -> p a b", a=2)),
                     reads=[pbk_], writes=[("pT", pti)])

            def s3(hh):
                for dvb in range(2):
                    pst, psk = next_ps(C)

                    def pvf(e, pst=pst, dvb=dvb):
                        ins = None
                        for tb in range(4):
                            for mb_ in range(2):
                                ins = e.matmul(pst[:, tb * 128:(tb + 1) * 128], lhsT=vtok[:, mb_, hh * 256 + dvb * 128:hh * 256 + (dvb + 1) * 128],
                                               rhs=pT[(hh % 2) * 4 + tb][:, mb_, :], start=(mb_ == 0), stop=(mb_ == 1))
                        return ins
                    P.op(PE, pvf, reads=[("vtok",)] + [("pT", (hh % 2) * 4 + tb) for tb in range(4)], writes=[psk])
                    P.op(ACT, lambda e, pst=pst, dvb=dvb: e.copy(out=oT[:, 2 * hh + dvb, :], in_=pst[:, :]), reads=[psk], writes=[("oT", 2 * hh + dvb)])

            LAG = 3
            for i in range(len(units) + LAG):
                if i < len(units):
                    s1(i)
                if 0 <= i - 1 < len(units):
                    s1b(i - 1)
                k_ = i - LAG
                if 0 <= k_ < len(units):
                    s2(k_)
                    if units[k_][1] == 3:
                        s3(units[k_][0])
            if j + 1 < NT:
                norm_tile(P, C, xres, j + 1, xts[1 - bb_], ("xt", 1 - bb_), hns[1 - bb_], ("hn%d" % (1 - bb_),), sq, rstd, gain=g_xa)
            for fb in range(8):
                pst, psk = next_ps(C)
                mm_chain(P, pst[:, :], [(wo[:, c, fb * 128:(fb + 1) * 128], oT[:, c, :]) for c in range(8)],
                         reads=wkeys(C, "xa_wo", fb * 128) + [("oT", c) for c in range(8)], writes=[psk])
                P.op(DVE, lambda e, pst=pst, fb=fb, xt=xt: e.tensor_tensor(out=xt[:, fb, :], in0=pst[:, :], in1=xt[:, fb, :], op=ALU.add),
                     reads=[psk, XK], writes=[XK])
            dstv = xres.rearrange("(c p) t -> p c t", p=128)
            P.op(POOL, lambda e, j=j, xt=xt: e.dma_start(out=dstv[:, :, j * TT:(j + 1) * TT], in_=xt[:, :, :]), reads=[XK], dkey=("xst", bb_))
        P.barrier()


def stage_ffn(P, C, lp, final=False):
    nc = P.nc
    W = C.w
    xres = C.xres
    with ExitStack() as es:
        sb = lambda name, shape, dt=F32: es.enter_context(nc.sbuf_tensor("sb_" + lp + name, list(shape), dt))
        wup = sb("ff_wup", [128, 8, 2 * FFN], BF16)
        wdn = sb("ff_wdn", [128, NPAIR, 1024], BF16)
        cw = sb("ff_cw", [128, 3, 2 * NPAIR])
        g_f = C.gains[lp + "ffn_norm"]
        wload(P, C, wup, W[lp + "ffn_w_up"], 8, 2 * FFN, name="ff_wup")
        wload(P, C, wdn, W[lp + "ffn_w_down"], NPAIR, 1024, gain=None, name="ff_wdn")
        P.op(SP, lambda e: e.dma_start(out=cw[:, :, :], in_=W[lp + "ffn_conv"]), writes=[("cw",)], dkey=("cw",))
        P.barrier()
        xts = [sb(f"ff_xt{i}", [128, 8, TT]) for i in range(2)]
        hn = sb("ff_hn", [128, 8, TT], BF16)
        rstd = sb("ff_rstd", [128, TT])
        rstd2 = rstd
        act = sb("ff_act", [128, NPAIR, TT], BF16)
        sq = act[:, 0:8, :]
        sqk = [("act", c) for c in range(8)]
        acc = [sb(f"ff_acc{i}", [128, TT]) for i in range(4)]
        halo = sb("ff_halo", [128, 2 * NPAIR, 2])
        corr = sb("ff_corr", [128, 2 * NPAIR, 2])
        ctmp = sb("ff_ctmp", [128, 2 * NPAIR])
        sqs = [sb("ff_sqs0", [128, TT], BF16)]
        load_xt(P, C, xres, 0, xts[0], ("xt", 0))
        norm_chunked(P, C, xts[0], ("xt", 0), hn, ("hn",), sqs, rstd, g_f)
        for j in range(NT):
            xt = xts[j % 2]
            fo = xt
            XK = ("xt", j % 2)
            if j + 1 < NT:
                load_xt(P, C, xres, j + 1, xts[1 - j % 2], ("xt", 1 - j % 2))
            if j > 0:
                hk = [("halo", b) for b in range(2 * NPAIR)]
                P.op(POOL, lambda e: e.tensor_tensor(out=corr[:, :, 0], in0=halo[:, :, 1], in1=cw[:, 1, :], op=ALU.mult), reads=hk, writes=[("corr",)])
                P.op(POOL, lambda e: e.tensor_tensor(out=ctmp[:, :], in0=halo[:, :, 0], in1=cw[:, 0, :], op=ALU.mult), reads=hk, writes=[("ctmp",)])
                P.op(POOL, lambda e: e.tensor_tensor(out=corr[:, :, 0], in0=corr[:, :, 0], in1=ctmp[:, :], op=ALU.add), reads=[("corr",), ("ctmp",)], writes=[("corr",)])
                P.op(POOL, lambda e: e.tensor_tensor(out=corr[:, :, 1], in0=halo[:, :, 1], in1=cw[:, 0, :], op=ALU.mult), reads=hk + [("corr",)], writes=[("corr",)])
            pend_pairs = []
            for pr in range(NPAIR):
                accs = {}
                for kind in range(2):
                    blk = kind * NPAIR + pr
                    u = (pr * 2 + kind) % 4
                    pst, psk = next_ps(C)
                    mm_chain(P, pst[:, :], [(wup[:, c, blk * 128:(blk + 1) * 128], hn[:, c, :]) for c in range(8)],
                             reads=[*wkeys(C, "ff_wup", blk * 128), ("hn", "a"), ("hn", "b")], writes=[psk])
                    a_ = acc[u]
                    P.op(ACT, lambda e, a_=a_, pst=pst, blk=blk: e.activation(out=a_[:, :], in_=pst[:, :], func=AF.Copy, scale=cw[:, 2, blk:blk + 1]),
                         reads=[psk], writes=[("acc", u)])
                    if j < NT - 1:
                        P.op(ACT, lambda e, pst=pst, blk=blk: e.copy(out=halo[:, blk, :], in_=pst[:, TT - 2:TT]), reads=[psk], writes=[("halo", blk)])
                    P.op(DVE, lambda e, a_=a_, pst=pst, blk=blk: e.scalar_tensor_tensor(out=a_[:, 1:TT], in0=pst[:, 0:TT - 1], scalar=cw[:, 1, blk:blk + 1], in1=a_[:, 1:TT],
                                                                                     op0=ALU.mult, op1=ALU.add),
                         reads=[psk, ("acc", u)], writes=[("acc", u)])
                    P.op(DVE, lambda e, a_=a_, pst=pst, blk=blk: e.scalar_tensor_tensor(out=a_[:, 2:TT], in0=pst[:, 0:TT - 2], scalar=cw[:, 0, blk:blk + 1], in1=a_[:, 2:TT],
                                                                                     op0=ALU.mult, op1=ALU.add),
                         reads=[psk, ("acc", u)], writes=[("acc", u)])
                    if j > 0:
                        P.op(POOL, lambda e, a_=a_, blk=blk: e.tensor_tensor(out=a_[:, 0:2], in0=a_[:, 0:2], in1=corr[:, blk, :], op=ALU.add),
                             reads=[("corr",), ("acc", u)], writes=[("acc", u)])
                    accs[kind] = (a_, u)
                pend_pairs.append((accs[0], accs[1], pr))
                while len(pend_pairs) > (1 if pr < NPAIR - 1 else 0):
                    (au, uu), (ag, ug), pr_ = pend_pairs.pop(0)
                    P.op(ACT, lambda e, ag=ag: e.activation(out=ag[:, :], in_=ag[:, :], func=AF.Silu), reads=[("acc", ug)], writes=[("acc", ug)])
                    P.op(POOL, lambda e, ag=ag, au=au, pr_=pr_: e.tensor_tensor(out=act[:, pr_, :], in0=ag[:, :], in1=au[:, :], op=ALU.mult),
                         reads=[("acc", ug), ("acc", uu)], writes=[("act", pr_)])
            if j + 1 < NT:
                norm_chunked(P, C, xts[1 - j % 2], ("xt", 1 - j % 2), hn, ("hn",), sqs, rstd2, g_f)
            for fb in range(8):
                pst, psk = next_ps(C)
                mm_chain(P, pst[:, :], [(wdn[:, c, fb * 128:(fb + 1) * 128], act[:, c, :]) for c in range(NPAIR)],
                         reads=wkeys(C, "ff_wdn", fb * 128) + [("act", c) for c in range(NPAIR)], writes=[psk])
                P.op(DVE, lambda e, pst=pst, fb=fb, xt=xt: e.tensor_tensor(out=xt[:, fb, :], in0=pst[:, :], in1=xt[:, fb, :], op=ALU.add),
                     reads=[psk, XK], writes=[XK])
            if not final:
                dstv = xres.rearrange("(c p) t -> p c t", p=128)
                P.op(POOL, lambda e, j=j, xt=xt: e.dma_start(out=dstv[:, :, j * TT:(j + 1) * TT], in_=xt[:, :, :]), reads=[XK], dkey=("xst", j % 2))
            else:
                gfin = C.gains["final_norm"]
                P.op(ACT, lambda e, xt=xt: e.activation(out=sq[:, :, :], in_=xt[:, :, :], func=AF.Square), reads=[XK], writes=sqk)
                pst, psk = next_ps(C)
                mm_chain(P, pst[:, :], [(C.ones_bf[:, :], sq[:, c, :]) for c in range(8)], reads=sqk, writes=[psk])
                P.op(ACT, lambda e, pst=pst: e.activation(out=rstd[:, :], in_=pst[:, :], func=AF.Sqrt, scale=1.0 / D, bias=C.eps_col[:, 0:1]),
                     reads=[psk], writes=[("rstd",)])
                P.op(DVE, lambda e: e.reciprocal(out=rstd[:, :], in_=rstd[:, :]), reads=[("rstd",)], writes=[("rstd",)])
                for c in range(8):
                    P.op(DVE, lambda e, c=c, xt=xt, fo=fo: e.scalar_tensor_tensor(out=fo[:, c, :], in0=xt[:, c, :], scalar=gfin[:, c:c + 1], in1=rstd[:, :],
                                                                                          op0=ALU.mult, op1=ALU.mult),
                         reads=[XK, ("rstd",), gain_key(None)], writes=[XK])
                dstv = C.outT.rearrange("(c p) t -> p c t", p=128)
                P.op(POOL, lambda e, j=j, fo=fo: e.dma_start(out=dstv[:, :, j * TT:(j + 1) * TT], in_=fo[:, :, :]), reads=[XK], dkey=("ost", j % 2))
        P.barrier()


RET_GAMMA = [1.0 - 2.0 ** (-5.0 - h) for h in range(4)]


def stage_l0_inproj(P, C):
    nc = P.nc
    xres = C.xres
    with ExitStack() as es:
        sb = lambda name, shape, dt=F32: es.enter_context(nc.sbuf_tensor(name, list(shape), dt))
        W = sb("A_W", [128, 8, 2560], BF16)
        Wsw = sb("A_Wsw", [128, 8, 1024], BF16)
        g_mix = C.gains["l0_mix_norm"]
        wload(P, C, W, C.w["l0_w_in"], 8, 2560, name="A_W")
        wload(P, C, Wsw, C.w["l0_w_in_sw"], 8, 1024, name="A_Wsw")
        gq = sb("A_gq", [128, 4, TT])
        P.op(SP, lambda e: e.dma_start(out=gq[:, :, :], in_=C.cst["gq_tab"]), writes=[("gq",)], dkey=("gq",))
        P.barrier()
        xts = [sb(f"A_xt{i}", [128, 8, TT]) for i in range(2)]
        hn = sb("A_hn", [128, 8, TT], BF16)
        sq = sb("A_sq", [128, 8, TT], BF16)
        rstd = sb("A_rstd", [128, TT])
        rot = sb("A_rot", [128, 4, TT])
        t1 = [sb(f"A_t1{i}", [128, TT]) for i in range(2)]
        t2 = [sb(f"A_t2{i}", [128, TT]) for i in range(2)]
        qo = sb("A_qo", [128, 4, TT], BF16)
        qdo = sb("A_qdo", [128, 4, TT], BF16)
        ko = sb("A_ko", [128, 4, TT], BF16)
        go = sb("A_go", [128, 4, TT], BF16)
        uo = sb("A_uo", [128, 4, TT], BF16)
        vo = sb("A_vo", [128, 4, TT], BF16)
        cnt = 0
        load_xt(P, C, xres, 0, xts[0], ("xt", 0))
        for j in range(NT):
            norm_tile(P, C, xres, j, xts[j % 2], ("xt", j % 2), hn, ("hn",), sq, rstd, gain=g_mix, load=False)
            if j + 1 < NT:
                load_xt(P, C, xres, j + 1, xts[1 - j % 2], ("xt", 1 - j % 2))
            P.op(SP, lambda e, j=j: e.dma_start(out=rot[:, :, :], in_=C.cst["rot_tab"][:, :, j * TT:(j + 1) * TT]), writes=[("rot",)], dkey=("rot",))
            for kind in range(2):
                for h in range(4):
                    col = kind * 512 + h * 128
                    pa, pak = next_ps(C)
                    mm_chain(P, pa[:, :], [(W[:, c, col:col + 128], hn[:, c, :]) for c in range(8)], reads=[*wkeys(C, "A_W", col), ("hn", "a"), ("hn", "b")], writes=[pak])
                    pb, pbk = next_ps(C)
                    mm_chain(P, pb[:, :], [(Wsw[:, c, col:col + 128], hn[:, c, :]) for c in range(8)], reads=[*wkeys(C, "A_Wsw", col), ("hn", "a"), ("hn", "b")], writes=[pbk])
                    u = cnt % 2
                    cnt += 1
                    a_, b_ = t1[u], t2[u]
                    P.op(DVE, lambda e, a_=a_, pa=pa, kind=kind: e.tensor_tensor(out=a_[:, :], in0=pa[:, :], in1=rot[:, 2 * kind, :], op=ALU.mult),
                         reads=[pak, ("rot",)], writes=[("t1", u)])
                    P.op(DVE, lambda e, b_=b_, pb=pb, kind=kind: e.tensor_tensor(out=b_[:, :], in0=pb[:, :], in1=rot[:, 2 * kind + 1, :], op=ALU.mult),
                         reads=[pbk, ("rot",)], writes=[("t2", u)])
                    P.op(POOL, lambda e, a_=a_, b_=b_: e.tensor_tensor(out=a_[:, :], in0=a_[:, :], in1=b_[:, :], op=ALU.add),
                         reads=[("t1", u), ("t2", u)], writes=[("t1", u)])
                    if kind == 0:
                        P.op(ACT, lambda e, a_=a_, h=h: e.copy(out=qo[:, h, :], in_=a_[:, :]), reads=[("t1", u)], writes=[("qo",)])
                        P.op(POOL, lambda e, a_=a_, h=h: e.tensor_tensor(out=qdo[:, h, :], in0=a_[:, :], in1=gq[:, h, :], op=ALU.mult),
                             reads=[("t1", u)], writes=[("qdo",)])
                    else:
                        P.op(ACT, lambda e, a_=a_, h=h: e.copy(out=ko[:, h, :], in_=a_[:, :]), reads=[("t1", u)], writes=[("ko",)])
            for tb in range(4):
                pa, pak = next_ps(C)
                mm_chain(P, pa[:, :], [(hn[:, c, tb * 128:(tb + 1) * 128], W[:, c, 1024:1536]) for c in range(8)], reads=[*wkeys(C, "A_W", 1024, 512), ("hn", "a"), ("hn", "b")], writes=[pak])
                P.op(ACT, lambda e, pa=pa, tb=tb: e.copy(out=vo[:, tb, :], in_=pa[:, :]), reads=[pak], writes=[("vo",)])
            for fb in range(4):
                pa, pak = next_ps(C)
                mm_chain(P, pa[:, :], [(W[:, c, 1536 + fb * 128:1536 + (fb + 1) * 128], hn[:, c, :]) for c in range(8)], reads=[*wkeys(C, "A_W", 1536 + fb * 128), ("hn", "a"), ("hn", "b")], writes=[pak])
                P.op(ACT, lambda e, pa=pa, fb=fb: e.activation(out=go[:, fb, :], in_=pa[:, :], func=AF.Silu), reads=[pak], writes=[("go",)])
            for fb in range(4):
                pa, pak = next_ps(C)
                mm_chain(P, pa[:, :], [(W[:, c, 2048 + fb * 128:2048 + (fb + 1) * 128], hn[:, c, :]) for c in range(8)], reads=[*wkeys(C, "A_W", 2048 + fb * 128), ("hn", "a"), ("hn", "b")], writes=[pak])
                P.op(ACT, lambda e, pa=pa, fb=fb: e.copy(out=uo[:, fb, :], in_=pa[:, :]), reads=[pak], writes=[("uo",)])
            for nm, tl in [("qT", qo), ("qdT", qdo), ("kT", ko), ("gT", go), ("uT", uo)]:
                dv = C.scr[nm].rearrange("(h p) t -> p h t", p=128)
                P.op(POOL, lambda e, dv=dv, tl=tl, j=j: e.dma_start(out=dv[:, :, j * TT:(j + 1) * TT], in_=tl[:, :, :]),
                     reads=[({"qT": "qo", "qdT": "qdo", "kT": "ko", "gT": "go", "uT": "uo"}[nm],)], dkey=("Ast", nm))
            dv = C.scr["vtok"].rearrange("(n p) f -> p n f", p=128)
            P.op(POOL, lambda e, dv=dv, j=j: e.dma_start(out=dv[:, j * 4:(j + 1) * 4, :], in_=vo[:, :, :]), reads=[("vo",)], dkey=("Ast", "v"))
        P.barrier()


def stage_retention(P, C):
    nc = P.nc
    with ExitStack() as es:
        sb = lambda name, shape, dt=F32: es.enter_context(nc.sbuf_tensor(name, list(shape), dt))
        kT = [sb(f"R_kT{i}", [128, S], BF16) for i in range(2)]
        qT = [sb(f"R_qT{i}", [128, S], BF16) for i in range(2)]
        qdT = [sb(f"R_qdT{i}", [128, S], BF16) for i in range(2)]
        gT = [sb(f"R_gT{i}", [128, S], BF16) for i in range(2)]
        vt = [sb(f"R_vt{i}", [128, 32, 128], BF16) for i in range(2)]
        dtab = sb("R_dtab", [128, 4, 128])
        kd = sb("R_kd", [128, 4])
        rn = sb("R_rn", [128, 4])
        state = sb("R_state", [128, 128])
        state_bfs = [sb(f"R_state_bf{i}", [128, 128], BF16) for i in range(2)]
        scm = [sb(f"R_scm{i}", [128, 128], BF16) for i in range(2)]
        kdec = [sb(f"R_kdec{i}", [128, 128], BF16) for i in range(2)]
        o_sb = [sb(f"R_osb{i}", [128, TT]) for i in range(2)]
        osq = [sb(f"R_osq{i}", [128, TT], BF16) for i in range(2)]
        rr = [sb(f"R_rr{i}", [128, TT]) for i in range(2)]
        mo = [sb(f"R_mo{i}", [128, TT], BF16) for i in range(2)]
        P.op(SP, lambda e: e.dma_start(out=dtab[:, :, :], in_=C.cst["dt_tab"]), writes=[("dtab",)], dkey=("dtab",))
        P.op(SP, lambda e: e.dma_start(out=kd[:, :], in_=C.cst["kd_tab"]), writes=[("kd",)], dkey=("kd",))
        P.op(SP, lambda e: e.dma_start(out=rn[:, :], in_=C.w["l0_ret_norm"]), writes=[("rn",)], dkey=("rn",))
        vview = C.scr["vtok"].rearrange("(n p) f -> p n f", p=128)
        grp = 0
        for h in range(4):
            hb = h % 2
            for nm, tl in [("kT", kT), ("qT", qT), ("qdT", qdT), ("gT", gT)]:
                P.op(SP, lambda e, nm=nm, tl=tl, h=h, hb=hb: e.dma_start(out=tl[hb][:, :], in_=C.scr[nm][h * 128:(h + 1) * 128, :]),
                     writes=[(nm, hb)], dkey=("Rld", nm, hb))
            P.op(SP, lambda e, h=h, hb=hb: e.dma_start(out=vt[hb][:, :, :], in_=vview[:, :, h * 128:(h + 1) * 128]), writes=[("vt", hb)], dkey=("Rld", "v", hb))
            P.op(POOL, lambda e: e.memset(state[:, :], 0.0), writes=[("state",)])
            P.op(POOL, lambda e: e.memset(state_bfs[0][:, :], 0.0), writes=[("state_bf", 0)])
            gam_c = RET_GAMMA[h] ** 128
            for n in range(32):
                cs = slice(n * 128, (n + 1) * 128)
                u = n % 2
                if n % 4 == 0:
                    po, pok = next_ps(C, 0, 2)
                    grp += 1
                psc, psck = next_ps(C, 2, 6)
                mm_chain(P, psc[:, 0:128], [(kT[hb][:, cs], qT[hb][:, cs])], reads=[("kT", hb), ("qT", hb)], writes=[psck])
                P.op(DVE, lambda e, psc=psc, u=u, h=h: e.tensor_tensor(out=scm[u][:, :], in0=psc[:, 0:128], in1=dtab[:, h, :], op=ALU.mult),
                     reads=[psck, ("dtab",)], writes=[("scm", u)])
                pbt, pbk_ = next_psb(C)
                P.op(PE, lambda e, cs=cs, hb=hb, pbt=pbt: e.transpose(pbt[:, 0:128], kT[hb][:, cs], C.ident_bf[:, :]), reads=[("kT", hb)], writes=[pbk_])
                P.op(ACT, lambda e, u=u, h=h, pbt=pbt: e.activation(out=kdec[u][:, :], in_=pbt[:, 0:128], func=AF.Copy, scale=kd[:, h:h + 1]),
                     reads=[pbk_, ("kd",)], writes=[("kdec", u)])
                pkv, pkvk = next_ps(C, 2, 6)
                mm_chain(P, pkv[:, 0:128], [(kdec[u][:, :], vt[hb][:, n, :])], reads=[("kdec", u), ("vt", hb)], writes=[pkvk])
                P.op(DVE, lambda e, pkv=pkv, gam_c=gam_c: e.scalar_tensor_tensor(out=state[:, :], in0=state[:, :], scalar=gam_c, in1=pkv[:, 0:128],
                                                                              op0=ALU.mult, op1=ALU.add),
                     reads=[pkvk, ("state",)], writes=[("state",)])
                sb_n = state_bfs[(n + 1) % 2]
                P.op(ACT, lambda e, sb_n=sb_n: e.copy(out=sb_n[:, :], in_=state[:, :]), reads=[("state",)], writes=[("state_bf", (n + 1) % 2)])
                oc = slice((n % 4) * 128, (n % 4 + 1) * 128)
                sb_c = state_bfs[n % 2]
                mm_chain(P, po[:, oc], [(vt[hb][:, n, :], scm[u][:, :]), (sb_c[:, :], qdT[hb][:, cs])],
                         reads=[("vt", hb), ("scm", u), ("state_bf", n % 2), ("qdT", hb)], writes=[pok])
                if n % 4 == 3:
                    v = grp % 2
                    ts_ = slice((n // 4) * TT, (n // 4 + 1) * TT)
                    P.op(ACT, lambda e, po=po, v=v: e.copy(out=o_sb[v][:, :], in_=po[:, :]), reads=[pok], writes=[("osb", v)])
                    P.op(ACT, lambda e, po=po, v=v: e.activation(out=osq[v][:, :], in_=po[:, :], func=AF.Square), reads=[pok], writes=[("osq", v)])
                    pss, pssk = next_ps(C, 2, 6)
                    mm_chain(P, pss[:, :], [(C.ones_bf[:, :], osq[v][:, :])], reads=[("osq", v)], writes=[pssk])
                    P.op(ACT, lambda e, pss=pss, v=v: e.activation(out=rr[v][:, :], in_=pss[:, :], func=AF.Sqrt, scale=1.0 / 128, bias=C.eps_col[:, 0:1]),
                         reads=[pssk], writes=[("rr", v)])
                    P.op(DVE, lambda e, v=v: e.reciprocal(out=rr[v][:, :], in_=rr[v][:, :]), reads=[("rr", v)], writes=[("rr", v)])
                    P.op(DVE, lambda e, v=v: e.tensor_tensor(out=o_sb[v][:, :], in0=o_sb[v][:, :], in1=rr[v][:, :], op=ALU.mult),
                         reads=[("osb", v), ("rr", v)], writes=[("osb", v)])
                    P.op(DVE, lambda e, v=v, h=h, hb=hb, ts_=ts_: e.scalar_tensor_tensor(out=mo[v][:, :], in0=o_sb[v][:, :], scalar=rn[:, h:h + 1], in1=gT[hb][:, ts_],
                                                                                      op0=ALU.mult, op1=ALU.mult),
                         reads=[("osb", v), ("rn",), ("gT", hb)], writes=[("mo", v)])
                    P.op(POOL, lambda e, v=v, h=h, ts_=ts_: e.dma_start(out=C.scr["mT"][h * 128:(h + 1) * 128, ts_], in_=mo[v][:, :]),
                         reads=[("mo", v)], dkey=("Rst", v))
        P.barrier()


def stage_s5(P, C):
    nc = P.nc
    TWO_PI = 2.0 * math.pi
    with ExitStack() as es:
        sb = lambda name, shape, dt=F32: es.enter_context(nc.sbuf_tensor(name, list(shape), dt))

        def tt(eng, out, a, b, op, rk, wk):
            P.op(eng, lambda e: e.tensor_tensor(out=out, in0=a, in1=b, op=op), reads=rk, writes=wk)

        def ts(eng, out, a, s1, op0, rk=(), wk=()):
            P.op(eng, lambda e: e.tensor_scalar(out=out, in0=a, scalar1=s1, scalar2=None, op0=op0), reads=rk, writes=wk)

        def act(out, a, func, rk, wk, scale=1.0, bias=None):
            if bias is None:
                P.op(ACT, lambda e: e.activation(out=out, in_=a, func=func, scale=scale), reads=rk, writes=wk)
            else:
                P.op(ACT, lambda e: e.activation(out=out, in_=a, func=func, scale=scale, bias=bias), reads=rk, writes=wk)

        def frac_sincos(eng, x, xi, xf, sin_out, cos_out, key, hp, np_, outkey=None):
            P.op(eng, lambda e: e.tensor_copy(out=xi, in_=x), reads=[(key, "x")], writes=[(key, "xi")])
            P.op(eng, lambda e: e.tensor_copy(out=xf, in_=xi), reads=[(key, "xi")], writes=[(key, "xf")])
            P.op(eng, lambda e: e.tensor_tensor(out=x, in0=x, in1=xf, op=ALU.subtract), reads=[(key, "x"), (key, "xf")], writes=[(key, "x")])
            P.op(DVE, lambda e: e.scalar_tensor_tensor(out=xf, in0=x, scalar=-1.0, in1=x, op0=ALU.mult, op1=ALU.max),
                 reads=[(key, "x"), (key, "xf")], writes=[(key, "xf")])
            ok = key if outkey is None else outkey
            P.op(ACT, lambda e: e.activation(out=sin_out, in_=x, func=AF.Sin, scale=TWO_PI), reads=[(key, "x")], writes=[(ok, "sin")])
            P.op(ACT, lambda e: e.activation(out=cos_out, in_=xf, func=AF.Sin, scale=-TWO_PI, bias=hp[0:np_, 0:1]), reads=[(key, "xf")], writes=[(ok, "cos")])

        halfpi = sb("S_halfpi", [128, 1])
        r_s = sb("S_r", [128, 32])
        f_s = sb("S_f", [128, 32])
        cs_tab = sb("S_cstab", [128, 32, 8])
        sn_tab = sb("S_sntab", [128, 32, 8])
        fbd = sb("S_fbd", [16, 32, 128])
        B1f = sb("S_B1f", [16, 32, 128])
        B2f = sb("S_B2f", [16, 32, 128])
        C1f = sb("S_C1f", [128, 32, 16])
        C2f = sb("S_C2f", [128, 32, 16])
        Dd = sb("S_Dd", [16, 32, 16], BF16)
        tloc = sb("S_tloc", [128, TT])
        t0s = sb("S_t0s", [128, 32, 8])
        t0b = sb("S_t0b", [16, 8, 128])
        ones5 = sb("S_ones", [128, TT])
        es_in = ExitStack()
        tb = lambda name, shape, dt=F32: es_in.enter_context(nc.sbuf_tensor(name, list(shape), dt))
        P.op(POOL, lambda e: e.memset(halfpi[:, :], 0.5 * math.pi), writes=[("halfpi",)])
        P.op(POOL, lambda e: e.memset(ones5[:, :], 1.0), writes=[("ones5",)])
        P.op(SP, lambda e: e.dma_start(out=tloc[:, :], in_=C.cst["tpos"][:, 0:TT]), writes=[("tloc",)], dkey=("s5c", 9))
        P.op(SP, lambda e: e.dma_start(out=t0s[:, :, :], in_=C.cst["t0s"]), writes=[("t0s",)], dkey=("s5c", 10))
        P.op(SP, lambda e: e.dma_start(out=t0b[:, :, :], in_=C.cst["t0b"]), writes=[("t0b",)], dkey=("s5c", 11))
        lam_s = tb("S_lam_s", [128, 2, 32])
        ldt_s = tb("S_ldt_s", [128, 32])
        P.op(SP, lambda e: e.dma_start(out=lam_s[:, :, :], in_=C.w["s5_lam_s"]), writes=[("lam_s",)], dkey=("s5c", 0))
        P.op(SP, lambda e: e.dma_start(out=ldt_s[:, :], in_=C.w["s5_ldt_s"]), writes=[("ldt_s",)], dkey=("s5c", 1))
        act(ldt_s[:, :], ldt_s[:, :], AF.Exp, [("ldt_s",)], [("ldt_s",)])
        tt(DVE, r_s[:, :], lam_s[:, 0, :], ldt_s[:, :], ALU.mult, [("lam_s",), ("ldt_s",)], [("r_s",)])
        act(r_s[:, :], r_s[:, :], AF.Exp, [("r_s",)], [("r_s",)])
        tt(DVE, f_s[:, :], lam_s[:, 1, :], ldt_s[:, :], ALU.mult, [("lam_s",), ("ldt_s",)], [("f_s",)])
        ts(DVE, f_s[:, :], f_s[:, :], 1.0 / TWO_PI, ALU.mult, rk=[("f_s",)], wk=[("f_s",)])
        fi_s = tb("S_fi_s", [128, 32], I32)
        ff_s = tb("S_ff_s", [128, 32])
        P.op(DVE, lambda e: e.tensor_copy(out=fi_s[:, :], in_=f_s[:, :]), reads=[("f_s",)], writes=[("fi_s",)])
        P.op(DVE, lambda e: e.tensor_copy(out=ff_s[:, :], in_=fi_s[:, :]), reads=[("fi_s",)], writes=[("ff_s",)])
        tt(DVE, f_s[:, :], f_s[:, :], ff_s[:, :], ALU.subtract, [("f_s",), ("ff_s",)], [("f_s",)])
        xo = tb("S_xo", [128, 32, 8])
        xoi = tb("S_xoi", [128, 32, 8], I32)
        xof = tb("S_xof", [128, 32, 8])
        P.op(DVE, lambda e: e.tensor_tensor(out=xo[:, :, :], in0=t0s[:, :, :], in1=f_s[:, :].unsqueeze(2).to_broadcast([128, 32, 8]), op=ALU.mult),
             reads=[("f_s",), ("t0s",)], writes=[("xo", "x")])
        frac_sincos(DVE, xo[:, :, :], xoi[:, :, :], xof[:, :, :], sn_tab[:, :, :], cs_tab[:, :, :], "xo", halfpi, 128)
        NB = 32 * 64
        lamb = tb("S_lamb", [16, 2, NB])
        ldtb = tb("S_ldtb", [16, NB])
        bb = tb("S_bb", [16, 2, NB])
        P.op(SP, lambda e: e.dma_start(out=lamb[:, :, :], in_=C.w["s5_lam_b"]), writes=[("lamb",)], dkey=("s5c", 2))
        P.op(SP, lambda e: e.dma_start(out=ldtb[:, :], in_=C.w["s5_ldt_b"]), writes=[("ldtb",)], dkey=("s5c", 3))
        P.op(SP, lambda e: e.dma_start(out=bb[:, :, :], in_=C.w["s5_b_b"]), writes=[("bb",)], dkey=("s5c", 4))
        lr = lamb[:, 0, :]
        li = lamb[:, 1, :]
        mag = tb("S_mag", [16, NB])
        fb_ = tb("S_fb", [16, NB])
        fc_ = tb("S_fc", [16, NB])
        fbi = tb("S_fbi", [16, NB], I32)
        are = tb("S_are", [16, NB])
        aim = tb("S_aim", [16, NB])
        den = tb("S_den", [16, NB])
        tmp = tb("S_tmp", [16, NB])
        zre = tb("S_zre", [16, NB])
        zim = tb("S_zim", [16, NB])
        act(ldtb[:, :], ldtb[:, :], AF.Exp, [("ldtb",)], [("ldtb",)])
        tt(DVE, mag[:, :], lr, ldtb[:, :], ALU.mult, [("lamb",), ("ldtb",)], [("mag",)])
        act(mag[:, :], mag[:, :], AF.Exp, [("mag",)], [("mag",)])
        tt(DVE, fb_[:, :], li, ldtb[:, :], ALU.mult, [("lamb",), ("ldtb",)], [("fbq", "x")])
        ts(DVE, fb_[:, :], fb_[:, :], 1.0 / TWO_PI, ALU.mult, rk=[("fbq", "x")], wk=[("fbq", "x")])
        frac_sincos(DVE, fb_[:, :], fbi[:, :], fc_[:, :], aim[:, :], are[:, :], "fbq", halfpi, 16)
        fb3 = fb_[:, :].rearrange("h (g p) -> h g p", g=32)
        P.op(POOL, lambda e: e.tensor_copy(out=fbd[:, :, 0:64], in_=fb3), reads=[("fbq", "x")], writes=[("fbd", 0)])
        P.op(POOL, lambda e: e.tensor_copy(out=fbd[:, :, 64:128], in_=fb3), reads=[("fbq", "x")], writes=[("fbd", 1)])
        tt(DVE, aim[:, :], aim[:, :], mag[:, :], ALU.mult, [("fbq", "sin"), ("mag",)], [("aim",)])
        tt(DVE, are[:, :], are[:, :], mag[:, :], ALU.mult, [("fbq", "cos"), ("mag",)], [("are",)])
        ts(DVE, are[:, :], are[:, :], -1.0, ALU.add, rk=[("are",)], wk=[("are",)])
        tt(DVE, den[:, :], lr, lr, ALU.mult, [("lamb",)], [("den",)])
        tt(DVE, tmp[:, :], li, li, ALU.mult, [("lamb",)], [("tmp",)])
        tt(DVE, den[:, :], den[:, :], tmp[:, :], ALU.add, [("den",), ("tmp",)], [("den",)])
        P.op(DVE, lambda e: e.reciprocal(out=den[:, :], in_=den[:, :]), reads=[("den",)], writes=[("den",)])
        tt(DVE, zre[:, :], are[:, :], lr, ALU.mult, [("are",), ("lamb",)], [("zre",)])
        tt(DVE, tmp[:, :], aim[:, :], li, ALU.mult, [("aim",), ("lamb",)], [("tmp",)])
        tt(DVE, zre[:, :], zre[:, :], tmp[:, :], ALU.add, [("zre",), ("tmp",)], [("zre",)])
        tt(DVE, zre[:, :], zre[:, :], den[:, :], ALU.mult, [("zre",), ("den",)], [("zre",)])
        tt(DVE, zim[:, :], aim[:, :], lr, ALU.mult, [("aim",), ("lamb",)], [("zim",)])
        tt(DVE, tmp[:, :], are[:, :], li, ALU.mult, [("are",), ("lamb",)], [("tmp",)])
        tt(DVE, zim[:, :], zim[:, :], tmp[:, :], ALU.subtract, [("zim",), ("tmp",)], [("zim",)])
        tt(DVE, zim[:, :], zim[:, :], den[:, :], ALU.mult, [("zim",), ("den",)], [("zim",)])
        bbre = mag
        bbim = den
        br = bb[:, 0, :]
        bi = bb[:, 1, :]
        tt(DVE, bbre[:, :], zre[:, :], br, ALU.mult, [("zre",), ("bb",), ("mag",)], [("mag",)])
        tt(DVE, tmp[:, :], zim[:, :], bi, ALU.mult, [("zim",), ("bb",)], [("tmp",)])
        tt(DVE, bbre[:, :], bbre[:, :], tmp[:, :], ALU.subtract, [("mag",), ("tmp",)], [("mag",)])
        tt(DVE, bbim[:, :], zre[:, :], bi, ALU.mult, [("zre",), ("bb",), ("den",)], [("den",)])
        tt(DVE, tmp[:, :], zim[:, :], br, ALU.mult, [("zim",), ("bb",)], [("tmp",)])
        tt(DVE, bbim[:, :], bbim[:, :], tmp[:, :], ALU.add, [("den",), ("tmp",)], [("den",)])
        bbre3 = bbre[:, :].rearrange("h (g p) -> h g p", g=32)
        bbim3 = bbim[:, :].rearrange("h (g p) -> h g p", g=32)
        P.op(ACT, lambda e: e.copy(out=B1f[:, :, 0:64], in_=bbre3), reads=[("mag",)], writes=[("B1", 0)])
        P.op(ACT, lambda e: e.copy(out=B1f[:, :, 64:128], in_=bbim3), reads=[("den",)], writes=[("B1", 1)])
        P.op(ACT, lambda e: e.copy(out=B2f[:, :, 0:64], in_=bbim3), reads=[("den",)], writes=[("B2", 0)])
        P.op(ACT, lambda e: e.activation(out=B2f[:, :, 64:128], in_=bbre3, func=AF.Copy, scale=-1.0), reads=[("mag",)], writes=[("B2", 1)])
        c1f = tb("S_c1f", [128, 32, 16])
        c2f = tb("S_c2f", [128, 32, 16])
        P.op(SP, lambda e: e.dma_start(out=c1f[:, :, :], in_=C.w["s5_c1"]), writes=[("c1f",)], dkey=("s5c", 5))
        P.op(SP, lambda e: e.dma_start(out=c2f[:, :, :], in_=C.w["s5_c2"]), writes=[("c2f",)], dkey=("s5c", 6))
        P.op(ACT, lambda e: e.copy(out=C1f[0:64, :, :], in_=c1f[0:64, :, :]), reads=[("c1f",)], writes=[("C1", 0)])
        P.op(ACT, lambda e: e.activation(out=C1f[64:128, :, :], in_=c1f[64:128, :, :], func=AF.Copy, scale=-1.0), reads=[("c1f",)], writes=[("C1", 1)])
        P.op(ACT, lambda e: e.activation(out=C2f[:, :, :], in_=c2f[:, :, :], func=AF.Copy, scale=-1.0), reads=[("c2f",)], writes=[("C2",)])
        dt_ = tb("S_dt", [16, 32])
        id16 = tb("S_id16", [16, 16])
        P.op(SP, lambda e: e.dma_start(out=dt_[:, :], in_=C.w["s5_d_t"]), writes=[("dt_",)], dkey=("s5c", 7))
        P.op(SP, lambda e: e.dma_start(out=id16[:, :], in_=C.cst["id16"]), writes=[("id16",)], dkey=("s5c", 8))
        P.op(DVE, lambda e: e.tensor_tensor(out=Dd[:, :, :], in0=id16[:, :].unsqueeze(1).to_broadcast([16, 32, 16]),
                                            in1=dt_[:, :].unsqueeze(2).to_broadcast([16, 32, 16]), op=ALU.mult),
             reads=[("dt_",), ("id16",)], writes=[("Dd",)])
        P.barrier()
        es_in.close()
        ug = [sb(f"S_ug{i}", [16, S], BF16) for i in range(2)]
        yg = [sb(f"S_yg{i}", [16, S], BF16) for i in range(2)]
        rfull = [sb(f"S_rfull{i}", [128, TT]) for i in range(2)]
        _lx = sb("S_lx", [128, TT])
        lx = [_lx, _lx]
        _lxi = sb("S_lxi", [128, TT], I32)
        lxi = [_lxi, _lxi]
        _lxf = sb("S_lxf", [128, TT])
        lxf = [_lxf, _lxf]
        sinL = [sb(f"S_sinL{i}", [128, TT]) for i in range(2)]
        cosL = [sb(f"S_cosL{i}", [128, TT]) for i in range(2)]
        _bx = sb("S_bx", [16, 8, 128])
        bx = [_bx, _bx]
        _bxi = sb("S_bxi", [16, 8, 128], I32)
        bxi = [_bxi, _bxi]
        _bxf = sb("S_bxf", [16, 8, 128])
        bxf = [_bxf, _bxf]
        _bcc = sb("S_bcc", [16, 8, 128])
        bcc = [_bcc, _bcc]
        _bss = sb("S_bss", [16, 8, 128])
        bss = [_bss, _bss]
        _bt1 = sb("S_bt1", [16, 8, 128])
        bt1 = [_bt1, _bt1]
        _bt2 = sb("S_bt2", [16, 8, 128])
        bt2 = [_bt2, _bt2]
        B1p = [sb(f"S_B1p{i}", [16, 8, 128], BF16) for i in range(2)]
        B2p = [sb(f"S_B2p{i}", [16, 8, 128], BF16) for i in range(2)]
        _ct1 = sb("S_ct1", [128, 8, 16])
        ct1 = [_ct1, _ct1]
        _ct2 = sb("S_ct2", [128, 8, 16])
        ct2 = [_ct2, _ct2]
        C1p = [sb(f"S_C1p{i}", [128, 8, 16], BF16) for i in range(2)]
        C2p = [sb(f"S_C2p{i}", [128, 8, 16], BF16) for i in range(2)]
        NX = 4
        NW = 3
        X1 = [sb(f"S_X1{i}", [128, TT]) for i in range(NX)]
        X2 = [sb(f"S_X2{i}", [128, TT]) for i in range(NX)]
        wb = [sb(f"S_w{i}", [128, TT]) for i in range(2)]
        cW = [sb(f"S_cW{i}", [128, TT], BF16) for i in range(NW)]
        sW = [sb(f"S_sW{i}", [128, TT], BF16) for i in range(NW)]

        def prep_group(g):
            gb = g % 2
            P.op(SP, lambda e: e.dma_start(out=ug[gb][:, :], in_=C.scr["uT"][g * 16:(g + 1) * 16, :]), writes=[("ug", gb)], dkey=("S5ld", gb))
            P.op(ACT, lambda e: e.activation(out=rfull[gb][:, :], in_=ones5[:, :], func=AF.Copy, scale=r_s[:, g:g + 1]), reads=[], writes=[("rfull", gb)])
            P.op(ACT, lambda e: e.activation(out=lx[gb][:, :], in_=tloc[:, :], func=AF.Copy, scale=f_s[:, g:g + 1]), reads=[], writes=[(("lx", 0), "x")])
            frac_sincos(POOL, lx[gb][:, :], lxi[gb][:, :], lxf[gb][:, :], sinL[gb][:, :], cosL[gb][:, :], ("lx", 0), halfpi, 128, outkey=("lxo", gb))
            P.op(POOL, lambda e: e.tensor_tensor(out=bx[gb][:, :, :], in0=t0b[:, :, :], in1=fbd[:, g, :].unsqueeze(1).to_broadcast([16, 8, 128]), op=ALU.mult),
                 reads=[], writes=[(("bx", 0), "x")])
            frac_sincos(POOL, bx[gb][:, :, :], bxi[gb][:, :, :], bxf[gb][:, :, :], bss[gb][:, :, :], bcc[gb][:, :, :], ("bx", 0), halfpi, 16)
            b1 = B1f[:, g, :].unsqueeze(1).to_broadcast([16, 8, 128])
            b2 = B2f[:, g, :].unsqueeze(1).to_broadcast([16, 8, 128])
            kc, ks = (("bx", 0), "cos"), (("bx", 0), "sin")
            P.op(DVE, lambda e: e.tensor_tensor(out=bt1[gb][:, :, :], in0=bcc[gb][:, :, :], in1=b1, op=ALU.mult), reads=[kc], writes=[("bt1", 0)])
            P.op(DVE, lambda e: e.tensor_tensor(out=bt2[gb][:, :, :], in0=bss[gb][:, :, :], in1=b2, op=ALU.mult), reads=[ks], writes=[("bt2", 0)])
            P.op(POOL, lambda e: e.tensor_tensor(out=B1p[gb][:, :, :], in0=bt1[gb][:, :, :], in1=bt2[gb][:, :, :], op=ALU.add), reads=[("bt1", 0), ("bt2", 0)], writes=[("B1p", gb)])
            P.op(DVE, lambda e: e.tensor_tensor(out=bt1[gb][:, :, :], in0=bcc[gb][:, :, :], in1=b2, op=ALU.mult), reads=[kc, ("bt1", 0)], writes=[("bt1", 0)])
            P.op(DVE, lambda e: e.tensor_tensor(out=bt2[gb][:, :, :], in0=bss[gb][:, :, :], in1=b1, op=ALU.mult), reads=[ks, ("bt2", 0)], writes=[("bt2", 0)])
            P.op(POOL, lambda e: e.tensor_tensor(out=B2p[gb][:, :, :], in0=bt1[gb][:, :, :], in1=bt2[gb][:, :, :], op=ALU.subtract), reads=[("bt1", 0), ("bt2", 0)], writes=[("B2p", gb)])
            c1 = C1f[:, g, :].unsqueeze(1).to_broadcast([128, 8, 16])
            c2 = C2f[:, g, :].unsqueeze(1).to_broadcast([128, 8, 16])
            cc_ = cs_tab[:, g, :].unsqueeze(2).to_broadcast([128, 8, 16])
            ss_ = sn_tab[:, g, :].unsqueeze(2).to_broadcast([128, 8, 16])
            P.op(DVE, lambda e: e.tensor_tensor(out=ct1[gb][:, :, :], in0=c1, in1=cc_, op=ALU.mult), reads=[], writes=[("ct1", 0)])
            P.op(DVE, lambda e: e.tensor_tensor(out=ct2[gb][:, :, :], in0=c2, in1=ss_, op=ALU.mult), reads=[], writes=[("ct2", 0)])
            P.op(POOL, lambda e: e.tensor_tensor(out=C1p[gb][:, :, :], in0=ct1[gb][:, :, :], in1=ct2[gb][:, :, :], op=ALU.add), reads=[("ct1", 0), ("ct2", 0)], writes=[("C1p", gb)])
            P.op(DVE, lambda e: e.tensor_tensor(out=ct1[gb][:, :, :], in0=c2, in1=cc_, op=ALU.mult), reads=[("ct1", 0)], writes=[("ct1", 0)])
            P.op(DVE, lambda e: e.tensor_tensor(out=ct2[gb][:, :, :], in0=c1, in1=ss_, op=ALU.mult), reads=[("ct2", 0)], writes=[("ct2", 0)])
            P.op(POOL, lambda e: e.tensor_tensor(out=C2p[gb][:, :, :], in0=ct1[gb][:, :, :], in1=ct2[gb][:, :, :], op=ALU.subtract), reads=[("ct1", 0), ("ct2", 0)], writes=[("C2p", gb)])

        steps = [(g, j) for g in range(32) for j in range(NT)]
        NS = len(steps)

        def stA1(i):
            g, j = steps[i]
            gb = g % 2
            tsl = slice(j * TT, (j + 1) * TT)
            pa, pak = next_ps(C, 0, 5)
            mm_chain(P, pa[:, :], [(B1p[gb][:, j, :], ug[gb][:, tsl])], reads=[("ug", gb), ("B1p", gb)], writes=[pak])
            pb, pbk = next_ps(C, 0, 5)
            mm_chain(P, pb[:, :], [(B2p[gb][:, j, :], ug[gb][:, tsl])], reads=[("ug", gb), ("B2p", gb)], writes=[pbk])
            psA[i] = (pa, pak, pb, pbk)

        def stA2(i):
            g, j = steps[i]
            gb = g % 2
            x = i % NX
            pa, pak, pb, pbk = psA.pop(i)
            P.op(DVE, lambda e: e.tensor_tensor(out=X1[x][:, :], in0=pa[:, :], in1=cosL[gb][:, :], op=ALU.mult), reads=[pak, (("lxo", gb), "cos")], writes=[("X1", x)])
            P.op(DVE, lambda e: e.tensor_tensor(out=X2[x][:, :], in0=pb[:, :], in1=sinL[gb][:, :], op=ALU.mult), reads=[pbk, (("lxo", gb), "sin")], writes=[("X2", x)])

        def stB(i):
            x = i % NX
            P.op(POOL, lambda e: e.tensor_tensor(out=X1[x][:, :], in0=X1[x][:, :], in1=X2[x][:, :], op=ALU.add), reads=[("X1", x), ("X2", x)], writes=[("X1", x)])

        def stC1(i):
            g, j = steps[i]
            gb = g % 2
            x = i % NX
            u = i % 2
            v = i % NW
            init = 0.0 if j == 0 else wb[1 - u][:, TT - 1:TT]
            P.op(DVE, lambda e: e.tensor_tensor_scan(out=wb[u][:, :], data0=rfull[gb][:, :], data1=X1[x][:, :], initial=init, op0=ALU.mult, op1=ALU.add),
                 reads=[("X1", x), ("rfull", gb), ("w", 1 - u)], writes=[("w", u)])
            P.op(DVE, lambda e: e.tensor_tensor(out=cW[v][:, :], in0=wb[u][:, :], in1=cosL[gb][:, :], op=ALU.mult), reads=[("w", u), (("lxo", gb), "cos")], writes=[("cW", v)])
            P.op(POOL, lambda e: e.tensor_tensor(out=sW[v][:, :], in0=wb[u][:, :], in1=sinL[gb][:, :], op=ALU.mult), reads=[("w", u), (("lxo", gb), "sin")], writes=[("sW", v)])

        def stC2(i):
            g, j = steps[i]
            gb = g % 2
            v = i % NW
            tsl = slice(j * TT, (j + 1) * TT)
            py, pyk = next_ps(C, 5, 6)
            mm_chain(P, py[0:16, :], [(C1p[gb][:, j, :], cW[v][:, :]), (C2p[gb][:, j, :], sW[v][:, :]), (Dd[:, g, :], ug[gb][:, tsl])],
                     reads=[("cW", v), ("sW", v), ("ug", gb), ("C1p", gb), ("C2p", gb)], writes=[pyk])
            P.op(ACT, lambda e: e.activation(out=yg[gb][:, tsl], in_=py[0:16, :], func=AF.Gelu), reads=[pyk], writes=[("yg", gb)])
            if j == NT - 1:
                P.op(POOL, lambda e: e.dma_start(out=C.scr["ygT"][g * 16:(g + 1) * 16, :], in_=yg[gb][:, :]), reads=[("yg", gb)], dkey=("S5st", gb))

        psA = {}
        prep_group(0)
        LC2 = 5
        for i in range(NS + LC2):
            if i < NS and steps[i][1] == LC2 and steps[i][0] + 1 < 32:
                prep_group(steps[i][0] + 1)
            if i < NS:
                stA1(i)
            if 0 <= i - 1 < NS:
                stA2(i - 1)
            if 0 <= i - 2 < NS:
                stB(i - 2)
            if 0 <= i - 3 < NS:
                stC1(i - 3)
            if 0 <= i - LC2 < NS:
                stC2(i - LC2)
        P.barrier()


def stage_l0_out(P, C):
    nc = P.nc
    xres = C.xres
    with ExitStack() as es:
        sb = lambda name, shape, dt=F32: es.enter_context(nc.sbuf_tensor(name, list(shape), dt))
        wglu = sb("O_wglu", [128, 4, 512], BF16)
        wout = sb("O_wout", [128, 8, 1024], BF16)
        bglu = sb("O_bglu", [128, 4])
        wload(P, C, wglu, C.w["l0_s5_w_glu"], 4, 512, gain=None, name="O_wglu")
        wload(P, C, wout, C.w["l0_w_out"], 8, 1024, gain=None, name="O_wout")
        P.op(SP, lambda e: e.dma_start(out=bglu[:, :], in_=C.w["l0_s5_b_glu"]), writes=[("bglu",)], dkey=("bglu",))
        P.barrier()
        xt = [sb(f"O_xt{i}", [128, 8, TT]) for i in range(2)]
        mg = [sb(f"O_mg{i}", [128, 8, TT], BF16) for i in range(2)]
        ygt = [sb(f"O_yg{i}", [128, 4, TT], BF16) for i in range(2)]
        sg = [sb(f"O_sg{i}", [128, TT]) for i in range(2)]
        xv = xres.rearrange("(c p) t -> p c t", p=128)
        mv = C.scr["mT"].rearrange("(c p) t -> p c t", p=128)
        yv = C.scr["ygT"].rearrange("(c p) t -> p c t", p=128)
        it = 0
        for j in range(NT):
            u = j % 2
            tsl = slice(j * TT, (j + 1) * TT)
            P.op(SP, lambda e, u=u, tsl=tsl: e.dma_start(out=xt[u][:, :, :], in_=xv[:, :, tsl]), writes=[("xt", u)], dkey=("Old", "x", u))
            P.op(SP, lambda e, u=u, tsl=tsl: e.dma_start(out=mg[u][:, 0:4, :], in_=mv[:, 0:4, tsl]), writes=[("mg", u, "r")], dkey=("Old", "m", u))
            P.op(SP, lambda e, u=u, tsl=tsl: e.dma_start(out=ygt[u][:, :, :], in_=yv[:, :, tsl]), writes=[("ygt", u)], dkey=("Old", "y", u))
            for fb in range(4):
                pa, pak = next_ps(C)
                mm_chain(P, pa[:, :], [(wglu[:, c, fb * 128:(fb + 1) * 128], ygt[u][:, c, :]) for c in range(4)], reads=[*wkeys(C, "O_wglu", fb * 128), ("ygt", u)], writes=[pak])
                v = it % 2
                it += 1
                P.op(ACT, lambda e, pa=pa, v=v, fb=fb: e.activation(out=sg[v][:, :], in_=pa[:, :], func=AF.Sigmoid, bias=bglu[:, fb:fb + 1]),
                     reads=[pak, ("bglu",)], writes=[("sg", v)])
                P.op(DVE, lambda e, v=v, u=u, fb=fb: e.tensor_tensor(out=mg[u][:, 4 + fb, :], in0=sg[v][:, :], in1=ygt[u][:, fb, :], op=ALU.mult),
                     reads=[("sg", v), ("ygt", u)], writes=[("mg", u, fb)])
            for fb in range(8):
                pa, pak = next_ps(C)
                mm_chain(P, pa[:, :], [(wout[:, c, fb * 128:(fb + 1) * 128], mg[u][:, c, :]) for c in range(8)],
                         reads=wkeys(C, "O_wout", fb * 128) + [("mg", u, "r")] + [("mg", u, f) for f in range(4)], writes=[pak])
                P.op(DVE, lambda e, pa=pa, u=u, fb=fb: e.tensor_tensor(out=xt[u][:, fb, :], in0=pa[:, :], in1=xt[u][:, fb, :], op=ALU.add),
                     reads=[pak, ("xt", u)], writes=[("xt", u)])
            P.op(POOL, lambda e, u=u, tsl=tsl: e.dma_start(out=xv[:, :, tsl], in_=xt[u][:, :, :]), reads=[("xt", u)], dkey=("Ost", u))
        P.barrier()


def stage_l1_inproj(P, C):
    nc = P.nc
    xres = C.xres
    with ExitStack() as es:
        sb = lambda name, shape, dt=F32: es.enter_context(nc.sbuf_tensor(name, list(shape), dt))
        W = sb("E_W", [128, 8, 4112], BF16)
        g_mix = C.gains["l1_mix_norm"]
        wload(P, C, W, C.w["l1_w_in"], 8, 4112, name="E_W")
        cw = sb("E_cw", [128, 4, 24])
        P.op(SP, lambda e: e.dma_start(out=cw[:, :, :], in_=C.w["l1_conv"]), writes=[("cw",)], dkey=("Ecw",))
        hp = sb("E_hp", [8, 4])
        P.op(SP, lambda e: e.dma_start(out=hp[:, 0:2], in_=C.w["l1_hp"]), writes=[("hp",)], dkey=("Ehp",))
        cmask = sb("E_cmask", [8, TT])
        P.op(SP, lambda e: e.dma_start(out=cmask[:, :], in_=C.cst["cmask"]), writes=[("cmask",)], dkey=("Ecm",))
        P.barrier()
        P.op(ACT, lambda e: e.activation(out=hp[:, 2:3], in_=hp[:, 0:1], func=AF.Exp), reads=[("hp",)], writes=[("hp2",)])
        P.op(DVE, lambda e: e.tensor_scalar(out=hp[:, 2:3], in0=hp[:, 2:3], scalar1=-1.0, scalar2=None, op0=ALU.mult), reads=[("hp2",)], writes=[("hp2",)])
        P.barrier()
        xt = sb("E_xt", [128, 8, TT])
        hn = sb("E_hn", [128, 8, TT], BF16)
        sq = sb("E_sq", [128, 8, TT], BF16)
        rstd = sb("E_rstd", [128, TT])
        acc = [sb(f"E_acc{i}", [128, TT]) for i in range(4)]
        accq = sb("E_accq", [128, 16, TT])
        rn16 = sb("E_rn16", [16, TT])
        oh16 = sb("E_oh16", [128, 16, 16], BF16)
        sel16 = sb("E_sel16", [16, 16, 128])
        l2c = sb("E_l2c", [16, 2])
        pss16 = C.psum[5]
        P.op(SP, lambda e: e.dma_start(out=oh16[:, :, :], in_=C.cst_oh16), writes=[("oh16",)], dkey=("Eoh",))
        P.op(SP, lambda e: e.dma_start(out=sel16[:, :, :], in_=C.cst["sel16"]), writes=[("sel16",)], dkey=("Esel",))
        P.op(SP, lambda e: e.dma_start(out=l2c[:, :], in_=C.cst["l2c"]), writes=[("l2c",)], dkey=("El2c",))
        sqb = [sb(f"E_sqb{i}", [128, TT], BF16) for i in range(5)]
        pend_ss = []
        pend_act = []
        halo = sb("E_halo", [128, 24, 3])
        corr = sb("E_corr", [128, 24, 3])
        ctmp = sb("E_ctmp", [128, 24])
        outs = {nm: sb("E_o" + nm, [128, 8, TT], BF16) for nm in ["gq", "gk", "gv", "gz"]}
        gsb = sb("E_g", [8, TT])
        gcs = [sb(f"E_gc{i}", [8, TT]) for i in range(2)]
        bts = [sb(f"E_bt{i}", [8, TT]) for i in range(2)]
        it = 0
        for j in range(NT):
            tsl = slice(j * TT, (j + 1) * TT)
            norm_tile(P, C, xres, j, xt, ("xt",), hn, ("hn",), sq, rstd, pshi=5, gain=g_mix)
            if j > 0:
                hk = [("halo", b) for b in range(24)]
                terms = [(0, 0, 0), (0, 1, 1), (0, 2, 2), (1, 0, 1), (1, 1, 2), (2, 0, 2)]
                first = {}
                for (t_, k_, h_i) in terms:
                    if t_ not in first:
                        first[t_] = True
                        P.op(POOL, lambda e, t_=t_, k_=k_, h_i=h_i: e.tensor_tensor(out=corr[:, :, t_], in0=halo[:, :, h_i], in1=cw[:, k_, 0:24], op=ALU.mult),
                             reads=hk + [("corr",)], writes=[("corr",)])
                    else:
                        P.op(POOL, lambda e, k_=k_, h_i=h_i: e.tensor_tensor(out=ctmp[:, :], in0=halo[:, :, h_i], in1=cw[:, k_, 0:24], op=ALU.mult),
                             reads=hk + [("ctmp",)], writes=[("ctmp",)])
                        P.op(POOL, lambda e, t_=t_: e.tensor_tensor(out=corr[:, :, t_], in0=corr[:, :, t_], in1=ctmp[:, :], op=ALU.add),
                             reads=[("corr",), ("ctmp",)], writes=[("corr",)])
            for sec in range(3):
                nm = ["gq", "gk", "gv"][sec]
                for hh in range(8):
                    blk = sec * 8 + hh
                    col = blk * 128
                    u = it % 5
                    a3 = it % 4
                    it += 1
                    pst, psk = next_ps(C, 0, 5)
                    mm_chain(P, pst[:, :], [(W[:, c, col:col + 128], hn[:, c, :]) for c in range(8)], reads=[*wkeys(C, "E_W", col), ("hn", "a"), ("hn", "b")], writes=[psk])
                    if sec == 2:
                        a_ = acc[a3]
                        akey = ("acc", a3)
                    else:
                        a_ = accq[:, sec * 8 + hh, :]
                        akey = ("accq", sec * 8 + hh)
                    P.op(ACT, lambda e, a_=a_, pst=pst, blk=blk: e.activation(out=a_[:, :], in_=pst[:, :], func=AF.Copy, scale=cw[:, 3, blk:blk + 1]),
                         reads=[psk], writes=[akey])
                    if j < NT - 1:
                        P.op(ACT, lambda e, pst=pst, blk=blk: e.copy(out=halo[:, blk, :], in_=pst[:, TT - 3:TT]), reads=[psk], writes=[("halo", blk)])
                    for k in range(3):
                        d_ = 3 - k
                        P.op(DVE, lambda e, a_=a_, pst=pst, blk=blk, k=k, d_=d_: e.scalar_tensor_tensor(out=a_[:, d_:TT], in0=pst[:, 0:TT - d_], scalar=cw[:, k, blk:blk + 1], in1=a_[:, d_:TT],
                                                                                                  op0=ALU.mult, op1=ALU.add),
                             reads=[psk, akey], writes=[akey])
                    if j > 0:
                        P.op(POOL, lambda e, a_=a_, blk=blk: e.tensor_tensor(out=a_[:, 0:3], in0=a_[:, 0:3], in1=corr[:, blk, :], op=ALU.add),
                             reads=[("corr",), akey], writes=[akey])
                    pend_act.append((sec, hh, a_, akey, u, nm))
                    while len(pend_act) > (1 if blk < 23 else 0):
                        sec_, hh_, a2_, akey_, u2_, nm_ = pend_act.pop(0)
                        if sec_ == 2:
                            P.op(ACT, lambda e, a2_=a2_, hh_=hh_, nm_=nm_: e.activation(out=outs[nm_][:, hh_, :], in_=a2_[:, :], func=AF.Silu), reads=[akey_], writes=[(nm_, hh_)])
                        else:
                            P.op(ACT, lambda e, a2_=a2_: e.activation(out=a2_[:, :], in_=a2_[:, :], func=AF.Silu), reads=[akey_], writes=[akey_])
                            P.op(ACT, lambda e, a2_=a2_, u2_=u2_: e.activation(out=sqb[u2_][:, :], in_=a2_[:, :], func=AF.Square), reads=[akey_], writes=[("sqb", u2_)])
                            pend_ss.append((sec_ * 8 + hh_, u2_))
                    while len(pend_ss) > (3 if blk < 23 else 0):
                        qi_, u_ = pend_ss.pop(0)
                        P.op(PE, lambda e, qi_=qi_, u_=u_: e.matmul(pss16[0:16, :], lhsT=oh16[:, qi_, :], rhs=sqb[u_][:, :], start=(qi_ == 0), stop=(qi_ == 15)),
                             reads=[("sqb", u_)], writes=[("pss16",)])
            P.op(ACT, lambda e: e.activation(out=rn16[:, :], in_=pss16[0:16, :], func=AF.Sqrt, scale=l2c[:, 0:1], bias=l2c[:, 1:2]), reads=[("pss16",)], writes=[("rn16",)])
            P.op(DVE, lambda e: e.reciprocal(out=rn16[:, :], in_=rn16[:, :]), reads=[("rn16",)], writes=[("rn16",)])
            for qi in range(16):
                sec, hh = qi // 8, qi % 8
                nm = ["gq", "gk"][sec]
                pbc, pbck = next_ps(C, 0, 5)
                mm_chain(P, pbc[:, :], [(sel16[:, qi, :], rn16[:, :])], reads=[("rn16",)], writes=[pbck])
                P.op(DVE, lambda e, pbc=pbc, qi=qi, hh=hh, nm=nm: e.tensor_tensor(out=outs[nm][:, hh, :], in0=accq[:, qi, :], in1=pbc[:, :], op=ALU.mult),
                     reads=[pbck, ("accq", qi)], writes=[(nm, hh)])
            for hh in range(8):
                col = 3072 + hh * 128
                pst, psk = next_ps(C, 0, 5)
                mm_chain(P, pst[:, :], [(W[:, c, col:col + 128], hn[:, c, :]) for c in range(8)], reads=[*wkeys(C, "E_W", col), ("hn", "a"), ("hn", "b")], writes=[psk])
                P.op(ACT, lambda e, pst=pst, hh=hh: e.activation(out=outs["gz"][:, hh, :], in_=pst[:, :], func=AF.Silu), reads=[psk], writes=[("gz", hh)])
            pb_, pbk = next_ps(C, 0, 5)
            mm_chain(P, pb_[0:8, :], [(W[:, c, 4096:4104], hn[:, c, :]) for c in range(8)], reads=[*wkeys(C, "E_W", 4096, 8), ("hn", "a"), ("hn", "b")], writes=[pbk])
            jb = j % 2
            P.op(ACT, lambda e, pb_=pb_, jb=jb: e.activation(out=bts[jb][:, :], in_=pb_[0:8, :], func=AF.Sigmoid), reads=[pbk], writes=[("bts", jb)])
            pa_, pak = next_ps(C, 0, 5)
            mm_chain(P, pa_[0:8, :], [(W[:, c, 4104:4112], hn[:, c, :]) for c in range(8)], reads=[*wkeys(C, "E_W", 4104, 8), ("hn", "a"), ("hn", "b")], writes=[pak])
            P.op(ACT, lambda e, pa_=pa_: e.activation(out=gsb[:, :], in_=pa_[0:8, :], func=AF.Exp, bias=hp[:, 1:2]), reads=[pak, ("hp",)], writes=[("gsb",)])
            P.op(ACT, lambda e: e.activation(out=gsb[:, :], in_=gsb[:, :], func=AF.Ln, bias=C.one_col[0:8, 0:1]), reads=[("gsb",)], writes=[("gsb",)])
            P.op(DVE, lambda e: e.tensor_scalar(out=gsb[:, :], in0=gsb[:, :], scalar1=hp[:, 2:3], scalar2=None, op0=ALU.mult), reads=[("gsb",), ("hp2",)], writes=[("gsb",)])
            P.op(DVE, lambda e, jb=jb: e.tensor_tensor_scan(out=gcs[jb][:, :], data0=cmask[:, :], data1=gsb[:, :], initial=0.0, op0=ALU.mult, op1=ALU.add),
                 reads=[("gsb",)], writes=[("gcs", jb)])
            P.op(POOL, lambda e, jb=jb, tsl=tsl: e.dma_start(out=C.scr32["gcT"][:, tsl], in_=gcs[jb][:, :]), reads=[("gcs", jb)], dkey=("Est", "gc", jb))
            P.op(POOL, lambda e, jb=jb, tsl=tsl: e.dma_start(out=C.scr32["btT"][:, tsl], in_=bts[jb][:, :]), reads=[("bts", jb)], dkey=("Est", "bt", jb))
            for nm in ["gq", "gk", "gv", "gz"]:
                dv = C.scr[nm].rearrange("(h p) t -> p h t", p=128)
                P.op(POOL, lambda e, dv=dv, nm=nm, tsl=tsl: e.dma_start(out=dv[:, :, tsl], in_=outs[nm][:, :, :]), reads=[(nm, hh) for hh in range(8)], dkey=("Est", nm))
        P.barrier()


def stage_gdn(P, C):
    nc = P.nc
    NCH = 32
    with ExitStack() as es:
        sb = lambda name, shape, dt=F32: es.enter_context(nc.sbuf_tensor(name, list(shape), dt))
        masks = sb("G_masks", [128, 18, 128])
        P.op(SP, lambda e: e.dma_start(out=masks[:, :, :], in_=C.cst["gmasks"]), writes=[("masks",)], dkey=("Gm",))
        sel = sb("G_sel", [8, 8, 128])
        P.op(SP, lambda e: e.dma_start(out=sel[:, :, :], in_=C.cst["gsel"]), writes=[("sel",)], dkey=("Gs",))
        sel_last = sb("G_sellast", [128, 128])
        P.op(SP, lambda e: e.dma_start(out=sel_last[:, :], in_=C.cst["gsellast"]), writes=[("sellast",)], dkey=("Gsl",))
        identf = masks[:, 15, :]
        onorm = sb("G_onorm", [128, 1])
        P.op(SP, lambda e: e.dma_start(out=onorm[:, :], in_=C.w["l1_o_norm"]), writes=[("onorm",)], dkey=("Gon",))
        gcT = sb("G_gcT", [8, S])
        btT = sb("G_btT", [8, S])
        P.op(SP, lambda e: e.dma_start(out=gcT[:, :], in_=C.scr32["gcT"][:, :]), writes=[("gcT",)], dkey=("Ggc",))
        P.op(SP, lambda e: e.dma_start(out=btT[:, :], in_=C.scr32["btT"][:, :]), writes=[("btT",)], dkey=("Gbt",))
        gct = sb("G_gct", [128, NCH, 8])
        btt = sb("G_btt", [128, NCH, 8])
        glt = sb("G_glt", [128, NCH, 8])
        kbs = sb("G_kbs", [128, NCH, 8])
        kds = sb("G_kds", [128, NCH, 8])
        egl = sb("G_egl", [128, NCH, 8])
        P.barrier()
        for n in range(NCH):
            cs = slice(n * 128, (n + 1) * 128)
            pt, ptk = next_ps(C)
            P.op(PE, lambda e, pt=pt, cs=cs: e.transpose(pt[:, 0:8], gcT[:, cs], identf[0:8, 0:8]), reads=[("gcT",)], writes=[ptk])
            P.op(ACT, lambda e, pt=pt, n=n: e.copy(out=gct[:, n, :], in_=pt[:, 0:8]), reads=[ptk], writes=[("gct", n)])
            pt2, pt2k = next_ps(C)
            P.op(PE, lambda e, pt2=pt2, cs=cs: e.transpose(pt2[:, 0:8], btT[:, cs], identf[0:8, 0:8]), reads=[("btT",)], writes=[pt2k])
            P.op(DVE, lambda e, pt2=pt2, n=n: e.tensor_copy(out=btt[:, n, :], in_=pt2[:, 0:8]), reads=[pt2k], writes=[("btt", n)])
        P.barrier()
        gflat = lambda t: t[:, :, :].rearrange("p n h -> p (n h)")
        pg, pgk = next_ps(C)
        mm_chain(P, pg[:, 0:256], [(sel_last[:, :], gflat(gct))], reads=[], writes=[pgk])
        P.op(ACT, lambda e: e.copy(out=gflat(glt), in_=pg[:, 0:256]), reads=[pgk], writes=[("glt",)])
        P.op(ACT, lambda e: e.activation(out=gflat(egl), in_=pg[:, 0:256], func=AF.Exp), reads=[pgk], writes=[("egl",)])
        P.op(ACT, lambda e: e.activation(out=gflat(kbs), in_=gflat(gct), func=AF.Exp), reads=[], writes=[("kbs",)])
        P.op(DVE, lambda e: e.tensor_tensor(out=gflat(kbs), in0=gflat(kbs), in1=gflat(btt), op=ALU.mult), reads=[("kbs",)], writes=[("kbs",)])
        P.op(DVE, lambda e: e.tensor_tensor(out=gflat(kds), in0=gflat(glt), in1=gflat(gct), op=ALU.subtract), reads=[("glt",)], writes=[("kds",)])
        P.op(ACT, lambda e: e.activation(out=gflat(kds), in_=gflat(kds), func=AF.Exp), reads=[("kds",)], writes=[("kds",)])
        P.barrier()
        KT = [sb(f"G_KT{i}", [128, S], BF16) for i in range(2)]
        QT = [sb(f"G_QT{i}", [128, S], BF16) for i in range(2)]
        VT = [sb(f"G_VT{i}", [128, S], BF16) for i in range(2)]
        ZT = [sb(f"G_ZT{i}", [128, S], BF16) for i in range(2)]
        gcb = [sb(f"G_gcb{i}", [128, TT]) for i in range(2)]
        gcbA = [sb(f"G_gcbA{i}", [128, TT]) for i in range(2)]
        gcbB = [sb(f"G_gcbB{i}", [128, TT]) for i in range(2)]
        egcb = [sb(f"G_egcb{i}", [128, TT]) for i in range(2)]
        QdT = [sb(f"G_QdT{i}", [128, TT], BF16) for i in range(2)]
        NB_ = 2
        T4 = [128, 4, 128]
        xg = [sb(f"G_xg{i}", T4) for i in range(NB_)]
        tA = [sb(f"G_tA{i}", T4) for i in range(NB_)]
        tB = [sb(f"G_tB{i}", T4) for i in range(NB_)]
        a1 = [sb(f"G_a1{i}", T4) for i in range(NB_)]
        q1 = [sb(f"G_q1{i}", T4) for i in range(NB_)]
        tmpf = [sb(f"G_tmpf{i}", T4) for i in range(NB_)]
        A_ = [sb(f"G_A{i}", T4, BF16) for i in range(NB_)]
        AT_ = [sb(f"G_AT{i}", T4, BF16) for i in range(NB_)]
        qkT = [sb(f"G_qkT{i}", T4, BF16) for i in range(NB_)]
        Dd = [[sb(f"G_D{i}_{k}", T4, BF16) for k in range(2)] for i in range(NB_)]
        DTd = [[sb(f"G_DT{i}_{k}", T4, BF16) for k in range(2)] for i in range(NB_)]
        Xm = [sb(f"G_Xm{i}", T4, BF16) for i in range(NB_)]
        XTm = [sb(f"G_XTm{i}", T4, BF16) for i in range(NB_)]
        kbd = [sb(f"G_kbd{i}", T4, BF16) for i in range(NB_)]
        kdec = [sb(f"G_kdec{i}", T4, BF16) for i in range(NB_)]
        vb = [sb(f"G_vb{i}", T4, BF16) for i in range(NB_)]
        nwT = [sb(f"G_nwT{i}", T4, BF16) for i in range(NB_)]
        vnew = [sb(f"G_vnew{i}", [128, 128], BF16) for i in range(2)]
        Sst = sb("G_S", [128, 128])
        Sbf = sb("G_Sbf", [128, 128], BF16)
        o_sb = [sb(f"G_osb{i}", [128, TT]) for i in range(2)]
        osq = [sb(f"G_osq{i}", [128, TT], BF16) for i in range(2)]
        rr = [sb(f"G_rr{i}", [128, TT]) for i in range(2)]
        mo = [sb(f"G_mo{i}", [128, TT], BF16) for i in range(2)]
        ident_bf = C.ident_bf
        v4 = lambda t: t[:, :].rearrange("p (c k) -> p c k", c=4)
        mb = lambda k: masks[:, k, :].unsqueeze(1).to_broadcast(T4)

        def load_head(h):
            hb = h % 2
            for nm, tl in [("gk", KT), ("gq", QT), ("gv", VT), ("gz", ZT)]:
                P.op(SP, lambda e, nm=nm, tl=tl, h=h, hb=hb: e.dma_start(out=tl[hb][:, :], in_=C.scr[nm][h * 128:(h + 1) * 128, :]),
                     writes=[(nm, hb)], dkey=("Gld", nm, hb))

        def prep_half(h, jt, u, c0, ncn):
            hb = h % 2
            w = u
            n0 = 4 * jt
            hf = c0 // ncn
            tsl = slice(jt * TT, (jt + 1) * TT)
            cs = [slice((n0 + c0 + c) * 128, (n0 + c0 + c + 1) * 128) for c in range(ncn)]
            TH = [128, ncn, 128]
            hs = slice(c0, c0 + ncn)
            vh = lambda t: t[:, 0:ncn * 128].rearrange("p (c k) -> p c k", c=ncn)
            mh = lambda k: masks[:, k, :].unsqueeze(1).to_broadcast(TH)
            K_ = lambda nm: (nm, u, hf)
            if c0 == 0:
                pg, pgk = next_ps(C, 2, 6)
                mm_chain(P, pg[:, :], [(sel[:, h, :], gcT[:, tsl])], reads=[], writes=[pgk])
                P.op(ACT, lambda e: e.copy(out=gcb[w][:, :], in_=pg[:, :]), reads=[pgk], writes=[("gcb", w)])
                m4 = lambda k: masks[:, k, :].unsqueeze(1).to_broadcast([128, 4, 128])
                g4 = lambda t: t[:, :].rearrange("p (c k) -> p c k", c=4)
                P.op(DVE, lambda e: e.tensor_tensor(out=g4(gcbA[w]), in0=g4(gcb[w]), in1=m4(16), op=ALU.add), reads=[("gcb", w)], writes=[("gcbA", w)])
                P.op(POOL, lambda e: e.tensor_tensor(out=g4(gcbB[w]), in0=g4(gcb[w]), in1=m4(17), op=ALU.add), reads=[("gcb", w)], writes=[("gcbB", w)])
                P.op(ACT, lambda e: e.activation(out=egcb[w][:, :], in_=pg[:, :], func=AF.Exp), reads=[pgk], writes=[("egcb", w)])
                P.op(POOL, lambda e: e.tensor_tensor(out=QdT[w][:, :], in0=QT[hb][:, tsl], in1=egcb[w][:, :], op=ALU.mult),
                     reads=[("egcb", w), ("gq", hb)], writes=[("QdT", w)])
            gci = gct[:, n0 + c0:n0 + c0 + ncn, h:h + 1].to_broadcast(TH)
            bti = btt[:, n0 + c0:n0 + c0 + ncn, h:h + 1].to_broadcast(TH)
            kbi = kbs[:, n0 + c0:n0 + c0 + ncn, h:h + 1].to_broadcast(TH)
            kdi = kds[:, n0 + c0:n0 + c0 + ncn, h:h + 1].to_broadcast(TH)
            pkk, pkkk = next_ps(C, 2, 6)

            def f_kk(e):
                ins = None
                for c in range(ncn):
                    ins = e.matmul(pkk[:, c * 128:(c + 1) * 128], lhsT=KT[hb][:, cs[c]], rhs=KT[hb][:, cs[c]], start=True, stop=True)
                return ins
            P.op(PE, f_kk, reads=[("gk", hb)], writes=[pkkk])
            pqk, pqkk = next_ps(C, 2, 6)

            def f_qk(e):
                ins = None
                for c in range(ncn):
                    ins = e.matmul(pqk[:, c * 128:(c + 1) * 128], lhsT=KT[hb][:, cs[c]], rhs=QT[hb][:, cs[c]], start=True, stop=True)
                return ins
            P.op(PE, f_qk, reads=[("gk", hb), ("gq", hb)], writes=[pqkk])
            gcbAv = gcbA[w][:, :].rearrange("p (c k) -> p c k", c=4)[:, hs, :]
            gcbBv = gcbB[w][:, :].rearrange("p (c k) -> p c k", c=4)[:, hs, :]
            P.op(DVE, lambda e: e.tensor_tensor(out=xg[u][:, hs, :], in0=gcbAv, in1=gci, op=ALU.subtract), reads=[("gcbA", w)], writes=[K_("xg")])
            P.op(DVE, lambda e: e.tensor_tensor(out=tmpf[u][:, hs, :], in0=gcbBv, in1=gci, op=ALU.subtract), reads=[("gcbB", w)], writes=[K_("tmpf")])
            yield None
            P.op(ACT, lambda e: e.activation(out=tA[u][:, hs, :], in_=xg[u][:, hs, :], func=AF.Exp, scale=-1.0), reads=[K_("xg")], writes=[K_("tA")])
            P.op(ACT, lambda e: e.activation(out=tB[u][:, hs, :], in_=tmpf[u][:, hs, :], func=AF.Exp), reads=[K_("tmpf")], writes=[K_("tB")])
            yield None
            P.op(DVE, lambda e: e.tensor_tensor(out=a1[u][:, hs, :], in0=tA[u][:, hs, :], in1=vh(pkk), op=ALU.mult), reads=[pkkk, K_("tA")], writes=[K_("a1")])
            P.op(DVE, lambda e: e.tensor_tensor(out=qkT[u][:, hs, :], in0=tB[u][:, hs, :], in1=vh(pqk), op=ALU.mult), reads=[pqkk, K_("tB")], writes=[K_("qkT")])
            yield None
            for c in range(ncn):
                P.op(ACT, lambda e, c=c: e.activation(out=A_[u][:, c0 + c, :], in_=a1[u][:, c0 + c, :], func=AF.Copy, scale=btt[:, n0 + c0 + c, h:h + 1]),
                     reads=[K_("a1")], writes=[K_("A")])
            yield None
            pb1, pb1k = next_psb(C)

            def f_at(e):
                ins = None
                for c in range(ncn):
                    ins = e.transpose(pb1[:, c * 128:(c + 1) * 128], A_[u][:, c0 + c, :], ident_bf[:, :])
                return ins
            P.op(PE, f_at, reads=[K_("A")], writes=[pb1k])
            pb2, pb2k = next_psb(C)

            def f_kvt(e):
                ins = None
                for c in range(ncn):
                    e.transpose(pb2[:, c * 128:(c + 1) * 128], KT[hb][:, cs[c]], ident_bf[:, :])
                    ins = e.transpose(pb2[:, (ncn + c) * 128:(ncn + c + 1) * 128], VT[hb][:, cs[c]], ident_bf[:, :])
                return ins
            P.op(PE, f_kvt, reads=[("gk", hb), ("gv", hb)], writes=[pb2k])
            P.op(ACT, lambda e: e.copy(out=AT_[u][:, hs, :], in_=vh(pb1)), reads=[pb1k], writes=[K_("AT")])
            ktr = pb2[:, 0:ncn * 128].rearrange("p (c k) -> p c k", c=ncn)
            vtr = pb2[:, ncn * 128:2 * ncn * 128].rearrange("p (c k) -> p c k", c=ncn)
            P.op(DVE, lambda e: e.tensor_tensor(out=kbd[u][:, hs, :], in0=ktr, in1=kbi, op=ALU.mult), reads=[pb2k], writes=[K_("kbd")])
            P.op(DVE, lambda e: e.tensor_tensor(out=kdec[u][:, hs, :], in0=ktr, in1=kdi, op=ALU.mult), reads=[pb2k], writes=[K_("kdec")])
            P.op(DVE, lambda e: e.tensor_tensor(out=vb[u][:, hs, :], in0=vtr, in1=bti, op=ALU.mult), reads=[pb2k], writes=[K_("vb")])
            P.op(POOL, lambda e: e.tensor_tensor(out=tmpf[u][:, hs, :], in0=A_[u][:, hs, :], in1=mh(2), op=ALU.mult), reads=[K_("A")], writes=[K_("tmpf")])
            yield None
            P.op(POOL, lambda e: e.tensor_tensor(out=Dd[u][0][:, hs, :], in0=tmpf[u][:, hs, :], in1=mh(15), op=ALU.add), reads=[K_("tmpf")], writes=[("D", u, 0, hf)])
            P.op(DVE, lambda e: e.tensor_tensor(out=q1[u][:, hs, :], in0=AT_[u][:, hs, :], in1=mh(8), op=ALU.mult), reads=[K_("AT"), K_("q1")], writes=[K_("q1")])
            yield None
            P.op(DVE, lambda e: e.tensor_tensor(out=DTd[u][0][:, hs, :], in0=q1[u][:, hs, :], in1=mh(15), op=ALU.add), reads=[K_("q1")], writes=[("DT", u, 0, hf)])
            cur = 0
            for li in range(1, 7):
                yield None
                last = (li == 6)
                nxt = 1 - cur
                D_c, DT_c = Dd[u][cur], DTd[u][cur]
                kD, kDT = ("D", u, cur, hf), ("DT", u, cur, hf)
                if not last:
                    px, pxk = next_ps(C, 2, 6)

                    def f_x(e, px=px, D_c=D_c):
                        ins = None
                        for c in range(ncn):
                            ins = e.matmul(px[:, c * 128:(c + 1) * 128], lhsT=AT_[u][:, c0 + c, :], rhs=D_c[:, c0 + c, :], start=True, stop=True)
                        return ins
                    P.op(PE, f_x, reads=[K_("AT"), kD], writes=[pxk])
                px2, px2k = next_ps(C, 2, 6)

                def f_x2(e, px2=px2, DT_c=DT_c):
                    ins = None
                    for c in range(ncn):
                        ins = e.matmul(px2[:, c * 128:(c + 1) * 128], lhsT=A_[u][:, c0 + c, :], rhs=DT_c[:, c0 + c, :], start=True, stop=True)
                    return ins
                P.op(PE, f_x2, reads=[K_("A"), kDT], writes=[px2k])
                yield None
                if not last:
                    P.op(DVE, lambda e, px=px, li=li: e.tensor_tensor(out=Xm[u][:, hs, :], in0=vh(px), in1=mh(2 + li), op=ALU.mult), reads=[pxk], writes=[K_("Xm")])
                P.op(DVE, lambda e, px2=px2, li=li: e.tensor_tensor(out=XTm[u][:, hs, :], in0=vh(px2), in1=mh(8 + li), op=ALU.mult), reads=[px2k], writes=[K_("XTm")])
                yield None
                if not last:
                    pm, pmk = next_ps(C, 2, 6)

                    def f_m(e, pm=pm, D_c=D_c, DT_c=DT_c):
                        ins = None
                        for c in range(ncn):
                            ins = e.matmul(pm[:, c * 128:(c + 1) * 128], lhsT=DT_c[:, c0 + c, :], rhs=Xm[u][:, c0 + c, :], start=True, stop=True)
                        return ins
                    P.op(PE, f_m, reads=[kDT, K_("Xm"), kD], writes=[pmk])
                pm2, pm2k = next_ps(C, 2, 6)

                def f_m2(e, pm2=pm2, D_c=D_c, DT_c=DT_c):
                    ins = None
                    for c in range(ncn):
                        ins = e.matmul(pm2[:, c * 128:(c + 1) * 128], lhsT=D_c[:, c0 + c, :], rhs=XTm[u][:, c0 + c, :], start=True, stop=True)
                    return ins
                P.op(PE, f_m2, reads=[kD, K_("XTm"), kDT], writes=[pm2k])
                yield None
                if not last:
                    P.op(DVE, lambda e, pm=pm, nxt=nxt, D_c=D_c: e.tensor_tensor(out=Dd[u][nxt][:, hs, :], in0=vh(pm), in1=D_c[:, hs, :], op=ALU.add),
                         reads=[pmk, kD], writes=[("D", u, nxt, hf)])
                P.op(DVE, lambda e, pm2=pm2, nxt=nxt, DT_c=DT_c: e.tensor_tensor(out=DTd[u][nxt][:, hs, :], in0=vh(pm2), in1=DT_c[:, hs, :], op=ALU.add),
                     reads=[pm2k, kDT], writes=[("DT", u, nxt, hf)])
                cur = nxt
            yield None
            TT_ = DTd[u][cur]
            pw, pwk = next_ps(C, 2, 6)

            def f_w(e):
                ins = None
                for c in range(ncn):
                    ins = e.matmul(pw[:, c * 128:(c + 1) * 128], lhsT=kbd[u][:, c0 + c, :], rhs=TT_[:, c0 + c, :], start=True, stop=True)
                return ins
            P.op(PE, f_w, reads=[K_("kbd"), ("DT", u, cur, hf)], writes=[pwk])
            yield None
            P.op(ACT, lambda e: e.activation(out=nwT[u][:, hs, :], in_=vh(pw), func=AF.Copy, scale=-1.0), reads=[pwk], writes=[K_("nwT")])
            yield (TT_, cur)

        def seq(h, n, u, TT_, cur, po, pok):
            c = n % 4
            hf = c // 2
            w = u
            v2 = n % 2
            cl = slice(c * 128, (c + 1) * 128)
            K_ = lambda nm: (nm, u, hf)
            pv, pvk = next_ps(C, 1, 2)
            mm_chain(P, pv[:, 0:128], [(TT_[:, c, :], vb[u][:, c, :]), (nwT[u][:, c, :], Sbf[:, :])], reads=[("DT", u, cur, hf), K_("vb"), K_("nwT"), ("Sbf",)], writes=[pvk])
            P.op(ACT, lambda e: e.copy(out=vnew[v2][:, :], in_=pv[:, 0:128]), reads=[pvk], writes=[("vnew", v2)])
            mm_chain(P, po[:, cl], [(Sbf[:, :], QdT[w][:, cl]), (vnew[v2][:, :], qkT[u][:, c, :])], reads=[("Sbf",), ("QdT", w), ("vnew", v2), K_("qkT")], writes=[pok])
            pS, pSk = next_ps(C, 1, 2)
            mm_chain(P, pS[:, 0:128], [(kdec[u][:, c, :], vnew[v2][:, :])], reads=[K_("kdec"), ("vnew", v2)], writes=[pSk])
            P.op(DVE, lambda e: e.scalar_tensor_tensor(out=Sst[:, :], in0=Sst[:, :], scalar=egl[:, n, h:h + 1], in1=pS[:, 0:128], op0=ALU.mult, op1=ALU.add),
                 reads=[pSk, ("S",)], writes=[("S",)])
            P.op(ACT, lambda e: e.copy(out=Sbf[:, :], in_=Sst[:, :]), reads=[("S",)], writes=[("Sbf",)])

        def finish_tile(h, jt, v, po, pok):
            hb = h % 2
            tsl = slice(jt * TT, (jt + 1) * TT)
            P.op(ACT, lambda e: e.copy(out=o_sb[v][:, :], in_=po[:, :]), reads=[pok], writes=[("osb", v)])
            P.op(ACT, lambda e: e.activation(out=osq[v][:, :], in_=po[:, :], func=AF.Square), reads=[pok], writes=[("osq", v)])
            pss, pssk = next_ps(C, 1, 2)
            mm_chain(P, pss[:, :], [(C.ones_bf[:, :], osq[v][:, :])], reads=[("osq", v)], writes=[pssk])
            P.op(ACT, lambda e: e.activation(out=rr[v][:, :], in_=pss[:, :], func=AF.Sqrt, scale=1.0 / 128, bias=C.eps_col[:, 0:1]), reads=[pssk], writes=[("rr", v)])
            P.op(DVE, lambda e: e.reciprocal(out=rr[v][:, :], in_=rr[v][:, :]), reads=[("rr", v)], writes=[("rr", v)])
            P.op(POOL, lambda e: e.tensor_tensor(out=o_sb[v][:, :], in0=o_sb[v][:, :], in1=rr[v][:, :], op=ALU.mult), reads=[("osb", v), ("rr", v)], writes=[("osb", v)])
            P.op(DVE, lambda e: e.scalar_tensor_tensor(out=mo[v][:, :], in0=o_sb[v][:, :], scalar=onorm[:, 0:1], in1=ZT[hb][:, tsl], op0=ALU.mult, op1=ALU.mult),
                 reads=[("osb", v), ("gz", hb)], writes=[("mo", v)])
            P.op(POOL, lambda e: e.dma_start(out=C.scr["mT"][h * 128:(h + 1) * 128, tsl], in_=mo[v][:, :]), reads=[("mo", v)], dkey=("Gst", v))

        tiles = [(h, jt) for h in range(8) for jt in range(NT)]
        load_head(0)
        load_head(1)

        def run_pair(h, jt, u, hooks):
            g0 = prep_half(h, jt, u, 0, 2)
            g1 = prep_half(h, jt, u, 2, 2)
            res = [None, None]
            step = 0
            alive = [True, True]
            import os
            if os.environ.get("GDN_SEQ"):
                for gi, g in enumerate((g0, g1)):
                    for r in g:
                        if r is not None:
                            res[gi] = r
                alive = [False, False]
            while alive[0] or alive[1]:
                for gi, g in enumerate((g0, g1)):
                    if alive[gi]:
                        try:
                            r = next(g)
                            if r is not None:
                                res[gi] = r
                        except StopIteration:
                            alive[gi] = False
                if step in hooks:
                    hooks[step]()
                step += 1
            assert res[0][1] == res[1][1]
            return res[0]

        pend = run_pair(0, 0, 0, {})
        for k, (h, jt) in enumerate(tiles):
            u = k % 2
            if jt == 0:
                P.op(POOL, lambda e: e.memset(Sst[:, :], 0.0), writes=[("S",)])
                P.op(POOL, lambda e: e.memset(Sbf[:, :], 0.0), writes=[("Sbf",)])
            po, pok = next_ps(C, 0, 1)
            done = []

            def mk(c):
                def f():
                    seq(h, 4 * jt + c, u, pend[0], pend[1], po, pok)
                    done.append(c)
                return f
            nxt_p = None
            if k + 1 < len(tiles):
                h2, jt2 = tiles[k + 1]
                nxt_p = run_pair(h2, jt2, 1 - u, {4: mk(0), 10: mk(1), 16: mk(2), 22: mk(3)})
            for c in range(4):
                if c not in done:
                    seq(h, 4 * jt + c, u, pend[0], pend[1], po, pok)
            finish_tile(h, jt, u, po, pok)
            if jt == NT - 1 and h + 2 < 8:
                load_head(h + 2)
            pend = nxt_p
        P.barrier()


def stage_l1_out(P, C):
    nc = P.nc
    xres = C.xres
    with ExitStack() as es:
        sb = lambda name, shape, dt=F32: es.enter_context(nc.sbuf_tensor(name, list(shape), dt))
        wout = sb("O1_wout", [128, 8, 1024], BF16)
        wload(P, C, wout, C.w["l1_w_out"], 8, 1024, gain=None, name="O1_wout")
        P.barrier()
        xt = [sb(f"O1_xt{i}", [128, 8, TT]) for i in range(2)]
        mg = [sb(f"O1_mg{i}", [128, 8, TT], BF16) for i in range(2)]
        xv = xres.rearrange("(c p) t -> p c t", p=128)
        mv = C.scr["mT"].rearrange("(c p) t -> p c t", p=128)
        for j in range(NT):
            u = j % 2
            tsl = slice(j * TT, (j + 1) * TT)
            P.op(SP, lambda e, u=u, tsl=tsl: e.dma_start(out=xt[u][:, :, :], in_=xv[:, :, tsl]), writes=[("xt", u)], dkey=("O1ld", "x", u))
            P.op(SP, lambda e, u=u, tsl=tsl: e.dma_start(out=mg[u][:, :, :], in_=mv[:, :, tsl]), writes=[("mg", u)], dkey=("O1ld", "m", u))
            for fb in range(8):
                pa, pak = next_ps(C)
                mm_chain(P, pa[:, :], [(wout[:, c, fb * 128:(fb + 1) * 128], mg[u][:, c, :]) for c in range(8)], reads=[*wkeys(C, "O1_wout", fb * 128), ("mg", u)], writes=[pak])
                P.op(DVE, lambda e, pa=pa, u=u, fb=fb: e.tensor_tensor(out=xt[u][:, fb, :], in0=pa[:, :], in1=xt[u][:, fb, :], op=ALU.add),
                     reads=[pak, ("xt", u)], writes=[("xt", u)])
            P.op(POOL, lambda e, u=u, tsl=tsl: e.dma_start(out=xv[:, :, tsl], in_=xt[u][:, :, :]), reads=[("xt", u)], dkey=("O1st", u))
        P.barrier()


def stage_copy_in(P, C):
    P.op(SP, lambda e: e.dma_start(out=C.xres[:, :], in_=C.xT_in[:, :]), dkey=("cpin",))
    P.barrier()


def stage_copy_out(P, C):
    P.op(SP, lambda e: e.dma_start(out=C.outT[:, :], in_=C.xres[:, :]), dkey=("cpout",))
    P.barrier()


WNAMES = ["xa_wq", "xa_wkv", "xa_wo", "ffn_w_up", "ffn_conv", "ffn_w_down"]
GNAMES = ["xa_norm", "mem_norm", "ffn_norm"]


def build_program(plan):
    nc = bass.Bass("TRN2", target_bir_lowering=False)
    P = Prog(nc)
    C = Ctx()
    C.ps_cnt = {}
    C.wreg = {}
    dt_in = lambda name, shape, dt=F32: nc.dram_tensor(name, list(shape), dt, kind="ExternalInput").ap()
    C.xT_in = dt_in("xT", [D, S])
    C.memT = dt_in("memT", [D, NMEM])
    C.outT = nc.dram_tensor("outT", [D, S], F32, kind="ExternalOutput").ap()
    C.xres = nc.dram_tensor("xres", [D, S], F32, kind="Internal").ap()
    C.w = {}
    for lp in ["l0_", "l1_"]:
        C.w[lp + "xa_wq"] = dt_in(lp + "xa_wq", [D, D])
        C.w[lp + "xa_wkv"] = dt_in(lp + "xa_wkv", [D, 2 * D])
        C.w[lp + "xa_wo"] = dt_in(lp + "xa_wo", [D, D])
        C.w[lp + "ffn_w_up"] = dt_in(lp + "ffn_w_up", [D, 2 * FFN])
        C.w[lp + "ffn_conv"] = dt_in(lp + "ffn_conv", [128, 3, 2 * NPAIR])
        C.w[lp + "ffn_w_down"] = dt_in(lp + "ffn_w_down", [FFN, D])
    for n, shp in [("l0_w_in", [D, 2560]), ("l0_w_in_sw", [D, 1024]), ("l0_ret_norm", [128, 4]), ("l0_s5_w_glu", [512, 512]),
                   ("l0_s5_b_glu", [128, 4]), ("l0_w_out", [D, D]), ("s5_lam_s", [128, 2, 32]), ("s5_ldt_s", [128, 32]),
                   ("s5_lam_b", [16, 2, 2048]), ("s5_ldt_b", [16, 2048]), ("s5_b_b", [16, 2, 2048]), ("s5_c1", [128, 32, 16]),
                   ("s5_c2", [128, 32, 16]), ("s5_d_t", [16, 32])]:
        C.w[n] = dt_in(n, shp)
    for n, shp in [("l1_w_in", [D, 4112]), ("l1_conv", [128, 4, 24]), ("l1_hp", [8, 2]), ("l1_o_norm", [128, 1]), ("l1_w_out", [D, D])]:
        C.w[n] = dt_in(n, shp)
    C.cst = {}
    for n, shp in [("rot_tab", [128, 4, S]), ("gq_tab", [128, 4, TT]), ("dt_tab", [128, 4, 128]), ("kd_tab", [128, 4]),
                   ("id16", [16, 16]), ("tpos", [128, S]), ("t0s", [128, 32, 8]), ("t0b", [16, 8, 128]), ("cmask", [8, TT]), ("sel16", [16, 16, 128]), ("l2c", [16, 2]), ("gmasks", [128, 18, 128]), ("gsel", [8, 8, 128]),
                   ("gsellast", [128, 128])]:
        C.cst[n] = dt_in(n, shp)
    C.scr = {}
    for n, shp in [("qT", [512, S]), ("qdT", [512, S]), ("kT", [512, S]), ("gT", [512, S]), ("uT", [512, S]), ("vtok", [S, 512]),
                   ("mT", [D, S]), ("ygT", [512, S]), ("gq", [D, S]), ("gk", [D, S]), ("gv", [D, S]), ("gz", [D, S])]:
        C.scr[n] = nc.dram_tensor("scr_" + n, list(shp), BF16, kind="Internal").ap()
    C.scr32 = {n: nc.dram_tensor("scr_" + n, [8, S], F32, kind="Internal").ap() for n in ["gcT", "btT"]}
    gnames = [lp + g for lp in ["l0_", "l1_"] for g in GNAMES + ["mix_norm"]] + ["final_norm"]
    gains_d = dt_in("gains", [128, len(gnames), 8])
    consts_bf = dt_in("consts_bf", [128, 2, 128], BF16)
    C.cst_oh16 = dt_in("oh16", [128, 16, 16], BF16)

    gains_t = P.sb("gains_t", [128, len(gnames), 8])
    cbf = P.sb("cbf", [128, 2, 128], BF16)
    C.eps_col = P.sb("eps_col", [128, 1])
    C.one_col = P.sb("one_col", [128, 1])
    C.eps128_col = P.sb("eps128_col", [128, 1])
    C.psum = [P.ps(f"psum{i}", [128, 512]) for i in range(6)]
    C.psbs = [P.ps(f"psb{i}", [128, 1024], BF16) for i in range(2)]
    C.psb = C.psbs[0]
    C.psb_rr = 0
    P.op(SP, lambda e: e.dma_start(out=gains_t[:, :, :], in_=gains_d), writes=[gain_key(None)], dkey=("gains",))
    P.op(SP, lambda e: e.dma_start(out=cbf[:, :, :], in_=consts_bf), writes=[("cbf",)], dkey=("cbf",))
    P.op(POOL, lambda e: e.memset(C.eps_col[:, :], EPS), writes=[("eps",)])
    P.op(POOL, lambda e: e.memset(C.one_col[:, :], 1.0), writes=[("one",)])
    P.op(POOL, lambda e: e.memset(C.eps128_col[:, :], 128.0 * EPS), writes=[("eps128",)])
    P.barrier()
    C.gains = {n: gains_t[:, i, :] for i, n in enumerate(gnames)}
    C.ident_bf = cbf[:, 0, :]
    C.ones_bf = cbf[:, 1, :]

    for st in plan:
        if st == "copy_in":
            stage_copy_in(P, C)
        elif st == "copy_out":
            stage_copy_out(P, C)
        elif st == "l0_inproj":
            stage_l0_inproj(P, C)
        elif st == "l0_ret":
            stage_retention(P, C)
        elif st == "l0_s5":
            stage_s5(P, C)
        elif st == "l0_out":
            stage_l0_out(P, C)
        elif st == "l1_inproj":
            stage_l1_inproj(P, C)
        elif st == "l1_gdn":
            stage_gdn(P, C)
        elif st == "l1_out":
            stage_l1_out(P, C)
        elif st.endswith("_xa"):
            stage_xattn(P, C, st[:3])
        elif st.endswith("_ffn"):
            stage_ffn(P, C, st[:3], final=False)
        elif st.endswith("_ffnfinal"):
            stage_ffn(P, C, st[:3], final=True)
        else:
            raise ValueError(st)
        P.barrier(final=True)
    P.emit()
    return nc, P, gnames


def host_prep(inputs, gnames):
    shared = {}
    for lp in ["l0_", "l1_"]:
        for n in ["xa_wq", "xa_wkv", "xa_wo", "ffn_w_up", "ffn_w_down"]:
            shared[lp + n] = np.ascontiguousarray(inputs[lp + n], dtype=np.float32)
        cw = np.asarray(inputs[lp + "ffn_conv"], dtype=np.float32)
        shared[lp + "ffn_conv"] = np.ascontiguousarray(cw.reshape(3, 2 * NPAIR, 128).transpose(2, 0, 1))
    f32 = lambda a: np.ascontiguousarray(np.asarray(a, dtype=np.float32))
    w_in = f32(inputs["l0_w_in"])
    shared["l0_w_in"] = w_in
    qk = w_in[:, :1024].reshape(D, 8, 128)
    shared["l0_w_in_sw"] = f32(np.concatenate([qk[:, :, 64:], qk[:, :, :64]], axis=2).reshape(D, 1024))
    shared["l0_ret_norm"] = f32(np.asarray(inputs["l0_ret_norm"]).reshape(4, 128).T)
    shared["l0_s5_w_glu"] = f32(inputs["l0_s5_w_glu"])
    shared["l0_s5_b_glu"] = f32(np.asarray(inputs["l0_s5_b_glu"]).reshape(4, 128).T)
    shared["l0_w_out"] = f32(inputs["l0_w_out"])
    lre = np.asarray(inputs["l0_s5_lambda_re"], np.float32)
    lim = np.asarray(inputs["l0_s5_lambda_im"], np.float32)
    ldt = np.asarray(inputs["l0_s5_log_dt"], np.float32)
    lam_s = np.stack([np.concatenate([lre.T, lre.T], 0), np.concatenate([lim.T, lim.T], 0)], axis=1)
    shared["s5_lam_s"] = f32(lam_s)
    shared["s5_ldt_s"] = f32(np.broadcast_to(ldt[None, :], (128, 32)))
    shared["s5_lam_b"] = f32(np.broadcast_to(np.stack([lre.reshape(-1), lim.reshape(-1)], 0)[None], (16, 2, 2048)))
    shared["s5_ldt_b"] = f32(np.broadcast_to(np.repeat(ldt, 64)[None], (16, 2048)))
    bre = np.asarray(inputs["l0_s5_b_re"], np.float32).transpose(1, 0, 2).reshape(16, 2048)
    bim = np.asarray(inputs["l0_s5_b_im"], np.float32).transpose(1, 0, 2).reshape(16, 2048)
    shared["s5_b_b"] = f32(np.stack([bre, bim], axis=1))
    cre = np.asarray(inputs["l0_s5_c_re"], np.float32).transpose(1, 0, 2)
    cim = np.asarray(inputs["l0_s5_c_im"], np.float32).transpose(1, 0, 2)
    shared["s5_c1"] = f32(np.concatenate([cre, cim], 0))
    shared["s5_c2"] = f32(np.concatenate([cim, cre], 0))
    shared["s5_d_t"] = f32(np.asarray(inputs["l0_s5_d"], np.float32).T)
    shared["l1_w_in"] = f32(inputs["l1_w_in"])
    shared["l1_conv"] = f32(np.asarray(inputs["l1_conv"], np.float32).reshape(4, 24, 128).transpose(2, 0, 1))
    shared["l1_hp"] = f32(np.stack([np.asarray(inputs["l1_a_log"], np.float32), np.asarray(inputs["l1_dt_bias"], np.float32)], axis=1))
    shared["l1_o_norm"] = f32(np.asarray(inputs["l1_o_norm"], np.float32).reshape(128, 1))
    shared["l1_w_out"] = f32(inputs["l1_w_out"])
    cm = np.ones((8, TT), np.float32)
    cm[:, ::128] = 0.0
    shared["cmask"] = cm
    s16 = np.zeros((16, 16, 128), np.float32)
    oh = np.zeros((128, 16, 16), np.float32)
    for q_ in range(16):
        s16[q_, q_, :] = 1.0
        oh[:, q_, q_] = 1.0
    shared["sel16"] = s16
    shared["oh16"] = oh.astype(ml_dtypes.bfloat16)
    l2 = np.zeros((16, 2), np.float32)
    l2[:8, 0] = 128.0
    l2[:8, 1] = 128.0 * EPS
    l2[8:, 0] = 1.0
    l2[8:, 1] = EPS
    shared["l2c"] = l2
    ii_, jj_ = np.meshgrid(np.arange(128), np.arange(128), indexing="ij")
    gm = np.zeros((128, 18, 128), np.float32)
    gm[:, 0, :] = (ii_ > jj_)
    gm[:, 1, :] = (jj_ >= ii_)
    for li, s_ in enumerate([1, 2, 4, 8, 16, 32, 64]):
        m = ((ii_ // (2 * s_)) == (jj_ // (2 * s_))) & ((ii_ % (2 * s_)) >= s_) & ((jj_ % (2 * s_)) < s_)
        if li < 6:
            gm[:, 2 + li, :] = -m.astype(np.float32)
        gm[:, 8 + li, :] = -m.T.astype(np.float32)
    gm[:, 15, :] = np.eye(128, dtype=np.float32)
    BIG = 30000.0
    gm[:, 16, :] = BIG * (1.0 - gm[:, 0, :])
    gm[:, 17, :] = -BIG * gm[:, 0, :]
    shared["gmasks"] = gm
    gs = np.zeros((8, 8, 128), np.float32)
    for h_ in range(8):
        gs[h_, h_, :] = 1.0
    shared["gsel"] = gs
    sl = np.zeros((128, 128), np.float32)
    sl[127, :] = 1.0
    shared["gsellast"] = sl
    inv = np.exp(-math.log(10000.0) * np.arange(64, dtype=np.float32) / 64).astype(np.float32)
    ang = (np.arange(S, dtype=np.float32)[:, None] * inv[None, :]).astype(np.float32).astype(np.float64)
    cosT = np.cos(ang).T
    sinT = np.sin(ang).T
    cos128 = np.concatenate([cosT, cosT], 0)
    sin128 = np.concatenate([-sinT, sinT], 0)
    ksc = 128.0 ** -0.5
    shared["rot_tab"] = f32(np.stack([cos128, sin128, cos128 * ksc, sin128 * ksc], axis=1))
    gam = np.array(RET_GAMMA, np.float64)
    ii = np.arange(TT) % 128
    shared["gq_tab"] = f32(np.broadcast_to((gam[:, None] ** (ii[None, :] + 1))[None], (128, 4, TT)))
    jj = np.arange(128)
    diff = jj[None, :] - jj[:, None]
    dtab = np.where(diff[:, None, :] >= 0, gam[None, :, None] ** np.maximum(diff[:, None, :], 0), 0.0)
    shared["dt_tab"] = f32(dtab)
    shared["kd_tab"] = f32(gam[None, :] ** (127 - jj[:, None]))
    shared["id16"] = f32(np.eye(16))
    shared["tpos"] = f32(np.broadcast_to(np.arange(S, dtype=np.float32)[None], (128, S)))
    shared["t0s"] = f32(np.broadcast_to((np.arange(8, dtype=np.float32) * TT)[None, None, :], (128, 32, 8)))
    shared["t0b"] = f32(np.broadcast_to((np.arange(8, dtype=np.float32) * TT)[None, :, None], (16, 8, 128)))
    g = np.stack([np.asarray(inputs[n], np.float32).reshape(8, 128).T for n in gnames], axis=1)
    shared["gains"] = np.ascontiguousarray(g)
    cb = np.stack([np.eye(128, dtype=np.float32), np.ones((128, 128), np.float32)], axis=1)
    shared["consts_bf"] = np.ascontiguousarray(cb).astype(ml_dtypes.bfloat16)
    return shared


PLAN_FULL = ["copy_in", "l0_inproj", "l0_ret", "l0_s5", "l0_out", "l0_xa", "l0_ffn", "l1_inproj", "l1_gdn", "l1_out", "l1_xa", "l1_ffnfinal"]


def kernel(**inputs):
    nc, P, gnames = build_program(PLAN_FULL)
    shared = host_prep(inputs, gnames)
    x = np.asarray(inputs["x"], np.float32)
    mem = np.asarray(inputs["mem"], np.float32)
    in_maps = []
    for b in range(8):
        m = dict(shared)
        m["xT"] = np.ascontiguousarray(x[b].T)
        m["memT"] = np.ascontiguousarray(mem[b].T)
        in_maps.append(m)
    res = run_bass_kernel_spmd(nc, in_maps, core_ids=list(range(8)))
    out = np.stack([np.ascontiguousarray(res.results[b]["outT"].T) for b in range(8)], axis=0)
    return out.astype(np.float32)
```

```python
import math
from contextlib import ExitStack

import numpy as np
import ml_dtypes
import concourse.bass as bass
import concourse.mybir as mybir
from concourse.bass_utils import run_bass_kernel_spmd

F32 = mybir.dt.float32
BF16 = mybir.dt.bfloat16
I32 = mybir.dt.int32
ALU = mybir.AluOpType
AF = mybir.ActivationFunctionType
AX = mybir.AxisListType
PE, ACT, DVE, POOL, SP = "tensor", "scalar", "vector", "gpsimd", "sync"

D = 1024
S = 4096
TT = 512
NT = S // TT
NMEM = 256
EPS = 1e-6
FFN = 2816
NPAIR = FFN // 128


class Prog:
    def __init__(self, nc):
        self.nc = nc
        self.ops = []
        self.track = {}
        self.stack = ExitStack()
        self.fence = frozenset()
        self.last_eng = {}
        self.dma_since = []
        self.ps_rr = 0
        self.slot_map = {}
        self.wslot_map = {}

    def sb(self, name, shape, dt=F32):
        return self.stack.enter_context(self.nc.sbuf_tensor(name, list(shape), dt))

    def ps(self, name, shape, dt=F32):
        return self.stack.enter_context(self.nc.psum_tensor(name, list(shape), dt))

    def op(self, eng, fn, reads=(), writes=(), dkey=None):
        oid = len(self.ops)
        deps = set(self.fence)
        writes = list(writes) + [k for k in reads if k[0] in ("ps", "psb")]
        for k in reads:
            t = self.track.get(k)
            if t and t[0] is not None:
                deps.add(t[0])
        for k in writes:
            t = self.track.get(k)
            if t:
                if t[0] is not None:
                    deps.add(t[0])
                deps.update(t[1])
        for k in reads:
            t = self.track.setdefault(k, [None, []])
            t[1].append(oid)
        for k in writes:
            self.track[k] = [oid, []]
        deps.discard(oid)
        isw = dkey is not None and dkey[0] == "W"
        if dkey is not None:
            m = self.wslot_map if isw else self.slot_map
            if dkey not in m:
                m[dkey] = ("w" if isw else "d", len(m))
            dkey = m[dkey]
        self.ops.append(dict(eng=eng, fn=fn, deps=deps, dkey=dkey, sig=False))
        if not isw:
            self.last_eng[eng] = oid
            if dkey is not None:
                self.dma_since.append(oid)
        return oid

    def barrier(self, final=False):
        self.fence = frozenset(list(self.last_eng.values()) + self.dma_since)
        self.dma_since = []
        self.track = {k: v for k, v in self.track.items() if k[0] == "W"}
        self.slot_map = {}
        if final:
            self.wslot_map = {}

    def emit(self):
        nc = self.nc
        ops = self.ops
        for o in ops:
            for d in o["deps"]:
                p = ops[d]
                if p["dkey"] is None and p["eng"] == PE and o["eng"] == PE and o["dkey"] is None:
                    continue
                p["sig"] = True
        engs = [PE, ACT, DVE, POOL, SP]
        cnt = {e: 0 for e in engs}
        dcnt = {}
        for o in ops:
            if o["dkey"] is not None:
                dcnt[o["dkey"]] = dcnt.get(o["dkey"], 0) + 16
                o["semk"] = ("d", o["dkey"])
                o["seq"] = dcnt[o["dkey"]]
            elif o["sig"]:
                cnt[o["eng"]] += 1
                o["semk"] = ("e", o["eng"])
                o["seq"] = cnt[o["eng"]]
        semkeys = [("e", e) for e in engs if cnt[e] > 0] + [("d", k) for k in dcnt]
        sems = {}
        for sk in semkeys:
            sems[sk] = self.stack.enter_context(nc.semaphore("s_" + "_".join(str(x) for x in sk)))
        per_eng = {e: [] for e in engs}
        for i, o in enumerate(ops):
            per_eng[o["eng"]].append(i)
        self.stats = {e: len(per_eng[e]) for e in engs}
        self.stats["sems"] = len(sems)
        self.stats["maxcnt"] = dict(cnt)
        self.stats["dcnt"] = max(dcnt.values()) if dcnt else 0

        def run_engine(ename, eobj):
            waited = {}
            for i in per_eng[ename]:
                o = ops[i]
                need = {}
                for d in o["deps"]:
                    p = ops[d]
                    if "semk" not in p:
                        continue
                    if p["dkey"] is None and p["eng"] == PE and ename == PE and o["dkey"] is None:
                        continue
                    sk = p["semk"]
                    if p["seq"] > need.get(sk, 0):
                        need[sk] = p["seq"]
                for sk, v in need.items():
                    if waited.get(sk, 0) >= v:
                        continue
                    eobj.wait_ge(sems[sk], v)
                    waited[sk] = v
                ins = o["fn"](eobj)
                if o["dkey"] is not None:
                    ins.then_inc(sems[o["semk"]], 16)
                elif o["sig"]:
                    ins.then_inc(sems[o["semk"]], 1)
            last = {}
            for i in per_eng[ename]:
                o = ops[i]
                if o["dkey"] is not None:
                    last[o["semk"]] = max(last.get(o["semk"], 0), o["seq"])
            for sk, v in last.items():
                if waited.get(sk, 0) < v:
                    eobj.wait_ge(sems[sk], v)

        block = self.stack.enter_context(nc.Block())
        if per_eng[SP]:
            @block.sync
            def _(e):
                run_engine(SP, e)
        if per_eng[PE]:
            @block.tensor
            def _(e):
                run_engine(PE, e)
        if per_eng[ACT]:
            @block.scalar
            def _(e):
                run_engine(ACT, e)
        if per_eng[DVE]:
            @block.vector
            def _(e):
                run_engine(DVE, e)
        if per_eng[POOL]:
            @block.gpsimd
            def _(e):
                run_engine(POOL, e)
        self.stack.close()


class Ctx:
    pass


def mm_chain(P, ps_ap, pairs, reads, writes):
    pairs = list(pairs)

    def fn(e):
        n = len(pairs)
        ins = None
        for i, (l, r) in enumerate(pairs):
            ins = e.matmul(ps_ap, lhsT=l, rhs=r, start=(i == 0), stop=(i == n - 1))
        return ins
    return P.op(PE, fn, reads, writes)


def next_psb(C):
    b = C.psb_rr % 2
    C.psb_rr += 1
    return C.psbs[b], ("psb", b)


def next_ps(C, lo=0, hi=None):
    hi = len(C.psum) if hi is None else hi
    k = (lo, hi)
    r = C.ps_cnt.get(k, 0)
    C.ps_cnt[k] = r + 1
    b = lo + r % (hi - lo)
    return C.psum[b], ("ps", b)


def wload(P, C, dst, w_dram, kc, F, gain=None, name="w", f_dst0=0):
    assert gain is None
    FW = 1408
    pcs = []
    for f0 in range(0, F, FW):
        fw = min(FW, F - f0)
        pcs.append((f0, fw))
        for c in range(kc):
            P.op(POOL, lambda e, c=c, f0=f0, fw=fw: e.dma_start(out=dst[:, c, f0:f0 + fw], in_=w_dram[c * 128:(c + 1) * 128, f0:f0 + fw]),
                 writes=[("W", name, f0, c)], dkey=("W", name, f0))
    C.wreg[name] = (pcs, kc)


def wkeys(C, name, col=None, n=128):
    pcs, kc = C.wreg[name]
    out = []
    for (f0, fw) in pcs:
        if col is None or (f0 < col + n and col < f0 + fw):
            out += [("W", name, f0, c) for c in range(kc)]
    return out


def gain_key(g):
    return ("gains",)


def load_xt(P, C, src, j, xt, xt_key, ncols=TT, col0=None, dkey="xt"):
    c0 = j * ncols if col0 is None else col0
    srcv = src.rearrange("(c p) t -> p c t", p=128)
    P.op(SP, lambda e: e.dma_start(out=xt[:, :, :], in_=srcv[:, :, c0:c0 + ncols]), writes=[xt_key], dkey=(dkey, xt_key))


def norm_chunked(P, C, xt, xt_key, hn, hn_key, sqs, rstd, gain, kc=8):
    pst, psk = next_ps(C)
    for c in range(kc):
        q = c % len(sqs)
        P.op(ACT, lambda e, c=c, q=q: e.activation(out=sqs[q][:, :], in_=xt[:, c, :], func=AF.Square), reads=[xt_key], writes=[("sqs", q)])
        P.op(PE, lambda e, c=c, q=q: e.matmul(pst[:, :], lhsT=C.ones_bf[:, :], rhs=sqs[q][:, :], start=(c == 0), stop=(c == kc - 1)),
             reads=[("sqs", q)], writes=[psk])
    P.op(ACT, lambda e: e.activation(out=rstd[:, :], in_=pst[:, :], func=AF.Sqrt, scale=1.0 / D, bias=C.eps_col[:, 0:1]), reads=[psk], writes=[("rstd",)])
    P.op(DVE, lambda e: e.reciprocal(out=rstd[:, :], in_=rstd[:, :]), reads=[("rstd",)], writes=[("rstd",)])
    for c in range(kc):
        P.op(DVE, lambda e, c=c: e.scalar_tensor_tensor(out=hn[:, c, :], in0=xt[:, c, :], scalar=gain[:, c:c + 1], in1=rstd[:, :], op0=ALU.mult, op1=ALU.mult),
             reads=[xt_key, ("rstd",)], writes=[(hn_key[0], "a" if c < 4 else "b")])


def norm_tile(P, C, src, j, xt, xt_key, hn, hn_key, sq, rstd, ncols=TT, kc=8, col0=None, dkey="xt", sq_keys=(("sq",),), pshi=None, gain=None, load=True):
    if load:
        load_xt(P, C, src, j, xt, xt_key, ncols=ncols, col0=col0, dkey=dkey)
    P.op(ACT, lambda e: e.activation(out=sq[:, :, :ncols], in_=xt[:, :, :], func=AF.Square), reads=[xt_key], writes=list(sq_keys))
    pst, psk = next_ps(C, 0, pshi)
    mm_chain(P, pst[:, :ncols], [(C.ones_bf[:, :], sq[:, c, :ncols]) for c in range(kc)], reads=list(sq_keys), writes=[psk])
    P.op(ACT, lambda e: e.activation(out=rstd[:, :ncols], in_=pst[:, :ncols], func=AF.Sqrt, scale=1.0 / D, bias=C.eps_col[:, 0:1]),
         reads=[psk], writes=[("rstd",)])
    P.op(DVE, lambda e: e.reciprocal(out=rstd[:, :ncols], in_=rstd[:, :ncols]), reads=[("rstd",)], writes=[("rstd",)])
    for c in range(kc):
        P.op(DVE, lambda e, c=c: e.scalar_tensor_tensor(out=hn[:, c, :], in0=xt[:, c, :], scalar=gain[:, c:c + 1], in1=rstd[:, :ncols], op0=ALU.mult, op1=ALU.mult),
             reads=[xt_key, ("rstd",)], writes=[(hn_key[0], "a" if c < 4 else "b")])


def stage_xattn(P, C, lp):
    nc = P.nc
    W = C.w
    xres = C.xres
    scale = 256 ** -0.5
    with ExitStack() as es:
        sb = lambda name, shape, dt=F32: es.enter_context(nc.sbuf_tensor("sb_" + lp + name, list(shape), dt))
        wq = sb("xa_wq", [128, 8, 1024], BF16)
        wo = sb("xa_wo", [128, 8, 1024], BF16)
        kT = sb("xa_kT", [128, 8, NMEM], BF16)
        vtok = sb("xa_vtok", [128, 2, 1024], BF16)
        g_xa = C.gains[lp + "xa_norm"]
        g_mem = C.gains[lp + "mem_norm"]
        wload(P, C, wq, W[lp + "xa_wq"], 8, 1024, name="xa_wq")
        wload(P, C, wo, W[lp + "xa_wo"], 8, 1024, gain=None, name="xa_wo")
        P.barrier()
        xts = [sb(f"xa_xt{i}", [128, 8, TT]) for i in range(2)]
        hns = [sb(f"xa_hn{i}", [128, 8, TT], BF16) for i in range(2)]
        xt = xts[0]
        hn = hns[0]
        sq = sb("xa_sq", [128, 8, TT], BF16)
        rstd = sb("xa_rstd", [128, TT])
        with ExitStack() as es2:
            wkv = es2.enter_context(nc.sbuf_tensor("sb_" + lp + "xa_wkv", [128, 8, 2048], BF16))
            memx = es2.enter_context(nc.sbuf_tensor("sb_" + lp + "xa_memx", [128, 8, NMEM], F32))
            memn = es2.enter_context(nc.sbuf_tensor("sb_" + lp + "xa_memn", [128, 8, NMEM], BF16))
            wload(P, C, wkv, W[lp + "xa_wkv"], 8, 2048, name="xa_wkv")
            P.barrier()
            norm_tile(P, C, C.memT, 0, memx, ("memx",), memn, ("memn",), sq, rstd, ncols=NMEM, dkey="memx", gain=g_mem)
            for fb in range(8):
                pst, psk = next_ps(C)
                mm_chain(P, pst[:, :NMEM], [(wkv[:, c, fb * 128:(fb + 1) * 128], memn[:, c, :]) for c in range(8)],
                         reads=[*wkeys(C, "xa_wkv", fb * 128), ("memn", "a"), ("memn", "b")], writes=[psk])
                P.op(ACT, lambda e, pst=pst, fb=fb: e.copy(out=kT[:, fb, :], in_=pst[:, :NMEM]), reads=[psk], writes=[("kT",)])
            for mb in range(2):
                for hf in range(2):
                    pst, psk = next_ps(C)
                    mm_chain(P, pst[:, :], [(memn[:, c, mb * 128:(mb + 1) * 128], wkv[:, c, 1024 + hf * 512:1024 + (hf + 1) * 512]) for c in range(8)],
                             reads=[*wkeys(C, "xa_wkv", 1024 + hf * 512, 512), ("memn", "a"), ("memn", "b")], writes=[psk])
                    P.op(ACT, lambda e, pst=pst, mb=mb, hf=hf: e.copy(out=vtok[:, mb, hf * 512:(hf + 1) * 512], in_=pst[:, :]),
                         reads=[psk], writes=[("vtok",)])
            P.barrier()
        qT = sb("xa_qT", [128, 8, TT], BF16)
        oT = sb("xa_oT", [128, 8, TT], BF16)
        pexp = [sb(f"xa_pexp{i}", [128, NMEM]) for i in range(4)]
        pn = [sb(f"xa_pn{i}", [128, NMEM], BF16) for i in range(4)]
        pT = [sb(f"xa_pT{i}", [128, 2, 128], BF16) for i in range(8)]
        st = [sb(f"xa_st{i}", [128, 4]) for i in range(4)]
        norm_tile(P, C, xres, 0, xts[0], ("xt", 0), hns[0], ("hn0",), sq, rstd, gain=g_xa)
        for j in range(NT):
            bb_ = j % 2
            xt = xts[bb_]
            hn = hns[bb_]
            XK = ("xt", bb_)
            HN = [("hn%d" % bb_, "a"), ("hn%d" % bb_, "b")]
            for fb in range(8):
                pst, psk = next_ps(C)
                mm_chain(P, pst[:, :], [(wq[:, c, fb * 128:(fb + 1) * 128], hn[:, c, :]) for c in range(8)],
                         reads=[*wkeys(C, "xa_wq", fb * 128), *HN], writes=[psk])
                P.op(ACT, lambda e, pst=pst, fb=fb: e.copy(out=qT[:, fb, :], in_=pst[:, :]), reads=[psk], writes=[("qT", fb)])
            units = [(hh, tb) for hh in range(4) for tb in range(4)]

            def s1(i):
                hh, tb = units[i]
                u = i % 4
                pst, psk = next_ps(C)
                mm_chain(P, pst[:, :NMEM], [(qT[:, 2 * hh + dc, tb * 128:(tb + 1) * 128], kT[:, 2 * hh + dc, :]) for dc in range(2)],
                         reads=[("qT", 2 * hh), ("qT", 2 * hh + 1), ("kT",)], writes=[psk])
                s_ = st[u]
                P.op(DVE, lambda e: e.reduce_max(out=s_[:, 0:1], in_=pst[:, :NMEM], axis=AX.X), reads=[psk], writes=[("st", u, 0)])
                P.op(DVE, lambda e: e.tensor_scalar(out=s_[:, 1:2], in0=s_[:, 0:1], scalar1=-scale, scalar2=None, op0=ALU.mult),
                     reads=[("st", u, 0)], writes=[("st", u, 1)])
                P.op(ACT, lambda e: e.activation(out=pexp[u][:, :], in_=pst[:, :NMEM], func=AF.Exp, scale=scale, bias=s_[:, 1:2], accum_out=s_[:, 2:3]),
                     reads=[psk, ("st", u, 1)], writes=[("pexp", u), ("st", u, 2)])

            def s1b(i):
                u = i % 4
                s_ = st[u]
                P.op(DVE, lambda e: e.reciprocal(out=s_[:, 3:4], in_=s_[:, 2:3]), reads=[("st", u, 2)], writes=[("st", u, 3)])
                P.op(DVE, lambda e: e.tensor_scalar(out=pn[u][:, :], in0=pexp[u][:, :], scalar1=s_[:, 3:4], scalar2=None, op0=ALU.mult),
                     reads=[("pexp", u), ("st", u, 3)], writes=[("pn", u)])

            def s2(i):
                hh, tb = units[i]
                u = i % 4
                pbt, pbk_ = next_psb(C)

                def trf(e):
                    ins = None
                    for mb_ in range(2):
                        ins = e.transpose(pbt[:, mb_ * 128:(mb_ + 1) * 128], pn[u][:, mb_ * 128:(mb_ + 1) * 128], C.ident_bf[:, :])
                    return ins
                P.op(PE, trf, reads=[("pn", u)], writes=[pbk_])
                pti = (hh % 2) * 4 + tb
                P.op(ACT, lambda e: e.copy(out=pT[pti][:, :, :], in_=pbt[:, 0:256].rearrange("p (a b) -> p a b", a=2)),
                     reads=[pbk_], writes=[("pT", pti)])

            def s3(hh):
                for dvb in range(2):
                    pst, psk = next_ps(C)

                    def pvf(e, pst=pst, dvb=dvb):
                        ins = None
                        for tb in range(4):
                            for mb_ in range(2):
                                ins = e.matmul(pst[:, tb * 128:(tb + 1) * 128], lhsT=vtok[:, mb_, hh * 256 + dvb * 128:hh * 256 + (dvb + 1) * 128],
                                               rhs=pT[(hh % 2) * 4 + tb][:, mb_, :], start=(mb_ == 0), stop=(mb_ == 1))
                        return ins
                    P.op(PE, pvf, reads=[("vtok",)] + [("pT", (hh % 2) * 4 + tb) for tb in range(4)], writes=[psk])
                    P.op(ACT, lambda e, pst=pst, dvb=dvb: e.copy(out=oT[:, 2 * hh + dvb, :], in_=pst[:, :]), reads=[psk], writes=[("oT", 2 * hh + dvb)])

            LAG = 3
            for i in range(len(units) + LAG):
                if i < len(units):
                    s1(i)
                if 0 <= i - 1 < len(units):
                    s1b(i - 1)
                k_ = i - LAG
                if 0 <= k_ < len(units):
                    s2(k_)
                    if units[k_][1] == 3:
                        s3(units[k_][0])
            if j + 1 < NT:
                norm_tile(P, C, xres, j + 1, xts[1 - bb_], ("xt", 1 - bb_), hns[1 - bb_], ("hn%d" % (1 - bb_),), sq, rstd, gain=g_xa)
            for fb in range(8):
                pst, psk = next_ps(C)
                mm_chain(P, pst[:, :], [(wo[:, c, fb * 128:(fb + 1) * 128], oT[:, c, :]) for c in range(8)],
                         reads=wkeys(C, "xa_wo", fb * 128) + [("oT", c) for c in range(8)], writes=[psk])
                P.op(DVE, lambda e, pst=pst, fb=fb, xt=xt: e.tensor_tensor(out=xt[:, fb, :], in0=pst[:, :], in1=xt[:, fb, :], op=ALU.add),
                     reads=[psk, XK], writes=[XK])
            dstv = xres.rearrange("(c p) t -> p c t", p=128)
            P.op(POOL, lambda e, j=j, xt=xt: e.dma_start(out=dstv[:, :, j * TT:(j + 1) * TT], in_=xt[:, :, :]), reads=[XK], dkey=("xst", bb_))
        P.barrier()


def stage_ffn(P, C, lp, final=False):
    nc = P.nc
    W = C.w
    xres = C.xres
    with ExitStack() as es:
        sb = lambda name, shape, dt=F32: es.enter_context(nc.sbuf_tensor("sb_" + lp + name, list(shape), dt))
        wup = sb("ff_wup", [128, 8, 2 * FFN], BF16)
        wdn = sb("ff_wdn", [128, NPAIR, 1024], BF16)
        cw = sb("ff_cw", [128, 3, 2 * NPAIR])
        g_f = C.gains[lp + "ffn_norm"]
        wload(P, C, wup, W[lp + "ffn_w_up"], 8, 2 * FFN, name="ff_wup")
        wload(P, C, wdn, W[lp + "ffn_w_down"], NPAIR, 1024, gain=None, name="ff_wdn")
        P.op(SP, lambda e: e.dma_start(out=cw[:, :, :], in_=W[lp + "ffn_conv"]), writes=[("cw",)], dkey=("cw",))
        P.barrier()
        xts = [sb(f"ff_xt{i}", [128, 8, TT]) for i in range(2)]
        hn = sb("ff_hn", [128, 8, TT], BF16)
        rstd = sb("ff_rstd", [128, TT])
        rstd2 = rstd
        act = sb("ff_act", [128, NPAIR, TT], BF16)
        sq = act[:, 0:8, :]
        sqk = [("act", c) for c in range(8)]
        acc = [sb(f"ff_acc{i}", [128, TT]) for i in range(4)]
        halo = sb("ff_halo", [128, 2 * NPAIR, 2])
        corr = sb("ff_corr", [128, 2 * NPAIR, 2])
        ctmp = sb("ff_ctmp", [128, 2 * NPAIR])
        sqs = [sb("ff_sqs0", [128, TT], BF16)]
        load_xt(P, C, xres, 0, xts[0], ("xt", 0))
        norm_chunked(P, C, xts[0], ("xt", 0), hn, ("hn",), sqs, rstd, g_f)
        for j in range(NT):
            xt = xts[j % 2]
            fo = xt
            XK = ("xt", j % 2)
            if j + 1 < NT:
                load_xt(P, C, xres, j + 1, xts[1 - j % 2], ("xt", 1 - j % 2))
            if j > 0:
                hk = [("halo", b) for b in range(2 * NPAIR)]
                P.op(POOL, lambda e: e.tensor_tensor(out=corr[:, :, 0], in0=halo[:, :, 1], in1=cw[:, 1, :], op=ALU.mult), reads=hk, writes=[("corr",)])
                P.op(POOL, lambda e: e.tensor_tensor(out=ctmp[:, :], in0=halo[:, :, 0], in1=cw[:, 0, :], op=ALU.mult), reads=hk, writes=[("ctmp",)])
                P.op(POOL, lambda e: e.tensor_tensor(out=corr[:, :, 0], in0=corr[:, :, 0], in1=ctmp[:, :], op=ALU.add), reads=[("corr",), ("ctmp",)], writes=[("corr",)])
                P.op(POOL, lambda e: e.tensor_tensor(out=corr[:, :, 1], in0=halo[:, :, 1], in1=cw[:, 0, :], op=ALU.mult), reads=hk + [("corr",)], writes=[("corr",)])
            pend_pairs = []
            for pr in range(NPAIR):
                accs = {}
                for kind in range(2):
                    blk = kind * NPAIR + pr
                    u = (pr * 2 + kind) % 4
                    pst, psk = next_ps(C)
                    mm_chain(P, pst[:, :], [(wup[:, c, blk * 128:(blk + 1) * 128], hn[:, c, :]) for c in range(8)],
                             reads=[*wkeys(C, "ff_wup", blk * 128), ("hn", "a"), ("hn", "b")], writes=[psk])
                    a_ = acc[u]
                    P.op(ACT, lambda e, a_=a_, pst=pst, blk=blk: e.activation(out=a_[:, :], in_=pst[:, :], func=AF.Copy, scale=cw[:, 2, blk:blk + 1]),
                         reads=[psk], writes=[("acc", u)])
                    if j < NT - 1:
                        P.op(ACT, lambda e, pst=pst, blk=blk: e.copy(out=halo[:, blk, :], in_=pst[:, TT - 2:TT]), reads=[psk], writes=[("halo", blk)])
                    P.op(DVE, lambda e, a_=a_, pst=pst, blk=blk: e.scalar_tensor_tensor(out=a_[:, 1:TT], in0=pst[:, 0:TT - 1], scalar=cw[:, 1, blk:blk + 1], in1=a_[:, 1:TT],
                                                                                     op0=ALU.mult, op1=ALU.add),
                         reads=[psk, ("acc", u)], writes=[("acc", u)])
                    P.op(DVE, lambda e, a_=a_, pst=pst, blk=blk: e.scalar_tensor_tensor(out=a_[:, 2:TT], in0=pst[:, 0:TT - 2], scalar=cw[:, 0, blk:blk + 1], in1=a_[:, 2:TT],
                                                                                     op0=ALU.mult, op1=ALU.add),
                         reads=[psk, ("acc", u)], writes=[("acc", u)])
                    if j > 0:
                        P.op(POOL, lambda e, a_=a_, blk=blk: e.tensor_tensor(out=a_[:, 0:2], in0=a_[:, 0:2], in1=corr[:, blk, :], op=ALU.add),
                             reads=[("corr",), ("acc", u)], writes=[("acc", u)])
                    accs[kind] = (a_, u)
                pend_pairs.append((accs[0], accs[1], pr))
                while len(pend_pairs) > (1 if pr < NPAIR - 1 else 0):
                    (au, uu), (ag, ug), pr_ = pend_pairs.pop(0)
                    P.op(ACT, lambda e, ag=ag: e.activation(out=ag[:, :], in_=ag[:, :], func=AF.Silu), reads=[("acc", ug)], writes=[("acc", ug)])
                    P.op(POOL, lambda e, ag=ag, au=au, pr_=pr_: e.tensor_tensor(out=act[:, pr_, :], in0=ag[:, :], in1=au[:, :], op=ALU.mult),
                         reads=[("acc", ug), ("acc", uu)], writes=[("act", pr_)])
            if j + 1 < NT:
                norm_chunked(P, C, xts[1 - j % 2], ("xt", 1 - j % 2), hn, ("hn",), sqs, rstd2, g_f)
            for fb in range(8):
                pst, psk = next_ps(C)
                mm_chain(P, pst[:, :], [(wdn[:, c, fb * 128:(fb + 1) * 128], act[:, c, :]) for c in range(NPAIR)],
                         reads=wkeys(C, "ff_wdn", fb * 128) + [("act", c) for c in range(NPAIR)], writes=[psk])
                P.op(DVE, lambda e, pst=pst, fb=fb, xt=xt: e.tensor_tensor(out=xt[:, fb, :], in0=pst[:, :], in1=xt[:, fb, :], op=ALU.add),
                     reads=[psk, XK], writes=[XK])
            if not final:
                dstv = xres.rearrange("(c p) t -> p c t", p=128)
                P.op(POOL, lambda e, j=j, xt=xt: e.dma_start(out=dstv[:, :, j * TT:(j + 1) * TT], in_=xt[:, :, :]), reads=[XK], dkey=("xst", j % 2))
            else:
                gfin = C.gains["final_norm"]
                P.op(ACT, lambda e, xt=xt: e.activation(out=sq[:, :, :], in_=xt[:, :, :], func=AF.Square), reads=[XK], writes=sqk)
                pst, psk = next_ps(C)
                mm_chain(P, pst[:, :], [(C.ones_bf[:, :], sq[:, c, :]) for c in range(8)], reads=sqk, writes=[psk])
                P.op(ACT, lambda e, pst=pst: e.activation(out=rstd[:, :], in_=pst[:, :], func=AF.Sqrt, scale=1.0 / D, bias=C.eps_col[:, 0:1]),
                     reads=[psk], writes=[("rstd",)])
                P.op(DVE, lambda e: e.reciprocal(out=rstd[:, :], in_=rstd[:, :]), reads=[("rstd",)], writes=[("rstd",)])
                for c in range(8):
                    P.op(DVE, lambda e, c=c, xt=xt, fo=fo: e.scalar_tensor_tensor(out=fo[:, c, :], in0=xt[:, c, :], scalar=gfin[:, c:c + 1], in1=rstd[:, :],
                                                                                          op0=ALU.mult, op1=ALU.mult),
                         reads=[XK, ("rstd",), gain_key(None)], writes=[XK])
                dstv = C.outT.rearrange("(c p) t -> p c t", p=128)
                P.op(POOL, lambda e, j=j, fo=fo: e.dma_start(out=dstv[:, :, j * TT:(j + 1) * TT], in_=fo[:, :, :]), reads=[XK], dkey=("ost", j % 2))
        P.barrier()


RET_GAMMA = [1.0 - 2.0 ** (-5.0 - h) for h in range(4)]


def stage_l0_inproj(P, C):
    nc = P.nc
    xres = C.xres
    with ExitStack() as es:
        sb = lambda name, shape, dt=F32: es.enter_context(nc.sbuf_tensor(name, list(shape), dt))
        W = sb("A_W", [128, 8, 2560], BF16)
        Wsw = sb("A_Wsw", [128, 8, 1024], BF16)
        g_mix = C.gains["l0_mix_norm"]
        wload(P, C, W, C.w["l0_w_in"], 8, 2560, name="A_W")
        wload(P, C, Wsw, C.w["l0_w_in_sw"], 8, 1024, name="A_Wsw")
        gq = sb("A_gq", [128, 4, TT])
        P.op(SP, lambda e: e.dma_start(out=gq[:, :, :], in_=C.cst["gq_tab"]), writes=[("gq",)], dkey=("gq",))
        P.barrier()
        xts = [sb(f"A_xt{i}", [128, 8, TT]) for i in range(2)]
        hn = sb("A_hn", [128, 8, TT], BF16)
        sq = sb("A_sq", [128, 8, TT], BF16)
        rstd = sb("A_rstd", [128, TT])
        rot = sb("A_rot", [128, 4, TT])
        t1 = [sb(f"A_t1{i}", [128, TT]) for i in range(2)]
        t2 = [sb(f"A_t2{i}", [128, TT]) for i in range(2)]
        qo = sb("A_qo", [128, 4, TT], BF16)
        qdo = sb("A_qdo", [128, 4, TT], BF16)
        ko = sb("A_ko", [128, 4, TT], BF16)
        go = sb("A_go", [128, 4, TT], BF16)
        uo = sb("A_uo", [128, 4, TT], BF16)
        vo = sb("A_vo", [128, 4, TT], BF16)
        cnt = 0
        load_xt(P, C, xres, 0, xts[0], ("xt", 0))
        for j in range(NT):
            norm_tile(P, C, xres, j, xts[j % 2], ("xt", j % 2), hn, ("hn",), sq, rstd, gain=g_mix, load=False)
            if j + 1 < NT:
                load_xt(P, C, xres, j + 1, xts[1 - j % 2], ("xt", 1 - j % 2))
            P.op(SP, lambda e, j=j: e.dma_start(out=rot[:, :, :], in_=C.cst["rot_tab"][:, :, j * TT:(j + 1) * TT]), writes=[("rot",)], dkey=("rot",))
            for kind in range(2):
                for h in range(4):
                    col = kind * 512 + h * 128
                    pa, pak = next_ps(C)
                    mm_chain(P, pa[:, :], [(W[:, c, col:col + 128], hn[:, c, :]) for c in range(8)], reads=[*wkeys(C, "A_W", col), ("hn", "a"), ("hn", "b")], writes=[pak])
                    pb, pbk = next_ps(C)
                    mm_chain(P, pb[:, :], [(Wsw[:, c, col:col + 128], hn[:, c, :]) for c in range(8)], reads=[*wkeys(C, "A_Wsw", col), ("hn", "a"), ("hn", "b")], writes=[pbk])
                    u = cnt % 2
                    cnt += 1
                    a_, b_ = t1[u], t2[u]
                    P.op(DVE, lambda e, a_=a_, pa=pa, kind=kind: e.tensor_tensor(out=a_[:, :], in0=pa[:, :], in1=rot[:, 2 * kind, :], op=ALU.mult),
                         reads=[pak, ("rot",)], writes=[("t1", u)])
                    P.op(DVE, lambda e, b_=b_, pb=pb, kind=kind: e.tensor_tensor(out=b_[:, :], in0=pb[:, :], in1=rot[:, 2 * kind + 1, :], op=ALU.mult),
                         reads=[pbk, ("rot",)], writes=[("t2", u)])
                    P.op(POOL, lambda e, a_=a_, b_=b_: e.tensor_tensor(out=a_[:, :], in0=a_[:, :], in1=b_[:, :], op=ALU.add),
                         reads=[("t1", u), ("t2", u)], writes=[("t1", u)])
                    if kind == 0:
                        P.op(ACT, lambda e, a_=a_, h=h: e.copy(out=qo[:, h, :], in_=a_[:, :]), reads=[("t1", u)], writes=[("qo",)])
                        P.op(POOL, lambda e, a_=a_, h=h: e.tensor_tensor(out=qdo[:, h, :], in0=a_[:, :], in1=gq[:, h, :], op=ALU.mult),
                             reads=[("t1", u)], writes=[("qdo",)])
                    else:
                        P.op(ACT, lambda e, a_=a_, h=h: e.copy(out=ko[:, h, :], in_=a_[:, :]), reads=[("t1", u)], writes=[("ko",)])
            for tb in range(4):
                pa, pak = next_ps(C)
                mm_chain(P, pa[:, :], [(hn[:, c, tb * 128:(tb + 1) * 128], W[:, c, 1024:1536]) for c in range(8)], reads=[*wkeys(C, "A_W", 1024, 512), ("hn", "a"), ("hn", "b")], writes=[pak])
                P.op(ACT, lambda e, pa=pa, tb=tb: e.copy(out=vo[:, tb, :], in_=pa[:, :]), reads=[pak], writes=[("vo",)])
            for fb in range(4):
                pa, pak = next_ps(C)
                mm_chain(P, pa[:, :], [(W[:, c, 1536 + fb * 128:1536 + (fb + 1) * 128], hn[:, c, :]) for c in range(8)], reads=[*wkeys(C, "A_W", 1536 + fb * 128), ("hn", "a"), ("hn", "b")], writes=[pak])
                P.op(ACT, lambda e, pa=pa, fb=fb: e.activation(out=go[:, fb, :], in_=pa[:, :], func=AF.Silu), reads=[pak], writes=[("go",)])
            for fb in range(4):
                pa, pak = next_ps(C)
                mm_chain(P, pa[:, :], [(W[:, c, 2048 + fb * 128:2048 + (fb + 1) * 128], hn[:, c, :]) for c in range(8)], reads=[*wkeys(C, "A_W", 2048 + fb * 128), ("hn", "a"), ("hn", "b")], writes=[pak])
                P.op(ACT, lambda e, pa=pa, fb=fb: e.copy(out=uo[:, fb, :], in_=pa[:, :]), reads=[pak], writes=[("uo",)])
            for nm, tl in [("qT", qo), ("qdT", qdo), ("kT", ko), ("gT", go), ("uT", uo)]:
                dv = C.scr[nm].rearrange("(h p) t -> p h t", p=128)
                P.op(POOL, lambda e, dv=dv, tl=tl, j=j: e.dma_start(out=dv[:, :, j * TT:(j + 1) * TT], in_=tl[:, :, :]),
                     reads=[({"qT": "qo", "qdT": "qdo", "kT": "ko", "gT": "go", "uT": "uo"}[nm],)], dkey=("Ast", nm))
            dv = C.scr["vtok"].rearrange("(n p) f -> p n f", p=128)
            P.op(POOL, lambda e, dv=dv, j=j: e.dma_start(out=dv[:, j * 4:(j + 1) * 4, :], in_=vo[:, :, :]), reads=[("vo",)], dkey=("Ast", "v"))
        P.barrier()


def stage_retention(P, C):
    nc = P.nc
    with ExitStack() as es:
        sb = lambda name, shape, dt=F32: es.enter_context(nc.sbuf_tensor(name, list(shape), dt))
        kT = [sb(f"R_kT{i}", [128, S], BF16) for i in range(2)]
        qT = [sb(f"R_qT{i}", [128, S], BF16) for i in range(2)]
        qdT = [sb(f"R_qdT{i}", [128, S], BF16) for i in range(2)]
        gT = [sb(f"R_gT{i}", [128, S], BF16) for i in range(2)]
        vt = [sb(f"R_vt{i}", [128, 32, 128], BF16) for i in range(2)]
        dtab = sb("R_dtab", [128, 4, 128])
        kd = sb("R_kd", [128, 4])
        rn = sb("R_rn", [128, 4])
        state = sb("R_state", [128, 128])
        state_bfs = [sb(f"R_state_bf{i}", [128, 128], BF16) for i in range(2)]
        scm = [sb(f"R_scm{i}", [128, 128], BF16) for i in range(2)]
        kdec = [sb(f"R_kdec{i}", [128, 128], BF16) for i in range(2)]
        o_sb = [sb(f"R_osb{i}", [128, TT]) for i in range(2)]
        osq = [sb(f"R_osq{i}", [128, TT], BF16) for i in range(2)]
        rr = [sb(f"R_rr{i}", [128, TT]) for i in range(2)]
        mo = [sb(f"R_mo{i}", [128, TT], BF16) for i in range(2)]
        P.op(SP, lambda e: e.dma_start(out=dtab[:, :, :], in_=C.cst["dt_tab"]), writes=[("dtab",)], dkey=("dtab",))
        P.op(SP, lambda e: e.dma_start(out=kd[:, :], in_=C.cst["kd_tab"]), writes=[("kd",)], dkey=("kd",))
        P.op(SP, lambda e: e.dma_start(out=rn[:, :], in_=C.w["l0_ret_norm"]), writes=[("rn",)], dkey=("rn",))
        vview = C.scr["vtok"].rearrange("(n p) f -> p n f", p=128)
        grp = 0
        for h in range(4):
            hb = h % 2
            for nm, tl in [("kT", kT), ("qT", qT), ("qdT", qdT), ("gT", gT)]:
                P.op(SP, lambda e, nm=nm, tl=tl, h=h, hb=hb: e.dma_start(out=tl[hb][:, :], in_=C.scr[nm][h * 128:(h + 1) * 128, :]),
                     writes=[(nm, hb)], dkey=("Rld", nm, hb))
            P.op(SP, lambda e, h=h, hb=hb: e.dma_start(out=vt[hb][:, :, :], in_=vview[:, :, h * 128:(h + 1) * 128]), writes=[("vt", hb)], dkey=("Rld", "v", hb))
            P.op(POOL, lambda e: e.memset(state[:, :], 0.0), writes=[("state",)])
            P.op(POOL, lambda e: e.memset(state_bfs[0][:, :], 0.0), writes=[("state_bf", 0)])
            gam_c = RET_GAMMA[h] ** 128
            for n in range(32):
                cs = slice(n * 128, (n + 1) * 128)
                u = n % 2
                if n % 4 == 0:
                    po, pok = next_ps(C, 0, 2)
                    grp += 1
                psc, psck = next_ps(C, 2, 6)
                mm_chain(P, psc[:, 0:128], [(kT[hb][:, cs], qT[hb][:, cs])], reads=[("kT", hb), ("qT", hb)], writes=[psck])
                P.op(DVE, lambda e, psc=psc, u=u, h=h: e.tensor_tensor(out=scm[u][:, :], in0=psc[:, 0:128], in1=dtab[:, h, :], op=ALU.mult),
                     reads=[psck, ("dtab",)], writes=[("scm", u)])
                pbt, pbk_ = next_psb(C)
                P.op(PE, lambda e, cs=cs, hb=hb, pbt=pbt: e.transpose(pbt[:, 0:128], kT[hb][:, cs], C.ident_bf[:, :]), reads=[("kT", hb)], writes=[pbk_])
                P.op(ACT, lambda e, u=u, h=h, pbt=pbt: e.activation(out=kdec[u][:, :], in_=pbt[:, 0:128], func=AF.Copy, scale=kd[:, h:h + 1]),
                     reads=[pbk_, ("kd",)], writes=[("kdec", u)])
                pkv, pkvk = next_ps(C, 2, 6)
                mm_chain(P, pkv[:, 0:128], [(kdec[u][:, :], vt[hb][:, n, :])], reads=[("kdec", u), ("vt", hb)], writes=[pkvk])
                P.op(DVE, lambda e, pkv=pkv, gam_c=gam_c: e.scalar_tensor_tensor(out=state[:, :], in0=state[:, :], scalar=gam_c, in1=pkv[:, 0:128],
                                                                              op0=ALU.mult, op1=ALU.add),
                     reads=[pkvk, ("state",)], writes=[("state",)])
                sb_n = state_bfs[(n + 1) % 2]
                P.op(ACT, lambda e, sb_n=sb_n: e.copy(out=sb_n[:, :], in_=state[:, :]), reads=[("state",)], writes=[("state_bf", (n + 1) % 2)])
                oc = slice((n % 4) * 128, (n % 4 + 1) * 128)
                sb_c = state_bfs[n % 2]
                mm_chain(P, po[:, oc], [(vt[hb][:, n, :], scm[u][:, :]), (sb_c[:, :], qdT[hb][:, cs])],
                         reads=[("vt", hb), ("scm", u), ("state_bf", n % 2), ("qdT", hb)], writes=[pok])
                if n % 4 == 3:
                    v = grp % 2
                    ts_ = slice((n // 4) * TT, (n // 4 + 1) * TT)
                    P.op(ACT, lambda e, po=po, v=v: e.copy(out=o_sb[v][:, :], in_=po[:, :]), reads=[pok], writes=[("osb", v)])
                    P.op(ACT, lambda e, po=po, v=v: e.activation(out=osq[v][:, :], in_=po[:, :], func=AF.Square), reads=[pok], writes=[("osq", v)])
                    pss, pssk = next_ps(C, 2, 6)
                    mm_chain(P, pss[:, :], [(C.ones_bf[:, :], osq[v][:, :])], reads=[("osq", v)], writes=[pssk])
                    P.op(ACT, lambda e, pss=pss, v=v: e.activation(out=rr[v][:, :], in_=pss[:, :], func=AF.Sqrt, scale=1.0 / 128, bias=C.eps_col[:, 0:1]),
                         reads=[pssk], writes=[("rr", v)])
                    P.op(DVE, lambda e, v=v: e.reciprocal(out=rr[v][:, :], in_=rr[v][:, :]), reads=[("rr", v)], writes=[("rr", v)])
                    P.op(DVE, lambda e, v=v: e.tensor_tensor(out=o_sb[v][:, :], in0=o_sb[v][:, :], in1=rr[v][:, :], op=ALU.mult),
                         reads=[("osb", v), ("rr", v)], writes=[("osb", v)])
                    P.op(DVE, lambda e, v=v, h=h, hb=hb, ts_=ts_: e.scalar_tensor_tensor(out=mo[v][:, :], in0=o_sb[v][:, :], scalar=rn[:, h:h + 1], in1=gT[hb][:, ts_],
                                                                                      op0=ALU.mult, op1=ALU.mult),
                         reads=[("osb", v), ("rn",), ("gT", hb)], writes=[("mo", v)])
                    P.op(POOL, lambda e, v=v, h=h, ts_=ts_: e.dma_start(out=C.scr["mT"][h * 128:(h + 1) * 128, ts_], in_=mo[v][:, :]),
                         reads=[("mo", v)], dkey=("Rst", v))
        P.barrier()


def stage_s5(P, C):
    nc = P.nc
    TWO_PI = 2.0 * math.pi
    with ExitStack() as es:
        sb = lambda name, shape, dt=F32: es.enter_context(nc.sbuf_tensor(name, list(shape), dt))

        def tt(eng, out, a, b, op, rk, wk):
            P.op(eng, lambda e: e.tensor_tensor(out=out, in0=a, in1=b, op=op), reads=rk, writes=wk)

        def ts(eng, out, a, s1, op0, rk=(), wk=()):
            P.op(eng, lambda e: e.tensor_scalar(out=out, in0=a, scalar1=s1, scalar2=None, op0=op0), reads=rk, writes=wk)

        def act(out, a, func, rk, wk, scale=1.0, bias=None):
            if bias is None:
                P.op(ACT, lambda e: e.activation(out=out, in_=a, func=func, scale=scale), reads=rk, writes=wk)
            else:
                P.op(ACT, lambda e: e.activation(out=out, in_=a, func=func, scale=scale, bias=bias), reads=rk, writes=wk)

        def frac_sincos(eng, x, xi, xf, sin_out, cos_out, key, hp, np_, outkey=None):
            P.op(eng, lambda e: e.tensor_copy(out=xi, in_=x), reads=[(key, "x")], writes=[(key, "xi")])
            P.op(eng, lambda e: e.tensor_copy(out=xf, in_=xi), reads=[(key, "xi")], writes=[(key, "xf")])
            P.op(eng, lambda e: e.tensor_tensor(out=x, in0=x, in1=xf, op=ALU.subtract), reads=[(key, "x"), (key, "xf")], writes=[(key, "x")])
            P.op(DVE, lambda e: e.scalar_tensor_tensor(out=xf, in0=x, scalar=-1.0, in1=x, op0=ALU.mult, op1=ALU.max),
                 reads=[(key, "x"), (key, "xf")], writes=[(key, "xf")])
            ok = key if outkey is None else outkey
            P.op(ACT, lambda e: e.activation(out=sin_out, in_=x, func=AF.Sin, scale=TWO_PI), reads=[(key, "x")], writes=[(ok, "sin")])
            P.op(ACT, lambda e: e.activation(out=cos_out, in_=xf, func=AF.Sin, scale=-TWO_PI, bias=hp[0:np_, 0:1]), reads=[(key, "xf")], writes=[(ok, "cos")])

        halfpi = sb("S_halfpi", [128, 1])
        r_s = sb("S_r", [128, 32])
        f_s = sb("S_f", [128, 32])
        cs_tab = sb("S_cstab", [128, 32, 8])
        sn_tab = sb("S_sntab", [128, 32, 8])
        fbd = sb("S_fbd", [16, 32, 128])
        B1f = sb("S_B1f", [16, 32, 128])
        B2f = sb("S_B2f", [16, 32, 128])
        C1f = sb("S_C1f", [128, 32, 16])
        C2f = sb("S_C2f", [128, 32, 16])
        Dd = sb("S_Dd", [16, 32, 16], BF16)
        tloc = sb("S_tloc", [128, TT])
        t0s = sb("S_t0s", [128, 32, 8])
        t0b = sb("S_t0b", [16, 8, 128])
        ones5 = sb("S_ones", [128, TT])
        es_in = ExitStack()
        tb = lambda name, shape, dt=F32: es_in.enter_context(nc.sbuf_tensor(name, list(shape), dt))
        P.op(POOL, lambda e: e.memset(halfpi[:, :], 0.5 * math.pi), writes=[("halfpi",)])
        P.op(POOL, lambda e: e.memset(ones5[:, :], 1.0), writes=[("ones5",)])
        P.op(SP, lambda e: e.dma_start(out=tloc[:, :], in_=C.cst["tpos"][:, 0:TT]), writes=[("tloc",)], dkey=("s5c", 9))
        P.op(SP, lambda e: e.dma_start(out=t0s[:, :, :], in_=C.cst["t0s"]), writes=[("t0s",)], dkey=("s5c", 10))
        P.op(SP, lambda e: e.dma_start(out=t0b[:, :, :], in_=C.cst["t0b"]), writes=[("t0b",)], dkey=("s5c", 11))
        lam_s = tb("S_lam_s", [128, 2, 32])
        ldt_s = tb("S_ldt_s", [128, 32])
        P.op(SP, lambda e: e.dma_start(out=lam_s[:, :, :], in_=C.w["s5_lam_s"]), writes=[("lam_s",)], dkey=("s5c", 0))
        P.op(SP, lambda e: e.dma_start(out=ldt_s[:, :], in_=C.w["s5_ldt_s"]), writes=[("ldt_s",)], dkey=("s5c", 1))
        act(ldt_s[:, :], ldt_s[:, :], AF.Exp, [("ldt_s",)], [("ldt_s",)])
        tt(DVE, r_s[:, :], lam_s[:, 0, :], ldt_s[:, :], ALU.mult, [("lam_s",), ("ldt_s",)], [("r_s",)])
        act(r_s[:, :], r_s[:, :], AF.Exp, [("r_s",)], [("r_s",)])
        tt(DVE, f_s[:, :], lam_s[:, 1, :], ldt_s[:, :], ALU.mult, [("lam_s",), ("ldt_s",)], [("f_s",)])
        ts(DVE, f_s[:, :], f_s[:, :], 1.0 / TWO_PI, ALU.mult, rk=[("f_s",)], wk=[("f_s",)])
        fi_s = tb("S_fi_s", [128, 32], I32)
        ff_s = tb("S_ff_s", [128, 32])
        P.op(DVE, lambda e: e.tensor_copy(out=fi_s[:, :], in_=f_s[:, :]), reads=[("f_s",)], writes=[("fi_s",)])
        P.op(DVE, lambda e: e.tensor_copy(out=ff_s[:, :], in_=fi_s[:, :]), reads=[("fi_s",)], writes=[("ff_s",)])
        tt(DVE, f_s[:, :], f_s[:, :], ff_s[:, :], ALU.subtract, [("f_s",), ("ff_s",)], [("f_s",)])
        xo = tb("S_xo", [128, 32, 8])
        xoi = tb("S_xoi", [128, 32, 8], I32)
        xof = tb("S_xof", [128, 32, 8])
        P.op(DVE, lambda e: e.tensor_tensor(out=xo[:, :, :], in0=t0s[:, :, :], in1=f_s[:, :].unsqueeze(2).to_broadcast([128, 32, 8]), op=ALU.mult),
             reads=[("f_s",), ("t0s",)], writes=[("xo", "x")])
        frac_sincos(DVE, xo[:, :, :], xoi[:, :, :], xof[:, :, :], sn_tab[:, :, :], cs_tab[:, :, :], "xo", halfpi, 128)
        NB = 32 * 64
        lamb = tb("S_lamb", [16, 2, NB])
        ldtb = tb("S_ldtb", [16, NB])
        bb = tb("S_bb", [16, 2, NB])
        P.op(SP, lambda e: e.dma_start(out=lamb[:, :, :], in_=C.w["s5_lam_b"]), writes=[("lamb",)], dkey=("s5c", 2))
        P.op(SP, lambda e: e.dma_start(out=ldtb[:, :], in_=C.w["s5_ldt_b"]), writes=[("ldtb",)], dkey=("s5c", 3))
        P.op(SP, lambda e: e.dma_start(out=bb[:, :, :], in_=C.w["s5_b_b"]), writes=[("bb",)], dkey=("s5c", 4))
        lr = lamb[:, 0, :]
        li = lamb[:, 1, :]
        mag = tb("S_mag", [16, NB])
        fb_ = tb("S_fb", [16, NB])
        fc_ = tb("S_fc", [16, NB])
        fbi = tb("S_fbi", [16, NB], I32)
        are = tb("S_are", [16, NB])
        aim = tb("S_aim", [16, NB])
        den = tb("S_den", [16, NB])
        tmp = tb("S_tmp", [16, NB])
        zre = tb("S_zre", [16, NB])
        zim = tb("S_zim", [16, NB])
        act(ldtb[:, :], ldtb[:, :], AF.Exp, [("ldtb",)], [("ldtb",)])
        tt(DVE, mag[:, :], lr, ldtb[:, :], ALU.mult, [("lamb",), ("ldtb",)], [("mag",)])
        act(mag[:, :], mag[:, :], AF.Exp, [("mag",)], [("mag",)])
        tt(DVE, fb_[:, :], li, ldtb[:, :], ALU.mult, [("lamb",), ("ldtb",)], [("fbq", "x")])
        ts(DVE, fb_[:, :], fb_[:, :], 1.0 / TWO_PI, ALU.mult, rk=[("fbq", "x")], wk=[("fbq", "x")])
        frac_sincos(DVE, fb_[:, :], fbi[:, :], fc_[:, :], aim[:, :], are[:, :], "fbq", halfpi, 16)
        fb3 = fb_[:, :].rearrange("h (g p) -> h g p", g=32)
        P.op(POOL, lambda e: e.tensor_copy(out=fbd[:, :, 0:64], in_=fb3), reads=[("fbq", "x")], writes=[("fbd", 0)])
        P.op(POOL, lambda e: e.tensor_copy(out=fbd[:, :, 64:128], in_=fb3), reads=[("fbq", "x")], writes=[("fbd", 1)])
        tt(DVE, aim[:, :], aim[:, :], mag[:, :], ALU.mult, [("fbq", "sin"), ("mag",)], [("aim",)])
        tt(DVE, are[:, :], are[:, :], mag[:, :], ALU.mult, [("fbq", "cos"), ("mag",)], [("are",)])
        ts(DVE, are[:, :], are[:, :], -1.0, ALU.add, rk=[("are",)], wk=[("are",)])
        tt(DVE, den[:, :], lr, lr, ALU.mult, [("lamb",)], [("den",)])
        tt(DVE, tmp[:, :], li, li, ALU.mult, [("lamb",)], [("tmp",)])
        tt(DVE, den[:, :], den[:, :], tmp[:, :], ALU.add, [("den",), ("tmp",)], [("den",)])
        P.op(DVE, lambda e: e.reciprocal(out=den[:, :], in_=den[:, :]), reads=[("den",)], writes=[("den",)])
        tt(DVE, zre[:, :], are[:, :], lr, ALU.mult, [("are",), ("lamb",)], [("zre",)])
        tt(DVE, tmp[:, :], aim[:, :], li, ALU.mult, [("aim",), ("lamb",)], [("tmp",)])
        tt(DVE, zre[:, :], zre[:, :], tmp[:, :], ALU.add, [("zre",), ("tmp",)], [("zre",)])
        tt(DVE, zre[:, :], zre[:, :], den[:, :], ALU.mult, [("zre",), ("den",)], [("zre",)])
        tt(DVE, zim[:, :], aim[:, :], lr, ALU.mult, [("aim",), ("lamb",)], [("zim",)])
        tt(DVE, tmp[:, :], are[:, :], li, ALU.mult, [("are",), ("lamb",)], [("tmp",)])
        tt(DVE, zim[:, :], zim[:, :], tmp[:, :], ALU.subtract, [("zim",), ("tmp",)], [("zim",)])
        tt(DVE, zim[:, :], zim[:, :], den[:, :], ALU.mult, [("zim",), ("den",)], [("zim",)])
        bbre = mag
        bbim = den
        br = bb[:, 0, :]
        bi = bb[:, 1, :]
        tt(DVE, bbre[:, :], zre[:, :], br, ALU.mult, [("zre",), ("bb",), ("mag",)], [("mag",)])
        tt(DVE, tmp[:, :], zim[:, :], bi, ALU.mult, [("zim",), ("bb",)], [("tmp",)])
        tt(DVE, bbre[:, :], bbre[:, :], tmp[:, :], ALU.subtract, [("mag",), ("tmp",)], [("mag",)])
        tt(DVE, bbim[:, :], zre[:, :], bi, ALU.mult, [("zre",), ("bb",), ("den",)], [("den",)])
        tt(DVE, tmp[:, :], zim[:, :], br, ALU.mult, [("zim",), ("bb",)], [("tmp",)])
        tt(DVE, bbim[:, :], bbim[:, :], tmp[:, :], ALU.add, [("den",), ("tmp",)], [("den",)])
        bbre3 = bbre[:, :].rearrange("h (g p) -> h g p", g=32)
        bbim3 = bbim[:, :].rearrange("h (g p) -> h g p", g=32)
        P.op(ACT, lambda e: e.copy(out=B1f[:, :, 0:64], in_=bbre3), reads=[("mag",)], writes=[("B1", 0)])
        P.op(ACT, lambda e: e.copy(out=B1f[:, :, 64:128], in_=bbim3), reads=[("den",)], writes=[("B1", 1)])
        P.op(ACT, lambda e: e.copy(out=B2f[:, :, 0:64], in_=bbim3), reads=[("den",)], writes=[("B2", 0)])
        P.op(ACT, lambda e: e.activation(out=B2f[:, :, 64:128], in_=bbre3, func=AF.Copy, scale=-1.0), reads=[("mag",)], writes=[("B2", 1)])
        c1f = tb("S_c1f", [128, 32, 16])
        c2f = tb("S_c2f", [128, 32, 16])
        P.op(SP, lambda e: e.dma_start(out=c1f[:, :, :], in_=C.w["s5_c1"]), writes=[("c1f",)], dkey=("s5c", 5))
        P.op(SP, lambda e: e.dma_start(out=c2f[:, :, :], in_=C.w["s5_c2"]), writes=[("c2f",)], dkey=("s5c", 6))
        P.op(ACT, lambda e: e.copy(out=C1f[0:64, :, :], in_=c1f[0:64, :, :]), reads=[("c1f",)], writes=[("C1", 0)])
        P.op(ACT, lambda e: e.activation(out=C1f[64:128, :, :], in_=c1f[64:128, :, :], func=AF.Copy, scale=-1.0), reads=[("c1f",)], writes=[("C1", 1)])
        P.op(ACT, lambda e: e.activation(out=C2f[:, :, :], in_=c2f[:, :, :], func=AF.Copy, scale=-1.0), reads=[("c2f",)], writes=[("C2",)])
        dt_ = tb("S_dt", [16, 32])
        id16 = tb("S_id16", [16, 16])
        P.op(SP, lambda e: e.dma_start(out=dt_[:, :], in_=C.w["s5_d_t"]), writes=[("dt_",)], dkey=("s5c", 7))
        P.op(SP, lambda e: e.dma_start(out=id16[:, :], in_=C.cst["id16"]), writes=[("id16",)], dkey=("s5c", 8))
        P.op(DVE, lambda e: e.tensor_tensor(out=Dd[:, :, :], in0=id16[:, :].unsqueeze(1).to_broadcast([16, 32, 16]),
                                            in1=dt_[:, :].unsqueeze(2).to_broadcast([16, 32, 16]), op=ALU.mult),
             reads=[("dt_",), ("id16",)], writes=[("Dd",)])
        P.barrier()
        es_in.close()
        ug = [sb(f"S_ug{i}", [16, S], BF16) for i in range(2)]
        yg = [sb(f"S_yg{i}", [16, S], BF16) for i in range(2)]
        rfull = [sb(f"S_rfull{i}", [128, TT]) for i in range(2)]
        _lx = sb("S_lx", [128, TT])
        lx = [_lx, _lx]
        _lxi = sb("S_lxi", [128, TT], I32)
        lxi = [_lxi, _lxi]
        _lxf = sb("S_lxf", [128, TT])
        lxf = [_lxf, _lxf]
        sinL = [sb(f"S_sinL{i}", [128, TT]) for i in range(2)]
        cosL = [sb(f"S_cosL{i}", [128, TT]) for i in range(2)]
        _bx = sb("S_bx", [16, 8, 128])
        bx = [_bx, _bx]
        _bxi = sb("S_bxi", [16, 8, 128], I32)
        bxi = [_bxi, _bxi]
        _bxf = sb("S_bxf", [16, 8, 128])
        bxf = [_bxf, _bxf]
        _bcc = sb("S_bcc", [16, 8, 128])
        bcc = [_bcc, _bcc]
        _bss = sb("S_bss", [16, 8, 128])
        bss = [_bss, _bss]
        _bt1 = sb("S_bt1", [16, 8, 128])
        bt1 = [_bt1, _bt1]
        _bt2 = sb("S_bt2", [16, 8, 128])
        bt2 = [_bt2, _bt2]
        B1p = [sb(f"S_B1p{i}", [16, 8, 128], BF16) for i in range(2)]
        B2p = [sb(f"S_B2p{i}", [16, 8, 128], BF16) for i in range(2)]
        _ct1 = sb("S_ct1", [128, 8, 16])
        ct1 = [_ct1, _ct1]
        _ct2 = sb("S_ct2", [128, 8, 16])
        ct2 = [_ct2, _ct2]
        C1p = [sb(f"S_C1p{i}", [128, 8, 16], BF16) for i in range(2)]
        C2p = [sb(f"S_C2p{i}", [128, 8, 16], BF16) for i in range(2)]
        NX = 4
        NW = 3
        X1 = [sb(f"S_X1{i}", [128, TT]) for i in range(NX)]
        X2 = [sb(f"S_X2{i}", [128, TT]) for i in range(NX)]
        wb = [sb(f"S_w{i}", [128, TT]) for i in range(2)]
        cW = [sb(f"S_cW{i}", [128, TT], BF16) for i in range(NW)]
        sW = [sb(f"S_sW{i}", [128, TT], BF16) for i in range(NW)]

        def prep_group(g):
            gb = g % 2
            P.op(SP, lambda e: e.dma_start(out=ug[gb][:, :], in_=C.scr["uT"][g * 16:(g + 1) * 16, :]), writes=[("ug", gb)], dkey=("S5ld", gb))
            P.op(ACT, lambda e: e.activation(out=rfull[gb][:, :], in_=ones5[:, :], func=AF.Copy, scale=r_s[:, g:g + 1]), reads=[], writes=[("rfull", gb)])
            P.op(ACT, lambda e: e.activation(out=lx[gb][:, :], in_=tloc[:, :], func=AF.Copy, scale=f_s[:, g:g + 1]), reads=[], writes=[(("lx", 0), "x")])
            frac_sincos(POOL, lx[gb][:, :], lxi[gb][:, :], lxf[gb][:, :], sinL[gb][:, :], cosL[gb][:, :], ("lx", 0), halfpi, 128, outkey=("lxo", gb))
            P.op(POOL, lambda e: e.tensor_tensor(out=bx[gb][:, :, :], in0=t0b[:, :, :], in1=fbd[:, g, :].unsqueeze(1).to_broadcast([16, 8, 128]), op=ALU.mult),
                 reads=[], writes=[(("bx", 0), "x")])
            frac_sincos(POOL, bx[gb][:, :, :], bxi[gb][:, :, :], bxf[gb][:, :, :], bss[gb][:, :, :], bcc[gb][:, :, :], ("bx", 0), halfpi, 16)
            b1 = B1f[:, g, :].unsqueeze(1).to_broadcast([16, 8, 128])
            b2 = B2f[:, g, :].unsqueeze(1).to_broadcast([16, 8, 128])
            kc, ks = (("bx", 0), "cos"), (("bx", 0), "sin")
            P.op(DVE, lambda e: e.tensor_tensor(out=bt1[gb][:, :, :], in0=bcc[gb][:, :, :], in1=b1, op=ALU.mult), reads=[kc], writes=[("bt1", 0)])
            P.op(DVE, lambda e: e.tensor_tensor(out=bt2[gb][:, :, :], in0=bss[gb][:, :, :], in1=b2, op=ALU.mult), reads=[ks], writes=[("bt2", 0)])
            P.op(POOL, lambda e: e.tensor_tensor(out=B1p[gb][:, :, :], in0=bt1[gb][:, :, :], in1=bt2[gb][:, :, :], op=ALU.add), reads=[("bt1", 0), ("bt2", 0)], writes=[("B1p", gb)])
            P.op(DVE, lambda e: e.tensor_tensor(out=bt1[gb][:, :, :], in0=bcc[gb][:, :, :], in1=b2, op=ALU.mult), reads=[kc, ("bt1", 0)], writes=[("bt1", 0)])
            P.op(DVE, lambda e: e.tensor_tensor(out=bt2[gb][:, :, :], in0=bss[gb][:, :, :], in1=b1, op=ALU.mult), reads=[ks, ("bt2", 0)], writes=[("bt2", 0)])
            P.op(POOL, lambda e: e.tensor_tensor(out=B2p[gb][:, :, :], in0=bt1[gb][:, :, :], in1=bt2[gb][:, :, :], op=ALU.subtract), reads=[("bt1", 0), ("bt2", 0)], writes=[("B2p", gb)])
            c1 = C1f[:, g, :].unsqueeze(1).to_broadcast([128, 8, 16])
            c2 = C2f[:, g, :].unsqueeze(1).to_broadcast([128, 8, 16])
            cc_ = cs_tab[:, g, :].unsqueeze(2).to_broadcast([128, 8, 16])
            ss_ = sn_tab[:, g, :].unsqueeze(2).to_broadcast([128, 8, 16])
            P.op(DVE, lambda e: e.tensor_tensor(out=ct1[gb][:, :, :], in0=c1, in1=cc_, op=ALU.mult), reads=[], writes=[("ct1", 0)])
            P.op(DVE, lambda e: e.tensor_tensor(out=ct2[gb][:, :, :], in0=c2, in1=ss_, op=ALU.mult), reads=[], writes=[("ct2", 0)])
            P.op(POOL, lambda e: e.tensor_tensor(out=C1p[gb][:, :, :], in0=ct1[gb][:, :, :], in1=ct2[gb][:, :, :], op=ALU.add), reads=[("ct1", 0), ("ct2", 0)], writes=[("C1p", gb)])
            P.op(DVE, lambda e: e.tensor_tensor(out=ct1[gb][:, :, :], in0=c2, in1=cc_, op=ALU.mult), reads=[("ct1", 0)], writes=[("ct1", 0)])
            P.op(DVE, lambda e: e.tensor_tensor(out=ct2[gb][:, :, :], in0=c1, in1=ss_, op=ALU.mult), reads=[("ct2", 0)], writes=[("ct2", 0)])
            P.op(POOL, lambda e: e.tensor_tensor(out=C2p[gb][:, :, :], in0=ct1[gb][:, :, :], in1=ct2[gb][:, :, :], op=ALU.subtract), reads=[("ct1", 0), ("ct2", 0)], writes=[("C2p", gb)])

        steps = [(g, j) for g in range(32) for j in range(NT)]
        NS = len(steps)

        def stA1(i):
            g, j = steps[i]
            gb = g % 2
            tsl = slice(j * TT, (j + 1) * TT)
            pa, pak = next_ps(C, 0, 5)
            mm_chain(P, pa[:, :], [(B1p[gb][:, j, :], ug[gb][:, tsl])], reads=[("ug", gb), ("B1p", gb)], writes=[pak])
            pb, pbk = next_ps(C, 0, 5)
            mm_chain(P, pb[:, :], [(B2p[gb][:, j, :], ug[gb][:, tsl])], reads=[("ug", gb), ("B2p", gb)], writes=[pbk])
            psA[i] = (pa, pak, pb, pbk)

        def stA2(i):
            g, j = steps[i]
            gb = g % 2
            x = i % NX
            pa, pak, pb, pbk = psA.pop(i)
            P.op(DVE, lambda e: e.tensor_tensor(out=X1[x][:, :], in0=pa[:, :], in1=cosL[gb][:, :], op=ALU.mult), reads=[pak, (("lxo", gb), "cos")], writes=[("X1", x)])
            P.op(DVE, lambda e: e.tensor_tensor(out=X2[x][:, :], in0=pb[:, :], in1=sinL[gb][:, :], op=ALU.mult), reads=[pbk, (("lxo", gb), "sin")], writes=[("X2", x)])

        def stB(i):
            x = i % NX
            P.op(POOL, lambda e: e.tensor_tensor(out=X1[x][:, :], in0=X1[x][:, :], in1=X2[x][:, :], op=ALU.add), reads=[("X1", x), ("X2", x)], writes=[("X1", x)])

        def stC1(i):
            g, j = steps[i]
            gb = g % 2
            x = i % NX
            u = i % 2
            v = i % NW
            init = 0.0 if j == 0 else wb[1 - u][:, TT - 1:TT]
            P.op(DVE, lambda e: e.tensor_tensor_scan(out=wb[u][:, :], data0=rfull[gb][:, :], data1=X1[x][:, :], initial=init, op0=ALU.mult, op1=ALU.add),
                 reads=[("X1", x), ("rfull", gb), ("w", 1 - u)], writes=[("w", u)])
            P.op(DVE, lambda e: e.tensor_tensor(out=cW[v][:, :], in0=wb[u][:, :], in1=cosL[gb][:, :], op=ALU.mult), reads=[("w", u), (("lxo", gb), "cos")], writes=[("cW", v)])
            P.op(POOL, lambda e: e.tensor_tensor(out=sW[v][:, :], in0=wb[u][:, :], in1=sinL[gb][:, :], op=ALU.mult), reads=[("w", u), (("lxo", gb), "sin")], writes=[("sW", v)])

        def stC2(i):
            g, j = steps[i]
            gb = g % 2
            v = i % NW
            tsl = slice(j * TT, (j + 1) * TT)
            py, pyk = next_ps(C, 5, 6)
            mm_chain(P, py[0:16, :], [(C1p[gb][:, j, :], cW[v][:, :]), (C2p[gb][:, j, :], sW[v][:, :]), (Dd[:, g, :], ug[gb][:, tsl])],
                     reads=[("cW", v), ("sW", v), ("ug", gb), ("C1p", gb), ("C2p", gb)], writes=[pyk])
            P.op(ACT, lambda e: e.activation(out=yg[gb][:, tsl], in_=py[0:16, :], func=AF.Gelu), reads=[pyk], writes=[("yg", gb)])
            if j == NT - 1:
                P.op(POOL, lambda e: e.dma_start(out=C.scr["ygT"][g * 16:(g + 1) * 16, :], in_=yg[gb][:, :]), reads=[("yg", gb)], dkey=("S5st", gb))

        psA = {}
        prep_group(0)
        LC2 = 5
        for i in range(NS + LC2):
            if i < NS and steps[i][1] == LC2 and steps[i][0] + 1 < 32:
                prep_group(steps[i][0] + 1)
            if i < NS:
                stA1(i)
            if 0 <= i - 1 < NS:
                stA2(i - 1)
            if 0 <= i - 2 < NS:
                stB(i - 2)
            if 0 <= i - 3 < NS:
                stC1(i - 3)
            if 0 <= i - LC2 < NS:
                stC2(i - LC2)
        P.barrier()


def stage_l0_out(P, C):
    nc = P.nc
    xres = C.xres
    with ExitStack() as es:
        sb = lambda name, shape, dt=F32: es.enter_context(nc.sbuf_tensor(name, list(shape), dt))
        wglu = sb("O_wglu", [128, 4, 512], BF16)
        wout = sb("O_wout", [128, 8, 1024], BF16)
        bglu = sb("O_bglu", [128, 4])
        wload(P, C, wglu, C.w["l0_s5_w_glu"], 4, 512, gain=None, name="O_wglu")
        wload(P, C, wout, C.w["l0_w_out"], 8, 1024, gain=None, name="O_wout")
        P.op(SP, lambda e: e.dma_start(out=bglu[:, :], in_=C.w["l0_s5_b_glu"]), writes=[("bglu",)], dkey=("bglu",))
        P.barrier()
        xt = [sb(f"O_xt{i}", [128, 8, TT]) for i in range(2)]
        mg = [sb(f"O_mg{i}", [128, 8, TT], BF16) for i in range(2)]
        ygt = [sb(f"O_yg{i}", [128, 4, TT], BF16) for i in range(2)]
        sg = [sb(f"O_sg{i}", [128, TT]) for i in range(2)]
        xv = xres.rearrange("(c p) t -> p c t", p=128)
        mv = C.scr["mT"].rearrange("(c p) t -> p c t", p=128)
        yv = C.scr["ygT"].rearrange("(c p) t -> p c t", p=128)
        it = 0
        for j in range(NT):
            u = j % 2
            tsl = slice(j * TT, (j + 1) * TT)
            P.op(SP, lambda e, u=u, tsl=tsl: e.dma_start(out=xt[u][:, :, :], in_=xv[:, :, tsl]), writes=[("xt", u)], dkey=("Old", "x", u))
            P.op(SP, lambda e, u=u, tsl=tsl: e.dma_start(out=mg[u][:, 0:4, :], in_=mv[:, 0:4, tsl]), writes=[("mg", u, "r")], dkey=("Old", "m", u))
            P.op(SP, lambda e, u=u, tsl=tsl: e.dma_start(out=ygt[u][:, :, :], in_=yv[:, :, tsl]), writes=[("ygt", u)], dkey=("Old", "y", u))
            for fb in range(4):
                pa, pak = next_ps(C)
                mm_chain(P, pa[:, :], [(wglu[:, c, fb * 128:(fb + 1) * 128], ygt[u][:, c, :]) for c in range(4)], reads=[*wkeys(C, "O_wglu", fb * 128), ("ygt", u)], writes=[pak])
                v = it % 2
                it += 1
                P.op(ACT, lambda e, pa=pa, v=v, fb=fb: e.activation(out=sg[v][:, :], in_=pa[:, :], func=AF.Sigmoid, bias=bglu[:, fb:fb + 1]),
                     reads=[pak, ("bglu",)], writes=[("sg", v)])
                P.op(DVE, lambda e, v=v, u=u, fb=fb: e.tensor_tensor(out=mg[u][:, 4 + fb, :], in0=sg[v][:, :], in1=ygt[u][:, fb, :], op=ALU.mult),
                     reads=[("sg", v), ("ygt", u)], writes=[("mg", u, fb)])
            for fb in range(8):
                pa, pak = next_ps(C)
                mm_chain(P, pa[:, :], [(wout[:, c, fb * 128:(fb + 1) * 128], mg[u][:, c, :]) for c in range(8)],
                         reads=wkeys(C, "O_wout", fb * 128) + [("mg", u, "r")] + [("mg", u, f) for f in range(4)], writes=[pak])
                P.op(DVE, lambda e, pa=pa, u=u, fb=fb: e.tensor_tensor(out=xt[u][:, fb, :], in0=pa[:, :], in1=xt[u][:, fb, :], op=ALU.add),
                     reads=[pak, ("xt", u)], writes=[("xt", u)])
            P.op(POOL, lambda e, u=u, tsl=tsl: e.dma_start(out=xv[:, :, tsl], in_=xt[u][:, :, :]), reads=[("xt", u)], dkey=("Ost", u))
        P.barrier()


def stage_l1_inproj(P, C):
    nc = P.nc
    xres = C.xres
    with ExitStack() as es:
        sb = lambda name, shape, dt=F32: es.enter_context(nc.sbuf_tensor(name, list(shape), dt))
        W = sb("E_W", [128, 8, 4112], BF16)
        g_mix = C.gains["l1_mix_norm"]
        wload(P, C, W, C.w["l1_w_in"], 8, 4112, name="E_W")
        cw = sb("E_cw", [128, 4, 24])
        P.op(SP, lambda e: e.dma_start(out=cw[:, :, :], in_=C.w["l1_conv"]), writes=[("cw",)], dkey=("Ecw",))
        hp = sb("E_hp", [8, 4])
        P.op(SP, lambda e: e.dma_start(out=hp[:, 0:2], in_=C.w["l1_hp"]), writes=[("hp",)], dkey=("Ehp",))
        cmask = sb("E_cmask", [8, TT])
        P.op(SP, lambda e: e.dma_start(out=cmask[:, :], in_=C.cst["cmask"]), writes=[("cmask",)], dkey=("Ecm",))
        P.barrier()
        P.op(ACT, lambda e: e.activation(out=hp[:, 2:3], in_=hp[:, 0:1], func=AF.Exp), reads=[("hp",)], writes=[("hp2",)])
        P.op(DVE, lambda e: e.tensor_scalar(out=hp[:, 2:3], in0=hp[:, 2:3], scalar1=-1.0, scalar2=None, op0=ALU.mult), reads=[("hp2",)], writes=[("hp2",)])
        P.barrier()
        xt = sb("E_xt", [128, 8, TT])
        hn = sb("E_hn", [128, 8, TT], BF16)
        sq = sb("E_sq", [128, 8, TT], BF16)
        rstd = sb("E_rstd", [128, TT])
        acc = [sb(f"E_acc{i}", [128, TT]) for i in range(4)]
        accq = sb("E_accq", [128, 16, TT])
        rn16 = sb("E_rn16", [16, TT])
        oh16 = sb("E_oh16", [128, 16, 16], BF16)
        sel16 = sb("E_sel16", [16, 16, 128])
        l2c = sb("E_l2c", [16, 2])
        pss16 = C.psum[5]
        P.op(SP, lambda e: e.dma_start(out=oh16[:, :, :], in_=C.cst_oh16), writes=[("oh16",)], dkey=("Eoh",))
        P.op(SP, lambda e: e.dma_start(out=sel16[:, :, :], in_=C.cst["sel16"]), writes=[("sel16",)], dkey=("Esel",))
        P.op(SP, lambda e: e.dma_start(out=l2c[:, :], in_=C.cst["l2c"]), writes=[("l2c",)], dkey=("El2c",))
        sqb = [sb(f"E_sqb{i}", [128, TT], BF16) for i in range(5)]
        pend_ss = []
        pend_act = []
        halo = sb("E_halo", [128, 24, 3])
        corr = sb("E_corr", [128, 24, 3])
        ctmp = sb("E_ctmp", [128, 24])
        outs = {nm: sb("E_o" + nm, [128, 8, TT], BF16) for nm in ["gq", "gk", "gv", "gz"]}
        gsb = sb("E_g", [8, TT])
        gcs = [sb(f"E_gc{i}", [8, TT]) for i in range(2)]
        bts = [sb(f"E_bt{i}", [8, TT]) for i in range(2)]
        it = 0
        for j in range(NT):
            tsl = slice(j * TT, (j + 1) * TT)
            norm_tile(P, C, xres, j, xt, ("xt",), hn, ("hn",), sq, rstd, pshi=5, gain=g_mix)
            if j > 0:
                hk = [("halo", b) for b in range(24)]
                terms = [(0, 0, 0), (0, 1, 1), (0, 2, 2), (1, 0, 1), (1, 1, 2), (2, 0, 2)]
                first = {}
                for (t_, k_, h_i) in terms:
                    if t_ not in first:
                        first[t_] = True
                        P.op(POOL, lambda e, t_=t_, k_=k_, h_i=h_i: e.tensor_tensor(out=corr[:, :, t_], in0=halo[:, :, h_i], in1=cw[:, k_, 0:24], op=ALU.mult),
                             reads=hk + [("corr",)], writes=[("corr",)])
                    else:
                        P.op(POOL, lambda e, k_=k_, h_i=h_i: e.tensor_tensor(out=ctmp[:, :], in0=halo[:, :, h_i], in1=cw[:, k_, 0:24], op=ALU.mult),
                             reads=hk + [("ctmp",)], writes=[("ctmp",)])
                        P.op(POOL, lambda e, t_=t_: e.tensor_tensor(out=corr[:, :, t_], in0=corr[:, :, t_], in1=ctmp[:, :], op=ALU.add),
                             reads=[("corr",), ("ctmp",)], writes=[("corr",)])
            for sec in range(3):
                nm = ["gq", "gk", "gv"][sec]
                for hh in range(8):
                    blk = sec * 8 + hh
                    col = blk * 128
                    u = it % 5
                    a3 = it % 4
                    it += 1
                    pst, psk = next_ps(C, 0, 5)
                    mm_chain(P, pst[:, :], [(W[:, c, col:col + 128], hn[:, c, :]) for c in range(8)], reads=[*wkeys(C, "E_W", col), ("hn", "a"), ("hn", "b")], writes=[psk])
                    if sec == 2:
                        a_ = acc[a3]
                        akey = ("acc", a3)
                    else:
                        a_ = accq[:, sec * 8 + hh, :]
                        akey = ("accq", sec * 8 + hh)
                    P.op(ACT, lambda e, a_=a_, pst=pst, blk=blk: e.activation(out=a_[:, :], in_=pst[:, :], func=AF.Copy, scale=cw[:, 3, blk:blk + 1]),
                         reads=[psk], writes=[akey])
                    if j < NT - 1:
                        P.op(ACT, lambda e, pst=pst, blk=blk: e.copy(out=halo[:, blk, :], in_=pst[:, TT - 3:TT]), reads=[psk], writes=[("halo", blk)])
                    for k in range(3):
                        d_ = 3 - k
                        P.op(DVE, lambda e, a_=a_, pst=pst, blk=blk, k=k, d_=d_: e.scalar_tensor_tensor(out=a_[:, d_:TT], in0=pst[:, 0:TT - d_], scalar=cw[:, k, blk:blk + 1], in1=a_[:, d_:TT],
                                                                                                  op0=ALU.mult, op1=ALU.add),
                             reads=[psk, akey], writes=[akey])
                    if j > 0:
                        P.op(POOL, lambda e, a_=a_, blk=blk: e.tensor_tensor(out=a_[:, 0:3], in0=a_[:, 0:3], in1=corr[:, blk, :], op=ALU.add),
                             reads=[("corr",), akey], writes=[akey])
                    pend_act.append((sec, hh, a_, akey, u, nm))
                    while len(pend_act) > (1 if blk < 23 else 0):
                        sec_, hh_, a2_, akey_, u2_, nm_ = pend_act.pop(0)
                        if sec_ == 2:
                            P.op(ACT, lambda e, a2_=a2_, hh_=hh_, nm_=nm_: e.activation(out=outs[nm_][:, hh_, :], in_=a2_[:, :], func=AF.Silu), reads=[akey_], writes=[(nm_, hh_)])
                        else:
                            P.op(ACT, lambda e, a2_=a2_: e.activation(out=a2_[:, :], in_=a2_[:, :], func=AF.Silu), reads=[akey_], writes=[akey_])
                            P.op(ACT, lambda e, a2_=a2_, u2_=u2_: e.activation(out=sqb[u2_][:, :], in_=a2_[:, :], func=AF.Square), reads=[akey_], writes=[("sqb", u2_)])
                            pend_ss.append((sec_ * 8 + hh_, u2_))
                    while len(pend_ss) > (3 if blk < 23 else 0):
                        qi_, u_ = pend_ss.pop(0)
                        P.op(PE, lambda e, qi_=qi_, u_=u_: e.matmul(pss16[0:16, :], lhsT=oh16[:, qi_, :], rhs=sqb[u_][:, :], start=(qi_ == 0), stop=(qi_ == 15)),
                             reads=[("sqb", u_)], writes=[("pss16",)])
            P.op(ACT, lambda e: e.activation(out=rn16[:, :], in_=pss16[0:16, :], func=AF.Sqrt, scale=l2c[:, 0:1], bias=l2c[:, 1:2]), reads=[("pss16",)], writes=[("rn16",)])
            P.op(DVE, lambda e: e.reciprocal(out=rn16[:, :], in_=rn16[:, :]), reads=[("rn16",)], writes=[("rn16",)])
            for qi in range(16):
                sec, hh = qi // 8, qi % 8
                nm = ["gq", "gk"][sec]
                pbc, pbck = next_ps(C, 0, 5)
                mm_chain(P, pbc[:, :], [(sel16[:, qi, :], rn16[:, :])], reads=[("rn16",)], writes=[pbck])
                P.op(DVE, lambda e, pbc=pbc, qi=qi, hh=hh, nm=nm: e.tensor_tensor(out=outs[nm][:, hh, :], in0=accq[:, qi, :], in1=pbc[:, :], op=ALU.mult),
                     reads=[pbck, ("accq", qi)], writes=[(nm, hh)])
            for hh in range(8):
                col = 3072 + hh * 128
                pst, psk = next_ps(C, 0, 5)
                mm_chain(P, pst[:, :], [(W[:, c, col:col + 128], hn[:, c, :]) for c in range(8)], reads=[*wkeys(C, "E_W", col), ("hn", "a"), ("hn", "b")], writes=[psk])
                P.op(ACT, lambda e, pst=pst, hh=hh: e.activation(out=outs["gz"][:, hh, :], in_=pst[:, :], func=AF.Silu), reads=[psk], writes=[("gz", hh)])
            pb_, pbk = next_ps(C, 0, 5)
            mm_chain(P, pb_[0:8, :], [(W[:, c, 4096:4104], hn[:, c, :]) for c in range(8)], reads=[*wkeys(C, "E_W", 4096, 8), ("hn", "a"), ("hn", "b")], writes=[pbk])
            jb = j % 2
            P.op(ACT, lambda e, pb_=pb_, jb=jb: e.activation(out=bts[jb][:, :], in_=pb_[0:8, :], func=AF.Sigmoid), reads=[pbk], writes=[("bts", jb)])
            pa_, pak = next_ps(C, 0, 5)
            mm_chain(P, pa_[0:8, :], [(W[:, c, 4104:4112], hn[:, c, :]) for c in range(8)], reads=[*wkeys(C, "E_W", 4104, 8), ("hn", "a"), ("hn", "b")], writes=[pak])
            P.op(ACT, lambda e, pa_=pa_: e.activation(out=gsb[:, :], in_=pa_[0:8, :], func=AF.Exp, bias=hp[:, 1:2]), reads=[pak, ("hp",)], writes=[("gsb",)])
            P.op(ACT, lambda e: e.activation(out=gsb[:, :], in_=gsb[:, :], func=AF.Ln, bias=C.one_col[0:8, 0:1]), reads=[("gsb",)], writes=[("gsb",)])
            P.op(DVE, lambda e: e.tensor_scalar(out=gsb[:, :], in0=gsb[:, :], scalar1=hp[:, 2:3], scalar2=None, op0=ALU.mult), reads=[("gsb",), ("hp2",)], writes=[("gsb",)])
            P.op(DVE, lambda e, jb=jb: e.tensor_tensor_scan(out=gcs[jb][:, :], data0=cmask[:, :], data1=gsb[:, :], initial=0.0, op0=ALU.mult, op1=ALU.add),
                 reads=[("gsb",)], writes=[("gcs", jb)])
            P.op(POOL, lambda e, jb=jb, tsl=tsl: e.dma_start(out=C.scr32["gcT"][:, tsl], in_=gcs[jb][:, :]), reads=[("gcs", jb)], dkey=("Est", "gc", jb))
            P.op(POOL, lambda e, jb=jb, tsl=tsl: e.dma_start(out=C.scr32["btT"][:, tsl], in_=bts[jb][:, :]), reads=[("bts", jb)], dkey=("Est", "bt", jb))
            for nm in ["gq", "gk", "gv", "gz"]:
                dv = C.scr[nm].rearrange("(h p) t -> p h t", p=128)
                P.op(POOL, lambda e, dv=dv, nm=nm, tsl=tsl: e.dma_start(out=dv[:, :, tsl], in_=outs[nm][:, :, :]), reads=[(nm, hh) for hh in range(8)], dkey=("Est", nm))
        P.barrier()


def stage_gdn(P, C):
    nc = P.nc
    NCH = 32
    with ExitStack() as es:
        sb = lambda name, shape, dt=F32: es.enter_context(nc.sbuf_tensor(name, list(shape), dt))
        masks = sb("G_masks", [128, 18, 128])
        P.op(SP, lambda e: e.dma_start(out=masks[:, :, :], in_=C.cst["gmasks"]), writes=[("masks",)], dkey=("Gm",))
        sel = sb("G_sel", [8, 8, 128])
        P.op(SP, lambda e: e.dma_start(out=sel[:, :, :], in_=C.cst["gsel"]), writes=[("sel",)], dkey=("Gs",))
        sel_last = sb("G_sellast", [128, 128])
        P.op(SP, lambda e: e.dma_start(out=sel_last[:, :], in_=C.cst["gsellast"]), writes=[("sellast",)], dkey=("Gsl",))
        identf = masks[:, 15, :]
        onorm = sb("G_onorm", [128, 1])
        P.op(SP, lambda e: e.dma_start(out=onorm[:, :], in_=C.w["l1_o_norm"]), writes=[("onorm",)], dkey=("Gon",))
        gcT = sb("G_gcT", [8, S])
        btT = sb("G_btT", [8, S])
        P.op(SP, lambda e: e.dma_start(out=gcT[:, :], in_=C.scr32["gcT"][:, :]), writes=[("gcT",)], dkey=("Ggc",))
        P.op(SP, lambda e: e.dma_start(out=btT[:, :], in_=C.scr32["btT"][:, :]), writes=[("btT",)], dkey=("Gbt",))
        gct = sb("G_gct", [128, NCH, 8])
        btt = sb("G_btt", [128, NCH, 8])
        glt = sb("G_glt", [128, NCH, 8])
        kbs = sb("G_kbs", [128, NCH, 8])
        kds = sb("G_kds", [128, NCH, 8])
        egl = sb("G_egl", [128, NCH, 8])
        P.barrier()
        for n in range(NCH):
            cs = slice(n * 128, (n + 1) * 128)
            pt, ptk = next_ps(C)
            P.op(PE, lambda e, pt=pt, cs=cs: e.transpose(pt[:, 0:8], gcT[:, cs], identf[0:8, 0:8]), reads=[("gcT",)], writes=[ptk])
            P.op(ACT, lambda e, pt=pt, n=n: e.copy(out=gct[:, n, :], in_=pt[:, 0:8]), reads=[ptk], writes=[("gct", n)])
            pt2, pt2k = next_ps(C)
            P.op(PE, lambda e, pt2=pt2, cs=cs: e.transpose(pt2[:, 0:8], btT[:, cs], identf[0:8, 0:8]), reads=[("btT",)], writes=[pt2k])
            P.op(DVE, lambda e, pt2=pt2, n=n: e.tensor_copy(out=btt[:, n, :], in_=pt2[:, 0:8]), reads=[pt2k], writes=[("btt", n)])
        P.barrier()
        gflat = lambda t: t[:, :, :].rearrange("p n h -> p (n h)")
        pg, pgk = next_ps(C)
        mm_chain(P, pg[:, 0:256], [(sel_last[:, :], gflat(gct))], reads=[], writes=[pgk])
        P.op(ACT, lambda e: e.copy(out=gflat(glt), in_=pg[:, 0:256]), reads=[pgk], writes=[("glt",)])
        P.op(ACT, lambda e: e.activation(out=gflat(egl), in_=pg[:, 0:256], func=AF.Exp), reads=[pgk], writes=[("egl",)])
        P.op(ACT, lambda e: e.activation(out=gflat(kbs), in_=gflat(gct), func=AF.Exp), reads=[], writes=[("kbs",)])
        P.op(DVE, lambda e: e.tensor_tensor(out=gflat(kbs), in0=gflat(kbs), in1=gflat(btt), op=ALU.mult), reads=[("kbs",)], writes=[("kbs",)])
        P.op(DVE, lambda e: e.tensor_tensor(out=gflat(kds), in0=gflat(glt), in1=gflat(gct), op=ALU.subtract), reads=[("glt",)], writes=[("kds",)])
        P.op(ACT, lambda e: e.activation(out=gflat(kds), in_=gflat(kds), func=AF.Exp), reads=[("kds",)], writes=[("kds",)])
        P.barrier()
        KT = [sb(f"G_KT{i}", [128, S], BF16) for i in range(2)]
        QT = [sb(f"G_QT{i}", [128, S], BF16) for i in range(2)]
        VT = [sb(f"G_VT{i}", [128, S], BF16) for i in range(2)]
        ZT = [sb(f"G_ZT{i}", [128, S], BF16) for i in range(2)]
        gcb = [sb(f"G_gcb{i}", [128, TT]) for i in range(2)]
        gcbA = [sb(f"G_gcbA{i}", [128, TT]) for i in range(2)]
        gcbB = [sb(f"G_gcbB{i}", [128, TT]) for i in range(2)]
        egcb = [sb(f"G_egcb{i}", [128, TT]) for i in range(2)]
        QdT = [sb(f"G_QdT{i}", [128, TT], BF16) for i in range(2)]
        NB_ = 2
        T4 = [128, 4, 128]
        xg = [sb(f"G_xg{i}", T4) for i in range(NB_)]
        tA = [sb(f"G_tA{i}", T4) for i in range(NB_)]
        tB = [sb(f"G_tB{i}", T4) for i in range(NB_)]
        a1 = [sb(f"G_a1{i}", T4) for i in range(NB_)]
        q1 = [sb(f"G_q1{i}", T4) for i in range(NB_)]
        tmpf = [sb(f"G_tmpf{i}", T4) for i in range(NB_)]
        A_ = [sb(f"G_A{i}", T4, BF16) for i in range(NB_)]
        AT_ = [sb(f"G_AT{i}", T4, BF16) for i in range(NB_)]
        qkT = [sb(f"G_qkT{i}", T4, BF16) for i in range(NB_)]
        Dd = [[sb(f"G_D{i}_{k}", T4, BF16) for k in range(2)] for i in range(NB_)]
        DTd = [[sb(f"G_DT{i}_{k}", T4, BF16) for k in range(2)] for i in range(NB_)]
        Xm = [sb(f"G_Xm{i}", T4, BF16) for i in range(NB_)]
        XTm = [sb(f"G_XTm{i}", T4, BF16) for i in range(NB_)]
        kbd = [sb(f"G_kbd{i}", T4, BF16) for i in range(NB_)]
        kdec = [sb(f"G_kdec{i}", T4, BF16) for i in range(NB_)]
        vb = [sb(f"G_vb{i}", T4, BF16) for i in range(NB_)]
        nwT = [sb(f"G_nwT{i}", T4, BF16) for i in range(NB_)]
        vnew = [sb(f"G_vnew{i}", [128, 128], BF16) for i in range(2)]
        Sst = sb("G_S", [128, 128])
        Sbf = sb("G_Sbf", [128, 128], BF16)
        o_sb = [sb(f"G_osb{i}", [128, TT]) for i in range(2)]
        osq = [sb(f"G_osq{i}", [128, TT], BF16) for i in range(2)]
        rr = [sb(f"G_rr{i}", [128, TT]) for i in range(2)]
        mo = [sb(f"G_mo{i}", [128, TT], BF16) for i in range(2)]
        ident_bf = C.ident_bf
        v4 = lambda t: t[:, :].rearrange("p (c k) -> p c k", c=4)
        mb = lambda k: masks[:, k, :].unsqueeze(1).to_broadcast(T4)

        def load_head(h):
            hb = h % 2
            for nm, tl in [("gk", KT), ("gq", QT), ("gv", VT), ("gz", ZT)]:
                P.op(SP, lambda e, nm=nm, tl=tl, h=h, hb=hb: e.dma_start(out=tl[hb][:, :], in_=C.scr[nm][h * 128:(h + 1) * 128, :]),
                     writes=[(nm, hb)], dkey=("Gld", nm, hb))

        def prep_half(h, jt, u, c0, ncn):
            hb = h % 2
            w = u
            n0 = 4 * jt
            hf = c0 // ncn
            tsl = slice(jt * TT, (jt + 1) * TT)
            cs = [slice((n0 + c0 + c) * 128, (n0 + c0 + c + 1) * 128) for c in range(ncn)]
            TH = [128, ncn, 128]
            hs = slice(c0, c0 + ncn)
            vh = lambda t: t[:, 0:ncn * 128].rearrange("p (c k) -> p c k", c=ncn)
            mh = lambda k: masks[:, k, :].unsqueeze(1).to_broadcast(TH)
            K_ = lambda nm: (nm, u, hf)
            if c0 == 0:
                pg, pgk = next_ps(C, 2, 6)
                mm_chain(P, pg[:, :], [(sel[:, h, :], gcT[:, tsl])], reads=[], writes=[pgk])
                P.op(ACT, lambda e: e.copy(out=gcb[w][:, :], in_=pg[:, :]), reads=[pgk], writes=[("gcb", w)])
                m4 = lambda k: masks[:, k, :].unsqueeze(1).to_broadcast([128, 4, 128])
                g4 = lambda t: t[:, :].rearrange("p (c k) -> p c k", c=4)
                P.op(DVE, lambda e: e.tensor_tensor(out=g4(gcbA[w]), in0=g4(gcb[w]), in1=m4(16), op=ALU.add), reads=[("gcb", w)], writes=[("gcbA", w)])
                P.op(POOL, lambda e: e.tensor_tensor(out=g4(gcbB[w]), in0=g4(gcb[w]), in1=m4(17), op=ALU.add), reads=[("gcb", w)], writes=[("gcbB", w)])
                P.op(ACT, lambda e: e.activation(out=egcb[w][:, :], in_=pg[:, :], func=AF.Exp), reads=[pgk], writes=[("egcb", w)])
                P.op(POOL, lambda e: e.tensor_tensor(out=QdT[w][:, :], in0=QT[hb][:, tsl], in1=egcb[w][:, :], op=ALU.mult),
                     reads=[("egcb", w), ("gq", hb)], writes=[("QdT", w)])
            gci = gct[:, n0 + c0:n0 + c0 + ncn, h:h + 1].to_broadcast(TH)
            bti = btt[:, n0 + c0:n0 + c0 + ncn, h:h + 1].to_broadcast(TH)
            kbi = kbs[:, n0 + c0:n0 + c0 + ncn, h:h + 1].to_broadcast(TH)
            kdi = kds[:, n0 + c0:n0 + c0 + ncn, h:h + 1].to_broadcast(TH)
            pkk, pkkk = next_ps(C, 2, 6)

            def f_kk(e):
                ins = None
                for c in range(ncn):
                    ins = e.matmul(pkk[:, c * 128:(c + 1) * 128], lhsT=KT[hb][:, cs[c]], rhs=KT[hb][:, cs[c]], start=True, stop=True)
                return ins
            P.op(PE, f_kk, reads=[("gk", hb)], writes=[pkkk])
            pqk, pqkk = next_ps(C, 2, 6)

            def f_qk(e):
                ins = None
                for c in range(ncn):
                    ins = e.matmul(pqk[:, c * 128:(c + 1) * 128], lhsT=KT[hb][:, cs[c]], rhs=QT[hb][:, cs[c]], start=True, stop=True)
                return ins
            P.op(PE, f_qk, reads=[("gk", hb), ("gq", hb)], writes=[pqkk])
            gcbAv = gcbA[w][:, :].rearrange("p (c k) -> p c k", c=4)[:, hs, :]
            gcbBv = gcbB[w][:, :].rearrange("p (c k) -> p c k", c=4)[:, hs, :]
            P.op(DVE, lambda e: e.tensor_tensor(out=xg[u][:, hs, :], in0=gcbAv, in1=gci, op=ALU.subtract), reads=[("gcbA", w)], writes=[K_("xg")])
            P.op(DVE, lambda e: e.tensor_tensor(out=tmpf[u][:, hs, :], in0=gcbBv, in1=gci, op=ALU.subtract), reads=[("gcbB", w)], writes=[K_("tmpf")])
            yield None
            P.op(ACT, lambda e: e.activation(out=tA[u][:, hs, :], in_=xg[u][:, hs, :], func=AF.Exp, scale=-1.0), reads=[K_("xg")], writes=[K_("tA")])
            P.op(ACT, lambda e: e.activation(out=tB[u][:, hs, :], in_=tmpf[u][:, hs, :], func=AF.Exp), reads=[K_("tmpf")], writes=[K_("tB")])
            yield None
            P.op(DVE, lambda e: e.tensor_tensor(out=a1[u][:, hs, :], in0=tA[u][:, hs, :], in1=vh(pkk), op=ALU.mult), reads=[pkkk, K_("tA")], writes=[K_("a1")])
            P.op(DVE, lambda e: e.tensor_tensor(out=qkT[u][:, hs, :], in0=tB[u][:, hs, :], in1=vh(pqk), op=ALU.mult), reads=[pqkk, K_("tB")], writes=[K_("qkT")])
            yield None
            for c in range(ncn):
                P.op(ACT, lambda e, c=c: e.activation(out=A_[u][:, c0 + c, :], in_=a1[u][:, c0 + c, :], func=AF.Copy, scale=btt[:, n0 + c0 + c, h:h + 1]),
                     reads=[K_("a1")], writes=[K_("A")])
            yield None
            pb1, pb1k = next_psb(C)

            def f_at(e):
                ins = None
                for c in range(ncn):
                    ins = e.transpose(pb1[:, c * 128:(c + 1) * 128], A_[u][:, c0 + c, :], ident_bf[:, :])
                return ins
            P.op(PE, f_at, reads=[K_("A")], writes=[pb1k])
            pb2, pb2k = next_psb(C)

            def f_kvt(e):
                ins = None
                for c in range(ncn):
                    e.transpose(pb2[:, c * 128:(c + 1) * 128], KT[hb][:, cs[c]], ident_bf[:, :])
                    ins = e.transpose(pb2[:, (ncn + c) * 128:(ncn + c + 1) * 128], VT[hb][:, cs[c]], ident_bf[:, :])
                return ins
            P.op(PE, f_kvt, reads=[("gk", hb), ("gv", hb)], writes=[pb2k])
            P.op(ACT, lambda e: e.copy(out=AT_[u][:, hs, :], in_=vh(pb1)), reads=[pb1k], writes=[K_("AT")])
            ktr = pb2[:, 0:ncn * 128].rearrange("p (c k) -> p c k", c=ncn)
            vtr = pb2[:, ncn * 128:2 * ncn * 128].rearrange("p (c k) -> p c k", c=ncn)
            P.op(DVE, lambda e: e.tensor_tensor(out=kbd[u][:, hs, :], in0=ktr, in1=kbi, op=ALU.mult), reads=[pb2k], writes=[K_("kbd")])
            P.op(DVE, lambda e: e.tensor_tensor(out=kdec[u][:, hs, :], in0=ktr, in1=kdi, op=ALU.mult), reads=[pb2k], writes=[K_("kdec")])
            P.op(DVE, lambda e: e.tensor_tensor(out=vb[u][:, hs, :], in0=vtr, in1=bti, op=ALU.mult), reads=[pb2k], writes=[K_("vb")])
            P.op(POOL, lambda e: e.tensor_tensor(out=tmpf[u][:, hs, :], in0=A_[u][:, hs, :], in1=mh(2), op=ALU.mult), reads=[K_("A")], writes=[K_("tmpf")])
            yield None
            P.op(POOL, lambda e: e.tensor_tensor(out=Dd[u][0][:, hs, :], in0=tmpf[u][:, hs, :], in1=mh(15), op=ALU.add), reads=[K_("tmpf")], writes=[("D", u, 0, hf)])
            P.op(DVE, lambda e: e.tensor_tensor(out=q1[u][:, hs, :], in0=AT_[u][:, hs, :], in1=mh(8), op=ALU.mult), reads=[K_("AT"), K_("q1")], writes=[K_("q1")])
            yield None
            P.op(DVE, lambda e: e.tensor_tensor(out=DTd[u][0][:, hs, :], in0=q1[u][:, hs, :], in1=mh(15), op=ALU.add), reads=[K_("q1")], writes=[("DT", u, 0, hf)])
            cur = 0
            for li in range(1, 7):
                yield None
                last = (li == 6)
                nxt = 1 - cur
                D_c, DT_c = Dd[u][cur], DTd[u][cur]
                kD, kDT = ("D", u, cur, hf), ("DT", u, cur, hf)
                if not last:
                    px, pxk = next_ps(C, 2, 6)

                    def f_x(e, px=px, D_c=D_c):
                        ins = None
                        for c in range(ncn):
                            ins = e.matmul(px[:, c * 128:(c + 1) * 128], lhsT=AT_[u][:, c0 + c, :], rhs=D_c[:, c0 + c, :], start=True, stop=True)
                        return ins
                    P.op(PE, f_x, reads=[K_("AT"), kD], writes=[pxk])
                px2, px2k = next_ps(C, 2, 6)

                def f_x2(e, px2=px2, DT_c=DT_c):
                    ins = None
                    for c in range(ncn):
                        ins = e.matmul(px2[:, c * 128:(c + 1) * 128], lhsT=A_[u][:, c0 + c, :], rhs=DT_c[:, c0 + c, :], start=True, stop=True)
                    return ins
                P.op(PE, f_x2, reads=[K_("A"), kDT], writes=[px2k])
                yield None
                if not last:
                    P.op(DVE, lambda e, px=px, li=li: e.tensor_tensor(out=Xm[u][:, hs, :], in0=vh(px), in1=mh(2 + li), op=ALU.mult), reads=[pxk], writes=[K_("Xm")])
                P.op(DVE, lambda e, px2=px2, li=li: e.tensor_tensor(out=XTm[u][:, hs, :], in0=vh(px2), in1=mh(8 + li), op=ALU.mult), reads=[px2k], writes=[K_("XTm")])
                yield None
                if not last:
                    pm, pmk = next_ps(C, 2, 6)

                    def f_m(e, pm=pm, D_c=D_c, DT_c=DT_c):
                        ins = None
                        for c in range(ncn):
                            e.matmul(pm[:, c * 128:(c + 1) * 128], lhsT=DT_c[:, c0 + c, :], rhs=Xm[u][:, c0 + c, :], start=True, stop=False)
                            ins = e.matmul(pm[:, c * 128:(c + 1) * 128], lhsT=ident_bf[:, :], rhs=D_c[:, c0 + c, :], start=False, stop=True)
                        return ins
                    P.op(PE, f_m, reads=[kDT, K_("Xm"), kD], writes=[pmk])
                pm2, pm2k = next_ps(C, 2, 6)

                def f_m2(e, pm2=pm2, D_c=D_c, DT_c=DT_c):
                    ins = None
                    for c in range(ncn):
                        ins = e.matmul(pm2[:, c * 128:(c + 1) * 128], lhsT=D_c[:, c0 + c, :], rhs=XTm[u][:, c0 + c, :], start=True, stop=True)
                    return ins
                P.op(PE, f_m2, reads=[kD, K_("XTm"), kDT], writes=[pm2k])
                yield None
                if not last:
                    P.op(ACT, lambda e, pm=pm, nxt=nxt: e.copy(out=Dd[u][nxt][:, hs, :], in_=vh(pm)), reads=[pmk], writes=[("D", u, nxt, hf)])
                P.op(DVE, lambda e, pm2=pm2, nxt=nxt, DT_c=DT_c: e.tensor_tensor(out=DTd[u][nxt][:, hs, :], in0=vh(pm2), in1=DT_c[:, hs, :], op=ALU.add),
                     reads=[pm2k, kDT], writes=[("DT", u, nxt, hf)])
                cur = nxt
            yield None
            TT_ = DTd[u][cur]
            pw, pwk = next_ps(C, 2, 6)

            def f_w(e):
                ins = None
                for c in range(ncn):
                    ins = e.matmul(pw[:, c * 128:(c + 1) * 128], lhsT=kbd[u][:, c0 + c, :], rhs=TT_[:, c0 + c, :], start=True, stop=True)
                return ins
            P.op(PE, f_w, reads=[K_("kbd"), ("DT", u, cur, hf)], writes=[pwk])
            yield None
            P.op(ACT, lambda e: e.activation(out=nwT[u][:, hs, :], in_=vh(pw), func=AF.Copy, scale=-1.0), reads=[pwk], writes=[K_("nwT")])
            yield (TT_, cur)

        def seq(h, n, u, TT_, cur, po, pok):
            c = n % 4
            hf = c // 2
            w = u
            v2 = n % 2
            cl = slice(c * 128, (c + 1) * 128)
            K_ = lambda nm: (nm, u, hf)
            pv, pvk = next_ps(C, 1, 2)
            mm_chain(P, pv[:, 0:128], [(TT_[:, c, :], vb[u][:, c, :]), (nwT[u][:, c, :], Sbf[:, :])], reads=[("DT", u, cur, hf), K_("vb"), K_("nwT"), ("Sbf",)], writes=[pvk])
            P.op(ACT, lambda e: e.copy(out=vnew[v2][:, :], in_=pv[:, 0:128]), reads=[pvk], writes=[("vnew", v2)])
            mm_chain(P, po[:, cl], [(Sbf[:, :], QdT[w][:, cl]), (vnew[v2][:, :], qkT[u][:, c, :])], reads=[("Sbf",), ("QdT", w), ("vnew", v2), K_("qkT")], writes=[pok])
            pS, pSk = next_ps(C, 1, 2)
            mm_chain(P, pS[:, 0:128], [(kdec[u][:, c, :], vnew[v2][:, :])], reads=[K_("kdec"), ("vnew", v2)], writes=[pSk])
            P.op(DVE, lambda e: e.scalar_tensor_tensor(out=Sst[:, :], in0=Sst[:, :], scalar=egl[:, n, h:h + 1], in1=pS[:, 0:128], op0=ALU.mult, op1=ALU.add),
                 reads=[pSk, ("S",)], writes=[("S",)])
            P.op(ACT, lambda e: e.copy(out=Sbf[:, :], in_=Sst[:, :]), reads=[("S",)], writes=[("Sbf",)])

        def finish_tile(h, jt, v, po, pok):
            hb = h % 2
            tsl = slice(jt * TT, (jt + 1) * TT)
            P.op(ACT, lambda e: e.copy(out=o_sb[v][:, :], in_=po[:, :]), reads=[pok], writes=[("osb", v)])
            P.op(ACT, lambda e: e.activation(out=osq[v][:, :], in_=po[:, :], func=AF.Square), reads=[pok], writes=[("osq", v)])
            pss, pssk = next_ps(C, 1, 2)
            mm_chain(P, pss[:, :], [(C.ones_bf[:, :], osq[v][:, :])], reads=[("osq", v)], writes=[pssk])
            P.op(ACT, lambda e: e.activation(out=rr[v][:, :], in_=pss[:, :], func=AF.Sqrt, scale=1.0 / 128, bias=C.eps_col[:, 0:1]), reads=[pssk], writes=[("rr", v)])
            P.op(DVE, lambda e: e.reciprocal(out=rr[v][:, :], in_=rr[v][:, :]), reads=[("rr", v)], writes=[("rr", v)])
            P.op(POOL, lambda e: e.tensor_tensor(out=o_sb[v][:, :], in0=o_sb[v][:, :], in1=rr[v][:, :], op=ALU.mult), reads=[("osb", v), ("rr", v)], writes=[("osb", v)])
            P.op(DVE, lambda e: e.scalar_tensor_tensor(out=mo[v][:, :], in0=o_sb[v][:, :], scalar=onorm[:, 0:1], in1=ZT[hb][:, tsl], op0=ALU.mult, op1=ALU.mult),
                 reads=[("osb", v), ("gz", hb)], writes=[("mo", v)])
            P.op(POOL, lambda e: e.dma_start(out=C.scr["mT"][h * 128:(h + 1) * 128, tsl], in_=mo[v][:, :]), reads=[("mo", v)], dkey=("Gst", v))

        tiles = [(h, jt) for h in range(8) for jt in range(NT)]
        load_head(0)
        load_head(1)

        def run_pair(h, jt, u, hooks):
            g0 = prep_half(h, jt, u, 0, 2)
            g1 = prep_half(h, jt, u, 2, 2)
            res = [None, None]
            step = 0
            alive = [True, True]
            import os
            if os.environ.get("GDN_SEQ"):
                for gi, g in enumerate((g0, g1)):
                    for r in g:
                        if r is not None:
                            res[gi] = r
                alive = [False, False]
            while alive[0] or alive[1]:
                for gi, g in enumerate((g0, g1)):
                    if alive[gi]:
                        try:
                            r = next(g)
                            if r is not None:
                                res[gi] = r
                        except StopIteration:
                            alive[gi] = False
                if step in hooks:
                    hooks[step]()
                step += 1
            assert res[0][1] == res[1][1]
            return res[0]

        pend = run_pair(0, 0, 0, {})
        for k, (h, jt) in enumerate(tiles):
            u = k % 2
            if jt == 0:
                P.op(POOL, lambda e: e.memset(Sst[:, :], 0.0), writes=[("S",)])
                P.op(POOL, lambda e: e.memset(Sbf[:, :], 0.0), writes=[("Sbf",)])
            po, pok = next_ps(C, 0, 1)
            done = []

            def mk(c):
                def f():
                    seq(h, 4 * jt + c, u, pend[0], pend[1], po, pok)
                    done.append(c)
                return f
            nxt_p = None
            if k + 1 < len(tiles):
                h2, jt2 = tiles[k + 1]
                nxt_p = run_pair(h2, jt2, 1 - u, {4: mk(0), 10: mk(1), 16: mk(2), 22: mk(3)})
            for c in range(4):
                if c not in done:
                    seq(h, 4 * jt + c, u, pend[0], pend[1], po, pok)
            finish_tile(h, jt, u, po, pok)
            if jt == NT - 1 and h + 2 < 8:
                load_head(h + 2)
            pend = nxt_p
        P.barrier()


def stage_l1_out(P, C):
    nc = P.nc
    xres = C.xres
    with ExitStack() as es:
        sb = lambda name, shape, dt=F32: es.enter_context(nc.sbuf_tensor(name, list(shape), dt))
        wout = sb("O1_wout", [128, 8, 1024], BF16)
        wload(P, C, wout, C.w["l1_w_out"], 8, 1024, gain=None, name="O1_wout")
        P.barrier()
        xt = [sb(f"O1_xt{i}", [128, 8, TT]) for i in range(2)]
        mg = [sb(f"O1_mg{i}", [128, 8, TT], BF16) for i in range(2)]
        xv = xres.rearrange("(c p) t -> p c t", p=128)
        mv = C.scr["mT"].rearrange("(c p) t -> p c t", p=128)
        for j in range(NT):
            u = j % 2
            tsl = slice(j * TT, (j + 1) * TT)
            P.op(SP, lambda e, u=u, tsl=tsl: e.dma_start(out=xt[u][:, :, :], in_=xv[:, :, tsl]), writes=[("xt", u)], dkey=("O1ld", "x", u))
            P.op(SP, lambda e, u=u, tsl=tsl: e.dma_start(out=mg[u][:, :, :], in_=mv[:, :, tsl]), writes=[("mg", u)], dkey=("O1ld", "m", u))
            for fb in range(8):
                pa, pak = next_ps(C)
                mm_chain(P, pa[:, :], [(wout[:, c, fb * 128:(fb + 1) * 128], mg[u][:, c, :]) for c in range(8)], reads=[*wkeys(C, "O1_wout", fb * 128), ("mg", u)], writes=[pak])
                P.op(DVE, lambda e, pa=pa, u=u, fb=fb: e.tensor_tensor(out=xt[u][:, fb, :], in0=pa[:, :], in1=xt[u][:, fb, :], op=ALU.add),
                     reads=[pak, ("xt", u)], writes=[("xt", u)])
            P.op(POOL, lambda e, u=u, tsl=tsl: e.dma_start(out=xv[:, :, tsl], in_=xt[u][:, :, :]), reads=[("xt", u)], dkey=("O1st", u))
        P.barrier()


def stage_copy_in(P, C):
    P.op(SP, lambda e: e.dma_start(out=C.xres[:, :], in_=C.xT_in[:, :]), dkey=("cpin",))
    P.barrier()


def stage_copy_out(P, C):
    P.op(SP, lambda e: e.dma_start(out=C.outT[:, :], in_=C.xres[:, :]), dkey=("cpout",))
    P.barrier()


WNAMES = ["xa_wq", "xa_wkv", "xa_wo", "ffn_w_up", "ffn_conv", "ffn_w_down"]
GNAMES = ["xa_norm", "mem_norm", "ffn_norm"]


def build_program(plan):
    nc = bass.Bass("TRN2", target_bir_lowering=False)
    P = Prog(nc)
    C = Ctx()
    C.ps_cnt = {}
    C.wreg = {}
    dt_in = lambda name, shape, dt=F32: nc.dram_tensor(name, list(shape), dt, kind="ExternalInput").ap()
    C.xT_in = dt_in("xT", [D, S])
    C.memT = dt_in("memT", [D, NMEM])
    C.outT = nc.dram_tensor("outT", [D, S], F32, kind="ExternalOutput").ap()
    C.xres = nc.dram_tensor("xres", [D, S], F32, kind="Internal").ap()
    C.w = {}
    for lp in ["l0_", "l1_"]:
        C.w[lp + "xa_wq"] = dt_in(lp + "xa_wq", [D, D])
        C.w[lp + "xa_wkv"] = dt_in(lp + "xa_wkv", [D, 2 * D])
        C.w[lp + "xa_wo"] = dt_in(lp + "xa_wo", [D, D])
        C.w[lp + "ffn_w_up"] = dt_in(lp + "ffn_w_up", [D, 2 * FFN])
        C.w[lp + "ffn_conv"] = dt_in(lp + "ffn_conv", [128, 3, 2 * NPAIR])
        C.w[lp + "ffn_w_down"] = dt_in(lp + "ffn_w_down", [FFN, D])
    for n, shp in [("l0_w_in", [D, 2560]), ("l0_w_in_sw", [D, 1024]), ("l0_ret_norm", [128, 4]), ("l0_s5_w_glu", [512, 512]),
                   ("l0_s5_b_glu", [128, 4]), ("l0_w_out", [D, D]), ("s5_lam_s", [128, 2, 32]), ("s5_ldt_s", [128, 32]),
                   ("s5_lam_b", [16, 2, 2048]), ("s5_ldt_b", [16, 2048]), ("s5_b_b", [16, 2, 2048]), ("s5_c1", [128, 32, 16]),
                   ("s5_c2", [128, 32, 16]), ("s5_d_t", [16, 32])]:
        C.w[n] = dt_in(n, shp)
    for n, shp in [("l1_w_in", [D, 4112]), ("l1_conv", [128, 4, 24]), ("l1_hp", [8, 2]), ("l1_o_norm", [128, 1]), ("l1_w_out", [D, D])]:
        C.w[n] = dt_in(n, shp)
    C.cst = {}
    for n, shp in [("rot_tab", [128, 4, S]), ("gq_tab", [128, 4, TT]), ("dt_tab", [128, 4, 128]), ("kd_tab", [128, 4]),
                   ("id16", [16, 16]), ("tpos", [128, S]), ("t0s", [128, 32, 8]), ("t0b", [16, 8, 128]), ("cmask", [8, TT]), ("sel16", [16, 16, 128]), ("l2c", [16, 2]), ("gmasks", [128, 18, 128]), ("gsel", [8, 8, 128]),
                   ("gsellast", [128, 128])]:
        C.cst[n] = dt_in(n, shp)
    C.scr = {}
    for n, shp in [("qT", [512, S]), ("qdT", [512, S]), ("kT", [512, S]), ("gT", [512, S]), ("uT", [512, S]), ("vtok", [S, 512]),
                   ("mT", [D, S]), ("ygT", [512, S]), ("gq", [D, S]), ("gk", [D, S]), ("gv", [D, S]), ("gz", [D, S])]:
        C.scr[n] = nc.dram_tensor("scr_" + n, list(shp), BF16, kind="Internal").ap()
    C.scr32 = {n: nc.dram_tensor("scr_" + n, [8, S], F32, kind="Internal").ap() for n in ["gcT", "btT"]}
    gnames = [lp + g for lp in ["l0_", "l1_"] for g in GNAMES + ["mix_norm"]] + ["final_norm"]
    gains_d = dt_in("gains", [128, len(gnames), 8])
    consts_bf = dt_in("consts_bf", [128, 2, 128], BF16)
    C.cst_oh16 = dt_in("oh16", [128, 16, 16], BF16)

    gains_t = P.sb("gains_t", [128, len(gnames), 8])
    cbf = P.sb("cbf", [128, 2, 128], BF16)
    C.eps_col = P.sb("eps_col", [128, 1])
    C.one_col = P.sb("one_col", [128, 1])
    C.eps128_col = P.sb("eps128_col", [128, 1])
    C.psum = [P.ps(f"psum{i}", [128, 512]) for i in range(6)]
    C.psbs = [P.ps(f"psb{i}", [128, 1024], BF16) for i in range(2)]
    C.psb = C.psbs[0]
    C.psb_rr = 0
    P.op(SP, lambda e: e.dma_start(out=gains_t[:, :, :], in_=gains_d), writes=[gain_key(None)], dkey=("gains",))
    P.op(SP, lambda e: e.dma_start(out=cbf[:, :, :], in_=consts_bf), writes=[("cbf",)], dkey=("cbf",))
    P.op(POOL, lambda e: e.memset(C.eps_col[:, :], EPS), writes=[("eps",)])
    P.op(POOL, lambda e: e.memset(C.one_col[:, :], 1.0), writes=[("one",)])
    P.op(POOL, lambda e: e.memset(C.eps128_col[:, :], 128.0 * EPS), writes=[("eps128",)])
    P.barrier()
    C.gains = {n: gains_t[:, i, :] for i, n in enumerate(gnames)}
    C.ident_bf = cbf[:, 0, :]
    C.ones_bf = cbf[:, 1, :]

    for st in plan:
        if st == "copy_in":
            stage_copy_in(P, C)
        elif st == "copy_out":
            stage_copy_out(P, C)
        elif st == "l0_inproj":
            stage_l0_inproj(P, C)
        elif st == "l0_ret":
            stage_retention(P, C)
        elif st == "l0_s5":
            stage_s5(P, C)
        elif st == "l0_out":
            stage_l0_out(P, C)
        elif st == "l1_inproj":
            stage_l1_inproj(P, C)
        elif st == "l1_gdn":
            stage_gdn(P, C)
        elif st == "l1_out":
            stage_l1_out(P, C)
        elif st.endswith("_xa"):
            stage_xattn(P, C, st[:3])
        elif st.endswith("_ffn"):
            stage_ffn(P, C, st[:3], final=False)
        elif st.endswith("_ffnfinal"):
            stage_ffn(P, C, st[:3], final=True)
        else:
            raise ValueError(st)
        P.barrier(final=True)
    P.emit()
    return nc, P, gnames


def host_prep(inputs, gnames):
    shared = {}
    for lp in ["l0_", "l1_"]:
        for n in ["xa_wq", "xa_wkv", "xa_wo", "ffn_w_up", "ffn_w_down"]:
            shared[lp + n] = np.ascontiguousarray(inputs[lp + n], dtype=np.float32)
        cw = np.asarray(inputs[lp + "ffn_conv"], dtype=np.float32)
        shared[lp + "ffn_conv"] = np.ascontiguousarray(cw.reshape(3, 2 * NPAIR, 128).transpose(2, 0, 1))
    f32 = lambda a: np.ascontiguousarray(np.asarray(a, dtype=np.float32))
    w_in = f32(inputs["l0_w_in"])
    shared["l0_w_in"] = w_in
    qk = w_in[:, :1024].reshape(D, 8, 128)
    shared["l0_w_in_sw"] = f32(np.concatenate([qk[:, :, 64:], qk[:, :, :64]], axis=2).reshape(D, 1024))
    shared["l0_ret_norm"] = f32(np.asarray(inputs["l0_ret_norm"]).reshape(4, 128).T)
    shared["l0_s5_w_glu"] = f32(inputs["l0_s5_w_glu"])
    shared["l0_s5_b_glu"] = f32(np.asarray(inputs["l0_s5_b_glu"]).reshape(4, 128).T)
    shared["l0_w_out"] = f32(inputs["l0_w_out"])
    lre = np.asarray(inputs["l0_s5_lambda_re"], np.float32)
    lim = np.asarray(inputs["l0_s5_lambda_im"], np.float32)
    ldt = np.asarray(inputs["l0_s5_log_dt"], np.float32)
    lam_s = np.stack([np.concatenate([lre.T, lre.T], 0), np.concatenate([lim.T, lim.T], 0)], axis=1)
    shared["s5_lam_s"] = f32(lam_s)
    shared["s5_ldt_s"] = f32(np.broadcast_to(ldt[None, :], (128, 32)))
    shared["s5_lam_b"] = f32(np.broadcast_to(np.stack([lre.reshape(-1), lim.reshape(-1)], 0)[None], (16, 2, 2048)))
    shared["s5_ldt_b"] = f32(np.broadcast_to(np.repeat(ldt, 64)[None], (16, 2048)))
    bre = np.asarray(inputs["l0_s5_b_re"], np.float32).transpose(1, 0, 2).reshape(16, 2048)
    bim = np.asarray(inputs["l0_s5_b_im"], np.float32).transpose(1, 0, 2).reshape(16, 2048)
    shared["s5_b_b"] = f32(np.stack([bre, bim], axis=1))
    cre = np.asarray(inputs["l0_s5_c_re"], np.float32).transpose(1, 0, 2)
    cim = np.asarray(inputs["l0_s5_c_im"], np.float32).transpose(1, 0, 2)
    shared["s5_c1"] = f32(np.concatenate([cre, cim], 0))
    shared["s5_c2"] = f32(np.concatenate([cim, cre], 0))
    shared["s5_d_t"] = f32(np.asarray(inputs["l0_s5_d"], np.float32).T)
    shared["l1_w_in"] = f32(inputs["l1_w_in"])
    shared["l1_conv"] = f32(np.asarray(inputs["l1_conv"], np.float32).reshape(4, 24, 128).transpose(2, 0, 1))
    shared["l1_hp"] = f32(np.stack([np.asarray(inputs["l1_a_log"], np.float32), np.asarray(inputs["l1_dt_bias"], np.float32)], axis=1))
    shared["l1_o_norm"] = f32(np.asarray(inputs["l1_o_norm"], np.float32).reshape(128, 1))
    shared["l1_w_out"] = f32(inputs["l1_w_out"])
    cm = np.ones((8, TT), np.float32)
    cm[:, ::128] = 0.0
    shared["cmask"] = cm
    s16 = np.zeros((16, 16, 128), np.float32)
    oh = np.zeros((128, 16, 16), np.float32)
    for q_ in range(16):
        s16[q_, q_, :] = 1.0
        oh[:, q_, q_] = 1.0
    shared["sel16"] = s16
    shared["oh16"] = oh.astype(ml_dtypes.bfloat16)
    l2 = np.zeros((16, 2), np.float32)
    l2[:8, 0] = 128.0
    l2[:8, 1] = 128.0 * EPS
    l2[8:, 0] = 1.0
    l2[8:, 1] = EPS
    shared["l2c"] = l2
    ii_, jj_ = np.meshgrid(np.arange(128), np.arange(128), indexing="ij")
    gm = np.zeros((128, 18, 128), np.float32)
    gm[:, 0, :] = (ii_ > jj_)
    gm[:, 1, :] = (jj_ >= ii_)
    for li, s_ in enumerate([1, 2, 4, 8, 16, 32, 64]):
        m = ((ii_ // (2 * s_)) == (jj_ // (2 * s_))) & ((ii_ % (2 * s_)) >= s_) & ((jj_ % (2 * s_)) < s_)
        if li < 6:
            gm[:, 2 + li, :] = -m.astype(np.float32)
        gm[:, 8 + li, :] = -m.T.astype(np.float32)
    gm[:, 15, :] = np.eye(128, dtype=np.float32)
    BIG = 30000.0
    gm[:, 16, :] = BIG * (1.0 - gm[:, 0, :])
    gm[:, 17, :] = -BIG * gm[:, 0, :]
    shared["gmasks"] = gm
    gs = np.zeros((8, 8, 128), np.float32)
    for h_ in range(8):
        gs[h_, h_, :] = 1.0
    shared["gsel"] = gs
    sl = np.zeros((128, 128), np.float32)
    sl[127, :] = 1.0
    shared["gsellast"] = sl
    inv = np.exp(-math.log(10000.0) * np.arange(64, dtype=np.float32) / 64).astype(np.float32)
    ang = (np.arange(S, dtype=np.float32)[:, None] * inv[None, :]).astype(np.float32).astype(np.float64)
    cosT = np.cos(ang).T
    sinT = np.sin(ang).T
    cos128 = np.concatenate([cosT, cosT], 0)
    sin128 = np.concatenate([-sinT, sinT], 0)
    ksc = 128.0 ** -0.5
    shared["rot_tab"] = f32(np.stack([cos128, sin128, cos128 * ksc, sin128 * ksc], axis=1))
    gam = np.array(RET_GAMMA, np.float64)
    ii = np.arange(TT) % 128
    shared["gq_tab"] = f32(np.broadcast_to((gam[:, None] ** (ii[None, :] + 1))[None], (128, 4, TT)))
    jj = np.arange(128)
    diff = jj[None, :] - jj[:, None]
    dtab = np.where(diff[:, None, :] >= 0, gam[None, :, None] ** np.maximum(diff[:, None, :], 0), 0.0)
    shared["dt_tab"] = f32(dtab)
    shared["kd_tab"] = f32(gam[None, :] ** (127 - jj[:, None]))
    shared["id16"] = f32(np.eye(16))
    shared["tpos"] = f32(np.broadcast_to(np.arange(S, dtype=np.float32)[None], (128, S)))
    shared["t0s"] = f32(np.broadcast_to((np.arange(8, dtype=np.float32) * TT)[None, None, :], (128, 32, 8)))
    shared["t0b"] = f32(np.broadcast_to((np.arange(8, dtype=np.float32) * TT)[None, :, None], (16, 8, 128)))
    g = np.stack([np.asarray(inputs[n], np.float32).reshape(8, 128).T for n in gnames], axis=1)
    shared["gains"] = np.ascontiguousarray(g)
    cb = np.stack([np.eye(128, dtype=np.float32), np.ones((128, 128), np.float32)], axis=1)
    shared["consts_bf"] = np.ascontiguousarray(cb).astype(ml_dtypes.bfloat16)
    return shared


PLAN_FULL = ["copy_in", "l0_inproj", "l0_ret", "l0_s5", "l0_out", "l0_xa", "l0_ffn", "l1_inproj", "l1_gdn", "l1_out", "l1_xa", "l1_ffnfinal"]


def kernel(**inputs):
    nc, P, gnames = build_program(PLAN_FULL)
    shared = host_prep(inputs, gnames)
    x = np.asarray(inputs["x"], np.float32)
    mem = np.asarray(inputs["mem"], np.float32)
    in_maps = []
    for b in range(8):
        m = dict(shared)
        m["xT"] = np.ascontiguousarray(x[b].T)
        m["memT"] = np.ascontiguousarray(mem[b].T)
        in_maps.append(m)
    res = run_bass_kernel_spmd(nc, in_maps, core_ids=list(range(8)))
    out = np.stack([np.ascontiguousarray(res.results[b]["outT"].T) for b in range(8)], axis=0)
    return out.astype(np.float32)
```

```python
import math
from contextlib import ExitStack

import numpy as np
import ml_dtypes
import concourse.bass as bass
import concourse.mybir as mybir
from concourse.bass_utils import run_bass_kernel_spmd

F32 = mybir.dt.float32
BF16 = mybir.dt.bfloat16
I32 = mybir.dt.int32
ALU = mybir.AluOpType
AF = mybir.ActivationFunctionType
AX = mybir.AxisListType
PE, ACT, DVE, POOL, SP = "tensor", "scalar", "vector", "gpsimd", "sync"

D = 1024
S = 4096
TT = 512
NT = S // TT
NMEM = 256
EPS = 1e-6
FFN = 2816
NPAIR = FFN // 128


class Prog:
    def __init__(self, nc):
        self.nc = nc
        self.ops = []
        self.track = {}
        self.stack = ExitStack()
        self.fence = frozenset()
        self.last_eng = {}
        self.dma_since = []
        self.ps_rr = 0
        self.slot_map = {}
        self.wslot_map = {}

    def sb(self, name, shape, dt=F32):
        return self.stack.enter_context(self.nc.sbuf_tensor(name, list(shape), dt))

    def ps(self, name, shape, dt=F32):
        return self.stack.enter_context(self.nc.psum_tensor(name, list(shape), dt))

    def op(self, eng, fn, reads=(), writes=(), dkey=None):
        oid = len(self.ops)
        deps = set(self.fence)
        writes = list(writes) + [k for k in reads if k[0] in ("ps", "psb")]
        for k in reads:
            t = self.track.get(k)
            if t and t[0] is not None:
                deps.add(t[0])
        for k in writes:
            t = self.track.get(k)
            if t:
                if t[0] is not None:
                    deps.add(t[0])
                deps.update(t[1])
        for k in reads:
            t = self.track.setdefault(k, [None, []])
            t[1].append(oid)
        for k in writes:
            self.track[k] = [oid, []]
        deps.discard(oid)
        isw = dkey is not None and dkey[0] == "W"
        if dkey is not None:
            m = self.wslot_map if isw else self.slot_map
            if dkey not in m:
                m[dkey] = ("w" if isw else "d", len(m))
            dkey = m[dkey]
        self.ops.append(dict(eng=eng, fn=fn, deps=deps, dkey=dkey, sig=False))
        if not isw:
            self.last_eng[eng] = oid
            if dkey is not None:
                self.dma_since.append(oid)
        return oid

    def barrier(self, final=False):
        self.fence = frozenset(list(self.last_eng.values()) + self.dma_since)
        self.dma_since = []
        self.track = {k: v for k, v in self.track.items() if k[0] == "W"}
        self.slot_map = {}
        if final:
            self.wslot_map = {}

    def emit(self):
        nc = self.nc
        ops = self.ops
        for o in ops:
            for d in o["deps"]:
                p = ops[d]
                if p["dkey"] is None and p["eng"] == PE and o["eng"] == PE and o["dkey"] is None:
                    continue
                p["sig"] = True
        engs = [PE, ACT, DVE, POOL, SP]
        cnt = {e: 0 for e in engs}
        dcnt = {}
        for o in ops:
            if o["dkey"] is not None:
                dcnt[o["dkey"]] = dcnt.get(o["dkey"], 0) + 16
                o["semk"] = ("d", o["dkey"])
                o["seq"] = dcnt[o["dkey"]]
            elif o["sig"]:
                cnt[o["eng"]] += 1
                o["semk"] = ("e", o["eng"])
                o["seq"] = cnt[o["eng"]]
        semkeys = [("e", e) for e in engs if cnt[e] > 0] + [("d", k) for k in dcnt]
        sems = {}
        for sk in semkeys:
            sems[sk] = self.stack.enter_context(nc.semaphore("s_" + "_".join(str(x) for x in sk)))
        per_eng = {e: [] for e in engs}
        for i, o in enumerate(ops):
            per_eng[o["eng"]].append(i)
        self.stats = {e: len(per_eng[e]) for e in engs}
        self.stats["sems"] = len(sems)
        self.stats["maxcnt"] = dict(cnt)
        self.stats["dcnt"] = max(dcnt.values()) if dcnt else 0

        def run_engine(ename, eobj):
            waited = {}
            for i in per_eng[ename]:
                o = ops[i]
                need = {}
                for d in o["deps"]:
                    p = ops[d]
                    if "semk" not in p:
                        continue
                    if p["dkey"] is None and p["eng"] == PE and ename == PE and o["dkey"] is None:
                        continue
                    sk = p["semk"]
                    if p["seq"] > need.get(sk, 0):
                        need[sk] = p["seq"]
                for sk, v in need.items():
                    if waited.get(sk, 0) >= v:
                        continue
                    eobj.wait_ge(sems[sk], v)
                    waited[sk] = v
                ins = o["fn"](eobj)
                if o["dkey"] is not None:
                    ins.then_inc(sems[o["semk"]], 16)
                elif o["sig"]:
                    ins.then_inc(sems[o["semk"]], 1)
            last = {}
            for i in per_eng[ename]:
                o = ops[i]
                if o["dkey"] is not None:
                    last[o["semk"]] = max(last.get(o["semk"], 0), o["seq"])
            for sk, v in last.items():
                if waited.get(sk, 0) < v:
                    eobj.wait_ge(sems[sk], v)

        block = self.stack.enter_context(nc.Block())
        if per_eng[SP]:
            @block.sync
            def _(e):
                run_engine(SP, e)
        if per_eng[PE]:
            @block.tensor
            def _(e):
                run_engine(PE, e)
        if per_eng[ACT]:
            @block.scalar
            def _(e):
                run_engine(ACT, e)
        if per_eng[DVE]:
            @block.vector
            def _(e):
                run_engine(DVE, e)
        if per_eng[POOL]:
            @block.gpsimd
            def _(e):
                run_engine(POOL, e)
        self.stack.close()


class Ctx:
    pass


def mm_chain(P, ps_ap, pairs, reads, writes):
    pairs = list(pairs)

    def fn(e):
        n = len(pairs)
        ins = None
        for i, (l, r) in enumerate(pairs):
            ins = e.matmul(ps_ap, lhsT=l, rhs=r, start=(i == 0), stop=(i == n - 1))
        return ins
    return P.op(PE, fn, reads, writes)


def next_psb(C):
    b = C.psb_rr % 2
    C.psb_rr += 1
    return C.psbs[b], ("psb", b)


def next_ps(C, lo=0, hi=None):
    hi = len(C.psum) if hi is None else hi
    k = (lo, hi)
    r = C.ps_cnt.get(k, 0)
    C.ps_cnt[k] = r + 1
    b = lo + r % (hi - lo)
    return C.psum[b], ("ps", b)


def wload(P, C, dst, w_dram, kc, F, gain=None, name="w", f_dst0=0):
    assert gain is None
    FW = 1408
    pcs = []
    for f0 in range(0, F, FW):
        fw = min(FW, F - f0)
        pcs.append((f0, fw))
        for c in range(kc):
            P.op(POOL, lambda e, c=c, f0=f0, fw=fw: e.dma_start(out=dst[:, c, f0:f0 + fw], in_=w_dram[c * 128:(c + 1) * 128, f0:f0 + fw]),
                 writes=[("W", name, f0, c)], dkey=("W", name, f0))
    C.wreg[name] = (pcs, kc)


def wkeys(C, name, col=None, n=128):
    pcs, kc = C.wreg[name]
    out = []
    for (f0, fw) in pcs:
        if col is None or (f0 < col + n and col < f0 + fw):
            out += [("W", name, f0, c) for c in range(kc)]
    return out


def gain_key(g):
    return ("gains",)


def load_xt(P, C, src, j, xt, xt_key, ncols=TT, col0=None, dkey="xt"):
    c0 = j * ncols if col0 is None else col0
    srcv = src.rearrange("(c p) t -> p c t", p=128)
    P.op(SP, lambda e: e.dma_start(out=xt[:, :, :], in_=srcv[:, :, c0:c0 + ncols]), writes=[xt_key], dkey=(dkey, xt_key))


def norm_chunked(P, C, xt, xt_key, hn, hn_key, sqs, rstd, gain, kc=8):
    pst, psk = next_ps(C)
    for c in range(kc):
        q = c % len(sqs)
        P.op(ACT, lambda e, c=c, q=q: e.activation(out=sqs[q][:, :], in_=xt[:, c, :], func=AF.Square), reads=[xt_key], writes=[("sqs", q)])
        P.op(PE, lambda e, c=c, q=q: e.matmul(pst[:, :], lhsT=C.ones_bf[:, :], rhs=sqs[q][:, :], start=(c == 0), stop=(c == kc - 1)),
             reads=[("sqs", q)], writes=[psk])
    P.op(ACT, lambda e: e.activation(out=rstd[:, :], in_=pst[:, :], func=AF.Sqrt, scale=1.0 / D, bias=C.eps_col[:, 0:1]), reads=[psk], writes=[("rstd",)])
    P.op(DVE, lambda e: e.reciprocal(out=rstd[:, :], in_=rstd[:, :]), reads=[("rstd",)], writes=[("rstd",)])
    for c in range(kc):
        P.op(DVE, lambda e, c=c: e.scalar_tensor_tensor(out=hn[:, c, :], in0=xt[:, c, :], scalar=gain[:, c:c + 1], in1=rstd[:, :], op0=ALU.mult, op1=ALU.mult),
             reads=[xt_key, ("rstd",)], writes=[(hn_key[0], "a" if c < 4 else "b")])


def norm_tile(P, C, src, j, xt, xt_key, hn, hn_key, sq, rstd, ncols=TT, kc=8, col0=None, dkey="xt", sq_keys=(("sq",),), pshi=None, gain=None, load=True):
    if load:
        load_xt(P, C, src, j, xt, xt_key, ncols=ncols, col0=col0, dkey=dkey)
    P.op(ACT, lambda e: e.activation(out=sq[:, :, :ncols], in_=xt[:, :, :], func=AF.Square), reads=[xt_key], writes=list(sq_keys))
    pst, psk = next_ps(C, 0, pshi)
    mm_chain(P, pst[:, :ncols], [(C.ones_bf[:, :], sq[:, c, :ncols]) for c in range(kc)], reads=list(sq_keys), writes=[psk])
    P.op(ACT, lambda e: e.activation(out=rstd[:, :ncols], in_=pst[:, :ncols], func=AF.Sqrt, scale=1.0 / D, bias=C.eps_col[:, 0:1]),
         reads=[psk], writes=[("rstd",)])
    P.op(DVE, lambda e: e.reciprocal(out=rstd[:, :ncols], in_=rstd[:, :ncols]), reads=[("rstd",)], writes=[("rstd",)])
    for c in range(kc):
        P.op(DVE, lambda e, c=c: e.scalar_tensor_tensor(out=hn[:, c, :], in0=xt[:, c, :], scalar=gain[:, c:c + 1], in1=rstd[:, :ncols], op0=ALU.mult, op1=ALU.mult),
             reads=[xt_key, ("rstd",)], writes=[(hn_key[0], "a" if c < 4 else "b")])


def stage_xattn(P, C, lp):
    nc = P.nc
    W = C.w
    xres = C.xres
    scale = 256 ** -0.5
    with ExitStack() as es:
        sb = lambda name, shape, dt=F32: es.enter_context(nc.sbuf_tensor("sb_" + lp + name, list(shape), dt))
        wq = sb("xa_wq", [128, 8, 1024], BF16)
        wo = sb("xa_wo", [128, 8, 1024], BF16)
        kT = sb("xa_kT", [128, 8, NMEM], BF16)
        vtok = sb("xa_vtok", [128, 2, 1024], BF16)
        g_xa = C.gains[lp + "xa_norm"]
        g_mem = C.gains[lp + "mem_norm"]
        wload(P, C, wq, W[lp + "xa_wq"], 8, 1024, name="xa_wq")
        wload(P, C, wo, W[lp + "xa_wo"], 8, 1024, gain=None, name="xa_wo")
        P.barrier()
        xts = [sb(f"xa_xt{i}", [128, 8, TT]) for i in range(2)]
        hns = [sb(f"xa_hn{i}", [128, 8, TT], BF16) for i in range(2)]
        xt = xts[0]
        hn = hns[0]
        sq = sb("xa_sq", [128, 8, TT], BF16)
        rstd = sb("xa_rstd", [128, TT])
        with ExitStack() as es2:
            wkv = es2.enter_context(nc.sbuf_tensor("sb_" + lp + "xa_wkv", [128, 8, 2048], BF16))
            memx = es2.enter_context(nc.sbuf_tensor("sb_" + lp + "xa_memx", [128, 8, NMEM], F32))
            memn = es2.enter_context(nc.sbuf_tensor("sb_" + lp + "xa_memn", [128, 8, NMEM], BF16))
            wload(P, C, wkv, W[lp + "xa_wkv"], 8, 2048, name="xa_wkv")
            P.barrier()
            norm_tile(P, C, C.memT, 0, memx, ("memx",), memn, ("memn",), sq, rstd, ncols=NMEM, dkey="memx", gain=g_mem)
            for fb in range(8):
                pst, psk = next_ps(C)
                mm_chain(P, pst[:, :NMEM], [(wkv[:, c, fb * 128:(fb + 1) * 128], memn[:, c, :]) for c in range(8)],
                         reads=[*wkeys(C, "xa_wkv", fb * 128), ("memn", "a"), ("memn", "b")], writes=[psk])
                P.op(ACT, lambda e, pst=pst, fb=fb: e.copy(out=kT[:, fb, :], in_=pst[:, :NMEM]), reads=[psk], writes=[("kT",)])
            for mb in range(2):
                for hf in range(2):
                    pst, psk = next_ps(C)
                    mm_chain(P, pst[:, :], [(memn[:, c, mb * 128:(mb + 1) * 128], wkv[:, c, 1024 + hf * 512:1024 + (hf + 1) * 512]) for c in range(8)],
                             reads=[*wkeys(C, "xa_wkv", 1024 + hf * 512, 512), ("memn", "a"), ("memn", "b")], writes=[psk])
                    P.op(ACT, lambda e, pst=pst, mb=mb, hf=hf: e.copy(out=vtok[:, mb, hf * 512:(hf + 1) * 512], in_=pst[:, :]),
                         reads=[psk], writes=[("vtok",)])
            P.barrier()
        qT = sb("xa_qT", [128, 8, TT], BF16)
        oT = sb("xa_oT", [128, 8, TT], BF16)
        pexp = [sb(f"xa_pexp{i}", [128, NMEM]) for i in range(4)]
        pn = [sb(f"xa_pn{i}", [128, NMEM], BF16) for i in range(4)]
        pT = [sb(f"xa_pT{i}", [128, 2, 128], BF16) for i in range(8)]
        st = [sb(f"xa_st{i}", [128, 4]) for i in range(4)]
        norm_tile(P, C, xres, 0, xts[0], ("xt", 0), hns[0], ("hn0",), sq, rstd, gain=g_xa)
        for j in range(NT):
            bb_ = j % 2
            xt = xts[bb_]
            hn = hns[bb_]
            XK = ("xt", bb_)
            HN = [("hn%d" % bb_, "a"), ("hn%d" % bb_, "b")]
            for fb in range(8):
                pst, psk = next_ps(C)
                mm_chain(P, pst[:, :], [(wq[:, c, fb * 128:(fb + 1) * 128], hn[:, c, :]) for c in range(8)],
                         reads=[*wkeys(C, "xa_wq", fb * 128), *HN], writes=[psk])
                P.op(ACT, lambda e, pst=pst, fb=fb: e.copy(out=qT[:, fb, :], in_=pst[:, :]), reads=[psk], writes=[("qT", fb)])
            units = [(hh, tb) for hh in range(4) for tb in range(4)]

            def s1(i):
                hh, tb = units[i]
                u = i % 4
                pst, psk = next_ps(C)
                mm_chain(P, pst[:, :NMEM], [(qT[:, 2 * hh + dc, tb * 128:(tb + 1) * 128], kT[:, 2 * hh + dc, :]) for dc in range(2)],
                         reads=[("qT", 2 * hh), ("qT", 2 * hh + 1), ("kT",)], writes=[psk])
                s_ = st[u]
                P.op(DVE, lambda e: e.reduce_max(out=s_[:, 0:1], in_=pst[:, :NMEM], axis=AX.X), reads=[psk], writes=[("st", u, 0)])
                P.op(DVE, lambda e: e.tensor_scalar(out=s_[:, 1:2], in0=s_[:, 0:1], scalar1=-scale, scalar2=None, op0=ALU.mult),
                     reads=[("st", u, 0)], writes=[("st", u, 1)])
                P.op(ACT, lambda e: e.activation(out=pexp[u][:, :], in_=pst[:, :NMEM], func=AF.Exp, scale=scale, bias=s_[:, 1:2], accum_out=s_[:, 2:3]),
                     reads=[psk, ("st", u, 1)], writes=[("pexp", u), ("st", u, 2)])

            def s1b(i):
                u = i % 4
                s_ = st[u]
                P.op(DVE, lambda e: e.reciprocal(out=s_[:, 3:4], in_=s_[:, 2:3]), reads=[("st", u, 2)], writes=[("st", u, 3)])
                P.op(DVE, lambda e: e.tensor_scalar(out=pn[u][:, :], in0=pexp[u][:, :], scalar1=s_[:, 3:4], scalar2=None, op0=ALU.mult),
                     reads=[("pexp", u), ("st", u, 3)], writes=[("pn", u)])

            def s2(i):
                hh, tb = units[i]
                u = i % 4
                pbt, pbk_ = next_psb(C)

                def trf(e):
                    ins = None
                    for mb_ in range(2):
                        ins = e.transpose(pbt[:, mb_ * 128:(mb_ + 1) * 128], pn[u][:, mb_ * 128:(mb_ + 1) * 128], C.ident_bf[:, :])
                    return ins
                P.op(PE, trf, reads=[("pn", u)], writes=[pbk_])
                pti = (hh % 2) * 4 + tb
                P.op(ACT, lambda e: e.copy(out=pT[pti][:, :, :], in_=pbt[:, 0:256].rearrange("p (a b) -> p a b", a=2)),
                     reads=[pbk_], writes=[("pT", pti)])

            def s3(hh):
                for dvb in range(2):
                    pst, psk = next_ps(C)

                    def pvf(e, pst=pst, dvb=dvb):
                        ins = None
                        for tb in range(4):
                            for mb_ in range(2):
                                ins = e.matmul(pst[:, tb * 128:(tb + 1) * 128], lhsT=vtok[:, mb_, hh * 256 + dvb * 128:hh * 256 + (dvb + 1) * 128],
                                               rhs=pT[(hh % 2) * 4 + tb][:, mb_, :], start=(mb_ == 0), stop=(mb_ == 1))
                        return ins
                    P.op(PE, pvf, reads=[("vtok",)] + [("pT", (hh % 2) * 4 + tb) for tb in range(4)], writes=[psk])
                    P.op(ACT, lambda e, pst=pst, dvb=dvb: e.copy(out=oT[:, 2 * hh + dvb, :], in_=pst[:, :]), reads=[psk], writes=[("oT", 2 * hh + dvb)])

            LAG = 3
            for i in range(len(units) + LAG):
                if i < len(units):
                    s1(i)
                if 0 <= i - 1 < len(units):
                    s1b(i - 1)
                k_ = i - LAG
                if 0 <= k_ < len(units):
                    s2(k_)
                    if units[k_][1] == 3:
                        s3(units[k_][0])
            if j + 1 < NT:
                norm_tile(P, C, xres, j + 1, xts[1 - bb_], ("xt", 1 - bb_), hns[1 - bb_], ("hn%d" % (1 - bb_),), sq, rstd, gain=g_xa)
            for fb in range(8):
                pst, psk = next_ps(C)
                mm_chain(P, pst[:, :], [(wo[:, c, fb * 128:(fb + 1) * 128], oT[:, c, :]) for c in range(8)],
                         reads=wkeys(C, "xa_wo", fb * 128) + [("oT", c) for c in range(8)], writes=[psk])
                P.op(DVE, lambda e, pst=pst, fb=fb, xt=xt: e.tensor_tensor(out=xt[:, fb, :], in0=pst[:, :], in1=xt[:, fb, :], op=ALU.add),
                     reads=[psk, XK], writes=[XK])
            dstv = xres.rearrange("(c p) t -> p c t", p=128)
            P.op(POOL, lambda e, j=j, xt=xt: e.dma_start(out=dstv[:, :, j * TT:(j + 1) * TT], in_=xt[:, :, :]), reads=[XK], dkey=("xst", bb_))
        P.barrier()


def stage_ffn(P, C, lp, final=False):
    nc = P.nc
    W = C.w
    xres = C.xres
    with ExitStack() as es:
        sb = lambda name, shape, dt=F32: es.enter_context(nc.sbuf_tensor("sb_" + lp + name, list(shape), dt))
        wup = sb("ff_wup", [128, 8, 2 * FFN], BF16)
        wdn = sb("ff_wdn", [128, NPAIR, 1024], BF16)
        cw = sb("ff_cw", [128, 3, 2 * NPAIR])
        g_f = C.gains[lp + "ffn_norm"]
        wload(P, C, wup, W[lp + "ffn_w_up"], 8, 2 * FFN, name="ff_wup")
        wload(P, C, wdn, W[lp + "ffn_w_down"], NPAIR, 1024, gain=None, name="ff_wdn")
        P.op(SP, lambda e: e.dma_start(out=cw[:, :, :], in_=W[lp + "ffn_conv"]), writes=[("cw",)], dkey=("cw",))
        P.barrier()
        xts = [sb(f"ff_xt{i}", [128, 8, TT]) for i in range(2)]
        hn = sb("ff_hn", [128, 8, TT], BF16)
        rstd = sb("ff_rstd", [128, TT])
        rstd2 = rstd
        act = sb("ff_act", [128, NPAIR, TT], BF16)
        sq = act[:, 0:8, :]
        sqk = [("act", c) for c in range(8)]
        acc = [sb(f"ff_acc{i}", [128, TT]) for i in range(4)]
        halo = sb("ff_halo", [128, 2 * NPAIR, 2])
        corr = sb("ff_corr", [128, 2 * NPAIR, 2])
        ctmp = sb("ff_ctmp", [128, 2 * NPAIR])
        sqs = [sb("ff_sqs0", [128, TT], BF16)]
        load_xt(P, C, xres, 0, xts[0], ("xt", 0))
        norm_chunked(P, C, xts[0], ("xt", 0), hn, ("hn",), sqs, rstd, g_f)
        for j in range(NT):
            xt = xts[j % 2]
            fo = xt
            XK = ("xt", j % 2)
            if j + 1 < NT:
                load_xt(P, C, xres, j + 1, xts[1 - j % 2], ("xt", 1 - j % 2))
            if j > 0:
                hk = [("halo", b) for b in range(2 * NPAIR)]
                P.op(POOL, lambda e: e.tensor_tensor(out=corr[:, :, 0], in0=halo[:, :, 1], in1=cw[:, 1, :], op=ALU.mult), reads=hk, writes=[("corr",)])
                P.op(POOL, lambda e: e.tensor_tensor(out=ctmp[:, :], in0=halo[:, :, 0], in1=cw[:, 0, :], op=ALU.mult), reads=hk, writes=[("ctmp",)])
                P.op(POOL, lambda e: e.tensor_tensor(out=corr[:, :, 0], in0=corr[:, :, 0], in1=ctmp[:, :], op=ALU.add), reads=[("corr",), ("ctmp",)], writes=[("corr",)])
                P.op(POOL, lambda e: e.tensor_tensor(out=corr[:, :, 1], in0=halo[:, :, 1], in1=cw[:, 0, :], op=ALU.mult), reads=hk + [("corr",)], writes=[("corr",)])
            pend_pairs = []
            for pr in range(NPAIR):
                accs = {}
                for kind in range(2):
                    blk = kind * NPAIR + pr
                    u = (pr * 2 + kind) % 4
                    pst, psk = next_ps(C)
                    mm_chain(P, pst[:, :], [(wup[:, c, blk * 128:(blk + 1) * 128], hn[:, c, :]) for c in range(8)],
                             reads=[*wkeys(C, "ff_wup", blk * 128), ("hn", "a"), ("hn", "b")], writes=[psk])
                    a_ = acc[u]
                    P.op(ACT, lambda e, a_=a_, pst=pst, blk=blk: e.activation(out=a_[:, :], in_=pst[:, :], func=AF.Copy, scale=cw[:, 2, blk:blk + 1]),
                         reads=[psk], writes=[("acc", u)])
                    if j < NT - 1:
                        P.op(ACT, lambda e, pst=pst, blk=blk: e.copy(out=halo[:, blk, :], in_=pst[:, TT - 2:TT]), reads=[psk], writes=[("halo", blk)])
                    P.op(DVE, lambda e, a_=a_, pst=pst, blk=blk: e.scalar_tensor_tensor(out=a_[:, 1:TT], in0=pst[:, 0:TT - 1], scalar=cw[:, 1, blk:blk + 1], in1=a_[:, 1:TT],
                                                                                     op0=ALU.mult, op1=ALU.add),
                         reads=[psk, ("acc", u)], writes=[("acc", u)])
                    P.op(DVE, lambda e, a_=a_, pst=pst, blk=blk: e.scalar_tensor_tensor(out=a_[:, 2:TT], in0=pst[:, 0:TT - 2], scalar=cw[:, 0, blk:blk + 1], in1=a_[:, 2:TT],
                                                                                     op0=ALU.mult, op1=ALU.add),
                         reads=[psk, ("acc", u)], writes=[("acc", u)])
                    if j > 0:
                        P.op(POOL, lambda e, a_=a_, blk=blk: e.tensor_tensor(out=a_[:, 0:2], in0=a_[:, 0:2], in1=corr[:, blk, :], op=ALU.add),
                             reads=[("corr",), ("acc", u)], writes=[("acc", u)])
                    accs[kind] = (a_, u)
                pend_pairs.append((accs[0], accs[1], pr))
                while len(pend_pairs) > (1 if pr < NPAIR - 1 else 0):
                    (au, uu), (ag, ug), pr_ = pend_pairs.pop(0)
                    P.op(ACT, lambda e, ag=ag: e.activation(out=ag[:, :], in_=ag[:, :], func=AF.Silu), reads=[("acc", ug)], writes=[("acc", ug)])
                    P.op(POOL, lambda e, ag=ag, au=au, pr_=pr_: e.tensor_tensor(out=act[:, pr_, :], in0=ag[:, :], in1=au[:, :], op=ALU.mult),
                         reads=[("acc", ug), ("acc", uu)], writes=[("act", pr_)])
            if j + 1 < NT:
                norm_chunked(P, C, xts[1 - j % 2], ("xt", 1 - j % 2), hn, ("hn",), sqs, rstd2, g_f)
            for fb in range(8):
                pst, psk = next_ps(C)
                mm_chain(P, pst[:, :], [(wdn[:, c, fb * 128:(fb + 1) * 128], act[:, c, :]) for c in range(NPAIR)],
                         reads=wkeys(C, "ff_wdn", fb * 128) + [("act", c) for c in range(NPAIR)], writes=[psk])
                P.op(DVE, lambda e, pst=pst, fb=fb, xt=xt: e.tensor_tensor(out=xt[:, fb, :], in0=pst[:, :], in1=xt[:, fb, :], op=ALU.add),
                     reads=[psk, XK], writes=[XK])
            if not final:
                dstv = xres.rearrange("(c p) t -> p c t", p=128)
                P.op(POOL, lambda e, j=j, xt=xt: e.dma_start(out=dstv[:, :, j * TT:(j + 1) * TT], in_=xt[:, :, :]), reads=[XK], dkey=("xst", j % 2))
            else:
                gfin = C.gains["final_norm"]
                P.op(ACT, lambda e, xt=xt: e.activation(out=sq[:, :, :], in_=xt[:, :, :], func=AF.Square), reads=[XK], writes=sqk)
                pst, psk = next_ps(C)
                mm_chain(P, pst[:, :], [(C.ones_bf[:, :], sq[:, c, :]) for c in range(8)], reads=sqk, writes=[psk])
                P.op(ACT, lambda e, pst=pst: e.activation(out=rstd[:, :], in_=pst[:, :], func=AF.Sqrt, scale=1.0 / D, bias=C.eps_col[:, 0:1]),
                     reads=[psk], writes=[("rstd",)])
                P.op(DVE, lambda e: e.reciprocal(out=rstd[:, :], in_=rstd[:, :]), reads=[("rstd",)], writes=[("rstd",)])
                for c in range(8):
                    P.op(DVE, lambda e, c=c, xt=xt, fo=fo: e.scalar_tensor_tensor(out=fo[:, c, :], in0=xt[:, c, :], scalar=gfin[:, c:c + 1], in1=rstd[:, :],
                                                                                          op0=ALU.mult, op1=ALU.mult),
                         reads=[XK, ("rstd",), gain_key(None)], writes=[XK])
                dstv = C.outT.rearrange("(c p) t -> p c t", p=128)
                P.op(POOL, lambda e, j=j, fo=fo: e.dma_start(out=dstv[:, :, j * TT:(j + 1) * TT], in_=fo[:, :, :]), reads=[XK], dkey=("ost", j % 2))
        P.barrier()


RET_GAMMA = [1.0 - 2.0 ** (-5.0 - h) for h in range(4)]


def stage_l0_inproj(P, C):
    nc = P.nc
    xres = C.xres
    with ExitStack() as es:
        sb = lambda name, shape, dt=F32: es.enter_context(nc.sbuf_tensor(name, list(shape), dt))
        W = sb("A_W", [128, 8, 2560], BF16)
        Wsw = sb("A_Wsw", [128, 8, 1024], BF16)
        g_mix = C.gains["l0_mix_norm"]
        wload(P, C, W, C.w["l0_w_in"], 8, 2560, name="A_W")
        wload(P, C, Wsw, C.w["l0_w_in_sw"], 8, 1024, name="A_Wsw")
        gq = sb("A_gq", [128, 4, TT])
        P.op(SP, lambda e: e.dma_start(out=gq[:, :, :], in_=C.cst["gq_tab"]), writes=[("gq",)], dkey=("gq",))
        P.barrier()
        xts = [sb(f"A_xt{i}", [128, 8, TT]) for i in range(2)]
        hn = sb("A_hn", [128, 8, TT], BF16)
        sq = sb("A_sq", [128, 8, TT], BF16)
        rstd = sb("A_rstd", [128, TT])
        rot = sb("A_rot", [128, 4, TT])
        t1 = [sb(f"A_t1{i}", [128, TT]) for i in range(2)]
        t2 = [sb(f"A_t2{i}", [128, TT]) for i in range(2)]
        qo = sb("A_qo", [128, 4, TT], BF16)
        qdo = sb("A_qdo", [128, 4, TT], BF16)
        ko = sb("A_ko", [128, 4, TT], BF16)
        go = sb("A_go", [128, 4, TT], BF16)
        uo = sb("A_uo", [128, 4, TT], BF16)
        vo = sb("A_vo", [128, 4, TT], BF16)
        cnt = 0
        load_xt(P, C, xres, 0, xts[0], ("xt", 0))
        for j in range(NT):
            norm_tile(P, C, xres, j, xts[j % 2], ("xt", j % 2), hn, ("hn",), sq, rstd, gain=g_mix, load=False)
            if j + 1 < NT:
                load_xt(P, C, xres, j + 1, xts[1 - j % 2], ("xt", 1 - j % 2))
            P.op(SP, lambda e, j=j: e.dma_start(out=rot[:, :, :], in_=C.cst["rot_tab"][:, :, j * TT:(j + 1) * TT]), writes=[("rot",)], dkey=("rot",))
            for kind in range(2):
                for h in range(4):
                    col = kind * 512 + h * 128
                    pa, pak = next_ps(C)
                    mm_chain(P, pa[:, :], [(W[:, c, col:col + 128], hn[:, c, :]) for c in range(8)], reads=[*wkeys(C, "A_W", col), ("hn", "a"), ("hn", "b")], writes=[pak])
                    pb, pbk = next_ps(C)
                    mm_chain(P, pb[:, :], [(Wsw[:, c, col:col + 128], hn[:, c, :]) for c in range(8)], reads=[*wkeys(C, "A_Wsw", col), ("hn", "a"), ("hn", "b")], writes=[pbk])
                    u = cnt % 2
                    cnt += 1
                    a_, b_ = t1[u], t2[u]
                    P.op(DVE, lambda e, a_=a_, pa=pa, kind=kind: e.tensor_tensor(out=a_[:, :], in0=pa[:, :], in1=rot[:, 2 * kind, :], op=ALU.mult),
                         reads=[pak, ("rot",)], writes=[("t1", u)])
                    P.op(DVE, lambda e, b_=b_, pb=pb, kind=kind: e.tensor_tensor(out=b_[:, :], in0=pb[:, :], in1=rot[:, 2 * kind + 1, :], op=ALU.mult),
                         reads=[pbk, ("rot",)], writes=[("t2", u)])
                    P.op(POOL, lambda e, a_=a_, b_=b_: e.tensor_tensor(out=a_[:, :], in0=a_[:, :], in1=b_[:, :], op=ALU.add),
                         reads=[("t1", u), ("t2", u)], writes=[("t1", u)])
                    if kind == 0:
                        P.op(ACT, lambda e, a_=a_, h=h: e.copy(out=qo[:, h, :], in_=a_[:, :]), reads=[("t1", u)], writes=[("qo",)])
                        P.op(POOL, lambda e, a_=a_, h=h: e.tensor_tensor(out=qdo[:, h, :], in0=a_[:, :], in1=gq[:, h, :], op=ALU.mult),
                             reads=[("t1", u)], writes=[("qdo",)])
                    else:
                        P.op(ACT, lambda e, a_=a_, h=h: e.copy(out=ko[:, h, :], in_=a_[:, :]), reads=[("t1", u)], writes=[("ko",)])
            for tb in range(4):
                pa, pak = next_ps(C)
                mm_chain(P, pa[:, :], [(hn[:, c, tb * 128:(tb + 1) * 128], W[:, c, 1024:1536]) for c in range(8)], reads=[*wkeys(C, "A_W", 1024, 512), ("hn", "a"), ("hn", "b")], writes=[pak])
                P.op(ACT, lambda e, pa=pa, tb=tb: e.copy(out=vo[:, tb, :], in_=pa[:, :]), reads=[pak], writes=[("vo",)])
            for fb in range(4):
                pa, pak = next_ps(C)
                mm_chain(P, pa[:, :], [(W[:, c, 1536 + fb * 128:1536 + (fb + 1) * 128], hn[:, c, :]) for c in range(8)], reads=[*wkeys(C, "A_W", 1536 + fb * 128), ("hn", "a"), ("hn", "b")], writes=[pak])
                P.op(ACT, lambda e, pa=pa, fb=fb: e.activation(out=go[:, fb, :], in_=pa[:, :], func=AF.Silu), reads=[pak], writes=[("go",)])
            for fb in range(4):
                pa, pak = next_ps(C)
                mm_chain(P, pa[:, :], [(W[:, c, 2048 + fb * 128:2048 + (fb + 1) * 128], hn[:, c, :]) for c in range(8)], reads=[*wkeys(C, "A_W", 2048 + fb * 128), ("hn", "a"), ("hn", "b")], writes=[pak])
                P.op(ACT, lambda e, pa=pa, fb=fb: e.copy(out=uo[:, fb, :], in_=pa[:, :]), reads=[pak], writes=[("uo",)])
            for nm, tl in [("qT", qo), ("qdT", qdo), ("kT", ko), ("gT", go), ("uT", uo)]:
                dv = C.scr[nm].rearrange("(h p) t -> p h t", p=128)
                P.op(POOL, lambda e, dv=dv, tl=tl, j=j: e.dma_start(out=dv[:, :, j * TT:(j + 1) * TT], in_=tl[:, :, :]),
                     reads=[({"qT": "qo", "qdT": "qdo", "kT": "ko", "gT": "go", "uT": "uo"}[nm],)], dkey=("Ast", nm))
            dv = C.scr["vtok"].rearrange("(n p) f -> p n f", p=128)
            P.op(POOL, lambda e, dv=dv, j=j: e.dma_start(out=dv[:, j * 4:(j + 1) * 4, :], in_=vo[:, :, :]), reads=[("vo",)], dkey=("Ast", "v"))
        P.barrier()


def stage_retention(P, C):
    nc = P.nc
    with ExitStack() as es:
        sb = lambda name, shape, dt=F32: es.enter_context(nc.sbuf_tensor(name, list(shape), dt))
        kT = [sb(f"R_kT{i}", [128, S], BF16) for i in range(2)]
        qT = [sb(f"R_qT{i}", [128, S], BF16) for i in range(2)]
        qdT = [sb(f"R_qdT{i}", [128, S], BF16) for i in range(2)]
        gT = [sb(f"R_gT{i}", [128, S], BF16) for i in range(2)]
        vt = [sb(f"R_vt{i}", [128, 32, 128], BF16) for i in range(2)]
        dtab = sb("R_dtab", [128, 4, 128])
        kd = sb("R_kd", [128, 4])
        rn = sb("R_rn", [128, 4])
        state = sb("R_state", [128, 128])
        state_bfs = [sb(f"R_state_bf{i}", [128, 128], BF16) for i in range(2)]
        scm = [sb(f"R_scm{i}", [128, 128], BF16) for i in range(2)]
        kdec = [sb(f"R_kdec{i}", [128, 128], BF16) for i in range(2)]
        o_sb = [sb(f"R_osb{i}", [128, TT]) for i in range(2)]
        osq = [sb(f"R_osq{i}", [128, TT], BF16) for i in range(2)]
        rr = [sb(f"R_rr{i}", [128, TT]) for i in range(2)]
        mo = [sb(f"R_mo{i}", [128, TT], BF16) for i in range(2)]
        P.op(SP, lambda e: e.dma_start(out=dtab[:, :, :], in_=C.cst["dt_tab"]), writes=[("dtab",)], dkey=("dtab",))
        P.op(SP, lambda e: e.dma_start(out=kd[:, :], in_=C.cst["kd_tab"]), writes=[("kd",)], dkey=("kd",))
        P.op(SP, lambda e: e.dma_start(out=rn[:, :], in_=C.w["l0_ret_norm"]), writes=[("rn",)], dkey=("rn",))
        vview = C.scr["vtok"].rearrange("(n p) f -> p n f", p=128)
        grp = 0
        for h in range(4):
            hb = h % 2
            for nm, tl in [("kT", kT), ("qT", qT), ("qdT", qdT), ("gT", gT)]:
                P.op(SP, lambda e, nm=nm, tl=tl, h=h, hb=hb: e.dma_start(out=tl[hb][:, :], in_=C.scr[nm][h * 128:(h + 1) * 128, :]),
                     writes=[(nm, hb)], dkey=("Rld", nm, hb))
            P.op(SP, lambda e, h=h, hb=hb: e.dma_start(out=vt[hb][:, :, :], in_=vview[:, :, h * 128:(h + 1) * 128]), writes=[("vt", hb)], dkey=("Rld", "v", hb))
            P.op(POOL, lambda e: e.memset(state[:, :], 0.0), writes=[("state",)])
            P.op(POOL, lambda e: e.memset(state_bfs[0][:, :], 0.0), writes=[("state_bf", 0)])
            gam_c = RET_GAMMA[h] ** 128
            for n in range(32):
                cs = slice(n * 128, (n + 1) * 128)
                u = n % 2
                if n % 4 == 0:
                    po, pok = next_ps(C, 0, 2)
                    grp += 1
                psc, psck = next_ps(C, 2, 6)
                mm_chain(P, psc[:, 0:128], [(kT[hb][:, cs], qT[hb][:, cs])], reads=[("kT", hb), ("qT", hb)], writes=[psck])
                P.op(DVE, lambda e, psc=psc, u=u, h=h: e.tensor_tensor(out=scm[u][:, :], in0=psc[:, 0:128], in1=dtab[:, h, :], op=ALU.mult),
                     reads=[psck, ("dtab",)], writes=[("scm", u)])
                pbt, pbk_ = next_psb(C)
                P.op(PE, lambda e, cs=cs, hb=hb, pbt=pbt: e.transpose(pbt[:, 0:128], kT[hb][:, cs], C.ident_bf[:, :]), reads=[("kT", hb)], writes=[pbk_])
                P.op(ACT, lambda e, u=u, h=h, pbt=pbt: e.activation(out=kdec[u][:, :], in_=pbt[:, 0:128], func=AF.Copy, scale=kd[:, h:h + 1]),
                     reads=[pbk_, ("kd",)], writes=[("kdec", u)])
                pkv, pkvk = next_ps(C, 2, 6)
                mm_chain(P, pkv[:, 0:128], [(kdec[u][:, :], vt[hb][:, n, :])], reads=[("kdec", u), ("vt", hb)], writes=[pkvk])
                P.op(DVE, lambda e, pkv=pkv, gam_c=gam_c: e.scalar_tensor_tensor(out=state[:, :], in0=state[:, :], scalar=gam_c, in1=pkv[:, 0:128],
                                                                              op0=ALU.mult, op1=ALU.add),
                     reads=[pkvk, ("state",)], writes=[("state",)])
                sb_n = state_bfs[(n + 1) % 2]
                P.op(ACT, lambda e, sb_n=sb_n: e.copy(out=sb_n[:, :], in_=state[:, :]), reads=[("state",)], writes=[("state_bf", (n + 1) % 2)])
                oc = slice((n % 4) * 128, (n % 4 + 1) * 128)
                sb_c = state_bfs[n % 2]
                mm_chain(P, po[:, oc], [(vt[hb][:, n, :], scm[u][:, :]), (sb_c[:, :], qdT[hb][:, cs])],
                         reads=[("vt", hb), ("scm", u), ("state_bf", n % 2), ("qdT", hb)], writes=[pok])
                if n % 4 == 3:
                    v = grp % 2
                    ts_ = slice((n // 4) * TT, (n // 4 + 1) * TT)
                    P.op(ACT, lambda e, po=po, v=v: e.copy(out=o_sb[v][:, :], in_=po[:, :]), reads=[pok], writes=[("osb", v)])
                    P.op(ACT, lambda e, po=po, v=v: e.activation(out=osq[v][:, :], in_=po[:, :], func=AF.Square), reads=[pok], writes=[("osq", v)])
                    pss, pssk = next_ps(C, 2, 6)
                    mm_chain(P, pss[:, :], [(C.ones_bf[:, :], osq[v][:, :])], reads=[("osq", v)], writes=[pssk])
                    P.op(ACT, lambda e, pss=pss, v=v: e.activation(out=rr[v][:, :], in_=pss[:, :], func=AF.Ln, scale=1.0 / 128, bias=C.eps_col[:, 0:1]),
                         reads=[pssk], writes=[("rr", v)])
                    P.op(ACT, lambda e, v=v: e.activation(out=rr[v][:, :], in_=rr[v][:, :], func=AF.Exp, scale=-0.5), reads=[("rr", v)], writes=[("rr", v)])
                    P.op(DVE, lambda e, v=v: e.tensor_tensor(out=o_sb[v][:, :], in0=o_sb[v][:, :], in1=rr[v][:, :], op=ALU.mult),
                         reads=[("osb", v), ("rr", v)], writes=[("osb", v)])
                    P.op(DVE, lambda e, v=v, h=h, hb=hb, ts_=ts_: e.scalar_tensor_tensor(out=mo[v][:, :], in0=o_sb[v][:, :], scalar=rn[:, h:h + 1], in1=gT[hb][:, ts_],
                                                                                      op0=ALU.mult, op1=ALU.mult),
                         reads=[("osb", v), ("rn",), ("gT", hb)], writes=[("mo", v)])
                    P.op(POOL, lambda e, v=v, h=h, ts_=ts_: e.dma_start(out=C.scr["mT"][h * 128:(h + 1) * 128, ts_], in_=mo[v][:, :]),
                         reads=[("mo", v)], dkey=("Rst", v))
        P.barrier()


def stage_s5(P, C):
    nc = P.nc
    TWO_PI = 2.0 * math.pi
    with ExitStack() as es:
        sb = lambda name, shape, dt=F32: es.enter_context(nc.sbuf_tensor(name, list(shape), dt))

        def tt(eng, out, a, b, op, rk, wk):
            P.op(eng, lambda e: e.tensor_tensor(out=out, in0=a, in1=b, op=op), reads=rk, writes=wk)

        def ts(eng, out, a, s1, op0, rk=(), wk=()):
            P.op(eng, lambda e: e.tensor_scalar(out=out, in0=a, scalar1=s1, scalar2=None, op0=op0), reads=rk, writes=wk)

        def act(out, a, func, rk, wk, scale=1.0, bias=None):
            if bias is None:
                P.op(ACT, lambda e: e.activation(out=out, in_=a, func=func, scale=scale), reads=rk, writes=wk)
            else:
                P.op(ACT, lambda e: e.activation(out=out, in_=a, func=func, scale=scale, bias=bias), reads=rk, writes=wk)

        def frac_sincos(eng, x, xi, xf, sin_out, cos_out, key, hp, np_, outkey=None):
            P.op(eng, lambda e: e.tensor_copy(out=xi, in_=x), reads=[(key, "x")], writes=[(key, "xi")])
            P.op(eng, lambda e: e.tensor_copy(out=xf, in_=xi), reads=[(key, "xi")], writes=[(key, "xf")])
            P.op(eng, lambda e: e.tensor_tensor(out=x, in0=x, in1=xf, op=ALU.subtract), reads=[(key, "x"), (key, "xf")], writes=[(key, "x")])
            P.op(DVE, lambda e: e.scalar_tensor_tensor(out=xf, in0=x, scalar=-1.0, in1=x, op0=ALU.mult, op1=ALU.max),
                 reads=[(key, "x"), (key, "xf")], writes=[(key, "xf")])
            ok = key if outkey is None else outkey
            P.op(ACT, lambda e: e.activation(out=sin_out, in_=x, func=AF.Sin, scale=TWO_PI), reads=[(key, "x")], writes=[(ok, "sin")])
            P.op(ACT, lambda e: e.activation(out=cos_out, in_=xf, func=AF.Sin, scale=-TWO_PI, bias=hp[0:np_, 0:1]), reads=[(key, "xf")], writes=[(ok, "cos")])

        halfpi = sb("S_halfpi", [128, 1])
        r_s = sb("S_r", [128, 32])
        f_s = sb("S_f", [128, 32])
        cs_tab = sb("S_cstab", [128, 32, 8])
        sn_tab = sb("S_sntab", [128, 32, 8])
        fbd = sb("S_fbd", [16, 32, 128])
        B1f = sb("S_B1f", [16, 32, 128])
        B2f = sb("S_B2f", [16, 32, 128])
        C1f = sb("S_C1f", [128, 32, 16])
        C2f = sb("S_C2f", [128, 32, 16])
        Dd = sb("S_Dd", [16, 32, 16], BF16)
        tloc = sb("S_tloc", [128, TT])
        t0s = sb("S_t0s", [128, 32, 8])
        t0b = sb("S_t0b", [16, 8, 128])
        ones5 = sb("S_ones", [128, TT])
        es_in = ExitStack()
        tb = lambda name, shape, dt=F32: es_in.enter_context(nc.sbuf_tensor(name, list(shape), dt))
        P.op(POOL, lambda e: e.memset(halfpi[:, :], 0.5 * math.pi), writes=[("halfpi",)])
        P.op(POOL, lambda e: e.memset(ones5[:, :], 1.0), writes=[("ones5",)])
        P.op(SP, lambda e: e.dma_start(out=tloc[:, :], in_=C.cst["tpos"][:, 0:TT]), writes=[("tloc",)], dkey=("s5c", 9))
        P.op(SP, lambda e: e.dma_start(out=t0s[:, :, :], in_=C.cst["t0s"]), writes=[("t0s",)], dkey=("s5c", 10))
        P.op(SP, lambda e: e.dma_start(out=t0b[:, :, :], in_=C.cst["t0b"]), writes=[("t0b",)], dkey=("s5c", 11))
        lam_s = tb("S_lam_s", [128, 2, 32])
        ldt_s = tb("S_ldt_s", [128, 32])
        P.op(SP, lambda e: e.dma_start(out=lam_s[:, :, :], in_=C.w["s5_lam_s"]), writes=[("lam_s",)], dkey=("s5c", 0))
        P.op(SP, lambda e: e.dma_start(out=ldt_s[:, :], in_=C.w["s5_ldt_s"]), writes=[("ldt_s",)], dkey=("s5c", 1))
        act(ldt_s[:, :], ldt_s[:, :], AF.Exp, [("ldt_s",)], [("ldt_s",)])
        tt(DVE, r_s[:, :], lam_s[:, 0, :], ldt_s[:, :], ALU.mult, [("lam_s",), ("ldt_s",)], [("r_s",)])
        act(r_s[:, :], r_s[:, :], AF.Exp, [("r_s",)], [("r_s",)])
        tt(DVE, f_s[:, :], lam_s[:, 1, :], ldt_s[:, :], ALU.mult, [("lam_s",), ("ldt_s",)], [("f_s",)])
        ts(DVE, f_s[:, :], f_s[:, :], 1.0 / TWO_PI, ALU.mult, rk=[("f_s",)], wk=[("f_s",)])
        fi_s = tb("S_fi_s", [128, 32], I32)
        ff_s = tb("S_ff_s", [128, 32])
        P.op(DVE, lambda e: e.tensor_copy(out=fi_s[:, :], in_=f_s[:, :]), reads=[("f_s",)], writes=[("fi_s",)])
        P.op(DVE, lambda e: e.tensor_copy(out=ff_s[:, :], in_=fi_s[:, :]), reads=[("fi_s",)], writes=[("ff_s",)])
        tt(DVE, f_s[:, :], f_s[:, :], ff_s[:, :], ALU.subtract, [("f_s",), ("ff_s",)], [("f_s",)])
        xo = tb("S_xo", [128, 32, 8])
        xoi = tb("S_xoi", [128, 32, 8], I32)
        xof = tb("S_xof", [128, 32, 8])
        P.op(DVE, lambda e: e.tensor_tensor(out=xo[:, :, :], in0=t0s[:, :, :], in1=f_s[:, :].unsqueeze(2).to_broadcast([128, 32, 8]), op=ALU.mult),
             reads=[("f_s",), ("t0s",)], writes=[("xo", "x")])
        frac_sincos(DVE, xo[:, :, :], xoi[:, :, :], xof[:, :, :], sn_tab[:, :, :], cs_tab[:, :, :], "xo", halfpi, 128)
        NB = 32 * 64
        lamb = tb("S_lamb", [16, 2, NB])
        ldtb = tb("S_ldtb", [16, NB])
        bb = tb("S_bb", [16, 2, NB])
        P.op(SP, lambda e: e.dma_start(out=lamb[:, :, :], in_=C.w["s5_lam_b"]), writes=[("lamb",)], dkey=("s5c", 2))
        P.op(SP, lambda e: e.dma_start(out=ldtb[:, :], in_=C.w["s5_ldt_b"]), writes=[("ldtb",)], dkey=("s5c", 3))
        P.op(SP, lambda e: e.dma_start(out=bb[:, :, :], in_=C.w["s5_b_b"]), writes=[("bb",)], dkey=("s5c", 4))
        lr = lamb[:, 0, :]
        li = lamb[:, 1, :]
        mag = tb("S_mag", [16, NB])
        fb_ = tb("S_fb", [16, NB])
        fc_ = tb("S_fc", [16, NB])
        fbi = tb("S_fbi", [16, NB], I32)
        are = tb("S_are", [16, NB])
        aim = tb("S_aim", [16, NB])
        den = tb("S_den", [16, NB])
        tmp = tb("S_tmp", [16, NB])
        zre = tb("S_zre", [16, NB])
        zim = tb("S_zim", [16, NB])
        act(ldtb[:, :], ldtb[:, :], AF.Exp, [("ldtb",)], [("ldtb",)])
        tt(DVE, mag[:, :], lr, ldtb[:, :], ALU.mult, [("lamb",), ("ldtb",)], [("mag",)])
        act(mag[:, :], mag[:, :], AF.Exp, [("mag",)], [("mag",)])
        tt(DVE, fb_[:, :], li, ldtb[:, :], ALU.mult, [("lamb",), ("ldtb",)], [("fbq", "x")])
        ts(DVE, fb_[:, :], fb_[:, :], 1.0 / TWO_PI, ALU.mult, rk=[("fbq", "x")], wk=[("fbq", "x")])
        frac_sincos(DVE, fb_[:, :], fbi[:, :], fc_[:, :], aim[:, :], are[:, :], "fbq", halfpi, 16)
        fb3 = fb_[:, :].rearrange("h (g p) -> h g p", g=32)
        P.op(POOL, lambda e: e.tensor_copy(out=fbd[:, :, 0:64], in_=fb3), reads=[("fbq", "x")], writes=[("fbd", 0)])
        P.op(POOL, lambda e: e.tensor_copy(out=fbd[:, :, 64:128], in_=fb3), reads=[("fbq", "x")], writes=[("fbd", 1)])
        tt(DVE, aim[:, :], aim[:, :], mag[:, :], ALU.mult, [("fbq", "sin"), ("mag",)], [("aim",)])
        tt(DVE, are[:, :], are[:, :], mag[:, :], ALU.mult, [("fbq", "cos"), ("mag",)], [("are",)])
        ts(DVE, are[:, :], are[:, :], -1.0, ALU.add, rk=[("are",)], wk=[("are",)])
        tt(DVE, den[:, :], lr, lr, ALU.mult, [("lamb",)], [("den",)])
        tt(DVE, tmp[:, :], li, li, ALU.mult, [("lamb",)], [("tmp",)])
        tt(DVE, den[:, :], den[:, :], tmp[:, :], ALU.add, [("den",), ("tmp",)], [("den",)])
        P.op(DVE, lambda e: e.reciprocal(out=den[:, :], in_=den[:, :]), reads=[("den",)], writes=[("den",)])
        tt(DVE, zre[:, :], are[:, :], lr, ALU.mult, [("are",), ("lamb",)], [("zre",)])
        tt(DVE, tmp[:, :], aim[:, :], li, ALU.mult, [("aim",), ("lamb",)], [("tmp",)])
        tt(DVE, zre[:, :], zre[:, :], tmp[:, :], ALU.add, [("zre",), ("tmp",)], [("zre",)])
        tt(DVE, zre[:, :], zre[:, :], den[:, :], ALU.mult, [("zre",), ("den",)], [("zre",)])
        tt(DVE, zim[:, :], aim[:, :], lr, ALU.mult, [("aim",), ("lamb",)], [("zim",)])
        tt(DVE, tmp[:, :], are[:, :], li, ALU.mult, [("are",), ("lamb",)], [("tmp",)])
        tt(DVE, zim[:, :], zim[:, :], tmp[:, :], ALU.subtract, [("zim",), ("tmp",)], [("zim",)])
        tt(DVE, zim[:, :], zim[:, :], den[:, :], ALU.mult, [("zim",), ("den",)], [("zim",)])
        bbre = mag
        bbim = den
        br = bb[:, 0, :]
        bi = bb[:, 1, :]
        tt(DVE, bbre[:, :], zre[:, :], br, ALU.mult, [("zre",), ("bb",), ("mag",)], [("mag",)])
        tt(DVE, tmp[:, :], zim[:, :], bi, ALU.mult, [("zim",), ("bb",)], [("tmp",)])
        tt(DVE, bbre[:, :], bbre[:, :], tmp[:, :], ALU.subtract, [("mag",), ("tmp",)], [("mag",)])
        tt(DVE, bbim[:, :], zre[:, :], bi, ALU.mult, [("zre",), ("bb",), ("den",)], [("den",)])
        tt(DVE, tmp[:, :], zim[:, :], br, ALU.mult, [("zim",), ("bb",)], [("tmp",)])
        tt(DVE, bbim[:, :], bbim[:, :], tmp[:, :], ALU.add, [("den",), ("tmp",)], [("den",)])
        bbre3 = bbre[:, :].rearrange("h (g p) -> h g p", g=32)
        bbim3 = bbim[:, :].rearrange("h (g p) -> h g p", g=32)
        P.op(ACT, lambda e: e.copy(out=B1f[:, :, 0:64], in_=bbre3), reads=[("mag",)], writes=[("B1", 0)])
        P.op(ACT, lambda e: e.copy(out=B1f[:, :, 64:128], in_=bbim3), reads=[("den",)], writes=[("B1", 1)])
        P.op(ACT, lambda e: e.copy(out=B2f[:, :, 0:64], in_=bbim3), reads=[("den",)], writes=[("B2", 0)])
        P.op(ACT, lambda e: e.activation(out=B2f[:, :, 64:128], in_=bbre3, func=AF.Copy, scale=-1.0), reads=[("mag",)], writes=[("B2", 1)])
        c1f = tb("S_c1f", [128, 32, 16])
        c2f = tb("S_c2f", [128, 32, 16])
        P.op(SP, lambda e: e.dma_start(out=c1f[:, :, :], in_=C.w["s5_c1"]), writes=[("c1f",)], dkey=("s5c", 5))
        P.op(SP, lambda e: e.dma_start(out=c2f[:, :, :], in_=C.w["s5_c2"]), writes=[("c2f",)], dkey=("s5c", 6))
        P.op(ACT, lambda e: e.copy(out=C1f[0:64, :, :], in_=c1f[0:64, :, :]), reads=[("c1f",)], writes=[("C1", 0)])
        P.op(ACT, lambda e: e.activation(out=C1f[64:128, :, :], in_=c1f[64:128, :, :], func=AF.Copy, scale=-1.0), reads=[("c1f",)], writes=[("C1", 1)])
        P.op(ACT, lambda e: e.activation(out=C2f[:, :, :], in_=c2f[:, :, :], func=AF.Copy, scale=-1.0), reads=[("c2f",)], writes=[("C2",)])
        dt_ = tb("S_dt", [16, 32])
        id16 = tb("S_id16", [16, 16])
        P.op(SP, lambda e: e.dma_start(out=dt_[:, :], in_=C.w["s5_d_t"]), writes=[("dt_",)], dkey=("s5c", 7))
        P.op(SP, lambda e: e.dma_start(out=id16[:, :], in_=C.cst["id16"]), writes=[("id16",)], dkey=("s5c", 8))
        P.op(DVE, lambda e: e.tensor_tensor(out=Dd[:, :, :], in0=id16[:, :].unsqueeze(1).to_broadcast([16, 32, 16]),
                                            in1=dt_[:, :].unsqueeze(2).to_broadcast([16, 32, 16]), op=ALU.mult),
             reads=[("dt_",), ("id16",)], writes=[("Dd",)])
        P.barrier()
        es_in.close()
        ug = [sb(f"S_ug{i}", [16, S], BF16) for i in range(2)]
        yg = [sb(f"S_yg{i}", [16, S], BF16) for i in range(2)]
        rfull = [sb(f"S_rfull{i}", [128, TT]) for i in range(2)]
        _lx = sb("S_lx", [128, TT])
        lx = [_lx, _lx]
        _lxi = sb("S_lxi", [128, TT], I32)
        lxi = [_lxi, _lxi]
        _lxf = sb("S_lxf", [128, TT])
        lxf = [_lxf, _lxf]
        sinL = [sb(f"S_sinL{i}", [128, TT]) for i in range(2)]
        cosL = [sb(f"S_cosL{i}", [128, TT]) for i in range(2)]
        _bx = sb("S_bx", [16, 8, 128])
        bx = [_bx, _bx]
        _bxi = sb("S_bxi", [16, 8, 128], I32)
        bxi = [_bxi, _bxi]
        _bxf = sb("S_bxf", [16, 8, 128])
        bxf = [_bxf, _bxf]
        _bcc = sb("S_bcc", [16, 8, 128])
        bcc = [_bcc, _bcc]
        _bss = sb("S_bss", [16, 8, 128])
        bss = [_bss, _bss]
        _bt1 = sb("S_bt1", [16, 8, 128])
        bt1 = [_bt1, _bt1]
        _bt2 = sb("S_bt2", [16, 8, 128])
        bt2 = [_bt2, _bt2]
        B1p = [sb(f"S_B1p{i}", [16, 8, 128], BF16) for i in range(2)]
        B2p = [sb(f"S_B2p{i}", [16, 8, 128], BF16) for i in range(2)]
        _ct1 = sb("S_ct1", [128, 8, 16])
        ct1 = [_ct1, _ct1]
        _ct2 = sb("S_ct2", [128, 8, 16])
        ct2 = [_ct2, _ct2]
        C1p = [sb(f"S_C1p{i}", [128, 8, 16], BF16) for i in range(2)]
        C2p = [sb(f"S_C2p{i}", [128, 8, 16], BF16) for i in range(2)]
        NX = 4
        NW = 3
        X1 = [sb(f"S_X1{i}", [128, TT]) for i in range(NX)]
        X2 = [sb(f"S_X2{i}", [128, TT]) for i in range(NX)]
        wb = [sb(f"S_w{i}", [128, TT]) for i in range(2)]
        cW = [sb(f"S_cW{i}", [128, TT], BF16) for i in range(NW)]
        sW = [sb(f"S_sW{i}", [128, TT], BF16) for i in range(NW)]

        def prep_group(g):
            gb = g % 2
            P.op(SP, lambda e: e.dma_start(out=ug[gb][:, :], in_=C.scr["uT"][g * 16:(g + 1) * 16, :]), writes=[("ug", gb)], dkey=("S5ld", gb))
            P.op(ACT, lambda e: e.activation(out=rfull[gb][:, :], in_=ones5[:, :], func=AF.Copy, scale=r_s[:, g:g + 1]), reads=[], writes=[("rfull", gb)])
            P.op(ACT, lambda e: e.activation(out=lx[gb][:, :], in_=tloc[:, :], func=AF.Copy, scale=f_s[:, g:g + 1]), reads=[], writes=[(("lx", 0), "x")])
            frac_sincos(POOL, lx[gb][:, :], lxi[gb][:, :], lxf[gb][:, :], sinL[gb][:, :], cosL[gb][:, :], ("lx", 0), halfpi, 128, outkey=("lxo", gb))
            P.op(POOL, lambda e: e.tensor_tensor(out=bx[gb][:, :, :], in0=t0b[:, :, :], in1=fbd[:, g, :].unsqueeze(1).to_broadcast([16, 8, 128]), op=ALU.mult),
                 reads=[], writes=[(("bx", 0), "x")])
            frac_sincos(POOL, bx[gb][:, :, :], bxi[gb][:, :, :], bxf[gb][:, :, :], bss[gb][:, :, :], bcc[gb][:, :, :], ("bx", 0), halfpi, 16)
            b1 = B1f[:, g, :].unsqueeze(1).to_broadcast([16, 8, 128])
            b2 = B2f[:, g, :].unsqueeze(1).to_broadcast([16, 8, 128])
            kc, ks = (("bx", 0), "cos"), (("bx", 0), "sin")
            P.op(DVE, lambda e: e.tensor_tensor(out=bt1[gb][:, :, :], in0=bcc[gb][:, :, :], in1=b1, op=ALU.mult), reads=[kc], writes=[("bt1", 0)])
            P.op(DVE, lambda e: e.tensor_tensor(out=bt2[gb][:, :, :], in0=bss[gb][:, :, :], in1=b2, op=ALU.mult), reads=[ks], writes=[("bt2", 0)])
            P.op(POOL, lambda e: e.tensor_tensor(out=B1p[gb][:, :, :], in0=bt1[gb][:, :, :], in1=bt2[gb][:, :, :], op=ALU.add), reads=[("bt1", 0), ("bt2", 0)], writes=[("B1p", gb)])
            P.op(DVE, lambda e: e.tensor_tensor(out=bt1[gb][:, :, :], in0=bcc[gb][:, :, :], in1=b2, op=ALU.mult), reads=[kc, ("bt1", 0)], writes=[("bt1", 0)])
            P.op(DVE, lambda e: e.tensor_tensor(out=bt2[gb][:, :, :], in0=bss[gb][:, :, :], in1=b1, op=ALU.mult), reads=[ks, ("bt2", 0)], writes=[("bt2", 0)])
            P.op(POOL, lambda e: e.tensor_tensor(out=B2p[gb][:, :, :], in0=bt1[gb][:, :, :], in1=bt2[gb][:, :, :], op=ALU.subtract), reads=[("bt1", 0), ("bt2", 0)], writes=[("B2p", gb)])
            c1 = C1f[:, g, :].unsqueeze(1).to_broadcast([128, 8, 16])
            c2 = C2f[:, g, :].unsqueeze(1).to_broadcast([128, 8, 16])
            cc_ = cs_tab[:, g, :].unsqueeze(2).to_broadcast([128, 8, 16])
            ss_ = sn_tab[:, g, :].unsqueeze(2).to_broadcast([128, 8, 16])
            P.op(DVE, lambda e: e.tensor_tensor(out=ct1[gb][:, :, :], in0=c1, in1=cc_, op=ALU.mult), reads=[], writes=[("ct1", 0)])
            P.op(DVE, lambda e: e.tensor_tensor(out=ct2[gb][:, :, :], in0=c2, in1=ss_, op=ALU.mult), reads=[], writes=[("ct2", 0)])
            P.op(POOL, lambda e: e.tensor_tensor(out=C1p[gb][:, :, :], in0=ct1[gb][:, :, :], in1=ct2[gb][:, :, :], op=ALU.add), reads=[("ct1", 0), ("ct2", 0)], writes=[("C1p", gb)])
            P.op(DVE, lambda e: e.tensor_tensor(out=ct1[gb][:, :, :], in0=c2, in1=cc_, op=ALU.mult), reads=[("ct1", 0)], writes=[("ct1", 0)])
            P.op(DVE, lambda e: e.tensor_tensor(out=ct2[gb][:, :, :], in0=c1, in1=ss_, op=ALU.mult), reads=[("ct2", 0)], writes=[("ct2", 0)])
            P.op(POOL, lambda e: e.tensor_tensor(out=C2p[gb][:, :, :], in0=ct1[gb][:, :, :], in1=ct2[gb][:, :, :], op=ALU.subtract), reads=[("ct1", 0), ("ct2", 0)], writes=[("C2p", gb)])

        steps = [(g, j) for g in range(32) for j in range(NT)]
        NS = len(steps)

        def stA1(i):
            g, j = steps[i]
            gb = g % 2
            tsl = slice(j * TT, (j + 1) * TT)
            pa, pak = next_ps(C, 0, 5)
            mm_chain(P, pa[:, :], [(B1p[gb][:, j, :], ug[gb][:, tsl])], reads=[("ug", gb), ("B1p", gb)], writes=[pak])
            pb, pbk = next_ps(C, 0, 5)
            mm_chain(P, pb[:, :], [(B2p[gb][:, j, :], ug[gb][:, tsl])], reads=[("ug", gb), ("B2p", gb)], writes=[pbk])
            psA[i] = (pa, pak, pb, pbk)

        def stA2(i):
            g, j = steps[i]
            gb = g % 2
            x = i % NX
            pa, pak, pb, pbk = psA.pop(i)
            P.op(DVE, lambda e: e.tensor_tensor(out=X1[x][:, :], in0=pa[:, :], in1=cosL[gb][:, :], op=ALU.mult), reads=[pak, (("lxo", gb), "cos")], writes=[("X1", x)])
            P.op(DVE, lambda e: e.tensor_tensor(out=X2[x][:, :], in0=pb[:, :], in1=sinL[gb][:, :], op=ALU.mult), reads=[pbk, (("lxo", gb), "sin")], writes=[("X2", x)])

        def stB(i):
            x = i % NX
            P.op(POOL, lambda e: e.tensor_tensor(out=X1[x][:, :], in0=X1[x][:, :], in1=X2[x][:, :], op=ALU.add), reads=[("X1", x), ("X2", x)], writes=[("X1", x)])

        def stC1(i):
            g, j = steps[i]
            gb = g % 2
            x = i % NX
            u = i % 2
            v = i % NW
            init = 0.0 if j == 0 else wb[1 - u][:, TT - 1:TT]
            P.op(DVE, lambda e: e.tensor_tensor_scan(out=wb[u][:, :], data0=rfull[gb][:, :], data1=X1[x][:, :], initial=init, op0=ALU.mult, op1=ALU.add),
                 reads=[("X1", x), ("rfull", gb), ("w", 1 - u)], writes=[("w", u)])
            P.op(DVE, lambda e: e.tensor_tensor(out=cW[v][:, :], in0=wb[u][:, :], in1=cosL[gb][:, :], op=ALU.mult), reads=[("w", u), (("lxo", gb), "cos")], writes=[("cW", v)])
            P.op(POOL, lambda e: e.tensor_tensor(out=sW[v][:, :], in0=wb[u][:, :], in1=sinL[gb][:, :], op=ALU.mult), reads=[("w", u), (("lxo", gb), "sin")], writes=[("sW", v)])

        def stC2(i):
            g, j = steps[i]
            gb = g % 2
            v = i % NW
            tsl = slice(j * TT, (j + 1) * TT)
            py, pyk = next_ps(C, 5, 6)
            mm_chain(P, py[0:16, :], [(C1p[gb][:, j, :], cW[v][:, :]), (C2p[gb][:, j, :], sW[v][:, :]), (Dd[:, g, :], ug[gb][:, tsl])],
                     reads=[("cW", v), ("sW", v), ("ug", gb), ("C1p", gb), ("C2p", gb)], writes=[pyk])
            P.op(ACT, lambda e: e.activation(out=yg[gb][:, tsl], in_=py[0:16, :], func=AF.Gelu), reads=[pyk], writes=[("yg", gb)])
            if j == NT - 1:
                P.op(POOL, lambda e: e.dma_start(out=C.scr["ygT"][g * 16:(g + 1) * 16, :], in_=yg[gb][:, :]), reads=[("yg", gb)], dkey=("S5st", gb))

        psA = {}
        prep_group(0)
        LC2 = 5
        for i in range(NS + LC2):
            if i < NS and steps[i][1] == LC2 and steps[i][0] + 1 < 32:
                prep_group(steps[i][0] + 1)
            if i < NS:
                stA1(i)
            if 0 <= i - 1 < NS:
                stA2(i - 1)
            if 0 <= i - 2 < NS:
                stB(i - 2)
            if 0 <= i - 3 < NS:
                stC1(i - 3)
            if 0 <= i - LC2 < NS:
                stC2(i - LC2)
        P.barrier()


def stage_l0_out(P, C):
    nc = P.nc
    xres = C.xres
    with ExitStack() as es:
        sb = lambda name, shape, dt=F32: es.enter_context(nc.sbuf_tensor(name, list(shape), dt))
        wglu = sb("O_wglu", [128, 4, 512], BF16)
        wout = sb("O_wout", [128, 8, 1024], BF16)
        bglu = sb("O_bglu", [128, 4])
        wload(P, C, wglu, C.w["l0_s5_w_glu"], 4, 512, gain=None, name="O_wglu")
        wload(P, C, wout, C.w["l0_w_out"], 8, 1024, gain=None, name="O_wout")
        P.op(SP, lambda e: e.dma_start(out=bglu[:, :], in_=C.w["l0_s5_b_glu"]), writes=[("bglu",)], dkey=("bglu",))
        P.barrier()
        xt = [sb(f"O_xt{i}", [128, 8, TT]) for i in range(2)]
        mg = [sb(f"O_mg{i}", [128, 8, TT], BF16) for i in range(2)]
        ygt = [sb(f"O_yg{i}", [128, 4, TT], BF16) for i in range(2)]
        sg = [sb(f"O_sg{i}", [128, TT]) for i in range(2)]
        xv = xres.rearrange("(c p) t -> p c t", p=128)
        mv = C.scr["mT"].rearrange("(c p) t -> p c t", p=128)
        yv = C.scr["ygT"].rearrange("(c p) t -> p c t", p=128)
        it = 0
        for j in range(NT):
            u = j % 2
            tsl = slice(j * TT, (j + 1) * TT)
            P.op(SP, lambda e, u=u, tsl=tsl: e.dma_start(out=xt[u][:, :, :], in_=xv[:, :, tsl]), writes=[("xt", u)], dkey=("Old", "x", u))
            P.op(SP, lambda e, u=u, tsl=tsl: e.dma_start(out=mg[u][:, 0:4, :], in_=mv[:, 0:4, tsl]), writes=[("mg", u, "r")], dkey=("Old", "m", u))
            P.op(SP, lambda e, u=u, tsl=tsl: e.dma_start(out=ygt[u][:, :, :], in_=yv[:, :, tsl]), writes=[("ygt", u)], dkey=("Old", "y", u))
            for fb in range(4):
                pa, pak = next_ps(C)
                mm_chain(P, pa[:, :], [(wglu[:, c, fb * 128:(fb + 1) * 128], ygt[u][:, c, :]) for c in range(4)], reads=[*wkeys(C, "O_wglu", fb * 128), ("ygt", u)], writes=[pak])
                v = it % 2
                it += 1
                P.op(ACT, lambda e, pa=pa, v=v, fb=fb: e.activation(out=sg[v][:, :], in_=pa[:, :], func=AF.Sigmoid, bias=bglu[:, fb:fb + 1]),
                     reads=[pak, ("bglu",)], writes=[("sg", v)])
                P.op(DVE, lambda e, v=v, u=u, fb=fb: e.tensor_tensor(out=mg[u][:, 4 + fb, :], in0=sg[v][:, :], in1=ygt[u][:, fb, :], op=ALU.mult),
                     reads=[("sg", v), ("ygt", u)], writes=[("mg", u, fb)])
            for fb in range(8):
                pa, pak = next_ps(C)
                mm_chain(P, pa[:, :], [(wout[:, c, fb * 128:(fb + 1) * 128], mg[u][:, c, :]) for c in range(8)],
                         reads=wkeys(C, "O_wout", fb * 128) + [("mg", u, "r")] + [("mg", u, f) for f in range(4)], writes=[pak])
                P.op(DVE, lambda e, pa=pa, u=u, fb=fb: e.tensor_tensor(out=xt[u][:, fb, :], in0=pa[:, :], in1=xt[u][:, fb, :], op=ALU.add),
                     reads=[pak, ("xt", u)], writes=[("xt", u)])
            P.op(POOL, lambda e, u=u, tsl=tsl: e.dma_start(out=xv[:, :, tsl], in_=xt[u][:, :, :]), reads=[("xt", u)], dkey=("Ost", u))
        P.barrier()


def stage_l1_inproj(P, C):
    nc = P.nc
    xres = C.xres
    with ExitStack() as es:
        sb = lambda name, shape, dt=F32: es.enter_context(nc.sbuf_tensor(name, list(shape), dt))
        W = sb("E_W", [128, 8, 4112], BF16)
        g_mix = C.gains["l1_mix_norm"]
        wload(P, C, W, C.w["l1_w_in"], 8, 4112, name="E_W")
        cw = sb("E_cw", [128, 4, 24])
        P.op(SP, lambda e: e.dma_start(out=cw[:, :, :], in_=C.w["l1_conv"]), writes=[("cw",)], dkey=("Ecw",))
        hp = sb("E_hp", [8, 4])
        P.op(SP, lambda e: e.dma_start(out=hp[:, 0:2], in_=C.w["l1_hp"]), writes=[("hp",)], dkey=("Ehp",))
        cmask = sb("E_cmask", [8, TT])
        P.op(SP, lambda e: e.dma_start(out=cmask[:, :], in_=C.cst["cmask"]), writes=[("cmask",)], dkey=("Ecm",))
        P.barrier()
        P.op(ACT, lambda e: e.activation(out=hp[:, 2:3], in_=hp[:, 0:1], func=AF.Exp), reads=[("hp",)], writes=[("hp2",)])
        P.op(DVE, lambda e: e.tensor_scalar(out=hp[:, 2:3], in0=hp[:, 2:3], scalar1=-1.0, scalar2=None, op0=ALU.mult), reads=[("hp2",)], writes=[("hp2",)])
        P.barrier()
        xt = sb("E_xt", [128, 8, TT])
        hn = sb("E_hn", [128, 8, TT], BF16)
        sq = sb("E_sq", [128, 8, TT], BF16)
        rstd = sb("E_rstd", [128, TT])
        acc = [sb(f"E_acc{i}", [128, TT]) for i in range(4)]
        accq = sb("E_accq", [128, 16, TT])
        rn16 = sb("E_rn16", [16, TT])
        oh16 = sb("E_oh16", [128, 16, 16], BF16)
        sel16 = sb("E_sel16", [16, 16, 128])
        l2c = sb("E_l2c", [16, 2])
        pss16 = C.psum[5]
        P.op(SP, lambda e: e.dma_start(out=oh16[:, :, :], in_=C.cst_oh16), writes=[("oh16",)], dkey=("Eoh",))
        P.op(SP, lambda e: e.dma_start(out=sel16[:, :, :], in_=C.cst["sel16"]), writes=[("sel16",)], dkey=("Esel",))
        P.op(SP, lambda e: e.dma_start(out=l2c[:, :], in_=C.cst["l2c"]), writes=[("l2c",)], dkey=("El2c",))
        sqb = [sb(f"E_sqb{i}", [128, TT], BF16) for i in range(5)]
        pend_ss = []
        pend_act = []
        halo = sb("E_halo", [128, 24, 3])
        corr = sb("E_corr", [128, 24, 3])
        ctmp = sb("E_ctmp", [128, 24])
        outs = {nm: sb("E_o" + nm, [128, 8, TT], BF16) for nm in ["gq", "gk", "gv", "gz"]}
        gsb = sb("E_g", [8, TT])
        gcs = [sb(f"E_gc{i}", [8, TT]) for i in range(2)]
        bts = [sb(f"E_bt{i}", [8, TT]) for i in range(2)]
        it = 0
        for j in range(NT):
            tsl = slice(j * TT, (j + 1) * TT)
            norm_tile(P, C, xres, j, xt, ("xt",), hn, ("hn",), sq, rstd, pshi=5, gain=g_mix)
            if j > 0:
                hk = [("halo", b) for b in range(24)]
                terms = [(0, 0, 0), (0, 1, 1), (0, 2, 2), (1, 0, 1), (1, 1, 2), (2, 0, 2)]
                first = {}
                for (t_, k_, h_i) in terms:
                    if t_ not in first:
                        first[t_] = True
                        P.op(POOL, lambda e, t_=t_, k_=k_, h_i=h_i: e.tensor_tensor(out=corr[:, :, t_], in0=halo[:, :, h_i], in1=cw[:, k_, 0:24], op=ALU.mult),
                             reads=hk + [("corr",)], writes=[("corr",)])
                    else:
                        P.op(POOL, lambda e, k_=k_, h_i=h_i: e.tensor_tensor(out=ctmp[:, :], in0=halo[:, :, h_i], in1=cw[:, k_, 0:24], op=ALU.mult),
                             reads=hk + [("ctmp",)], writes=[("ctmp",)])
                        P.op(POOL, lambda e, t_=t_: e.tensor_tensor(out=corr[:, :, t_], in0=corr[:, :, t_], in1=ctmp[:, :], op=ALU.add),
                             reads=[("corr",), ("ctmp",)], writes=[("corr",)])
            for sec in range(3):
                nm = ["gq", "gk", "gv"][sec]
                for hh in range(8):
                    blk = sec * 8 + hh
                    col = blk * 128
                    u = it % 5
                    a3 = it % 4
                    it += 1
                    pst, psk = next_ps(C, 0, 5)
                    mm_chain(P, pst[:, :], [(W[:, c, col:col + 128], hn[:, c, :]) for c in range(8)], reads=[*wkeys(C, "E_W", col), ("hn", "a"), ("hn", "b")], writes=[psk])
                    if sec == 2:
                        a_ = acc[a3]
                        akey = ("acc", a3)
                    else:
                        a_ = accq[:, sec * 8 + hh, :]
                        akey = ("accq", sec * 8 + hh)
                    P.op(ACT, lambda e, a_=a_, pst=pst, blk=blk: e.activation(out=a_[:, :], in_=pst[:, :], func=AF.Copy, scale=cw[:, 3, blk:blk + 1]),
                         reads=[psk], writes=[akey])
                    if j < NT - 1:
                        P.op(ACT, lambda e, pst=pst, blk=blk: e.copy(out=halo[:, blk, :], in_=pst[:, TT - 3:TT]), reads=[psk], writes=[("halo", blk)])
                    for k in range(3):
                        d_ = 3 - k
                        P.op(DVE, lambda e, a_=a_, pst=pst, blk=blk, k=k, d_=d_: e.scalar_tensor_tensor(out=a_[:, d_:TT], in0=pst[:, 0:TT - d_], scalar=cw[:, k, blk:blk + 1], in1=a_[:, d_:TT],
                                                                                                  op0=ALU.mult, op1=ALU.add),
                             reads=[psk, akey], writes=[akey])
                    if j > 0:
                        P.op(POOL, lambda e, a_=a_, blk=blk: e.tensor_tensor(out=a_[:, 0:3], in0=a_[:, 0:3], in1=corr[:, blk, :], op=ALU.add),
                             reads=[("corr",), akey], writes=[akey])
                    pend_act.append((sec, hh, a_, akey, u, nm))
                    while len(pend_act) > (1 if blk < 23 else 0):
                        sec_, hh_, a2_, akey_, u2_, nm_ = pend_act.pop(0)
                        if sec_ == 2:
                            P.op(ACT, lambda e, a2_=a2_, hh_=hh_, nm_=nm_: e.activation(out=outs[nm_][:, hh_, :], in_=a2_[:, :], func=AF.Silu), reads=[akey_], writes=[(nm_, hh_)])
                        else:
                            P.op(ACT, lambda e, a2_=a2_: e.activation(out=a2_[:, :], in_=a2_[:, :], func=AF.Silu), reads=[akey_], writes=[akey_])
                            P.op(ACT, lambda e, a2_=a2_, u2_=u2_: e.activation(out=sqb[u2_][:, :], in_=a2_[:, :], func=AF.Square), reads=[akey_], writes=[("sqb", u2_)])
                            pend_ss.append((sec_ * 8 + hh_, u2_))
                    while len(pend_ss) > (3 if blk < 23 else 0):
                        qi_, u_ = pend_ss.pop(0)
                        P.op(PE, lambda e, qi_=qi_, u_=u_: e.matmul(pss16[0:16, :], lhsT=oh16[:, qi_, :], rhs=sqb[u_][:, :], start=(qi_ == 0), stop=(qi_ == 15)),
                             reads=[("sqb", u_)], writes=[("pss16",)])
            P.op(ACT, lambda e: e.activation(out=rn16[:, :], in_=pss16[0:16, :], func=AF.Sqrt, scale=l2c[:, 0:1], bias=l2c[:, 1:2]), reads=[("pss16",)], writes=[("rn16",)])
            P.op(DVE, lambda e: e.reciprocal(out=rn16[:, :], in_=rn16[:, :]), reads=[("rn16",)], writes=[("rn16",)])
            for qi in range(16):
                sec, hh = qi // 8, qi % 8
                nm = ["gq", "gk"][sec]
                pbc, pbck = next_ps(C, 0, 5)
                mm_chain(P, pbc[:, :], [(sel16[:, qi, :], rn16[:, :])], reads=[("rn16",)], writes=[pbck])
                P.op(DVE, lambda e, pbc=pbc, qi=qi, hh=hh, nm=nm: e.tensor_tensor(out=outs[nm][:, hh, :], in0=accq[:, qi, :], in1=pbc[:, :], op=ALU.mult),
                     reads=[pbck, ("accq", qi)], writes=[(nm, hh)])
            for hh in range(8):
                col = 3072 + hh * 128
                pst, psk = next_ps(C, 0, 5)
                mm_chain(P, pst[:, :], [(W[:, c, col:col + 128], hn[:, c, :]) for c in range(8)], reads=[*wkeys(C, "E_W", col), ("hn", "a"), ("hn", "b")], writes=[psk])
                P.op(ACT, lambda e, pst=pst, hh=hh: e.activation(out=outs["gz"][:, hh, :], in_=pst[:, :], func=AF.Silu), reads=[psk], writes=[("gz", hh)])
            pb_, pbk = next_ps(C, 0, 5)
            mm_chain(P, pb_[0:8, :], [(W[:, c, 4096:4104], hn[:, c, :]) for c in range(8)], reads=[*wkeys(C, "E_W", 4096, 8), ("hn", "a"), ("hn", "b")], writes=[pbk])
            jb = j % 2
            P.op(ACT, lambda e, pb_=pb_, jb=jb: e.activation(out=bts[jb][:, :], in_=pb_[0:8, :], func=AF.Sigmoid), reads=[pbk], writes=[("bts", jb)])
            pa_, pak = next_ps(C, 0, 5)
            mm_chain(P, pa_[0:8, :], [(W[:, c, 4104:4112], hn[:, c, :]) for c in range(8)], reads=[*wkeys(C, "E_W", 4104, 8), ("hn", "a"), ("hn", "b")], writes=[pak])
            P.op(ACT, lambda e, pa_=pa_: e.activation(out=gsb[:, :], in_=pa_[0:8, :], func=AF.Exp, bias=hp[:, 1:2]), reads=[pak, ("hp",)], writes=[("gsb",)])
            P.op(ACT, lambda e: e.activation(out=gsb[:, :], in_=gsb[:, :], func=AF.Ln, bias=C.one_col[0:8, 0:1]), reads=[("gsb",)], writes=[("gsb",)])
            P.op(DVE, lambda e: e.tensor_scalar(out=gsb[:, :], in0=gsb[:, :], scalar1=hp[:, 2:3], scalar2=None, op0=ALU.mult), reads=[("gsb",), ("hp2",)], writes=[("gsb",)])
            P.op(DVE, lambda e, jb=jb: e.tensor_tensor_scan(out=gcs[jb][:, :], data0=cmask[:, :], data1=gsb[:, :], initial=0.0, op0=ALU.mult, op1=ALU.add),
                 reads=[("gsb",)], writes=[("gcs", jb)])
            P.op(POOL, lambda e, jb=jb, tsl=tsl: e.dma_start(out=C.scr32["gcT"][:, tsl], in_=gcs[jb][:, :]), reads=[("gcs", jb)], dkey=("Est", "gc", jb))
            P.op(POOL, lambda e, jb=jb, tsl=tsl: e.dma_start(out=C.scr32["btT"][:, tsl], in_=bts[jb][:, :]), reads=[("bts", jb)], dkey=("Est", "bt", jb))
            for nm in ["gq", "gk", "gv", "gz"]:
                dv = C.scr[nm].rearrange("(h p) t -> p h t", p=128)
                P.op(POOL, lambda e, dv=dv, nm=nm, tsl=tsl: e.dma_start(out=dv[:, :, tsl], in_=outs[nm][:, :, :]), reads=[(nm, hh) for hh in range(8)], dkey=("Est", nm))
        P.barrier()


def stage_gdn(P, C):
    nc = P.nc
    NCH = 32
    with ExitStack() as es:
        sb = lambda name, shape, dt=F32: es.enter_context(nc.sbuf_tensor(name, list(shape), dt))
        masks = sb("G_masks", [128, 18, 128])
        P.op(SP, lambda e: e.dma_start(out=masks[:, :, :], in_=C.cst["gmasks"]), writes=[("masks",)], dkey=("Gm",))
        sel = sb("G_sel", [8, 8, 128])
        P.op(SP, lambda e: e.dma_start(out=sel[:, :, :], in_=C.cst["gsel"]), writes=[("sel",)], dkey=("Gs",))
        sel_last = sb("G_sellast", [128, 128])
        P.op(SP, lambda e: e.dma_start(out=sel_last[:, :], in_=C.cst["gsellast"]), writes=[("sellast",)], dkey=("Gsl",))
        identf = masks[:, 15, :]
        onorm = sb("G_onorm", [128, 1])
        P.op(SP, lambda e: e.dma_start(out=onorm[:, :], in_=C.w["l1_o_norm"]), writes=[("onorm",)], dkey=("Gon",))
        gcT = sb("G_gcT", [8, S])
        btT = sb("G_btT", [8, S])
        P.op(SP, lambda e: e.dma_start(out=gcT[:, :], in_=C.scr32["gcT"][:, :]), writes=[("gcT",)], dkey=("Ggc",))
        P.op(SP, lambda e: e.dma_start(out=btT[:, :], in_=C.scr32["btT"][:, :]), writes=[("btT",)], dkey=("Gbt",))
        gct = sb("G_gct", [128, NCH, 8])
        btt = sb("G_btt", [128, NCH, 8])
        glt = sb("G_glt", [128, NCH, 8])
        kbs = sb("G_kbs", [128, NCH, 8])
        kds = sb("G_kds", [128, NCH, 8])
        egl = sb("G_egl", [128, NCH, 8])
        P.barrier()
        for n in range(NCH):
            cs = slice(n * 128, (n + 1) * 128)
            pt, ptk = next_ps(C)
            P.op(PE, lambda e, pt=pt, cs=cs: e.transpose(pt[:, 0:8], gcT[:, cs], identf[0:8, 0:8]), reads=[("gcT",)], writes=[ptk])
            P.op(ACT, lambda e, pt=pt, n=n: e.copy(out=gct[:, n, :], in_=pt[:, 0:8]), reads=[ptk], writes=[("gct", n)])
            pt2, pt2k = next_ps(C)
            P.op(PE, lambda e, pt2=pt2, cs=cs: e.transpose(pt2[:, 0:8], btT[:, cs], identf[0:8, 0:8]), reads=[("btT",)], writes=[pt2k])
            P.op(DVE, lambda e, pt2=pt2, n=n: e.tensor_copy(out=btt[:, n, :], in_=pt2[:, 0:8]), reads=[pt2k], writes=[("btt", n)])
        P.barrier()
        gflat = lambda t: t[:, :, :].rearrange("p n h -> p (n h)")
        pg, pgk = next_ps(C)
        mm_chain(P, pg[:, 0:256], [(sel_last[:, :], gflat(gct))], reads=[], writes=[pgk])
        P.op(ACT, lambda e: e.copy(out=gflat(glt), in_=pg[:, 0:256]), reads=[pgk], writes=[("glt",)])
        P.op(ACT, lambda e: e.activation(out=gflat(egl), in_=pg[:, 0:256], func=AF.Exp), reads=[pgk], writes=[("egl",)])
        P.op(ACT, lambda e: e.activation(out=gflat(kbs), in_=gflat(gct), func=AF.Exp), reads=[], writes=[("kbs",)])
        P.op(DVE, lambda e: e.tensor_tensor(out=gflat(kbs), in0=gflat(kbs), in1=gflat(btt), op=ALU.mult), reads=[("kbs",)], writes=[("kbs",)])
        P.op(DVE, lambda e: e.tensor_tensor(out=gflat(kds), in0=gflat(glt), in1=gflat(gct), op=ALU.subtract), reads=[("glt",)], writes=[("kds",)])
        P.op(ACT, lambda e: e.activation(out=gflat(kds), in_=gflat(kds), func=AF.Exp), reads=[("kds",)], writes=[("kds",)])
        P.barrier()
        KT = [sb(f"G_KT{i}", [128, S], BF16) for i in range(2)]
        QT = [sb(f"G_QT{i}", [128, S], BF16) for i in range(2)]
        VT = [sb(f"G_VT{i}", [128, S], BF16) for i in range(2)]
        ZT = [sb(f"G_ZT{i}", [128, S], BF16) for i in range(2)]
        gcb = [sb(f"G_gcb{i}", [128, TT]) for i in range(2)]
        gcbA = [sb(f"G_gcbA{i}", [128, TT]) for i in range(2)]
        gcbB = [sb(f"G_gcbB{i}", [128, TT]) for i in range(2)]
        egcb = [sb(f"G_egcb{i}", [128, TT]) for i in range(2)]
        QdT = [sb(f"G_QdT{i}", [128, TT], BF16) for i in range(2)]
        NB_ = 2
        T4 = [128, 4, 128]
        xg = [sb(f"G_xg{i}", T4) for i in range(NB_)]
        tA = [sb(f"G_tA{i}", T4) for i in range(NB_)]
        tB = [sb(f"G_tB{i}", T4) for i in range(NB_)]
        a1 = [sb(f"G_a1{i}", T4) for i in range(NB_)]
        q1 = [sb(f"G_q1{i}", T4) for i in range(NB_)]
        tmpf = [sb(f"G_tmpf{i}", T4) for i in range(NB_)]
        A_ = [sb(f"G_A{i}", T4, BF16) for i in range(NB_)]
        AT_ = [sb(f"G_AT{i}", T4, BF16) for i in range(NB_)]
        qkT = [sb(f"G_qkT{i}", T4, BF16) for i in range(NB_)]
        Dd = [[sb(f"G_D{i}_{k}", T4, BF16) for k in range(2)] for i in range(NB_)]
        DTd = [[sb(f"G_DT{i}_{k}", T4, BF16) for k in range(2)] for i in range(NB_)]
        Xm = [sb(f"G_Xm{i}", T4, BF16) for i in range(NB_)]
        XTm = [sb(f"G_XTm{i}", T4, BF16) for i in range(NB_)]
        kbd = [sb(f"G_kbd{i}", T4, BF16) for i in range(NB_)]
        kdec = [sb(f"G_kdec{i}", T4, BF16) for i in range(NB_)]
        vb = [sb(f"G_vb{i}", T4, BF16) for i in range(NB_)]
        nwT = [sb(f"G_nwT{i}", T4, BF16) for i in range(NB_)]
        vnew = [sb(f"G_vnew{i}", [128, 128], BF16) for i in range(2)]
        Sst = sb("G_S", [128, 128])
        Sbf = sb("G_Sbf", [128, 128], BF16)
        o_sb = [sb(f"G_osb{i}", [128, TT]) for i in range(2)]
        osq = [sb(f"G_osq{i}", [128, TT], BF16) for i in range(2)]
        rr = [sb(f"G_rr{i}", [128, TT]) for i in range(2)]
        mo = [sb(f"G_mo{i}", [128, TT], BF16) for i in range(2)]
        ident_bf = C.ident_bf
        v4 = lambda t: t[:, :].rearrange("p (c k) -> p c k", c=4)
        mb = lambda k: masks[:, k, :].unsqueeze(1).to_broadcast(T4)

        def load_head(h):
            hb = h % 2
            for nm, tl in [("gk", KT), ("gq", QT), ("gv", VT), ("gz", ZT)]:
                P.op(SP, lambda e, nm=nm, tl=tl, h=h, hb=hb: e.dma_start(out=tl[hb][:, :], in_=C.scr[nm][h * 128:(h + 1) * 128, :]),
                     writes=[(nm, hb)], dkey=("Gld", nm, hb))

        def prep_half(h, jt, u, c0, ncn):
            hb = h % 2
            w = u
            n0 = 4 * jt
            hf = c0 // ncn
            tsl = slice(jt * TT, (jt + 1) * TT)
            cs = [slice((n0 + c0 + c) * 128, (n0 + c0 + c + 1) * 128) for c in range(ncn)]
            TH = [128, ncn, 128]
            hs = slice(c0, c0 + ncn)
            vh = lambda t: t[:, 0:ncn * 128].rearrange("p (c k) -> p c k", c=ncn)
            mh = lambda k: masks[:, k, :].unsqueeze(1).to_broadcast(TH)
            K_ = lambda nm: (nm, u, hf)
            if c0 == 0:
                pg, pgk = next_ps(C, 2, 6)
                mm_chain(P, pg[:, :], [(sel[:, h, :], gcT[:, tsl])], reads=[], writes=[pgk])
                P.op(ACT, lambda e: e.copy(out=gcb[w][:, :], in_=pg[:, :]), reads=[pgk], writes=[("gcb", w)])
                m4 = lambda k: masks[:, k, :].unsqueeze(1).to_broadcast([128, 4, 128])
                g4 = lambda t: t[:, :].rearrange("p (c k) -> p c k", c=4)
                P.op(DVE, lambda e: e.tensor_tensor(out=g4(gcbA[w]), in0=g4(gcb[w]), in1=m4(16), op=ALU.add), reads=[("gcb", w)], writes=[("gcbA", w)])
                P.op(POOL, lambda e: e.tensor_tensor(out=g4(gcbB[w]), in0=g4(gcb[w]), in1=m4(17), op=ALU.add), reads=[("gcb", w)], writes=[("gcbB", w)])
                P.op(ACT, lambda e: e.activation(out=egcb[w][:, :], in_=pg[:, :], func=AF.Exp), reads=[pgk], writes=[("egcb", w)])
                P.op(POOL, lambda e: e.tensor_tensor(out=QdT[w][:, :], in0=QT[hb][:, tsl], in1=egcb[w][:, :], op=ALU.mult),
                     reads=[("egcb", w), ("gq", hb)], writes=[("QdT", w)])
            gci = gct[:, n0 + c0:n0 + c0 + ncn, h:h + 1].to_broadcast(TH)
            bti = btt[:, n0 + c0:n0 + c0 + ncn, h:h + 1].to_broadcast(TH)
            kbi = kbs[:, n0 + c0:n0 + c0 + ncn, h:h + 1].to_broadcast(TH)
            kdi = kds[:, n0 + c0:n0 + c0 + ncn, h:h + 1].to_broadcast(TH)
            pkk, pkkk = next_ps(C, 2, 6)

            def f_kk(e):
                ins = None
                for c in range(ncn):
                    ins = e.matmul(pkk[:, c * 128:(c + 1) * 128], lhsT=KT[hb][:, cs[c]], rhs=KT[hb][:, cs[c]], start=True, stop=True)
                return ins
            P.op(PE, f_kk, reads=[("gk", hb)], writes=[pkkk])
            pqk, pqkk = next_ps(C, 2, 6)

            def f_qk(e):
                ins = None
                for c in range(ncn):
                    ins = e.matmul(pqk[:, c * 128:(c + 1) * 128], lhsT=KT[hb][:, cs[c]], rhs=QT[hb][:, cs[c]], start=True, stop=True)
                return ins
            P.op(PE, f_qk, reads=[("gk", hb), ("gq", hb)], writes=[pqkk])
            gcbAv = gcbA[w][:, :].rearrange("p (c k) -> p c k", c=4)[:, hs, :]
            gcbBv = gcbB[w][:, :].rearrange("p (c k) -> p c k", c=4)[:, hs, :]
            P.op(DVE, lambda e: e.tensor_tensor(out=xg[u][:, hs, :], in0=gcbAv, in1=gci, op=ALU.subtract), reads=[("gcbA", w)], writes=[K_("xg")])
            P.op(DVE, lambda e: e.tensor_tensor(out=tmpf[u][:, hs, :], in0=gcbBv, in1=gci, op=ALU.subtract), reads=[("gcbB", w)], writes=[K_("tmpf")])
            yield None
            P.op(ACT, lambda e: e.activation(out=tA[u][:, hs, :], in_=xg[u][:, hs, :], func=AF.Exp, scale=-1.0), reads=[K_("xg")], writes=[K_("tA")])
            P.op(ACT, lambda e: e.activation(out=tB[u][:, hs, :], in_=tmpf[u][:, hs, :], func=AF.Exp), reads=[K_("tmpf")], writes=[K_("tB")])
            yield None
            P.op(DVE, lambda e: e.tensor_tensor(out=a1[u][:, hs, :], in0=tA[u][:, hs, :], in1=vh(pkk), op=ALU.mult), reads=[pkkk, K_("tA")], writes=[K_("a1")])
            P.op(DVE, lambda e: e.tensor_tensor(out=qkT[u][:, hs, :], in0=tB[u][:, hs, :], in1=vh(pqk), op=ALU.mult), reads=[pqkk, K_("tB")], writes=[K_("qkT")])
            yield None
            for c in range(ncn):
                P.op(ACT, lambda e, c=c: e.activation(out=A_[u][:, c0 + c, :], in_=a1[u][:, c0 + c, :], func=AF.Copy, scale=btt[:, n0 + c0 + c, h:h + 1]),
                     reads=[K_("a1")], writes=[K_("A")])
            yield None
            pb1, pb1k = next_psb(C)

            def f_at(e):
                ins = None
                for c in range(ncn):
                    ins = e.transpose(pb1[:, c * 128:(c + 1) * 128], A_[u][:, c0 + c, :], ident_bf[:, :])
                return ins
            P.op(PE, f_at, reads=[K_("A")], writes=[pb1k])
            pb2, pb2k = next_psb(C)

            def f_kvt(e):
                ins = None
                for c in range(ncn):
                    e.transpose(pb2[:, c * 128:(c + 1) * 128], KT[hb][:, cs[c]], ident_bf[:, :])
                    ins = e.transpose(pb2[:, (ncn + c) * 128:(ncn + c + 1) * 128], VT[hb][:, cs[c]], ident_bf[:, :])
                return ins
            P.op(PE, f_kvt, reads=[("gk", hb), ("gv", hb)], writes=[pb2k])
            P.op(ACT, lambda e: e.copy(out=AT_[u][:, hs, :], in_=vh(pb1)), reads=[pb1k], writes=[K_("AT")])
            ktr = pb2[:, 0:ncn * 128].rearrange("p (c k) -> p c k", c=ncn)
            vtr = pb2[:, ncn * 128:2 * ncn * 128].rearrange("p (c k) -> p c k", c=ncn)
            P.op(DVE, lambda e: e.tensor_tensor(out=kbd[u][:, hs, :], in0=ktr, in1=kbi, op=ALU.mult), reads=[pb2k], writes=[K_("kbd")])
            P.op(DVE, lambda e: e.tensor_tensor(out=kdec[u][:, hs, :], in0=ktr, in1=kdi, op=ALU.mult), reads=[pb2k], writes=[K_("kdec")])
            P.op(DVE, lambda e: e.tensor_tensor(out=vb[u][:, hs, :], in0=vtr, in1=bti, op=ALU.mult), reads=[pb2k], writes=[K_("vb")])
            P.op(POOL, lambda e: e.tensor_tensor(out=tmpf[u][:, hs, :], in0=A_[u][:, hs, :], in1=mh(2), op=ALU.mult), reads=[K_("A")], writes=[K_("tmpf")])
            yield None
            P.op(POOL, lambda e: e.tensor_tensor(out=Dd[u][0][:, hs, :], in0=tmpf[u][:, hs, :], in1=mh(15), op=ALU.add), reads=[K_("tmpf")], writes=[("D", u, 0, hf)])
            P.op(DVE, lambda e: e.tensor_tensor(out=q1[u][:, hs, :], in0=AT_[u][:, hs, :], in1=mh(8), op=ALU.mult), reads=[K_("AT"), K_("q1")], writes=[K_("q1")])
            yield None
            P.op(DVE, lambda e: e.tensor_tensor(out=DTd[u][0][:, hs, :], in0=q1[u][:, hs, :], in1=mh(15), op=ALU.add), reads=[K_("q1")], writes=[("DT", u, 0, hf)])
            cur = 0
            for li in range(1, 7):
                yield None
                last = (li == 6)
                nxt = 1 - cur
                D_c, DT_c = Dd[u][cur], DTd[u][cur]
                kD, kDT = ("D", u, cur, hf), ("DT", u, cur, hf)
                if not last:
                    px, pxk = next_ps(C, 2, 6)

                    def f_x(e, px=px, D_c=D_c):
                        ins = None
                        for c in range(ncn):
                            ins = e.matmul(px[:, c * 128:(c + 1) * 128], lhsT=AT_[u][:, c0 + c, :], rhs=D_c[:, c0 + c, :], start=True, stop=True)
                        return ins
                    P.op(PE, f_x, reads=[K_("AT"), kD], writes=[pxk])
                px2, px2k = next_ps(C, 2, 6)

                def f_x2(e, px2=px2, DT_c=DT_c):
                    ins = None
                    for c in range(ncn):
                        ins = e.matmul(px2[:, c * 128:(c + 1) * 128], lhsT=A_[u][:, c0 + c, :], rhs=DT_c[:, c0 + c, :], start=True, stop=True)
                    return ins
                P.op(PE, f_x2, reads=[K_("A"), kDT], writes=[px2k])
                yield None
                if not last:
                    P.op(DVE, lambda e, px=px, li=li: e.tensor_tensor(out=Xm[u][:, hs, :], in0=vh(px), in1=mh(2 + li), op=ALU.mult), reads=[pxk], writes=[K_("Xm")])
                P.op(DVE, lambda e, px2=px2, li=li: e.tensor_tensor(out=XTm[u][:, hs, :], in0=vh(px2), in1=mh(8 + li), op=ALU.mult), reads=[px2k], writes=[K_("XTm")])
                yield None
                if not last:
                    pm, pmk = next_ps(C, 2, 6)

                    def f_m(e, pm=pm, D_c=D_c, DT_c=DT_c):
                        ins = None
                        for c in range(ncn):
                            e.matmul(pm[:, c * 128:(c + 1) * 128], lhsT=DT_c[:, c0 + c, :], rhs=Xm[u][:, c0 + c, :], start=True, stop=False)
                            ins = e.matmul(pm[:, c * 128:(c + 1) * 128], lhsT=ident_bf[:, :], rhs=D_c[:, c0 + c, :], start=False, stop=True)
                        return ins
                    P.op(PE, f_m, reads=[kDT, K_("Xm"), kD], writes=[pmk])
                pm2, pm2k = next_ps(C, 2, 6)

                def f_m2(e, pm2=pm2, D_c=D_c, DT_c=DT_c):
                    ins = None
                    for c in range(ncn):
                        ins = e.matmul(pm2[:, c * 128:(c + 1) * 128], lhsT=D_c[:, c0 + c, :], rhs=XTm[u][:, c0 + c, :], start=True, stop=True)
                    return ins
                P.op(PE, f_m2, reads=[kD, K_("XTm"), kDT], writes=[pm2k])
                yield None
                if not last:
                    P.op(ACT, lambda e, pm=pm, nxt=nxt: e.copy(out=Dd[u][nxt][:, hs, :], in_=vh(pm)), reads=[pmk], writes=[("D", u, nxt, hf)])
                P.op(DVE, lambda e, pm2=pm2, nxt=nxt, DT_c=DT_c: e.tensor_tensor(out=DTd[u][nxt][:, hs, :], in0=vh(pm2), in1=DT_c[:, hs, :], op=ALU.add),
                     reads=[pm2k, kDT], writes=[("DT", u, nxt, hf)])
                cur = nxt
            yield None
            TT_ = DTd[u][cur]
            pw, pwk = next_ps(C, 2, 6)

            def f_w(e):
                ins = None
                for c in range(ncn):
                    ins = e.matmul(pw[:, c * 128:(c + 1) * 128], lhsT=kbd[u][:, c0 + c, :], rhs=TT_[:, c0 + c, :], start=True, stop=True)
                return ins
            P.op(PE, f_w, reads=[K_("kbd"), ("DT", u, cur, hf)], writes=[pwk])
            yield None
            P.op(ACT, lambda e: e.activation(out=nwT[u][:, hs, :], in_=vh(pw), func=AF.Copy, scale=-1.0), reads=[pwk], writes=[K_("nwT")])
            yield (TT_, cur)

        def seq(h, n, u, TT_, cur, po, pok):
            c = n % 4
            hf = c // 2
            w = u
            v2 = n % 2
            cl = slice(c * 128, (c + 1) * 128)
            K_ = lambda nm: (nm, u, hf)
            pv, pvk = next_ps(C, 1, 2)
            mm_chain(P, pv[:, 0:128], [(TT_[:, c, :], vb[u][:, c, :]), (nwT[u][:, c, :], Sbf[:, :])], reads=[("DT", u, cur, hf), K_("vb"), K_("nwT"), ("Sbf",)], writes=[pvk])
            P.op(ACT, lambda e: e.copy(out=vnew[v2][:, :], in_=pv[:, 0:128]), reads=[pvk], writes=[("vnew", v2)])
            mm_chain(P, po[:, cl], [(Sbf[:, :], QdT[w][:, cl]), (vnew[v2][:, :], qkT[u][:, c, :])], reads=[("Sbf",), ("QdT", w), ("vnew", v2), K_("qkT")], writes=[pok])
            pS, pSk = next_ps(C, 1, 2)
            mm_chain(P, pS[:, 0:128], [(kdec[u][:, c, :], vnew[v2][:, :])], reads=[K_("kdec"), ("vnew", v2)], writes=[pSk])
            P.op(DVE, lambda e: e.scalar_tensor_tensor(out=Sst[:, :], in0=Sst[:, :], scalar=egl[:, n, h:h + 1], in1=pS[:, 0:128], op0=ALU.mult, op1=ALU.add),
                 reads=[pSk, ("S",)], writes=[("S",)])
            P.op(ACT, lambda e: e.copy(out=Sbf[:, :], in_=Sst[:, :]), reads=[("S",)], writes=[("Sbf",)])

        def finish_tile(h, jt, v, po, pok):
            hb = h % 2
            tsl = slice(jt * TT, (jt + 1) * TT)
            P.op(ACT, lambda e: e.copy(out=o_sb[v][:, :], in_=po[:, :]), reads=[pok], writes=[("osb", v)])
            P.op(ACT, lambda e: e.activation(out=osq[v][:, :], in_=po[:, :], func=AF.Square), reads=[pok], writes=[("osq", v)])
            pss, pssk = next_ps(C, 1, 2)
            mm_chain(P, pss[:, :], [(C.ones_bf[:, :], osq[v][:, :])], reads=[("osq", v)], writes=[pssk])
            P.op(ACT, lambda e: e.activation(out=rr[v][:, :], in_=pss[:, :], func=AF.Ln, scale=1.0 / 128, bias=C.eps_col[:, 0:1]), reads=[pssk], writes=[("rr", v)])
            P.op(ACT, lambda e: e.activation(out=rr[v][:, :], in_=rr[v][:, :], func=AF.Exp, scale=-0.5), reads=[("rr", v)], writes=[("rr", v)])
            P.op(POOL, lambda e: e.tensor_tensor(out=o_sb[v][:, :], in0=o_sb[v][:, :], in1=rr[v][:, :], op=ALU.mult), reads=[("osb", v), ("rr", v)], writes=[("osb", v)])
            P.op(DVE, lambda e: e.scalar_tensor_tensor(out=mo[v][:, :], in0=o_sb[v][:, :], scalar=onorm[:, 0:1], in1=ZT[hb][:, tsl], op0=ALU.mult, op1=ALU.mult),
                 reads=[("osb", v), ("gz", hb)], writes=[("mo", v)])
            P.op(POOL, lambda e: e.dma_start(out=C.scr["mT"][h * 128:(h + 1) * 128, tsl], in_=mo[v][:, :]), reads=[("mo", v)], dkey=("Gst", v))

        tiles = [(h, jt) for h in range(8) for jt in range(NT)]
        load_head(0)
        load_head(1)

        def run_pair(h, jt, u, hooks):
            g0 = prep_half(h, jt, u, 0, 2)
            g1 = prep_half(h, jt, u, 2, 2)
            res = [None, None]
            step = 0
            alive = [True, True]
            import os
            if os.environ.get("GDN_SEQ"):
                for gi, g in enumerate((g0, g1)):
                    for r in g:
                        if r is not None:
                            res[gi] = r
                alive = [False, False]
            while alive[0] or alive[1]:
                for gi, g in enumerate((g0, g1)):
                    if alive[gi]:
                        try:
                            r = next(g)
                            if r is not None:
                                res[gi] = r
                        except StopIteration:
                            alive[gi] = False
                if step in hooks:
                    hooks[step]()
                step += 1
            assert res[0][1] == res[1][1]
            return res[0]

        pend = run_pair(0, 0, 0, {})
        for k, (h, jt) in enumerate(tiles):
            u = k % 2
            if jt == 0:
                P.op(POOL, lambda e: e.memset(Sst[:, :], 0.0), writes=[("S",)])
                P.op(POOL, lambda e: e.memset(Sbf[:, :], 0.0), writes=[("Sbf",)])
            po, pok = next_ps(C, 0, 1)
            done = []

            def mk(c):
                def f():
                    seq(h, 4 * jt + c, u, pend[0], pend[1], po, pok)
                    done.append(c)
                return f
            nxt_p = None
            if k + 1 < len(tiles):
                h2, jt2 = tiles[k + 1]
                nxt_p = run_pair(h2, jt2, 1 - u, {4: mk(0), 10: mk(1), 16: mk(2), 22: mk(3)})
            for c in range(4):
                if c not in done:
                    seq(h, 4 * jt + c, u, pend[0], pend[1], po, pok)
            finish_tile(h, jt, u, po, pok)
            if jt == NT - 1 and h + 2 < 8:
                load_head(h + 2)
            pend = nxt_p
        P.barrier()


def stage_l1_out(P, C):
    nc = P.nc
    xres = C.xres
    with ExitStack() as es:
        sb = lambda name, shape, dt=F32: es.enter_context(nc.sbuf_tensor(name, list(shape), dt))
        wout = sb("O1_wout", [128, 8, 1024], BF16)
        wload(P, C, wout, C.w["l1_w_out"], 8, 1024, gain=None, name="O1_wout")
        P.barrier()
        xt = [sb(f"O1_xt{i}", [128, 8, TT]) for i in range(2)]
        mg = [sb(f"O1_mg{i}", [128, 8, TT], BF16) for i in range(2)]
        xv = xres.rearrange("(c p) t -> p c t", p=128)
        mv = C.scr["mT"].rearrange("(c p) t -> p c t", p=128)
        for j in range(NT):
            u = j % 2
            tsl = slice(j * TT, (j + 1) * TT)
            P.op(SP, lambda e, u=u, tsl=tsl: e.dma_start(out=xt[u][:, :, :], in_=xv[:, :, tsl]), writes=[("xt", u)], dkey=("O1ld", "x", u))
            P.op(SP, lambda e, u=u, tsl=tsl: e.dma_start(out=mg[u][:, :, :], in_=mv[:, :, tsl]), writes=[("mg", u)], dkey=("O1ld", "m", u))
            for fb in range(8):
                pa, pak = next_ps(C)
                mm_chain(P, pa[:, :], [(wout[:, c, fb * 128:(fb + 1) * 128], mg[u][:, c, :]) for c in range(8)], reads=[*wkeys(C, "O1_wout", fb * 128), ("mg", u)], writes=[pak])
                P.op(DVE, lambda e, pa=pa, u=u, fb=fb: e.tensor_tensor(out=xt[u][:, fb, :], in0=pa[:, :], in1=xt[u][:, fb, :], op=ALU.add),
                     reads=[pak, ("xt", u)], writes=[("xt", u)])
            P.op(POOL, lambda e, u=u, tsl=tsl: e.dma_start(out=xv[:, :, tsl], in_=xt[u][:, :, :]), reads=[("xt", u)], dkey=("O1st", u))
        P.barrier()


def stage_copy_in(P, C):
    P.op(SP, lambda e: e.dma_start(out=C.xres[:, :], in_=C.xT_in[:, :]), dkey=("cpin",))
    P.barrier()


def stage_copy_out(P, C):
    P.op(SP, lambda e: e.dma_start(out=C.outT[:, :], in_=C.xres[:, :]), dkey=("cpout",))
    P.barrier()


WNAMES = ["xa_wq", "xa_wkv", "xa_wo", "ffn_w_up", "ffn_conv", "ffn_w_down"]
GNAMES = ["xa_norm", "mem_norm", "ffn_norm"]


def build_program(plan):
    nc = bass.Bass("TRN2", target_bir_lowering=False)
    P = Prog(nc)
    C = Ctx()
    C.ps_cnt = {}
    C.wreg = {}
    dt_in = lambda name, shape, dt=F32: nc.dram_tensor(name, list(shape), dt, kind="ExternalInput").ap()
    C.xT_in = dt_in("xT", [D, S])
    C.memT = dt_in("memT", [D, NMEM])
    C.outT = nc.dram_tensor("outT", [D, S], F32, kind="ExternalOutput").ap()
    C.xres = nc.dram_tensor("xres", [D, S], F32, kind="Internal").ap()
    C.w = {}
    for lp in ["l0_", "l1_"]:
        C.w[lp + "xa_wq"] = dt_in(lp + "xa_wq", [D, D])
        C.w[lp + "xa_wkv"] = dt_in(lp + "xa_wkv", [D, 2 * D])
        C.w[lp + "xa_wo"] = dt_in(lp + "xa_wo", [D, D])
        C.w[lp + "ffn_w_up"] = dt_in(lp + "ffn_w_up", [D, 2 * FFN])
        C.w[lp + "ffn_conv"] = dt_in(lp + "ffn_conv", [128, 3, 2 * NPAIR])
        C.w[lp + "ffn_w_down"] = dt_in(lp + "ffn_w_down", [FFN, D])
    for n, shp in [("l0_w_in", [D, 2560]), ("l0_w_in_sw", [D, 1024]), ("l0_ret_norm", [128, 4]), ("l0_s5_w_glu", [512, 512]),
                   ("l0_s5_b_glu", [128, 4]), ("l0_w_out", [D, D]), ("s5_lam_s", [128, 2, 32]), ("s5_ldt_s", [128, 32]),
                   ("s5_lam_b", [16, 2, 2048]), ("s5_ldt_b", [16, 2048]), ("s5_b_b", [16, 2, 2048]), ("s5_c1", [128, 32, 16]),
                   ("s5_c2", [128, 32, 16]), ("s5_d_t", [16, 32])]:
        C.w[n] = dt_in(n, shp)
    for n, shp in [("l1_w_in", [D, 4112]), ("l1_conv", [128, 4, 24]), ("l1_hp", [8, 2]), ("l1_o_norm", [128, 1]), ("l1_w_out", [D, D])]:
        C.w[n] = dt_in(n, shp)
    C.cst = {}
    for n, shp in [("rot_tab", [128, 4, S]), ("gq_tab", [128, 4, TT]), ("dt_tab", [128, 4, 128]), ("kd_tab", [128, 4]),
                   ("id16", [16, 16]), ("tpos", [128, S]), ("t0s", [128, 32, 8]), ("t0b", [16, 8, 128]), ("cmask", [8, TT]), ("sel16", [16, 16, 128]), ("l2c", [16, 2]), ("gmasks", [128, 18, 128]), ("gsel", [8, 8, 128]),
                   ("gsellast", [128, 128])]:
        C.cst[n] = dt_in(n, shp)
    C.scr = {}
    for n, shp in [("qT", [512, S]), ("qdT", [512, S]), ("kT", [512, S]), ("gT", [512, S]), ("uT", [512, S]), ("vtok", [S, 512]),
                   ("mT", [D, S]), ("ygT", [512, S]), ("gq", [D, S]), ("gk", [D, S]), ("gv", [D, S]), ("gz", [D, S])]:
        C.scr[n] = nc.dram_tensor("scr_" + n, list(shp), BF16, kind="Internal").ap()
    C.scr32 = {n: nc.dram_tensor("scr_" + n, [8, S], F32, kind="Internal").ap() for n in ["gcT", "btT"]}
    gnames = [lp + g for lp in ["l0_", "l1_"] for g in GNAMES + ["mix_norm"]] + ["final_norm"]
    gains_d = dt_in("gains", [128, len(gnames), 8])
    consts_bf = dt_in("consts_bf", [128, 2, 128], BF16)
    C.cst_oh16 = dt_in("oh16", [128, 16, 16], BF16)

    gains_t = P.sb("gains_t", [128, len(gnames), 8])
    cbf = P.sb("cbf", [128, 2, 128], BF16)
    C.eps_col = P.sb("eps_col", [128, 1])
    C.one_col = P.sb("one_col", [128, 1])
    C.eps128_col = P.sb("eps128_col", [128, 1])
    C.psum = [P.ps(f"psum{i}", [128, 512]) for i in range(6)]
    C.psbs = [P.ps(f"psb{i}", [128, 1024], BF16) for i in range(2)]
    C.psb = C.psbs[0]
    C.psb_rr = 0
    P.op(SP, lambda e: e.dma_start(out=gains_t[:, :, :], in_=gains_d), writes=[gain_key(None)], dkey=("gains",))
    P.op(SP, lambda e: e.dma_start(out=cbf[:, :, :], in_=consts_bf), writes=[("cbf",)], dkey=("cbf",))
    P.op(POOL, lambda e: e.memset(C.eps_col[:, :], EPS), writes=[("eps",)])
    P.op(POOL, lambda e: e.memset(C.one_col[:, :], 1.0), writes=[("one",)])
    P.op(POOL, lambda e: e.memset(C.eps128_col[:, :], 128.0 * EPS), writes=[("eps128",)])
    P.barrier()
    C.gains = {n: gains_t[:, i, :] for i, n in enumerate(gnames)}
    C.ident_bf = cbf[:, 0, :]
    C.ones_bf = cbf[:, 1, :]

    for st in plan:
        if st == "copy_in":
            stage_copy_in(P, C)
        elif st == "copy_out":
            stage_copy_out(P, C)
        elif st == "l0_inproj":
            stage_l0_inproj(P, C)
        elif st == "l0_ret":
            stage_retention(P, C)
        elif st == "l0_s5":
            stage_s5(P, C)
        elif st == "l0_out":
            stage_l0_out(P, C)
        elif st == "l1_inproj":
            stage_l1_inproj(P, C)
        elif st == "l1_gdn":
            stage_gdn(P, C)
        elif st == "l1_out":
            stage_l1_out(P, C)
        elif st.endswith("_xa"):
            stage_xattn(P, C, st[:3])
        elif st.endswith("_ffn"):
            stage_ffn(P, C, st[:3], final=False)
        elif st.endswith("_ffnfinal"):
            stage_ffn(P, C, st[:3], final=True)
        else:
            raise ValueError(st)
        P.barrier(final=True)
    P.emit()
    return nc, P, gnames


def host_prep(inputs, gnames):
    shared = {}
    for lp in ["l0_", "l1_"]:
        for n in ["xa_wq", "xa_wkv", "xa_wo", "ffn_w_up", "ffn_w_down"]:
            shared[lp + n] = np.ascontiguousarray(inputs[lp + n], dtype=np.float32)
        cw = np.asarray(inputs[lp + "ffn_conv"], dtype=np.float32)
        shared[lp + "ffn_conv"] = np.ascontiguousarray(cw.reshape(3, 2 * NPAIR, 128).transpose(2, 0, 1))
    f32 = lambda a: np.ascontiguousarray(np.asarray(a, dtype=np.float32))
    w_in = f32(inputs["l0_w_in"])
    shared["l0_w_in"] = w_in
    qk = w_in[:, :1024].reshape(D, 8, 128)
    shared["l0_w_in_sw"] = f32(np.concatenate([qk[:, :, 64:], qk[:, :, :64]], axis=2).reshape(D, 1024))
    shared["l0_ret_norm"] = f32(np.asarray(inputs["l0_ret_norm"]).reshape(4, 128).T)
    shared["l0_s5_w_glu"] = f32(inputs["l0_s5_w_glu"])
    shared["l0_s5_b_glu"] = f32(np.asarray(inputs["l0_s5_b_glu"]).reshape(4, 128).T)
    shared["l0_w_out"] = f32(inputs["l0_w_out"])
    lre = np.asarray(inputs["l0_s5_lambda_re"], np.float32)
    lim = np.asarray(inputs["l0_s5_lambda_im"], np.float32)
    ldt = np.asarray(inputs["l0_s5_log_dt"], np.float32)
    lam_s = np.stack([np.concatenate([lre.T, lre.T], 0), np.concatenate([lim.T, lim.T], 0)], axis=1)
    shared["s5_lam_s"] = f32(lam_s)
    shared["s5_ldt_s"] = f32(np.broadcast_to(ldt[None, :], (128, 32)))
    shared["s5_lam_b"] = f32(np.broadcast_to(np.stack([lre.reshape(-1), lim.reshape(-1)], 0)[None], (16, 2, 2048)))
    shared["s5_ldt_b"] = f32(np.broadcast_to(np.repeat(ldt, 64)[None], (16, 2048)))
    bre = np.asarray(inputs["l0_s5_b_re"], np.float32).transpose(1, 0, 2).reshape(16, 2048)
    bim = np.asarray(inputs["l0_s5_b_im"], np.float32).transpose(1, 0, 2).reshape(16, 2048)
    shared["s5_b_b"] = f32(np.stack([bre, bim], axis=1))
    cre = np.asarray(inputs["l0_s5_c_re"], np.float32).transpose(1, 0, 2)
    cim = np.asarray(inputs["l0_s5_c_im"], np.float32).transpose(1, 0, 2)
    shared["s5_c1"] = f32(np.concatenate([cre, cim], 0))
    shared["s5_c2"] = f32(np.concatenate([cim, cre], 0))
    shared["s5_d_t"] = f32(np.asarray(inputs["l0_s5_d"], np.float32).T)
    shared["l1_w_in"] = f32(inputs["l1_w_in"])
    shared["l1_conv"] = f32(np.asarray(inputs["l1_conv"], np.float32).reshape(4, 24, 128).transpose(2, 0, 1))
    shared["l1_hp"] = f32(np.stack([np.asarray(inputs["l1_a_log"], np.float32), np.asarray(inputs["l1_dt_bias"], np.float32)], axis=1))
    shared["l1_o_norm"] = f32(np.asarray(inputs["l1_o_norm"], np.float32).reshape(128, 1))
    shared["l1_w_out"] = f32(inputs["l1_w_out"])
    cm = np.ones((8, TT), np.float32)
    cm[:, ::128] = 0.0
    shared["cmask"] = cm
    s16 = np.zeros((16, 16, 128), np.float32)
    oh = np.zeros((128, 16, 16), np.float32)
    for q_ in range(16):
        s16[q_, q_, :] = 1.0
        oh[:, q_, q_] = 1.0
    shared["sel16"] = s16
    shared["oh16"] = oh.astype(ml_dtypes.bfloat16)
    l2 = np.zeros((16, 2), np.float32)
    l2[:8, 0] = 128.0
    l2[:8, 1] = 128.0 * EPS
    l2[8:, 0] = 1.0
    l2[8:, 1] = EPS
    shared["l2c"] = l2
    ii_, jj_ = np.meshgrid(np.arange(128), np.arange(128), indexing="ij")
    gm = np.zeros((128, 18, 128), np.float32)
    gm[:, 0, :] = (ii_ > jj_)
    gm[:, 1, :] = (jj_ >= ii_)
    for li, s_ in enumerate([1, 2, 4, 8, 16, 32, 64]):
        m = ((ii_ // (2 * s_)) == (jj_ // (2 * s_))) & ((ii_ % (2 * s_)) >= s_) & ((jj_ % (2 * s_)) < s_)
        if li < 6:
            gm[:, 2 + li, :] = -m.astype(np.float32)
        gm[:, 8 + li, :] = -m.T.astype(np.float32)
    gm[:, 15, :] = np.eye(128, dtype=np.float32)
    BIG = 30000.0
    gm[:, 16, :] = BIG * (1.0 - gm[:, 0, :])
    gm[:, 17, :] = -BIG * gm[:, 0, :]
    shared["gmasks"] = gm
    gs = np.zeros((8, 8, 128), np.float32)
    for h_ in range(8):
        gs[h_, h_, :] = 1.0
    shared["gsel"] = gs
    sl = np.zeros((128, 128), np.float32)
    sl[127, :] = 1.0
    shared["gsellast"] = sl
    inv = np.exp(-math.log(10000.0) * np.arange(64, dtype=np.float32) / 64).astype(np.float32)
    ang = (np.arange(S, dtype=np.float32)[:, None] * inv[None, :]).astype(np.float32).astype(np.float64)
    cosT = np.cos(ang).T
    sinT = np.sin(ang).T
    cos128 = np.concatenate([cosT, cosT], 0)
    sin128 = np.concatenate([-sinT, sinT], 0)
    ksc = 128.0 ** -0.5
    shared["rot_tab"] = f32(np.stack([cos128, sin128, cos128 * ksc, sin128 * ksc], axis=1))
    gam = np.array(RET_GAMMA, np.float64)
    ii = np.arange(TT) % 128
    shared["gq_tab"] = f32(np.broadcast_to((gam[:, None] ** (ii[None, :] + 1))[None], (128, 4, TT)))
    jj = np.arange(128)
    diff = jj[None, :] - jj[:, None]
    dtab = np.where(diff[:, None, :] >= 0, gam[None, :, None] ** np.maximum(diff[:, None, :], 0), 0.0)
    shared["dt_tab"] = f32(dtab)
    shared["kd_tab"] = f32(gam[None, :] ** (127 - jj[:, None]))
    shared["id16"] = f32(np.eye(16))
    shared["tpos"] = f32(np.broadcast_to(np.arange(S, dtype=np.float32)[None], (128, S)))
    shared["t0s"] = f32(np.broadcast_to((np.arange(8, dtype=np.float32) * TT)[None, None, :], (128, 32, 8)))
    shared["t0b"] = f32(np.broadcast_to((np.arange(8, dtype=np.float32) * TT)[None, :, None], (16, 8, 128)))
    g = np.stack([np.asarray(inputs[n], np.float32).reshape(8, 128).T for n in gnames], axis=1)
    shared["gains"] = np.ascontiguousarray(g)
    cb = np.stack([np.eye(128, dtype=np.float32), np.ones((128, 128), np.float32)], axis=1)
    shared["consts_bf"] = np.ascontiguousarray(cb).astype(ml_dtypes.bfloat16)
    return shared


PLAN_FULL = ["copy_in", "l0_inproj", "l0_ret", "l0_s5", "l0_out", "l0_xa", "l0_ffn", "l1_inproj", "l1_gdn", "l1_out", "l1_xa", "l1_ffnfinal"]


def kernel(**inputs):
    nc, P, gnames = build_program(PLAN_FULL)
    shared = host_prep(inputs, gnames)
    x = np.asarray(inputs["x"], np.float32)
    mem = np.asarray(inputs["mem"], np.float32)
    in_maps = []
    for b in range(8):
        m = dict(shared)
        m["xT"] = np.ascontiguousarray(x[b].T)
        m["memT"] = np.ascontiguousarray(mem[b].T)
        in_maps.append(m)
    res = run_bass_kernel_spmd(nc, in_maps, core_ids=list(range(8)))
    out = np.stack([np.ascontiguousarray(res.results[b]["outT"].T) for b in range(8)], axis=0)
    return out.astype(np.float32)
```

```python
import math
from contextlib import ExitStack

import numpy as np
import ml_dtypes
import concourse.bass as bass
import concourse.mybir as mybir
from concourse.bass_utils import run_bass_kernel_spmd

F32 = mybir.dt.float32
BF16 = mybir.dt.bfloat16
I32 = mybir.dt.int32
ALU = mybir.AluOpType
AF = mybir.ActivationFunctionType
AX = mybir.AxisListType
PE, ACT, DVE, POOL, SP = "tensor", "scalar", "vector", "gpsimd", "sync"

D = 1024
S = 4096
TT = 512
NT = S // TT
NMEM = 256
EPS = 1e-6
FFN = 2816
NPAIR = FFN // 128


class Prog:
    def __init__(self, nc):
        self.nc = nc
        self.ops = []
        self.track = {}
        self.stack = ExitStack()
        self.fence = frozenset()
        self.last_eng = {}
        self.dma_since = []
        self.ps_rr = 0
        self.slot_map = {}
        self.wslot_map = {}

    def sb(self, name, shape, dt=F32):
        return self.stack.enter_context(self.nc.sbuf_tensor(name, list(shape), dt))

    def ps(self, name, shape, dt=F32):
        return self.stack.enter_context(self.nc.psum_tensor(name, list(shape), dt))

    def op(self, eng, fn, reads=(), writes=(), dkey=None):
        oid = len(self.ops)
        deps = set(self.fence)
        writes = list(writes) + [k for k in reads if k[0] in ("ps", "psb")]
        for k in reads:
            t = self.track.get(k)
            if t and t[0] is not None:
                deps.add(t[0])
        for k in writes:
            t = self.track.get(k)
            if t:
                if t[0] is not None:
                    deps.add(t[0])
                deps.update(t[1])
        for k in reads:
            t = self.track.setdefault(k, [None, []])
            t[1].append(oid)
        for k in writes:
            self.track[k] = [oid, []]
        deps.discard(oid)
        isw = dkey is not None and dkey[0] == "W"
        if dkey is not None:
            m = self.wslot_map if isw else self.slot_map
            if dkey not in m:
                m[dkey] = ("w" if isw else "d", len(m))
            dkey = m[dkey]
        self.ops.append(dict(eng=eng, fn=fn, deps=deps, dkey=dkey, sig=False))
        if not isw:
            self.last_eng[eng] = oid
            if dkey is not None:
                self.dma_since.append(oid)
        return oid

    def barrier(self, final=False):
        self.fence = frozenset(list(self.last_eng.values()) + self.dma_since)
        self.dma_since = []
        self.track = {k: v for k, v in self.track.items() if k[0] == "W"}
        self.slot_map = {}
        if final:
            self.wslot_map = {}

    def emit(self):
        nc = self.nc
        ops = self.ops
        for o in ops:
            for d in o["deps"]:
                p = ops[d]
                if p["dkey"] is None and p["eng"] == PE and o["eng"] == PE and o["dkey"] is None:
                    continue
                p["sig"] = True
        engs = [PE, ACT, DVE, POOL, SP]
        cnt = {e: 0 for e in engs}
        dcnt = {}
        for o in ops:
            if o["dkey"] is not None:
                dcnt[o["dkey"]] = dcnt.get(o["dkey"], 0) + 16
                o["semk"] = ("d", o["dkey"])
                o["seq"] = dcnt[o["dkey"]]
            elif o["sig"]:
                cnt[o["eng"]] += 1
                o["semk"] = ("e", o["eng"])
                o["seq"] = cnt[o["eng"]]
        semkeys = [("e", e) for e in engs if cnt[e] > 0] + [("d", k) for k in dcnt]
        sems = {}
        for sk in semkeys:
            sems[sk] = self.stack.enter_context(nc.semaphore("s_" + "_".join(str(x) for x in sk)))
        per_eng = {e: [] for e in engs}
        for i, o in enumerate(ops):
            per_eng[o["eng"]].append(i)
        self.stats = {e: len(per_eng[e]) for e in engs}
        self.stats["sems"] = len(sems)
        self.stats["maxcnt"] = dict(cnt)
        self.stats["dcnt"] = max(dcnt.values()) if dcnt else 0

        def run_engine(ename, eobj):
            waited = {}
            for i in per_eng[ename]:
                o = ops[i]
                need = {}
                for d in o["deps"]:
                    p = ops[d]
                    if "semk" not in p:
                        continue
                    if p["dkey"] is None and p["eng"] == PE and ename == PE and o["dkey"] is None:
                        continue
                    sk = p["semk"]
                    if p["seq"] > need.get(sk, 0):
                        need[sk] = p["seq"]
                for sk, v in need.items():
                    if waited.get(sk, 0) >= v:
                        continue
                    eobj.wait_ge(sems[sk], v)
                    waited[sk] = v
                ins = o["fn"](eobj)
                if o["dkey"] is not None:
                    ins.then_inc(sems[o["semk"]], 16)
                elif o["sig"]:
                    ins.then_inc(sems[o["semk"]], 1)
            last = {}
            for i in per_eng[ename]:
                o = ops[i]
                if o["dkey"] is not None:
                    last[o["semk"]] = max(last.get(o["semk"], 0), o["seq"])
            for sk, v in last.items():
                if waited.get(sk, 0) < v:
                    eobj.wait_ge(sems[sk], v)

        block = self.stack.enter_context(nc.Block())
        if per_eng[SP]:
            @block.sync
            def _(e):
                run_engine(SP, e)
        if per_eng[PE]:
            @block.tensor
            def _(e):
                run_engine(PE, e)
        if per_eng[ACT]:
            @block.scalar
            def _(e):
                run_engine(ACT, e)
        if per_eng[DVE]:
            @block.vector
            def _(e):
                run_engine(DVE, e)
        if per_eng[POOL]:
            @block.gpsimd
            def _(e):
                run_engine(POOL, e)
        self.stack.close()


class Ctx:
    pass


def mm_chain(P, ps_ap, pairs, reads, writes):
    pairs = list(pairs)

    def fn(e):
        n = len(pairs)
        ins = None
        for i, (l, r) in enumerate(pairs):
            ins = e.matmul(ps_ap, lhsT=l, rhs=r, start=(i == 0), stop=(i == n - 1))
        return ins
    return P.op(PE, fn, reads, writes)


def next_psb(C):
    b = C.psb_rr % 2
    C.psb_rr += 1
    return C.psbs[b], ("psb", b)


def next_ps(C, lo=0, hi=None):
    hi = len(C.psum) if hi is None else hi
    k = (lo, hi)
    r = C.ps_cnt.get(k, 0)
    C.ps_cnt[k] = r + 1
    b = lo + r % (hi - lo)
    return C.psum[b], ("ps", b)


def wload(P, C, dst, w_dram, kc, F, gain=None, name="w", f_dst0=0):
    assert gain is None
    FW = 1408
    pcs = []
    for f0 in range(0, F, FW):
        fw = min(FW, F - f0)
        pcs.append((f0, fw))
        for c in range(kc):
            P.op(POOL, lambda e, c=c, f0=f0, fw=fw: e.dma_start(out=dst[:, c, f0:f0 + fw], in_=w_dram[c * 128:(c + 1) * 128, f0:f0 + fw]),
                 writes=[("W", name, f0, c)], dkey=("W", name, f0))
    C.wreg[name] = (pcs, kc)


def wkeys(C, name, col=None, n=128):
    pcs, kc = C.wreg[name]
    out = []
    for (f0, fw) in pcs:
        if col is None or (f0 < col + n and col < f0 + fw):
            out += [("W", name, f0, c) for c in range(kc)]
    return out


def gain_key(g):
    return ("gains",)


def load_xt(P, C, src, j, xt, xt_key, ncols=TT, col0=None, dkey="xt"):
    c0 = j * ncols if col0 is None else col0
    srcv = src.rearrange("(c p) t -> p c t", p=128)
    P.op(SP, lambda e: e.dma_start(out=xt[:, :, :], in_=srcv[:, :, c0:c0 + ncols]), writes=[xt_key], dkey=(dkey, xt_key))


def norm_chunked(P, C, xt, xt_key, hn, hn_key, sqs, rstd, gain, kc=8):
    pst, psk = next_ps(C)
    for c in range(kc):
        q = c % len(sqs)
        P.op(ACT, lambda e, c=c, q=q: e.activation(out=sqs[q][:, :], in_=xt[:, c, :], func=AF.Square), reads=[xt_key], writes=[("sqs", q)])
        P.op(PE, lambda e, c=c, q=q: e.matmul(pst[:, :], lhsT=C.ones_bf[:, :], rhs=sqs[q][:, :], start=(c == 0), stop=(c == kc - 1)),
             reads=[("sqs", q)], writes=[psk])
    P.op(ACT, lambda e: e.activation(out=rstd[:, :], in_=pst[:, :], func=AF.Ln, scale=1.0 / D, bias=C.eps_col[:, 0:1]), reads=[psk], writes=[("rstd",)])
    P.op(ACT, lambda e: e.activation(out=rstd[:, :], in_=rstd[:, :], func=AF.Exp, scale=-0.5), reads=[("rstd",)], writes=[("rstd",)])
    for c in range(kc):
        P.op(DVE, lambda e, c=c: e.scalar_tensor_tensor(out=hn[:, c, :], in0=xt[:, c, :], scalar=gain[:, c:c + 1], in1=rstd[:, :], op0=ALU.mult, op1=ALU.mult),
             reads=[xt_key, ("rstd",)], writes=[(hn_key[0], "a" if c < 4 else "b")])


def norm_tile(P, C, src, j, xt, xt_key, hn, hn_key, sq, rstd, ncols=TT, kc=8, col0=None, dkey="xt", sq_keys=(("sq",),), pshi=None, gain=None, load=True):
    if load:
        load_xt(P, C, src, j, xt, xt_key, ncols=ncols, col0=col0, dkey=dkey)
    P.op(ACT, lambda e: e.activation(out=sq[:, :, :ncols], in_=xt[:, :, :], func=AF.Square), reads=[xt_key], writes=list(sq_keys))
    pst, psk = next_ps(C, 0, pshi)
    mm_chain(P, pst[:, :ncols], [(C.ones_bf[:, :], sq[:, c, :ncols]) for c in range(kc)], reads=list(sq_keys), writes=[psk])
    P.op(ACT, lambda e: e.activation(out=rstd[:, :ncols], in_=pst[:, :ncols], func=AF.Ln, scale=1.0 / D, bias=C.eps_col[:, 0:1]),
         reads=[psk], writes=[("rstd",)])
    P.op(ACT, lambda e: e.activation(out=rstd[:, :ncols], in_=rstd[:, :ncols], func=AF.Exp, scale=-0.5), reads=[("rstd",)], writes=[("rstd",)])
    for c in range(kc):
        P.op(DVE, lambda e, c=c: e.scalar_tensor_tensor(out=hn[:, c, :], in0=xt[:, c, :], scalar=gain[:, c:c + 1], in1=rstd[:, :ncols], op0=ALU.mult, op1=ALU.mult),
             reads=[xt_key, ("rstd",)], writes=[(hn_key[0], "a" if c < 4 else "b")])


def stage_xattn(P, C, lp):
    nc = P.nc
    W = C.w
    xres = C.xres
    scale = 256 ** -0.5
    with ExitStack() as es:
        sb = lambda name, shape, dt=F32: es.enter_context(nc.sbuf_tensor("sb_" + lp + name, list(shape), dt))
        wq = sb("xa_wq", [128, 8, 1024], BF16)
        wo = sb("xa_wo", [128, 8, 1024], BF16)
        kT = sb("xa_kT", [128, 8, NMEM], BF16)
        vtok = sb("xa_vtok", [128, 2, 1024], BF16)
        g_xa = C.gains[lp + "xa_norm"]
        g_mem = C.gains[lp + "mem_norm"]
        wload(P, C, wq, W[lp + "xa_wq"], 8, 1024, name="xa_wq")
        wload(P, C, wo, W[lp + "xa_wo"], 8, 1024, gain=None, name="xa_wo")
        P.barrier()
        xts = [sb(f"xa_xt{i}", [128, 8, TT]) for i in range(2)]
        hns = [sb(f"xa_hn{i}", [128, 8, TT], BF16) for i in range(2)]
        xt = xts[0]
        hn = hns[0]
        sq = sb("xa_sq", [128, 8, TT], BF16)
        rstd = sb("xa_rstd", [128, TT])
        with ExitStack() as es2:
            wkv = es2.enter_context(nc.sbuf_tensor("sb_" + lp + "xa_wkv", [128, 8, 2048], BF16))
            memx = es2.enter_context(nc.sbuf_tensor("sb_" + lp + "xa_memx", [128, 8, NMEM], F32))
            memn = es2.enter_context(nc.sbuf_tensor("sb_" + lp + "xa_memn", [128, 8, NMEM], BF16))
            wload(P, C, wkv, W[lp + "xa_wkv"], 8, 2048, name="xa_wkv")
            P.barrier()
            norm_tile(P, C, C.memT, 0, memx, ("memx",), memn, ("memn",), sq, rstd, ncols=NMEM, dkey="memx", gain=g_mem)
            for fb in range(8):
                pst, psk = next_ps(C)
                mm_chain(P, pst[:, :NMEM], [(wkv[:, c, fb * 128:(fb + 1) * 128], memn[:, c, :]) for c in range(8)],
                         reads=[*wkeys(C, "xa_wkv", fb * 128), ("memn", "a"), ("memn", "b")], writes=[psk])
                P.op(ACT, lambda e, pst=pst, fb=fb: e.copy(out=kT[:, fb, :], in_=pst[:, :NMEM]), reads=[psk], writes=[("kT",)])
            for mb in range(2):
                for hf in range(2):
                    pst, psk = next_ps(C)
                    mm_chain(P, pst[:, :], [(memn[:, c, mb * 128:(mb + 1) * 128], wkv[:, c, 1024 + hf * 512:1024 + (hf + 1) * 512]) for c in range(8)],
                             reads=[*wkeys(C, "xa_wkv", 1024 + hf * 512, 512), ("memn", "a"), ("memn", "b")], writes=[psk])
                    P.op(ACT, lambda e, pst=pst, mb=mb, hf=hf: e.copy(out=vtok[:, mb, hf * 512:(hf + 1) * 512], in_=pst[:, :]),
                         reads=[psk], writes=[("vtok",)])
            P.barrier()
        qT = sb("xa_qT", [128, 8, TT], BF16)
        oT = sb("xa_oT", [128, 8, TT], BF16)
        pexp = [sb(f"xa_pexp{i}", [128, NMEM]) for i in range(4)]
        pn = [sb(f"xa_pn{i}", [128, NMEM], BF16) for i in range(4)]
        pT = [sb(f"xa_pT{i}", [128, 2, 128], BF16) for i in range(8)]
        st = [sb(f"xa_st{i}", [128, 4]) for i in range(4)]
        norm_tile(P, C, xres, 0, xts[0], ("xt", 0), hns[0], ("hn0",), sq, rstd, gain=g_xa)
        for j in range(NT):
            bb_ = j % 2
            xt = xts[bb_]
            hn = hns[bb_]
            XK = ("xt", bb_)
            HN = [("hn%d" % bb_, "a"), ("hn%d" % bb_, "b")]
            for fb in range(8):
                pst, psk = next_ps(C)
                mm_chain(P, pst[:, :], [(wq[:, c, fb * 128:(fb + 1) * 128], hn[:, c, :]) for c in range(8)],
                         reads=[*wkeys(C, "xa_wq", fb * 128), *HN], writes=[psk])
                P.op(ACT, lambda e, pst=pst, fb=fb: e.copy(out=qT[:, fb, :], in_=pst[:, :]), reads=[psk], writes=[("qT", fb)])
            units = [(hh, tb) for hh in range(4) for tb in range(4)]

            def s1(i):
                hh, tb = units[i]
                u = i % 4
                pst, psk = next_ps(C)
                mm_chain(P, pst[:, :NMEM], [(qT[:, 2 * hh + dc, tb * 128:(tb + 1) * 128], kT[:, 2 * hh + dc, :]) for dc in range(2)],
                         reads=[("qT", 2 * hh), ("qT", 2 * hh + 1), ("kT",)], writes=[psk])
                s_ = st[u]
                P.op(DVE, lambda e: e.reduce_max(out=s_[:, 0:1], in_=pst[:, :NMEM], axis=AX.X), reads=[psk], writes=[("st", u, 0)])
                P.op(DVE, lambda e: e.tensor_scalar(out=s_[:, 1:2], in0=s_[:, 0:1], scalar1=-scale, scalar2=None, op0=ALU.mult),
                     reads=[("st", u, 0)], writes=[("st", u, 1)])
                P.op(ACT, lambda e: e.activation(out=pexp[u][:, :], in_=pst[:, :NMEM], func=AF.Exp, scale=scale, bias=s_[:, 1:2], accum_out=s_[:, 2:3]),
                     reads=[psk, ("st", u, 1)], writes=[("pexp", u), ("st", u, 2)])

            def s1b(i):
                u = i % 4
                s_ = st[u]
                P.op(DVE, lambda e: e.reciprocal(out=s_[:, 3:4], in_=s_[:, 2:3]), reads=[("st", u, 2)], writes=[("st", u, 3)])
                P.op(DVE, lambda e: e.tensor_scalar(out=pn[u][:, :], in0=pexp[u][:, :], scalar1=s_[:, 3:4], scalar2=None, op0=ALU.mult),
                     reads=[("pexp", u), ("st", u, 3)], writes=[("pn", u)])

            def s2(i):
                hh, tb = units[i]
                u = i % 4
                pbt, pbk_ = next_psb(C)

                def trf(e):
                    ins = None
                    for mb_ in range(2):
                        ins = e.transpose(pbt[:, mb_ * 128:(mb_ + 1) * 128], pn[u][:, mb_ * 128:(mb_ + 1) * 128], C.ident_bf[:, :])
                    return ins
                P.op(PE, trf, reads=[("pn", u)], writes=[pbk_])
                pti = (hh % 2) * 4 + tb
                P.op(ACT, lambda e: e.copy(out=pT[pti][:, :, :], in_=pbt[:, 0:256].rearrange("p (a b) -> p a b", a=2)),
                     reads=[pbk_], writes=[("pT", pti)])

            def s3(hh):
                for dvb in range(2):
                    pst, psk = next_ps(C)

                    def pvf(e, pst=pst, dvb=dvb):
                        ins = None
                        for tb in range(4):
                            for mb_ in range(2):
                                ins = e.matmul(pst[:, tb * 128:(tb + 1) * 128], lhsT=vtok[:, mb_, hh * 256 + dvb * 128:hh * 256 + (dvb + 1) * 128],
                                               rhs=pT[(hh % 2) * 4 + tb][:, mb_, :], start=(mb_ == 0), stop=(mb_ == 1))
                        return ins
                    P.op(PE, pvf, reads=[("vtok",)] + [("pT", (hh % 2) * 4 + tb) for tb in range(4)], writes=[psk])
                    P.op(ACT, lambda e, pst=pst, dvb=dvb: e.copy(out=oT[:, 2 * hh + dvb, :], in_=pst[:, :]), reads=[psk], writes=[("oT", 2 * hh + dvb)])

            LAG = 3
            for i in range(len(units) + LAG):
                if i < len(units):
                    s1(i)
                if 0 <= i - 1 < len(units):
                    s1b(i - 1)
                k_ = i - LAG
                if 0 <= k_ < len(units):
                    s2(k_)
                    if units[k_][1] == 3:
                        s3(units[k_][0])
            if j + 1 < NT:
                norm_tile(P, C, xres, j + 1, xts[1 - bb_], ("xt", 1 - bb_), hns[1 - bb_], ("hn%d" % (1 - bb_),), sq, rstd, gain=g_xa)
            for fb in range(8):
                pst, psk = next_ps(C)
                mm_chain(P, pst[:, :], [(wo[:, c, fb * 128:(fb + 1) * 128], oT[:, c, :]) for c in range(8)],
                         reads=wkeys(C, "xa_wo", fb * 128) + [("oT", c) for c in range(8)], writes=[psk])
                P.op(DVE, lambda e, pst=pst, fb=fb, xt=xt: e.tensor_tensor(out=xt[:, fb, :], in0=pst[:, :], in1=xt[:, fb, :], op=ALU.add),
                     reads=[psk, XK], writes=[XK])
            dstv = xres.rearrange("(c p) t -> p c t", p=128)
            P.op(POOL, lambda e, j=j, xt=xt: e.dma_start(out=dstv[:, :, j * TT:(j + 1) * TT], in_=xt[:, :, :]), reads=[XK], dkey=("xst", bb_))
        P.barrier()


def stage_ffn(P, C, lp, final=False):
    nc = P.nc
    W = C.w
    xres = C.xres
    with ExitStack() as es:
        sb = lambda name, shape, dt=F32: es.enter_context(nc.sbuf_tensor("sb_" + lp + name, list(shape), dt))
        wup = sb("ff_wup", [128, 8, 2 * FFN], BF16)
        wdn = sb("ff_wdn", [128, NPAIR, 1024], BF16)
        cw = sb("ff_cw", [128, 3, 2 * NPAIR])
        g_f = C.gains[lp + "ffn_norm"]
        wload(P, C, wup, W[lp + "ffn_w_up"], 8, 2 * FFN, name="ff_wup")
        wload(P, C, wdn, W[lp + "ffn_w_down"], NPAIR, 1024, gain=None, name="ff_wdn")
        P.op(SP, lambda e: e.dma_start(out=cw[:, :, :], in_=W[lp + "ffn_conv"]), writes=[("cw",)], dkey=("cw",))
        P.barrier()
        xts = [sb(f"ff_xt{i}", [128, 8, TT]) for i in range(2)]
        hn = sb("ff_hn", [128, 8, TT], BF16)
        rstd = sb("ff_rstd", [128, TT])
        rstd2 = rstd
        act = sb("ff_act", [128, NPAIR, TT], BF16)
        sq = act[:, 0:8, :]
        sqk = [("act", c) for c in range(8)]
        acc = [sb(f"ff_acc{i}", [128, TT]) for i in range(4)]
        halo = sb("ff_halo", [128, 2 * NPAIR, 2])
        corr = sb("ff_corr", [128, 2 * NPAIR, 2])
        ctmp = sb("ff_ctmp", [128, 2 * NPAIR])
        sqs = [sb("ff_sqs0", [128, TT], BF16)]
        load_xt(P, C, xres, 0, xts[0], ("xt", 0))
        norm_chunked(P, C, xts[0], ("xt", 0), hn, ("hn",), sqs, rstd, g_f)
        for j in range(NT):
            xt = xts[j % 2]
            fo = xt
            XK = ("xt", j % 2)
            if j + 1 < NT:
                load_xt(P, C, xres, j + 1, xts[1 - j % 2], ("xt", 1 - j % 2))
            if j > 0:
                hk = [("halo", b) for b in range(2 * NPAIR)]
                P.op(POOL, lambda e: e.tensor_tensor(out=corr[:, :, 0], in0=halo[:, :, 1], in1=cw[:, 1, :], op=ALU.mult), reads=hk, writes=[("corr",)])
                P.op(POOL, lambda e: e.tensor_tensor(out=ctmp[:, :], in0=halo[:, :, 0], in1=cw[:, 0, :], op=ALU.mult), reads=hk, writes=[("ctmp",)])
                P.op(POOL, lambda e: e.tensor_tensor(out=corr[:, :, 0], in0=corr[:, :, 0], in1=ctmp[:, :], op=ALU.add), reads=[("corr",), ("ctmp",)], writes=[("corr",)])
                P.op(POOL, lambda e: e.tensor_tensor(out=corr[:, :, 1], in0=halo[:, :, 1], in1=cw[:, 0, :], op=ALU.mult), reads=hk + [("corr",)], writes=[("corr",)])
            pend_pairs = []
            for pr in range(NPAIR):
                accs = {}
                for kind in range(2):
                    blk = kind * NPAIR + pr
                    u = (pr * 2 + kind) % 4
                    pst, psk = next_ps(C)
                    mm_chain(P, pst[:, :], [(wup[:, c, blk * 128:(blk + 1) * 128], hn[:, c, :]) for c in range(8)],
                             reads=[*wkeys(C, "ff_wup", blk * 128), ("hn", "a"), ("hn", "b")], writes=[psk])
                    a_ = acc[u]
                    P.op(ACT, lambda e, a_=a_, pst=pst, blk=blk: e.activation(out=a_[:, :], in_=pst[:, :], func=AF.Copy, scale=cw[:, 2, blk:blk + 1]),
                         reads=[psk], writes=[("acc", u)])
                    if j < NT - 1:
                        P.op(ACT, lambda e, pst=pst, blk=blk: e.copy(out=halo[:, blk, :], in_=pst[:, TT - 2:TT]), reads=[psk], writes=[("halo", blk)])
                    P.op(DVE, lambda e, a_=a_, pst=pst, blk=blk: e.scalar_tensor_tensor(out=a_[:, 1:TT], in0=pst[:, 0:TT - 1], scalar=cw[:, 1, blk:blk + 1], in1=a_[:, 1:TT],
                                                                                     op0=ALU.mult, op1=ALU.add),
                         reads=[psk, ("acc", u)], writes=[("acc", u)])
                    P.op(DVE, lambda e, a_=a_, pst=pst, blk=blk: e.scalar_tensor_tensor(out=a_[:, 2:TT], in0=pst[:, 0:TT - 2], scalar=cw[:, 0, blk:blk + 1], in1=a_[:, 2:TT],
                                                                                     op0=ALU.mult, op1=ALU.add),
                         reads=[psk, ("acc", u)], writes=[("acc", u)])
                    if j > 0:
                        P.op(POOL, lambda e, a_=a_, blk=blk: e.tensor_tensor(out=a_[:, 0:2], in0=a_[:, 0:2], in1=corr[:, blk, :], op=ALU.add),
                             reads=[("corr",), ("acc", u)], writes=[("acc", u)])
                    accs[kind] = (a_, u)
                pend_pairs.append((accs[0], accs[1], pr))
                while len(pend_pairs) > (1 if pr < NPAIR - 1 else 0):
                    (au, uu), (ag, ug), pr_ = pend_pairs.pop(0)
                    P.op(ACT, lambda e, ag=ag: e.activation(out=ag[:, :], in_=ag[:, :], func=AF.Silu), reads=[("acc", ug)], writes=[("acc", ug)])
                    P.op(POOL, lambda e, ag=ag, au=au, pr_=pr_: e.tensor_tensor(out=act[:, pr_, :], in0=ag[:, :], in1=au[:, :], op=ALU.mult),
                         reads=[("acc", ug), ("acc", uu)], writes=[("act", pr_)])
            if j + 1 < NT:
                norm_chunked(P, C, xts[1 - j % 2], ("xt", 1 - j % 2), hn, ("hn",), sqs, rstd2, g_f)
            for fb in range(8):
                pst, psk = next_ps(C)
                mm_chain(P, pst[:, :], [(wdn[:, c, fb * 128:(fb + 1) * 128], act[:, c, :]) for c in range(NPAIR)],
                         reads=wkeys(C, "ff_wdn", fb * 128) + [("act", c) for c in range(NPAIR)], writes=[psk])
                P.op(DVE, lambda e, pst=pst, fb=fb, xt=xt: e.tensor_tensor(out=xt[:, fb, :], in0=pst[:, :], in1=xt[:, fb, :], op=ALU.add),
                     reads=[psk, XK], writes=[XK])
            if not final:
                dstv = xres.rearrange("(c p) t -> p c t", p=128)
                P.op(POOL, lambda e, j=j, xt=xt: e.dma_start(out=dstv[:, :, j * TT:(j + 1) * TT], in_=xt[:, :, :]), reads=[XK], dkey=("xst", j % 2))
            else:
                gfin = C.gains["final_norm"]
                P.op(ACT, lambda e, xt=xt: e.activation(out=sq[:, :, :], in_=xt[:, :, :], func=AF.Square), reads=[XK], writes=sqk)
                pst, psk = next_ps(C)
                mm_chain(P, pst[:, :], [(C.ones_bf[:, :], sq[:, c, :]) for c in range(8)], reads=sqk, writes=[psk])
                P.op(ACT, lambda e, pst=pst: e.activation(out=rstd[:, :], in_=pst[:, :], func=AF.Ln, scale=1.0 / D, bias=C.eps_col[:, 0:1]),
                     reads=[psk], writes=[("rstd",)])
                P.op(ACT, lambda e: e.activation(out=rstd[:, :], in_=rstd[:, :], func=AF.Exp, scale=-0.5), reads=[("rstd",)], writes=[("rstd",)])
                for c in range(8):
                    P.op(DVE, lambda e, c=c, xt=xt, fo=fo: e.scalar_tensor_tensor(out=fo[:, c, :], in0=xt[:, c, :], scalar=gfin[:, c:c + 1], in1=rstd[:, :],
                                                                                          op0=ALU.mult, op1=ALU.mult),
                         reads=[XK, ("rstd",), gain_key(None)], writes=[XK])
                dstv = C.outT.rearrange("(c p) t -> p c t", p=128)
                P.op(POOL, lambda e, j=j, fo=fo: e.dma_start(out=dstv[:, :, j * TT:(j + 1) * TT], in_=fo[:, :, :]), reads=[XK], dkey=("ost", j % 2))
        P.barrier()


RET_GAMMA = [1.0 - 2.0 ** (-5.0 - h) for h in range(4)]


def stage_l0_inproj(P, C):
    nc = P.nc
    xres = C.xres
    with ExitStack() as es:
        sb = lambda name, shape, dt=F32: es.enter_context(nc.sbuf_tensor(name, list(shape), dt))
        W = sb("A_W", [128, 8, 2560], BF16)
        Wsw = sb("A_Wsw", [128, 8, 1024], BF16)
        g_mix = C.gains["l0_mix_norm"]
        wload(P, C, W, C.w["l0_w_in"], 8, 2560, name="A_W")
        wload(P, C, Wsw, C.w["l0_w_in_sw"], 8, 1024, name="A_Wsw")
        gq = sb("A_gq", [128, 4, TT])
        P.op(SP, lambda e: e.dma_start(out=gq[:, :, :], in_=C.cst["gq_tab"]), writes=[("gq",)], dkey=("gq",))
        P.barrier()
        xts = [sb(f"A_xt{i}", [128, 8, TT]) for i in range(2)]
        hn = sb("A_hn", [128, 8, TT], BF16)
        sq = sb("A_sq", [128, 8, TT], BF16)
        rstd = sb("A_rstd", [128, TT])
        rot = sb("A_rot", [128, 4, TT])
        t1 = [sb(f"A_t1{i}", [128, TT]) for i in range(2)]
        t2 = [sb(f"A_t2{i}", [128, TT]) for i in range(2)]
        qo = sb("A_qo", [128, 4, TT], BF16)
        qdo = sb("A_qdo", [128, 4, TT], BF16)
        ko = sb("A_ko", [128, 4, TT], BF16)
        go = sb("A_go", [128, 4, TT], BF16)
        uo = sb("A_uo", [128, 4, TT], BF16)
        vo = sb("A_vo", [128, 4, TT], BF16)
        cnt = 0
        load_xt(P, C, xres, 0, xts[0], ("xt", 0))
        for j in range(NT):
            norm_tile(P, C, xres, j, xts[j % 2], ("xt", j % 2), hn, ("hn",), sq, rstd, gain=g_mix, load=False)
            if j + 1 < NT:
                load_xt(P, C, xres, j + 1, xts[1 - j % 2], ("xt", 1 - j % 2))
            P.op(SP, lambda e, j=j: e.dma_start(out=rot[:, :, :], in_=C.cst["rot_tab"][:, :, j * TT:(j + 1) * TT]), writes=[("rot",)], dkey=("rot",))
            for kind in range(2):
                for h in range(4):
                    col = kind * 512 + h * 128
                    pa, pak = next_ps(C)
                    mm_chain(P, pa[:, :], [(W[:, c, col:col + 128], hn[:, c, :]) for c in range(8)], reads=[*wkeys(C, "A_W", col), ("hn", "a"), ("hn", "b")], writes=[pak])
                    pb, pbk = next_ps(C)
                    mm_chain(P, pb[:, :], [(Wsw[:, c, col:col + 128], hn[:, c, :]) for c in range(8)], reads=[*wkeys(C, "A_Wsw", col), ("hn", "a"), ("hn", "b")], writes=[pbk])
                    u = cnt % 2
                    cnt += 1
                    a_, b_ = t1[u], t2[u]
                    P.op(DVE, lambda e, a_=a_, pa=pa, kind=kind: e.tensor_tensor(out=a_[:, :], in0=pa[:, :], in1=rot[:, 2 * kind, :], op=ALU.mult),
                         reads=[pak, ("rot",)], writes=[("t1", u)])
                    P.op(DVE, lambda e, b_=b_, pb=pb, kind=kind: e.tensor_tensor(out=b_[:, :], in0=pb[:, :], in1=rot[:, 2 * kind + 1, :], op=ALU.mult),
                         reads=[pbk, ("rot",)], writes=[("t2", u)])
                    P.op(POOL, lambda e, a_=a_, b_=b_: e.tensor_tensor(out=a_[:, :], in0=a_[:, :], in1=b_[:, :], op=ALU.add),
                         reads=[("t1", u), ("t2", u)], writes=[("t1", u)])
                    if kind == 0:
                        P.op(ACT, lambda e, a_=a_, h=h: e.copy(out=qo[:, h, :], in_=a_[:, :]), reads=[("t1", u)], writes=[("qo",)])
                        P.op(POOL, lambda e, a_=a_, h=h: e.tensor_tensor(out=qdo[:, h, :], in0=a_[:, :], in1=gq[:, h, :], op=ALU.mult),
                             reads=[("t1", u)], writes=[("qdo",)])
                    else:
                        P.op(ACT, lambda e, a_=a_, h=h: e.copy(out=ko[:, h, :], in_=a_[:, :]), reads=[("t1", u)], writes=[("ko",)])
            for tb in range(4):
                pa, pak = next_ps(C)
                mm_chain(P, pa[:, :], [(hn[:, c, tb * 128:(tb + 1) * 128], W[:, c, 1024:1536]) for c in range(8)], reads=[*wkeys(C, "A_W", 1024, 512), ("hn", "a"), ("hn", "b")], writes=[pak])
                P.op(ACT, lambda e, pa=pa, tb=tb: e.copy(out=vo[:, tb, :], in_=pa[:, :]), reads=[pak], writes=[("vo",)])
            for fb in range(4):
                pa, pak = next_ps(C)
                mm_chain(P, pa[:, :], [(W[:, c, 1536 + fb * 128:1536 + (fb + 1) * 128], hn[:, c, :]) for c in range(8)], reads=[*wkeys(C, "A_W", 1536 + fb * 128), ("hn", "a"), ("hn", "b")], writes=[pak])
                P.op(ACT, lambda e, pa=pa, fb=fb: e.activation(out=go[:, fb, :], in_=pa[:, :], func=AF.Silu), reads=[pak], writes=[("go",)])
            for fb in range(4):
                pa, pak = next_ps(C)
                mm_chain(P, pa[:, :], [(W[:, c, 2048 + fb * 128:2048 + (fb + 1) * 128], hn[:, c, :]) for c in range(8)], reads=[*wkeys(C, "A_W", 2048 + fb * 128), ("hn", "a"), ("hn", "b")], writes=[pak])
                P.op(ACT, lambda e, pa=pa, fb=fb: e.copy(out=uo[:, fb, :], in_=pa[:, :]), reads=[pak], writes=[("uo",)])
            for nm, tl in [("qT", qo), ("qdT", qdo), ("kT", ko), ("gT", go), ("uT", uo)]:
                dv = C.scr[nm].rearrange("(h p) t -> p h t", p=128)
                P.op(POOL, lambda e, dv=dv, tl=tl, j=j: e.dma_start(out=dv[:, :, j * TT:(j + 1) * TT], in_=tl[:, :, :]),
                     reads=[({"qT": "qo", "qdT": "qdo", "kT": "ko", "gT": "go", "uT": "uo"}[nm],)], dkey=("Ast", nm))
            dv = C.scr["vtok"].rearrange("(n p) f -> p n f", p=128)
            P.op(POOL, lambda e, dv=dv, j=j: e.dma_start(out=dv[:, j * 4:(j + 1) * 4, :], in_=vo[:, :, :]), reads=[("vo",)], dkey=("Ast", "v"))
        P.barrier()


def stage_retention(P, C):
    nc = P.nc
    with ExitStack() as es:
        sb = lambda name, shape, dt=F32: es.enter_context(nc.sbuf_tensor(name, list(shape), dt))
        kT = [sb(f"R_kT{i}", [128, S], BF16) for i in range(2)]
        qT = [sb(f"R_qT{i}", [128, S], BF16) for i in range(2)]
        qdT = [sb(f"R_qdT{i}", [128, S], BF16) for i in range(2)]
        gT = [sb(f"R_gT{i}", [128, S], BF16) for i in range(2)]
        vt = [sb(f"R_vt{i}", [128, 32, 128], BF16) for i in range(2)]
        dtab = sb("R_dtab", [128, 4, 128])
        kd = sb("R_kd", [128, 4])
        rn = sb("R_rn", [128, 4])
        state = sb("R_state", [128, 128])
        state_bfs = [sb(f"R_state_bf{i}", [128, 128], BF16) for i in range(2)]
        scm = [sb(f"R_scm{i}", [128, 128], BF16) for i in range(2)]
        kdec = [sb(f"R_kdec{i}", [128, 128], BF16) for i in range(2)]
        o_sb = [sb(f"R_osb{i}", [128, TT]) for i in range(2)]
        osq = [sb(f"R_osq{i}", [128, TT], BF16) for i in range(2)]
        rr = [sb(f"R_rr{i}", [128, TT]) for i in range(2)]
        mo = [sb(f"R_mo{i}", [128, TT], BF16) for i in range(2)]
        P.op(SP, lambda e: e.dma_start(out=dtab[:, :, :], in_=C.cst["dt_tab"]), writes=[("dtab",)], dkey=("dtab",))
        P.op(SP, lambda e: e.dma_start(out=kd[:, :], in_=C.cst["kd_tab"]), writes=[("kd",)], dkey=("kd",))
        P.op(SP, lambda e: e.dma_start(out=rn[:, :], in_=C.w["l0_ret_norm"]), writes=[("rn",)], dkey=("rn",))
        vview = C.scr["vtok"].rearrange("(n p) f -> p n f", p=128)
        grp = 0
        for h in range(4):
            hb = h % 2
            for nm, tl in [("kT", kT), ("qT", qT), ("qdT", qdT), ("gT", gT)]:
                P.op(SP, lambda e, nm=nm, tl=tl, h=h, hb=hb: e.dma_start(out=tl[hb][:, :], in_=C.scr[nm][h * 128:(h + 1) * 128, :]),
                     writes=[(nm, hb)], dkey=("Rld", nm, hb))
            P.op(SP, lambda e, h=h, hb=hb: e.dma_start(out=vt[hb][:, :, :], in_=vview[:, :, h * 128:(h + 1) * 128]), writes=[("vt", hb)], dkey=("Rld", "v", hb))
            P.op(POOL, lambda e: e.memset(state[:, :], 0.0), writes=[("state",)])
            P.op(POOL, lambda e: e.memset(state_bfs[0][:, :], 0.0), writes=[("state_bf", 0)])
            gam_c = RET_GAMMA[h] ** 128
            for n in range(32):
                cs = slice(n * 128, (n + 1) * 128)
                u = n % 2
                if n % 4 == 0:
                    po, pok = next_ps(C, 0, 2)
                    grp += 1
                psc, psck = next_ps(C, 2, 6)
                mm_chain(P, psc[:, 0:128], [(kT[hb][:, cs], qT[hb][:, cs])], reads=[("kT", hb), ("qT", hb)], writes=[psck])
                P.op(DVE, lambda e, psc=psc, u=u, h=h: e.tensor_tensor(out=scm[u][:, :], in0=psc[:, 0:128], in1=dtab[:, h, :], op=ALU.mult),
                     reads=[psck, ("dtab",)], writes=[("scm", u)])
                pbt, pbk_ = next_psb(C)
                P.op(PE, lambda e, cs=cs, hb=hb, pbt=pbt: e.transpose(pbt[:, 0:128], kT[hb][:, cs], C.ident_bf[:, :]), reads=[("kT", hb)], writes=[pbk_])
                P.op(ACT, lambda e, u=u, h=h, pbt=pbt: e.activation(out=kdec[u][:, :], in_=pbt[:, 0:128], func=AF.Copy, scale=kd[:, h:h + 1]),
                     reads=[pbk_, ("kd",)], writes=[("kdec", u)])
                pkv, pkvk = next_ps(C, 2, 6)
                mm_chain(P, pkv[:, 0:128], [(kdec[u][:, :], vt[hb][:, n, :])], reads=[("kdec", u), ("vt", hb)], writes=[pkvk])
                P.op(DVE, lambda e, pkv=pkv, gam_c=gam_c: e.scalar_tensor_tensor(out=state[:, :], in0=state[:, :], scalar=gam_c, in1=pkv[:, 0:128],
                                                                              op0=ALU.mult, op1=ALU.add),
                     reads=[pkvk, ("state",)], writes=[("state",)])
                sb_n = state_bfs[(n + 1) % 2]
                P.op(ACT, lambda e, sb_n=sb_n: e.copy(out=sb_n[:, :], in_=state[:, :]), reads=[("state",)], writes=[("state_bf", (n + 1) % 2)])
                oc = slice((n % 4) * 128, (n % 4 + 1) * 128)
                sb_c = state_bfs[n % 2]
                mm_chain(P, po[:, oc], [(vt[hb][:, n, :], scm[u][:, :]), (sb_c[:, :], qdT[hb][:, cs])],
                         reads=[("vt", hb), ("scm", u), ("state_bf", n % 2), ("qdT", hb)], writes=[pok])
                if n % 4 == 3:
                    v = grp % 2
                    ts_ = slice((n // 4) * TT, (n // 4 + 1) * TT)
                    P.op(ACT, lambda e, po=po, v=v: e.copy(out=o_sb[v][:, :], in_=po[:, :]), reads=[pok], writes=[("osb", v)])
                    P.op(ACT, lambda e, po=po, v=v: e.activation(out=osq[v][:, :], in_=po[:, :], func=AF.Square), reads=[pok], writes=[("osq", v)])
                    pss, pssk = next_ps(C, 2, 6)
                    mm_chain(P, pss[:, :], [(C.ones_bf[:, :], osq[v][:, :])], reads=[("osq", v)], writes=[pssk])
                    P.op(ACT, lambda e, pss=pss, v=v: e.activation(out=rr[v][:, :], in_=pss[:, :], func=AF.Ln, scale=1.0 / 128, bias=C.eps_col[:, 0:1]),
                         reads=[pssk], writes=[("rr", v)])
                    P.op(ACT, lambda e, v=v: e.activation(out=rr[v][:, :], in_=rr[v][:, :], func=AF.Exp, scale=-0.5), reads=[("rr", v)], writes=[("rr", v)])
                    P.op(DVE, lambda e, v=v: e.tensor_tensor(out=o_sb[v][:, :], in0=o_sb[v][:, :], in1=rr[v][:, :], op=ALU.mult),
                         reads=[("osb", v), ("rr", v)], writes=[("osb", v)])
                    P.op(DVE, lambda e, v=v, h=h, hb=hb, ts_=ts_: e.scalar_tensor_tensor(out=mo[v][:, :], in0=o_sb[v][:, :], scalar=rn[:, h:h + 1], in1=gT[hb][:, ts_],
                                                                                      op0=ALU.mult, op1=ALU.mult),
                         reads=[("osb", v), ("rn",), ("gT", hb)], writes=[("mo", v)])
                    P.op(POOL, lambda e, v=v, h=h, ts_=ts_: e.dma_start(out=C.scr["mT"][h * 128:(h + 1) * 128, ts_], in_=mo[v][:, :]),
                         reads=[("mo", v)], dkey=("Rst", v))
        P.barrier()


def stage_s5(P, C):
    nc = P.nc
    TWO_PI = 2.0 * math.pi
    with ExitStack() as es:
        sb = lambda name, shape, dt=F32: es.enter_context(nc.sbuf_tensor(name, list(shape), dt))

        def tt(eng, out, a, b, op, rk, wk):
            P.op(eng, lambda e: e.tensor_tensor(out=out, in0=a, in1=b, op=op), reads=rk, writes=wk)

        def ts(eng, out, a, s1, op0, rk=(), wk=()):
            P.op(eng, lambda e: e.tensor_scalar(out=out, in0=a, scalar1=s1, scalar2=None, op0=op0), reads=rk, writes=wk)

        def act(out, a, func, rk, wk, scale=1.0, bias=None):
            if bias is None:
                P.op(ACT, lambda e: e.activation(out=out, in_=a, func=func, scale=scale), reads=rk, writes=wk)
            else:
                P.op(ACT, lambda e: e.activation(out=out, in_=a, func=func, scale=scale, bias=bias), reads=rk, writes=wk)

        def frac_sincos(eng, x, xi, xf, sin_out, cos_out, key, hp, np_, outkey=None):
            P.op(eng, lambda e: e.tensor_copy(out=xi, in_=x), reads=[(key, "x")], writes=[(key, "xi")])
            P.op(eng, lambda e: e.tensor_copy(out=xf, in_=xi), reads=[(key, "xi")], writes=[(key, "xf")])
            P.op(eng, lambda e: e.tensor_tensor(out=x, in0=x, in1=xf, op=ALU.subtract), reads=[(key, "x"), (key, "xf")], writes=[(key, "x")])
            P.op(DVE, lambda e: e.scalar_tensor_tensor(out=xf, in0=x, scalar=-1.0, in1=x, op0=ALU.mult, op1=ALU.max),
                 reads=[(key, "x"), (key, "xf")], writes=[(key, "xf")])
            ok = key if outkey is None else outkey
            P.op(ACT, lambda e: e.activation(out=sin_out, in_=x, func=AF.Sin, scale=TWO_PI), reads=[(key, "x")], writes=[(ok, "sin")])
            P.op(ACT, lambda e: e.activation(out=cos_out, in_=xf, func=AF.Sin, scale=-TWO_PI, bias=hp[0:np_, 0:1]), reads=[(key, "xf")], writes=[(ok, "cos")])

        halfpi = sb("S_halfpi", [128, 1])
        r_s = sb("S_r", [128, 32])
        f_s = sb("S_f", [128, 32])
        cs_tab = sb("S_cstab", [128, 32, 8])
        sn_tab = sb("S_sntab", [128, 32, 8])
        fbd = sb("S_fbd", [16, 32, 128])
        B1f = sb("S_B1f", [16, 32, 128])
        B2f = sb("S_B2f", [16, 32, 128])
        C1f = sb("S_C1f", [128, 32, 16])
        C2f = sb("S_C2f", [128, 32, 16])
        Dd = sb("S_Dd", [16, 32, 16], BF16)
        tloc = sb("S_tloc", [128, TT])
        t0s = sb("S_t0s", [128, 32, 8])
        t0b = sb("S_t0b", [16, 8, 128])
        ones5 = sb("S_ones", [128, TT])
        es_in = ExitStack()
        tb = lambda name, shape, dt=F32: es_in.enter_context(nc.sbuf_tensor(name, list(shape), dt))
        P.op(POOL, lambda e: e.memset(halfpi[:, :], 0.5 * math.pi), writes=[("halfpi",)])
        P.op(POOL, lambda e: e.memset(ones5[:, :], 1.0), writes=[("ones5",)])
        P.op(SP, lambda e: e.dma_start(out=tloc[:, :], in_=C.cst["tpos"][:, 0:TT]), writes=[("tloc",)], dkey=("s5c", 9))
        P.op(SP, lambda e: e.dma_start(out=t0s[:, :, :], in_=C.cst["t0s"]), writes=[("t0s",)], dkey=("s5c", 10))
        P.op(SP, lambda e: e.dma_start(out=t0b[:, :, :], in_=C.cst["t0b"]), writes=[("t0b",)], dkey=("s5c", 11))
        lam_s = tb("S_lam_s", [128, 2, 32])
        ldt_s = tb("S_ldt_s", [128, 32])
        P.op(SP, lambda e: e.dma_start(out=lam_s[:, :, :], in_=C.w["s5_lam_s"]), writes=[("lam_s",)], dkey=("s5c", 0))
        P.op(SP, lambda e: e.dma_start(out=ldt_s[:, :], in_=C.w["s5_ldt_s"]), writes=[("ldt_s",)], dkey=("s5c", 1))
        act(ldt_s[:, :], ldt_s[:, :], AF.Exp, [("ldt_s",)], [("ldt_s",)])
        tt(DVE, r_s[:, :], lam_s[:, 0, :], ldt_s[:, :], ALU.mult, [("lam_s",), ("ldt_s",)], [("r_s",)])
        act(r_s[:, :], r_s[:, :], AF.Exp, [("r_s",)], [("r_s",)])
        tt(DVE, f_s[:, :], lam_s[:, 1, :], ldt_s[:, :], ALU.mult, [("lam_s",), ("ldt_s",)], [("f_s",)])
        ts(DVE, f_s[:, :], f_s[:, :], 1.0 / TWO_PI, ALU.mult, rk=[("f_s",)], wk=[("f_s",)])
        fi_s = tb("S_fi_s", [128, 32], I32)
        ff_s = tb("S_ff_s", [128, 32])
        P.op(DVE, lambda e: e.tensor_copy(out=fi_s[:, :], in_=f_s[:, :]), reads=[("f_s",)], writes=[("fi_s",)])
        P.op(DVE, lambda e: e.tensor_copy(out=ff_s[:, :], in_=fi_s[:, :]), reads=[("fi_s",)], writes=[("ff_s",)])
        tt(DVE, f_s[:, :], f_s[:, :], ff_s[:, :], ALU.subtract, [("f_s",), ("ff_s",)], [("f_s",)])
        xo = tb("S_xo", [128, 32, 8])
        xoi = tb("S_xoi", [128, 32, 8], I32)
        xof = tb("S_xof", [128, 32, 8])
        P.op(DVE, lambda e: e.tensor_tensor(out=xo[:, :, :], in0=t0s[:, :, :], in1=f_s[:, :].unsqueeze(2).to_broadcast([128, 32, 8]), op=ALU.mult),
             reads=[("f_s",), ("t0s",)], writes=[("xo", "x")])
        frac_sincos(DVE, xo[:, :, :], xoi[:, :, :], xof[:, :, :], sn_tab[:, :, :], cs_tab[:, :, :], "xo", halfpi, 128)
        NB = 32 * 64
        lamb = tb("S_lamb", [16, 2, NB])
        ldtb = tb("S_ldtb", [16, NB])
        bb = tb("S_bb", [16, 2, NB])
        P.op(SP, lambda e: e.dma_start(out=lamb[:, :, :], in_=C.w["s5_lam_b"]), writes=[("lamb",)], dkey=("s5c", 2))
        P.op(SP, lambda e: e.dma_start(out=ldtb[:, :], in_=C.w["s5_ldt_b"]), writes=[("ldtb",)], dkey=("s5c", 3))
        P.op(SP, lambda e: e.dma_start(out=bb[:, :, :], in_=C.w["s5_b_b"]), writes=[("bb",)], dkey=("s5c", 4))
        lr = lamb[:, 0, :]
        li = lamb[:, 1, :]
        mag = tb("S_mag", [16, NB])
        fb_ = tb("S_fb", [16, NB])
        fc_ = tb("S_fc", [16, NB])
        fbi = tb("S_fbi", [16, NB], I32)
        are = tb("S_are", [16, NB])
        aim = tb("S_aim", [16, NB])
        den = tb("S_den", [16, NB])
        tmp = tb("S_tmp", [16, NB])
        zre = tb("S_zre", [16, NB])
        zim = tb("S_zim", [16, NB])
        act(ldtb[:, :], ldtb[:, :], AF.Exp, [("ldtb",)], [("ldtb",)])
        tt(DVE, mag[:, :], lr, ldtb[:, :], ALU.mult, [("lamb",), ("ldtb",)], [("mag",)])
        act(mag[:, :], mag[:, :], AF.Exp, [("mag",)], [("mag",)])
        tt(DVE, fb_[:, :], li, ldtb[:, :], ALU.mult, [("lamb",), ("ldtb",)], [("fbq", "x")])
        ts(DVE, fb_[:, :], fb_[:, :], 1.0 / TWO_PI, ALU.mult, rk=[("fbq", "x")], wk=[("fbq", "x")])
        frac_sincos(DVE, fb_[:, :], fbi[:, :], fc_[:, :], aim[:, :], are[:, :], "fbq", halfpi, 16)
        fb3 = fb_[:, :].rearrange("h (g p) -> h g p", g=32)
        P.op(POOL, lambda e: e.tensor_copy(out=fbd[:, :, 0:64], in_=fb3), reads=[("fbq", "x")], writes=[("fbd", 0)])
        P.op(POOL, lambda e: e.tensor_copy(out=fbd[:, :, 64:128], in_=fb3), reads=[("fbq", "x")], writes=[("fbd", 1)])
        tt(DVE, aim[:, :], aim[:, :], mag[:, :], ALU.mult, [("fbq", "sin"), ("mag",)], [("aim",)])
        tt(DVE, are[:, :], are[:, :], mag[:, :], ALU.mult, [("fbq", "cos"), ("mag",)], [("are",)])
        ts(DVE, are[:, :], are[:, :], -1.0, ALU.add, rk=[("are",)], wk=[("are",)])
        tt(DVE, den[:, :], lr, lr, ALU.mult, [("lamb",)], [("den",)])
        tt(DVE, tmp[:, :], li, li, ALU.mult, [("lamb",)], [("tmp",)])
        tt(DVE, den[:, :], den[:, :], tmp[:, :], ALU.add, [("den",), ("tmp",)], [("den",)])
        P.op(DVE, lambda e: e.reciprocal(out=den[:, :], in_=den[:, :]), reads=[("den",)], writes=[("den",)])
        tt(DVE, zre[:, :], are[:, :], lr, ALU.mult, [("are",), ("lamb",)], [("zre",)])
        tt(DVE, tmp[:, :], aim[:, :], li, ALU.mult, [("aim",), ("lamb",)], [("tmp",)])
        tt(DVE, zre[:, :], zre[:, :], tmp[:, :], ALU.add, [("zre",), ("tmp",)], [("zre",)])
        tt(DVE, zre[:, :], zre[:, :], den[:, :], ALU.mult, [("zre",), ("den",)], [("zre",)])
        tt(DVE, zim[:, :], aim[:, :], lr, ALU.mult, [("aim",), ("lamb",)], [("zim",)])
        tt(DVE, tmp[:, :], are[:, :], li, ALU.mult, [("are",), ("lamb",)], [("tmp",)])
        tt(DVE, zim[:, :], zim[:, :], tmp[:, :], ALU.subtract, [("zim",), ("tmp",)], [("zim",)])
        tt(DVE, zim[:, :], zim[:, :], den[:, :], ALU.mult, [("zim",), ("den",)], [("zim",)])
        bbre = mag
        bbim = den
        br = bb[:, 0, :]
        bi = bb[:, 1, :]
        tt(DVE, bbre[:, :], zre[:, :], br, ALU.mult, [("zre",), ("bb",), ("mag",)], [("mag",)])
        tt(DVE, tmp[:, :], zim[:, :], bi, ALU.mult, [("zim",), ("bb",)], [("tmp",)])
        tt(DVE, bbre[:, :], bbre[:, :], tmp[:, :], ALU.subtract, [("mag",), ("tmp",)], [("mag",)])
        tt(DVE, bbim[:, :], zre[:, :], bi, ALU.mult, [("zre",), ("bb",), ("den",)], [("den",)])
        tt(DVE, tmp[:, :], zim[:, :], br, ALU.mult, [("zim",), ("bb",)], [("tmp",)])
        tt(DVE, bbim[:, :], bbim[:, :], tmp[:, :], ALU.add, [("den",), ("tmp",)], [("den",)])
        bbre3 = bbre[:, :].rearrange("h (g p) -> h g p", g=32)
        bbim3 = bbim[:, :].rearrange("h (g p) -> h g p", g=32)
        P.op(ACT, lambda e: e.copy(out=B1f[:, :, 0:64], in_=bbre3), reads=[("mag",)], writes=[("B1", 0)])
        P.op(ACT, lambda e: e.copy(out=B1f[:, :, 64:128], in_=bbim3), reads=[("den",)], writes=[("B1", 1)])
        P.op(ACT, lambda e: e.copy(out=B2f[:, :, 0:64], in_=bbim3), reads=[("den",)], writes=[("B2", 0)])
        P.op(ACT, lambda e: e.activation(out=B2f[:, :, 64:128], in_=bbre3, func=AF.Copy, scale=-1.0), reads=[("mag",)], writes=[("B2", 1)])
        c1f = tb("S_c1f", [128, 32, 16])
        c2f = tb("S_c2f", [128, 32, 16])
        P.op(SP, lambda e: e.dma_start(out=c1f[:, :, :], in_=C.w["s5_c1"]), writes=[("c1f",)], dkey=("s5c", 5))
        P.op(SP, lambda e: e.dma_start(out=c2f[:, :, :], in_=C.w["s5_c2"]), writes=[("c2f",)], dkey=("s5c", 6))
        P.op(ACT, lambda e: e.copy(out=C1f[0:64, :, :], in_=c1f[0:64, :, :]), reads=[("c1f",)], writes=[("C1", 0)])
        P.op(ACT, lambda e: e.activation(out=C1f[64:128, :, :], in_=c1f[64:128, :, :], func=AF.Copy, scale=-1.0), reads=[("c1f",)], writes=[("C1", 1)])
        P.op(ACT, lambda e: e.activation(out=C2f[:, :, :], in_=c2f[:, :, :], func=AF.Copy, scale=-1.0), reads=[("c2f",)], writes=[("C2",)])
        dt_ = tb("S_dt", [16, 32])
        id16 = tb("S_id16", [16, 16])
        P.op(SP, lambda e: e.dma_start(out=dt_[:, :], in_=C.w["s5_d_t"]), writes=[("dt_",)], dkey=("s5c", 7))
        P.op(SP, lambda e: e.dma_start(out=id16[:, :], in_=C.cst["id16"]), writes=[("id16",)], dkey=("s5c", 8))
        P.op(DVE, lambda e: e.tensor_tensor(out=Dd[:, :, :], in0=id16[:, :].unsqueeze(1).to_broadcast([16, 32, 16]),
                                            in1=dt_[:, :].unsqueeze(2).to_broadcast([16, 32, 16]), op=ALU.mult),
             reads=[("dt_",), ("id16",)], writes=[("Dd",)])
        P.barrier()
        es_in.close()
        ug = [sb(f"S_ug{i}", [16, S], BF16) for i in range(2)]
        yg = [sb(f"S_yg{i}", [16, S], BF16) for i in range(2)]
        rfull = [sb(f"S_rfull{i}", [128, TT]) for i in range(2)]
        _lx = sb("S_lx", [128, TT])
        lx = [_lx, _lx]
        _lxi = sb("S_lxi", [128, TT], I32)
        lxi = [_lxi, _lxi]
        _lxf = sb("S_lxf", [128, TT])
        lxf = [_lxf, _lxf]
        sinL = [sb(f"S_sinL{i}", [128, TT]) for i in range(2)]
        cosL = [sb(f"S_cosL{i}", [128, TT]) for i in range(2)]
        _bx = sb("S_bx", [16, 8, 128])
        bx = [_bx, _bx]
        _bxi = sb("S_bxi", [16, 8, 128], I32)
        bxi = [_bxi, _bxi]
        _bxf = sb("S_bxf", [16, 8, 128])
        bxf = [_bxf, _bxf]
        _bcc = sb("S_bcc", [16, 8, 128])
        bcc = [_bcc, _bcc]
        _bss = sb("S_bss", [16, 8, 128])
        bss = [_bss, _bss]
        _bt1 = sb("S_bt1", [16, 8, 128])
        bt1 = [_bt1, _bt1]
        _bt2 = sb("S_bt2", [16, 8, 128])
        bt2 = [_bt2, _bt2]
        B1p = [sb(f"S_B1p{i}", [16, 8, 128], BF16) for i in range(2)]
        B2p = [sb(f"S_B2p{i}", [16, 8, 128], BF16) for i in range(2)]
        _ct1 = sb("S_ct1", [128, 8, 16])
        ct1 = [_ct1, _ct1]
        _ct2 = sb("S_ct2", [128, 8, 16])
        ct2 = [_ct2, _ct2]
        C1p = [sb(f"S_C1p{i}", [128, 8, 16], BF16) for i in range(2)]
        C2p = [sb(f"S_C2p{i}", [128, 8, 16], BF16) for i in range(2)]
        NX = 4
        NW = 3
        X1 = [sb(f"S_X1{i}", [128, TT]) for i in range(NX)]
        X2 = [sb(f"S_X2{i}", [128, TT]) for i in range(NX)]
        wb = [sb(f"S_w{i}", [128, TT]) for i in range(2)]
        cW = [sb(f"S_cW{i}", [128, TT], BF16) for i in range(NW)]
        sW = [sb(f"S_sW{i}", [128, TT], BF16) for i in range(NW)]

        def prep_group(g):
            gb = g % 2
            P.op(SP, lambda e: e.dma_start(out=ug[gb][:, :], in_=C.scr["uT"][g * 16:(g + 1) * 16, :]), writes=[("ug", gb)], dkey=("S5ld", gb))
            P.op(ACT, lambda e: e.activation(out=rfull[gb][:, :], in_=ones5[:, :], func=AF.Copy, scale=r_s[:, g:g + 1]), reads=[], writes=[("rfull", gb)])
            P.op(ACT, lambda e: e.activation(out=lx[gb][:, :], in_=tloc[:, :], func=AF.Copy, scale=f_s[:, g:g + 1]), reads=[], writes=[(("lx", 0), "x")])
            frac_sincos(POOL, lx[gb][:, :], lxi[gb][:, :], lxf[gb][:, :], sinL[gb][:, :], cosL[gb][:, :], ("lx", 0), halfpi, 128, outkey=("lxo", gb))
            P.op(POOL, lambda e: e.tensor_tensor(out=bx[gb][:, :, :], in0=t0b[:, :, :], in1=fbd[:, g, :].unsqueeze(1).to_broadcast([16, 8, 128]), op=ALU.mult),
                 reads=[], writes=[(("bx", 0), "x")])
            frac_sincos(POOL, bx[gb][:, :, :], bxi[gb][:, :, :], bxf[gb][:, :, :], bss[gb][:, :, :], bcc[gb][:, :, :], ("bx", 0), halfpi, 16)
            b1 = B1f[:, g, :].unsqueeze(1).to_broadcast([16, 8, 128])
            b2 = B2f[:, g, :].unsqueeze(1).to_broadcast([16, 8, 128])
            kc, ks = (("bx", 0), "cos"), (("bx", 0), "sin")
            P.op(DVE, lambda e: e.tensor_tensor(out=bt1[gb][:, :, :], in0=bcc[gb][:, :, :], in1=b1, op=ALU.mult), reads=[kc], writes=[("bt1", 0)])
            P.op(DVE, lambda e: e.tensor_tensor(out=bt2[gb][:, :, :], in0=bss[gb][:, :, :], in1=b2, op=ALU.mult), reads=[ks], writes=[("bt2", 0)])
            P.op(POOL, lambda e: e.tensor_tensor(out=B1p[gb][:, :, :], in0=bt1[gb][:, :, :], in1=bt2[gb][:, :, :], op=ALU.add), reads=[("bt1", 0), ("bt2", 0)], writes=[("B1p", gb)])
            P.op(DVE, lambda e: e.tensor_tensor(out=bt1[gb][:, :, :], in0=bcc[gb][:, :, :], in1=b2, op=ALU.mult), reads=[kc, ("bt1", 0)], writes=[("bt1", 0)])
            P.op(DVE, lambda e: e.tensor_tensor(out=bt2[gb][:, :, :], in0=bss[gb][:, :, :], in1=b1, op=ALU.mult), reads=[ks, ("bt2", 0)], writes=[("bt2", 0)])
            P.op(POOL, lambda e: e.tensor_tensor(out=B2p[gb][:, :, :], in0=bt1[gb][:, :, :], in1=bt2[gb][:, :, :], op=ALU.subtract), reads=[("bt1", 0), ("bt2", 0)], writes=[("B2p", gb)])
            c1 = C1f[:, g, :].unsqueeze(1).to_broadcast([128, 8, 16])
            c2 = C2f[:, g, :].unsqueeze(1).to_broadcast([128, 8, 16])
            cc_ = cs_tab[:, g, :].unsqueeze(2).to_broadcast([128, 8, 16])
            ss_ = sn_tab[:, g, :].unsqueeze(2).to_broadcast([128, 8, 16])
            P.op(DVE, lambda e: e.tensor_tensor(out=ct1[gb][:, :, :], in0=c1, in1=cc_, op=ALU.mult), reads=[], writes=[("ct1", 0)])
            P.op(DVE, lambda e: e.tensor_tensor(out=ct2[gb][:, :, :], in0=c2, in1=ss_, op=ALU.mult), reads=[], writes=[("ct2", 0)])
            P.op(POOL, lambda e: e.tensor_tensor(out=C1p[gb][:, :, :], in0=ct1[gb][:, :, :], in1=ct2[gb][:, :, :], op=ALU.add), reads=[("ct1", 0), ("ct2", 0)], writes=[("C1p", gb)])
            P.op(DVE, lambda e: e.tensor_tensor(out=ct1[gb][:, :, :], in0=c2, in1=cc_, op=ALU.mult), reads=[("ct1", 0)], writes=[("ct1", 0)])
            P.op(DVE, lambda e: e.tensor_tensor(out=ct2[gb][:, :, :], in0=c1, in1=ss_, op=ALU.mult), reads=[("ct2", 0)], writes=[("ct2", 0)])
            P.op(POOL, lambda e: e.tensor_tensor(out=C2p[gb][:, :, :], in0=ct1[gb][:, :, :], in1=ct2[gb][:, :, :], op=ALU.subtract), reads=[("ct1", 0), ("ct2", 0)], writes=[("C2p", gb)])

        steps = [(g, j) for g in range(32) for j in range(NT)]
        NS = len(steps)

        def stA1(i):
            g, j = steps[i]
            gb = g % 2
            tsl = slice(j * TT, (j + 1) * TT)
            pa, pak = next_ps(C, 0, 5)
            mm_chain(P, pa[:, :], [(B1p[gb][:, j, :], ug[gb][:, tsl])], reads=[("ug", gb), ("B1p", gb)], writes=[pak])
            pb, pbk = next_ps(C, 0, 5)
            mm_chain(P, pb[:, :], [(B2p[gb][:, j, :], ug[gb][:, tsl])], reads=[("ug", gb), ("B2p", gb)], writes=[pbk])
            psA[i] = (pa, pak, pb, pbk)

        def stA2(i):
            g, j = steps[i]
            gb = g % 2
            x = i % NX
            pa, pak, pb, pbk = psA.pop(i)
            P.op(DVE, lambda e: e.tensor_tensor(out=X1[x][:, :], in0=pa[:, :], in1=cosL[gb][:, :], op=ALU.mult), reads=[pak, (("lxo", gb), "cos")], writes=[("X1", x)])
            P.op(DVE, lambda e: e.tensor_tensor(out=X2[x][:, :], in0=pb[:, :], in1=sinL[gb][:, :], op=ALU.mult), reads=[pbk, (("lxo", gb), "sin")], writes=[("X2", x)])

        def stB(i):
            x = i % NX
            P.op(POOL, lambda e: e.tensor_tensor(out=X1[x][:, :], in0=X1[x][:, :], in1=X2[x][:, :], op=ALU.add), reads=[("X1", x), ("X2", x)], writes=[("X1", x)])

        def stC1(i):
            g, j = steps[i]
            gb = g % 2
            x = i % NX
            u = i % 2
            v = i % NW
            init = 0.0 if j == 0 else wb[1 - u][:, TT - 1:TT]
            P.op(DVE, lambda e: e.tensor_tensor_scan(out=wb[u][:, :], data0=rfull[gb][:, :], data1=X1[x][:, :], initial=init, op0=ALU.mult, op1=ALU.add),
                 reads=[("X1", x), ("rfull", gb), ("w", 1 - u)], writes=[("w", u)])
            P.op(DVE, lambda e: e.tensor_tensor(out=cW[v][:, :], in0=wb[u][:, :], in1=cosL[gb][:, :], op=ALU.mult), reads=[("w", u), (("lxo", gb), "cos")], writes=[("cW", v)])
            P.op(POOL, lambda e: e.tensor_tensor(out=sW[v][:, :], in0=wb[u][:, :], in1=sinL[gb][:, :], op=ALU.mult), reads=[("w", u), (("lxo", gb), "sin")], writes=[("sW", v)])

        def stC2(i):
            g, j = steps[i]
            gb = g % 2
            v = i % NW
            tsl = slice(j * TT, (j + 1) * TT)
            py, pyk = next_ps(C, 5, 6)
            mm_chain(P, py[0:16, :], [(C1p[gb][:, j, :], cW[v][:, :]), (C2p[gb][:, j, :], sW[v][:, :]), (Dd[:, g, :], ug[gb][:, tsl])],
                     reads=[("cW", v), ("sW", v), ("ug", gb), ("C1p", gb), ("C2p", gb)], writes=[pyk])
            P.op(ACT, lambda e: e.activation(out=yg[gb][:, tsl], in_=py[0:16, :], func=AF.Gelu), reads=[pyk], writes=[("yg", gb)])
            if j == NT - 1:
                P.op(POOL, lambda e: e.dma_start(out=C.scr["ygT"][g * 16:(g + 1) * 16, :], in_=yg[gb][:, :]), reads=[("yg", gb)], dkey=("S5st", gb))

        psA = {}
        prep_group(0)
        LC2 = 5
        for i in range(NS + LC2):
            if i < NS and steps[i][1] == LC2 and steps[i][0] + 1 < 32:
                prep_group(steps[i][0] + 1)
            if i < NS:
                stA1(i)
            if 0 <= i - 1 < NS:
                stA2(i - 1)
            if 0 <= i - 2 < NS:
                stB(i - 2)
            if 0 <= i - 3 < NS:
                stC1(i - 3)
            if 0 <= i - LC2 < NS:
                stC2(i - LC2)
        P.barrier()


def stage_l0_out(P, C):
    nc = P.nc
    xres = C.xres
    with ExitStack() as es:
        sb = lambda name, shape, dt=F32: es.enter_context(nc.sbuf_tensor(name, list(shape), dt))
        wglu = sb("O_wglu", [128, 4, 512], BF16)
        wout = sb("O_wout", [128, 8, 1024], BF16)
        bglu = sb("O_bglu", [128, 4])
        wload(P, C, wglu, C.w["l0_s5_w_glu"], 4, 512, gain=None, name="O_wglu")
        wload(P, C, wout, C.w["l0_w_out"], 8, 1024, gain=None, name="O_wout")
        P.op(SP, lambda e: e.dma_start(out=bglu[:, :], in_=C.w["l0_s5_b_glu"]), writes=[("bglu",)], dkey=("bglu",))
        P.barrier()
        xt = [sb(f"O_xt{i}", [128, 8, TT]) for i in range(2)]
        mg = [sb(f"O_mg{i}", [128, 8, TT], BF16) for i in range(2)]
        ygt = [sb(f"O_yg{i}", [128, 4, TT], BF16) for i in range(2)]
        sg = [sb(f"O_sg{i}", [128, TT]) for i in range(2)]
        xv = xres.rearrange("(c p) t -> p c t", p=128)
        mv = C.scr["mT"].rearrange("(c p) t -> p c t", p=128)
        yv = C.scr["ygT"].rearrange("(c p) t -> p c t", p=128)
        it = 0
        for j in range(NT):
            u = j % 2
            tsl = slice(j * TT, (j + 1) * TT)
            P.op(SP, lambda e, u=u, tsl=tsl: e.dma_start(out=xt[u][:, :, :], in_=xv[:, :, tsl]), writes=[("xt", u)], dkey=("Old", "x", u))
            P.op(SP, lambda e, u=u, tsl=tsl: e.dma_start(out=mg[u][:, 0:4, :], in_=mv[:, 0:4, tsl]), writes=[("mg", u, "r")], dkey=("Old", "m", u))
            P.op(SP, lambda e, u=u, tsl=tsl: e.dma_start(out=ygt[u][:, :, :], in_=yv[:, :, tsl]), writes=[("ygt", u)], dkey=("Old", "y", u))
            for fb in range(4):
                pa, pak = next_ps(C)
                mm_chain(P, pa[:, :], [(wglu[:, c, fb * 128:(fb + 1) * 128], ygt[u][:, c, :]) for c in range(4)], reads=[*wkeys(C, "O_wglu", fb * 128), ("ygt", u)], writes=[pak])
                v = it % 2
                it += 1
                P.op(ACT, lambda e, pa=pa, v=v, fb=fb: e.activation(out=sg[v][:, :], in_=pa[:, :], func=AF.Sigmoid, bias=bglu[:, fb:fb + 1]),
                     reads=[pak, ("bglu",)], writes=[("sg", v)])
                P.op(DVE, lambda e, v=v, u=u, fb=fb: e.tensor_tensor(out=mg[u][:, 4 + fb, :], in0=sg[v][:, :], in1=ygt[u][:, fb, :], op=ALU.mult),
                     reads=[("sg", v), ("ygt", u)], writes=[("mg", u, fb)])
            for fb in range(8):
                pa, pak = next_ps(C)
                mm_chain(P, pa[:, :], [(wout[:, c, fb * 128:(fb + 1) * 128], mg[u][:, c, :]) for c in range(8)],
                         reads=wkeys(C, "O_wout", fb * 128) + [("mg", u, "r")] + [("mg", u, f) for f in range(4)], writes=[pak])
                P.op(DVE, lambda e, pa=pa, u=u, fb=fb: e.tensor_tensor(out=xt[u][:, fb, :], in0=pa[:, :], in1=xt[u][:, fb, :], op=ALU.add),
                     reads=[pak, ("xt", u)], writes=[("xt", u)])
            P.op(POOL, lambda e, u=u, tsl=tsl: e.dma_start(out=xv[:, :, tsl], in_=xt[u][:, :, :]), reads=[("xt", u)], dkey=("Ost", u))
        P.barrier()


def stage_l1_inproj(P, C):
    nc = P.nc
    xres = C.xres
    with ExitStack() as es:
        sb = lambda name, shape, dt=F32: es.enter_context(nc.sbuf_tensor(name, list(shape), dt))
        W = sb("E_W", [128, 8, 4112], BF16)
        g_mix = C.gains["l1_mix_norm"]
        wload(P, C, W, C.w["l1_w_in"], 8, 4112, name="E_W")
        cw = sb("E_cw", [128, 4, 24])
        P.op(SP, lambda e: e.dma_start(out=cw[:, :, :], in_=C.w["l1_conv"]), writes=[("cw",)], dkey=("Ecw",))
        hp = sb("E_hp", [8, 4])
        P.op(SP, lambda e: e.dma_start(out=hp[:, 0:2], in_=C.w["l1_hp"]), writes=[("hp",)], dkey=("Ehp",))
        cmask = sb("E_cmask", [8, TT])
        P.op(SP, lambda e: e.dma_start(out=cmask[:, :], in_=C.cst["cmask"]), writes=[("cmask",)], dkey=("Ecm",))
        P.barrier()
        P.op(ACT, lambda e: e.activation(out=hp[:, 2:3], in_=hp[:, 0:1], func=AF.Exp), reads=[("hp",)], writes=[("hp2",)])
        P.op(DVE, lambda e: e.tensor_scalar(out=hp[:, 2:3], in0=hp[:, 2:3], scalar1=-1.0, scalar2=None, op0=ALU.mult), reads=[("hp2",)], writes=[("hp2",)])
        P.barrier()
        xt = sb("E_xt", [128, 8, TT])
        hn = sb("E_hn", [128, 8, TT], BF16)
        sq = sb("E_sq", [128, 8, TT], BF16)
        rstd = sb("E_rstd", [128, TT])
        acc = [sb(f"E_acc{i}", [128, TT]) for i in range(4)]
        accq = sb("E_accq", [128, 16, TT])
        rn16 = sb("E_rn16", [16, TT])
        oh16 = sb("E_oh16", [128, 16, 16], BF16)
        sel16 = sb("E_sel16", [16, 16, 128])
        l2c = sb("E_l2c", [16, 2])
        pss16 = C.psum[5]
        P.op(SP, lambda e: e.dma_start(out=oh16[:, :, :], in_=C.cst_oh16), writes=[("oh16",)], dkey=("Eoh",))
        P.op(SP, lambda e: e.dma_start(out=sel16[:, :, :], in_=C.cst["sel16"]), writes=[("sel16",)], dkey=("Esel",))
        P.op(SP, lambda e: e.dma_start(out=l2c[:, :], in_=C.cst["l2c"]), writes=[("l2c",)], dkey=("El2c",))
        sqb = [sb(f"E_sqb{i}", [128, TT], BF16) for i in range(5)]
        pend_ss = []
        pend_act = []
        halo = sb("E_halo", [128, 24, 3])
        corr = sb("E_corr", [128, 24, 3])
        ctmp = sb("E_ctmp", [128, 24])
        outs = {nm: sb("E_o" + nm, [128, 8, TT], BF16) for nm in ["gq", "gk", "gv", "gz"]}
        gsb = sb("E_g", [8, TT])
        gcs = [sb(f"E_gc{i}", [8, TT]) for i in range(2)]
        bts = [sb(f"E_bt{i}", [8, TT]) for i in range(2)]
        it = 0
        for j in range(NT):
            tsl = slice(j * TT, (j + 1) * TT)
            norm_tile(P, C, xres, j, xt, ("xt",), hn, ("hn",), sq, rstd, pshi=5, gain=g_mix)
            if j > 0:
                hk = [("halo", b) for b in range(24)]
                terms = [(0, 0, 0), (0, 1, 1), (0, 2, 2), (1, 0, 1), (1, 1, 2), (2, 0, 2)]
                first = {}
                for (t_, k_, h_i) in terms:
                    if t_ not in first:
                        first[t_] = True
                        P.op(POOL, lambda e, t_=t_, k_=k_, h_i=h_i: e.tensor_tensor(out=corr[:, :, t_], in0=halo[:, :, h_i], in1=cw[:, k_, 0:24], op=ALU.mult),
                             reads=hk + [("corr",)], writes=[("corr",)])
                    else:
                        P.op(POOL, lambda e, k_=k_, h_i=h_i: e.tensor_tensor(out=ctmp[:, :], in0=halo[:, :, h_i], in1=cw[:, k_, 0:24], op=ALU.mult),
                             reads=hk + [("ctmp",)], writes=[("ctmp",)])
                        P.op(POOL, lambda e, t_=t_: e.tensor_tensor(out=corr[:, :, t_], in0=corr[:, :, t_], in1=ctmp[:, :], op=ALU.add),
                             reads=[("corr",), ("ctmp",)], writes=[("corr",)])
            for sec in range(3):
                nm = ["gq", "gk", "gv"][sec]
                for hh in range(8):
                    blk = sec * 8 + hh
                    col = blk * 128
                    u = it % 5
                    a3 = it % 4
                    it += 1
                    pst, psk = next_ps(C, 0, 5)
                    mm_chain(P, pst[:, :], [(W[:, c, col:col + 128], hn[:, c, :]) for c in range(8)], reads=[*wkeys(C, "E_W", col), ("hn", "a"), ("hn", "b")], writes=[psk])
                    if sec == 2:
                        a_ = acc[a3]
                        akey = ("acc", a3)
                    else:
                        a_ = accq[:, sec * 8 + hh, :]
                        akey = ("accq", sec * 8 + hh)
                    P.op(ACT, lambda e, a_=a_, pst=pst, blk=blk: e.activation(out=a_[:, :], in_=pst[:, :], func=AF.Copy, scale=cw[:, 3, blk:blk + 1]),
                         reads=[psk], writes=[akey])
                    if j < NT - 1:
                        P.op(ACT, lambda e, pst=pst, blk=blk: e.copy(out=halo[:, blk, :], in_=pst[:, TT - 3:TT]), reads=[psk], writes=[("halo", blk)])
                    for k in range(3):
                        d_ = 3 - k
                        P.op(DVE, lambda e, a_=a_, pst=pst, blk=blk, k=k, d_=d_: e.scalar_tensor_tensor(out=a_[:, d_:TT], in0=pst[:, 0:TT - d_], scalar=cw[:, k, blk:blk + 1], in1=a_[:, d_:TT],
                                                                                                  op0=ALU.mult, op1=ALU.add),
                             reads=[psk, akey], writes=[akey])
                    if j > 0:
                        P.op(POOL, lambda e, a_=a_, blk=blk: e.tensor_tensor(out=a_[:, 0:3], in0=a_[:, 0:3], in1=corr[:, blk, :], op=ALU.add),
                             reads=[("corr",), akey], writes=[akey])
                    pend_act.append((sec, hh, a_, akey, u, nm))
                    while len(pend_act) > (1 if blk < 23 else 0):
                        sec_, hh_, a2_, akey_, u2_, nm_ = pend_act.pop(0)
                        if sec_ == 2:
                            P.op(ACT, lambda e, a2_=a2_, hh_=hh_, nm_=nm_: e.activation(out=outs[nm_][:, hh_, :], in_=a2_[:, :], func=AF.Silu), reads=[akey_], writes=[(nm_, hh_)])
                        else:
                            P.op(ACT, lambda e, a2_=a2_: e.activation(out=a2_[:, :], in_=a2_[:, :], func=AF.Silu), reads=[akey_], writes=[akey_])
                            P.op(ACT, lambda e, a2_=a2_, u2_=u2_: e.activation(out=sqb[u2_][:, :], in_=a2_[:, :], func=AF.Square), reads=[akey_], writes=[("sqb", u2_)])
                            pend_ss.append((sec_ * 8 + hh_, u2_))
                    while len(pend_ss) > (3 if blk < 23 else 0):
                        qi_, u_ = pend_ss.pop(0)
                        P.op(PE, lambda e, qi_=qi_, u_=u_: e.matmul(pss16[0:16, :], lhsT=oh16[:, qi_, :], rhs=sqb[u_][:, :], start=(qi_ == 0), stop=(qi_ == 15)),
                             reads=[("sqb", u_)], writes=[("pss16",)])
            P.op(ACT, lambda e: e.activation(out=rn16[:, :], in_=pss16[0:16, :], func=AF.Ln, scale=l2c[:, 0:1], bias=l2c[:, 1:2]), reads=[("pss16",)], writes=[("rn16",)])
            P.op(ACT, lambda e: e.activation(out=rn16[:, :], in_=rn16[:, :], func=AF.Exp, scale=-0.5), reads=[("rn16",)], writes=[("rn16",)])
            for qi in range(16):
                sec, hh = qi // 8, qi % 8
                nm = ["gq", "gk"][sec]
                pbc, pbck = next_ps(C, 0, 5)
                mm_chain(P, pbc[:, :], [(sel16[:, qi, :], rn16[:, :])], reads=[("rn16",)], writes=[pbck])
                P.op(DVE, lambda e, pbc=pbc, qi=qi, hh=hh, nm=nm: e.tensor_tensor(out=outs[nm][:, hh, :], in0=accq[:, qi, :], in1=pbc[:, :], op=ALU.mult),
                     reads=[pbck, ("accq", qi)], writes=[(nm, hh)])
            for hh in range(8):
                col = 3072 + hh * 128
                pst, psk = next_ps(C, 0, 5)
                mm_chain(P, pst[:, :], [(W[:, c, col:col + 128], hn[:, c, :]) for c in range(8)], reads=[*wkeys(C, "E_W", col), ("hn", "a"), ("hn", "b")], writes=[psk])
                P.op(ACT, lambda e, pst=pst, hh=hh: e.activation(out=outs["gz"][:, hh, :], in_=pst[:, :], func=AF.Silu), reads=[psk], writes=[("gz", hh)])
            pb_, pbk = next_ps(C, 0, 5)
            mm_chain(P, pb_[0:8, :], [(W[:, c, 4096:4104], hn[:, c, :]) for c in range(8)], reads=[*wkeys(C, "E_W", 4096, 8), ("hn", "a"), ("hn", "b")], writes=[pbk])
            jb = j % 2
            P.op(ACT, lambda e, pb_=pb_, jb=jb: e.activation(out=bts[jb][:, :], in_=pb_[0:8, :], func=AF.Sigmoid), reads=[pbk], writes=[("bts", jb)])
            pa_, pak = next_ps(C, 0, 5)
            mm_chain(P, pa_[0:8, :], [(W[:, c, 4104:4112], hn[:, c, :]) for c in range(8)], reads=[*wkeys(C, "E_W", 4104, 8), ("hn", "a"), ("hn", "b")], writes=[pak])
            P.op(ACT, lambda e, pa_=pa_: e.activation(out=gsb[:, :], in_=pa_[0:8, :], func=AF.Exp, bias=hp[:, 1:2]), reads=[pak, ("hp",)], writes=[("gsb",)])
            P.op(ACT, lambda e: e.activation(out=gsb[:, :], in_=gsb[:, :], func=AF.Ln, bias=C.one_col[0:8, 0:1]), reads=[("gsb",)], writes=[("gsb",)])
            P.op(DVE, lambda e: e.tensor_scalar(out=gsb[:, :], in0=gsb[:, :], scalar1=hp[:, 2:3], scalar2=None, op0=ALU.mult), reads=[("gsb",), ("hp2",)], writes=[("gsb",)])
            P.op(DVE, lambda e, jb=jb: e.tensor_tensor_scan(out=gcs[jb][:, :], data0=cmask[:, :], data1=gsb[:, :], initial=0.0, op0=ALU.mult, op1=ALU.add),
                 reads=[("gsb",)], writes=[("gcs", jb)])
            P.op(POOL, lambda e, jb=jb, tsl=tsl: e.dma_start(out=C.scr32["gcT"][:, tsl], in_=gcs[jb][:, :]), reads=[("gcs", jb)], dkey=("Est", "gc", jb))
            P.op(POOL, lambda e, jb=jb, tsl=tsl: e.dma_start(out=C.scr32["btT"][:, tsl], in_=bts[jb][:, :]), reads=[("bts", jb)], dkey=("Est", "bt", jb))
            for nm in ["gq", "gk", "gv", "gz"]:
                dv = C.scr[nm].rearrange("(h p) t -> p h t", p=128)
                P.op(POOL, lambda e, dv=dv, nm=nm, tsl=tsl: e.dma_start(out=dv[:, :, tsl], in_=outs[nm][:, :, :]), reads=[(nm, hh) for hh in range(8)], dkey=("Est", nm))
        P.barrier()


def stage_gdn(P, C):
    nc = P.nc
    NCH = 32
    with ExitStack() as es:
        sb = lambda name, shape, dt=F32: es.enter_context(nc.sbuf_tensor(name, list(shape), dt))
        masks = sb("G_masks", [128, 18, 128])
        P.op(SP, lambda e: e.dma_start(out=masks[:, :, :], in_=C.cst["gmasks"]), writes=[("masks",)], dkey=("Gm",))
        sel = sb("G_sel", [8, 8, 128])
        P.op(SP, lambda e: e.dma_start(out=sel[:, :, :], in_=C.cst["gsel"]), writes=[("sel",)], dkey=("Gs",))
        sel_last = sb("G_sellast", [128, 128])
        P.op(SP, lambda e: e.dma_start(out=sel_last[:, :], in_=C.cst["gsellast"]), writes=[("sellast",)], dkey=("Gsl",))
        identf = masks[:, 15, :]
        onorm = sb("G_onorm", [128, 1])
        P.op(SP, lambda e: e.dma_start(out=onorm[:, :], in_=C.w["l1_o_norm"]), writes=[("onorm",)], dkey=("Gon",))
        gcT = sb("G_gcT", [8, S])
        btT = sb("G_btT", [8, S])
        P.op(SP, lambda e: e.dma_start(out=gcT[:, :], in_=C.scr32["gcT"][:, :]), writes=[("gcT",)], dkey=("Ggc",))
        P.op(SP, lambda e: e.dma_start(out=btT[:, :], in_=C.scr32["btT"][:, :]), writes=[("btT",)], dkey=("Gbt",))
        gct = sb("G_gct", [128, NCH, 8])
        btt = sb("G_btt", [128, NCH, 8])
        glt = sb("G_glt", [128, NCH, 8])
        kbs = sb("G_kbs", [128, NCH, 8])
        kds = sb("G_kds", [128, NCH, 8])
        egl = sb("G_egl", [128, NCH, 8])
        P.barrier()
        for n in range(NCH):
            cs = slice(n * 128, (n + 1) * 128)
            pt, ptk = next_ps(C)
            P.op(PE, lambda e, pt=pt, cs=cs: e.transpose(pt[:, 0:8], gcT[:, cs], identf[0:8, 0:8]), reads=[("gcT",)], writes=[ptk])
            P.op(ACT, lambda e, pt=pt, n=n: e.copy(out=gct[:, n, :], in_=pt[:, 0:8]), reads=[ptk], writes=[("gct", n)])
            pt2, pt2k = next_ps(C)
            P.op(PE, lambda e, pt2=pt2, cs=cs: e.transpose(pt2[:, 0:8], btT[:, cs], identf[0:8, 0:8]), reads=[("btT",)], writes=[pt2k])
            P.op(DVE, lambda e, pt2=pt2, n=n: e.tensor_copy(out=btt[:, n, :], in_=pt2[:, 0:8]), reads=[pt2k], writes=[("btt", n)])
        P.barrier()
        gflat = lambda t: t[:, :, :].rearrange("p n h -> p (n h)")
        pg, pgk = next_ps(C)
        mm_chain(P, pg[:, 0:256], [(sel_last[:, :], gflat(gct))], reads=[], writes=[pgk])
        P.op(ACT, lambda e: e.copy(out=gflat(glt), in_=pg[:, 0:256]), reads=[pgk], writes=[("glt",)])
        P.op(ACT, lambda e: e.activation(out=gflat(egl), in_=pg[:, 0:256], func=AF.Exp), reads=[pgk], writes=[("egl",)])
        P.op(ACT, lambda e: e.activation(out=gflat(kbs), in_=gflat(gct), func=AF.Exp), reads=[], writes=[("kbs",)])
        P.op(DVE, lambda e: e.tensor_tensor(out=gflat(kbs), in0=gflat(kbs), in1=gflat(btt), op=ALU.mult), reads=[("kbs",)], writes=[("kbs",)])
        P.op(DVE, lambda e: e.tensor_tensor(out=gflat(kds), in0=gflat(glt), in1=gflat(gct), op=ALU.subtract), reads=[("glt",)], writes=[("kds",)])
        P.op(ACT, lambda e: e.activation(out=gflat(kds), in_=gflat(kds), func=AF.Exp), reads=[("kds",)], writes=[("kds",)])
        P.barrier()
        KT = [sb(f"G_KT{i}", [128, S], BF16) for i in range(2)]
        QT = [sb(f"G_QT{i}", [128, S], BF16) for i in range(2)]
        VT = [sb(f"G_VT{i}", [128, S], BF16) for i in range(2)]
        ZT = [sb(f"G_ZT{i}", [128, S], BF16) for i in range(2)]
        gcb = [sb(f"G_gcb{i}", [128, TT]) for i in range(2)]
        gcbA = [sb(f"G_gcbA{i}", [128, TT]) for i in range(2)]
        gcbB = [sb(f"G_gcbB{i}", [128, TT]) for i in range(2)]
        egcb = [sb(f"G_egcb{i}", [128, TT]) for i in range(2)]
        QdT = [sb(f"G_QdT{i}", [128, TT], BF16) for i in range(2)]
        NB_ = 2
        T4 = [128, 4, 128]
        xg = [sb(f"G_xg{i}", T4) for i in range(NB_)]
        tA = [sb(f"G_tA{i}", T4) for i in range(NB_)]
        tB = [sb(f"G_tB{i}", T4) for i in range(NB_)]
        a1 = [sb(f"G_a1{i}", T4) for i in range(NB_)]
        q1 = [sb(f"G_q1{i}", T4) for i in range(NB_)]
        tmpf = [sb(f"G_tmpf{i}", T4) for i in range(NB_)]
        A_ = [sb(f"G_A{i}", T4, BF16) for i in range(NB_)]
        AT_ = [sb(f"G_AT{i}", T4, BF16) for i in range(NB_)]
        qkT = [sb(f"G_qkT{i}", T4, BF16) for i in range(NB_)]
        Dd = [[sb(f"G_D{i}_{k}", T4, BF16) for k in range(2)] for i in range(NB_)]
        DTd = [[sb(f"G_DT{i}_{k}", T4, BF16) for k in range(2)] for i in range(NB_)]
        Xm = [sb(f"G_Xm{i}", T4, BF16) for i in range(NB_)]
        XTm = [sb(f"G_XTm{i}", T4, BF16) for i in range(NB_)]
        kbd = [sb(f"G_kbd{i}", T4, BF16) for i in range(NB_)]
        kdec = [sb(f"G_kdec{i}", T4, BF16) for i in range(NB_)]
        vb = [sb(f"G_vb{i}", T4, BF16) for i in range(NB_)]
        nwT = [sb(f"G_nwT{i}", T4, BF16) for i in range(NB_)]
        vnew = [sb(f"G_vnew{i}", [128, 128], BF16) for i in range(2)]
        Sst = sb("G_S", [128, 128])
        Sbf = sb("G_Sbf", [128, 128], BF16)
        o_sb = [sb(f"G_osb{i}", [128, TT]) for i in range(2)]
        osq = [sb(f"G_osq{i}", [128, TT], BF16) for i in range(2)]
        rr = [sb(f"G_rr{i}", [128, TT]) for i in range(2)]
        mo = [sb(f"G_mo{i}", [128, TT], BF16) for i in range(2)]
        ident_bf = C.ident_bf
        v4 = lambda t: t[:, :].rearrange("p (c k) -> p c k", c=4)
        mb = lambda k: masks[:, k, :].unsqueeze(1).to_broadcast(T4)

        def load_head(h):
            hb = h % 2
            for nm, tl in [("gk", KT), ("gq", QT), ("gv", VT), ("gz", ZT)]:
                P.op(SP, lambda e, nm=nm, tl=tl, h=h, hb=hb: e.dma_start(out=tl[hb][:, :], in_=C.scr[nm][h * 128:(h + 1) * 128, :]),
                     writes=[(nm, hb)], dkey=("Gld", nm, hb))

        def prep_half(h, jt, u, c0, ncn):
            hb = h % 2
            w = u
            n0 = 4 * jt
            hf = c0 // ncn
            tsl = slice(jt * TT, (jt + 1) * TT)
            cs = [slice((n0 + c0 + c) * 128, (n0 + c0 + c + 1) * 128) for c in range(ncn)]
            TH = [128, ncn, 128]
            hs = slice(c0, c0 + ncn)
            vh = lambda t: t[:, 0:ncn * 128].rearrange("p (c k) -> p c k", c=ncn)
            mh = lambda k: masks[:, k, :].unsqueeze(1).to_broadcast(TH)
            K_ = lambda nm: (nm, u, hf)
            if c0 == 0:
                pg, pgk = next_ps(C, 2, 6)
                mm_chain(P, pg[:, :], [(sel[:, h, :], gcT[:, tsl])], reads=[], writes=[pgk])
                P.op(ACT, lambda e: e.copy(out=gcb[w][:, :], in_=pg[:, :]), reads=[pgk], writes=[("gcb", w)])
                m4 = lambda k: masks[:, k, :].unsqueeze(1).to_broadcast([128, 4, 128])
                g4 = lambda t: t[:, :].rearrange("p (c k) -> p c k", c=4)
                P.op(DVE, lambda e: e.tensor_tensor(out=g4(gcbA[w]), in0=g4(gcb[w]), in1=m4(16), op=ALU.add), reads=[("gcb", w)], writes=[("gcbA", w)])
                P.op(POOL, lambda e: e.tensor_tensor(out=g4(gcbB[w]), in0=g4(gcb[w]), in1=m4(17), op=ALU.add), reads=[("gcb", w)], writes=[("gcbB", w)])
                P.op(ACT, lambda e: e.activation(out=egcb[w][:, :], in_=pg[:, :], func=AF.Exp), reads=[pgk], writes=[("egcb", w)])
                P.op(POOL, lambda e: e.tensor_tensor(out=QdT[w][:, :], in0=QT[hb][:, tsl], in1=egcb[w][:, :], op=ALU.mult),
                     reads=[("egcb", w), ("gq", hb)], writes=[("QdT", w)])
            gci = gct[:, n0 + c0:n0 + c0 + ncn, h:h + 1].to_broadcast(TH)
            bti = btt[:, n0 + c0:n0 + c0 + ncn, h:h + 1].to_broadcast(TH)
            kbi = kbs[:, n0 + c0:n0 + c0 + ncn, h:h + 1].to_broadcast(TH)
            kdi = kds[:, n0 + c0:n0 + c0 + ncn, h:h + 1].to_broadcast(TH)
            pkk, pkkk = next_ps(C, 2, 6)

            def f_kk(e):
                ins = None
                for c in range(ncn):
                    ins = e.matmul(pkk[:, c * 128:(c + 1) * 128], lhsT=KT[hb][:, cs[c]], rhs=KT[hb][:, cs[c]], start=True, stop=True)
                return ins
            P.op(PE, f_kk, reads=[("gk", hb)], writes=[pkkk])
            pqk, pqkk = next_ps(C, 2, 6)

            def f_qk(e):
                ins = None
                for c in range(ncn):
                    ins = e.matmul(pqk[:, c * 128:(c + 1) * 128], lhsT=KT[hb][:, cs[c]], rhs=QT[hb][:, cs[c]], start=True, stop=True)
                return ins
            P.op(PE, f_qk, reads=[("gk", hb), ("gq", hb)], writes=[pqkk])
            gcbAv = gcbA[w][:, :].rearrange("p (c k) -> p c k", c=4)[:, hs, :]
            gcbBv = gcbB[w][:, :].rearrange("p (c k) -> p c k", c=4)[:, hs, :]
            P.op(DVE, lambda e: e.tensor_tensor(out=xg[u][:, hs, :], in0=gcbAv, in1=gci, op=ALU.subtract), reads=[("gcbA", w)], writes=[K_("xg")])
            P.op(DVE, lambda e: e.tensor_tensor(out=tmpf[u][:, hs, :], in0=gcbBv, in1=gci, op=ALU.subtract), reads=[("gcbB", w)], writes=[K_("tmpf")])
            yield None
            P.op(ACT, lambda e: e.activation(out=tA[u][:, hs, :], in_=xg[u][:, hs, :], func=AF.Exp, scale=-1.0), reads=[K_("xg")], writes=[K_("tA")])
            P.op(ACT, lambda e: e.activation(out=tB[u][:, hs, :], in_=tmpf[u][:, hs, :], func=AF.Exp), reads=[K_("tmpf")], writes=[K_("tB")])
            yield None
            P.op(DVE, lambda e: e.tensor_tensor(out=a1[u][:, hs, :], in0=tA[u][:, hs, :], in1=vh(pkk), op=ALU.mult), reads=[pkkk, K_("tA")], writes=[K_("a1")])
            P.op(DVE, lambda e: e.tensor_tensor(out=qkT[u][:, hs, :], in0=tB[u][:, hs, :], in1=vh(pqk), op=ALU.mult), reads=[pqkk, K_("tB")], writes=[K_("qkT")])
            yield None
            for c in range(ncn):
                P.op(ACT, lambda e, c=c: e.activation(out=A_[u][:, c0 + c, :], in_=a1[u][:, c0 + c, :], func=AF.Copy, scale=btt[:, n0 + c0 + c, h:h + 1]),
                     reads=[K_("a1")], writes=[K_("A")])
            yield None
            pb1, pb1k = next_psb(C)

            def f_at(e):
                ins = None
                for c in range(ncn):
                    ins = e.transpose(pb1[:, c * 128:(c + 1) * 128], A_[u][:, c0 + c, :], ident_bf[:, :])
                return ins
            P.op(PE, f_at, reads=[K_("A")], writes=[pb1k])
            pb2, pb2k = next_psb(C)

            def f_kvt(e):
                ins = None
                for c in range(ncn):
                    e.transpose(pb2[:, c * 128:(c + 1) * 128], KT[hb][:, cs[c]], ident_bf[:, :])
                    ins = e.transpose(pb2[:, (ncn + c) * 128:(ncn + c + 1) * 128], VT[hb][:, cs[c]], ident_bf[:, :])
                return ins
            P.op(PE, f_kvt, reads=[("gk", hb), ("gv", hb)], writes=[pb2k])
            P.op(ACT, lambda e: e.copy(out=AT_[u][:, hs, :], in_=vh(pb1)), reads=[pb1k], writes=[K_("AT")])
            ktr = pb2[:, 0:ncn * 128].rearrange("p (c k) -> p c k", c=ncn)
            vtr = pb2[:, ncn * 128:2 * ncn * 128].rearrange("p (c k) -> p c k", c=ncn)
            P.op(DVE, lambda e: e.tensor_tensor(out=kbd[u][:, hs, :], in0=ktr, in1=kbi, op=ALU.mult), reads=[pb2k], writes=[K_("kbd")])
            P.op(DVE, lambda e: e.tensor_tensor(out=kdec[u][:, hs, :], in0=ktr, in1=kdi, op=ALU.mult), reads=[pb2k], writes=[K_("kdec")])
            P.op(DVE, lambda e: e.tensor_tensor(out=vb[u][:, hs, :], in0=vtr, in1=bti, op=ALU.mult), reads=[pb2k], writes=[K_("vb")])
            P.op(POOL, lambda e: e.tensor_tensor(out=tmpf[u][:, hs, :], in0=A_[u][:, hs, :], in1=mh(2), op=ALU.mult), reads=[K_("A")], writes=[K_("tmpf")])
            yield None
            P.op(POOL, lambda e: e.tensor_tensor(out=Dd[u][0][:, hs, :], in0=tmpf[u][:, hs, :], in1=mh(15), op=ALU.add), reads=[K_("tmpf")], writes=[("D", u, 0, hf)])
            P.op(DVE, lambda e: e.tensor_tensor(out=q1[u][:, hs, :], in0=AT_[u][:, hs, :], in1=mh(8), op=ALU.mult), reads=[K_("AT"), K_("q1")], writes=[K_("q1")])
            yield None
            P.op(DVE, lambda e: e.tensor_tensor(out=DTd[u][0][:, hs, :], in0=q1[u][:, hs, :], in1=mh(15), op=ALU.add), reads=[K_("q1")], writes=[("DT", u, 0, hf)])
            cur = 0
            for li in range(1, 7):
                yield None
                last = (li == 6)
                nxt = 1 - cur
                D_c, DT_c = Dd[u][cur], DTd[u][cur]
                kD, kDT = ("D", u, cur, hf), ("DT", u, cur, hf)
                if not last:
                    px, pxk = next_ps(C, 2, 6)

                    def f_x(e, px=px, D_c=D_c):
                        ins = None
                        for c in range(ncn):
                            ins = e.matmul(px[:, c * 128:(c + 1) * 128], lhsT=AT_[u][:, c0 + c, :], rhs=D_c[:, c0 + c, :], start=True, stop=True)
                        return ins
                    P.op(PE, f_x, reads=[K_("AT"), kD], writes=[pxk])
                px2, px2k = next_ps(C, 2, 6)

                def f_x2(e, px2=px2, DT_c=DT_c):
                    ins = None
                    for c in range(ncn):
                        ins = e.matmul(px2[:, c * 128:(c + 1) * 128], lhsT=A_[u][:, c0 + c, :], rhs=DT_c[:, c0 + c, :], start=True, stop=True)
                    return ins
                P.op(PE, f_x2, reads=[K_("A"), kDT], writes=[px2k])
                yield None
                if not last:
                    P.op(DVE, lambda e, px=px, li=li: e.tensor_tensor(out=Xm[u][:, hs, :], in0=vh(px), in1=mh(2 + li), op=ALU.mult), reads=[pxk], writes=[K_("Xm")])
                P.op(DVE, lambda e, px2=px2, li=li: e.tensor_tensor(out=XTm[u][:, hs, :], in0=vh(px2), in1=mh(8 + li), op=ALU.mult), reads=[px2k], writes=[K_("XTm")])
                yield None
                if not last:
                    pm, pmk = next_ps(C, 2, 6)

                    def f_m(e, pm=pm, D_c=D_c, DT_c=DT_c):
                        ins = None
                        for c in range(ncn):
                            e.matmul(pm[:, c * 128:(c + 1) * 128], lhsT=DT_c[:, c0 + c, :], rhs=Xm[u][:, c0 + c, :], start=True, stop=False)
                            ins = e.matmul(pm[:, c * 128:(c + 1) * 128], lhsT=ident_bf[:, :], rhs=D_c[:, c0 + c, :], start=False, stop=True)
                        return ins
                    P.op(PE, f_m, reads=[kDT, K_("Xm"), kD], writes=[pmk])
                pm2, pm2k = next_ps(C, 2, 6)

                def f_m2(e, pm2=pm2, D_c=D_c, DT_c=DT_c):
                    ins = None
                    for c in range(ncn):
                        ins = e.matmul(pm2[:, c * 128:(c + 1) * 128], lhsT=D_c[:, c0 + c, :], rhs=XTm[u][:, c0 + c, :], start=True, stop=True)
                    return ins
                P.op(PE, f_m2, reads=[kD, K_("XTm"), kDT], writes=[pm2k])
                yield None
                if not last:
                    P.op(ACT, lambda e, pm=pm, nxt=nxt: e.copy(out=Dd[u][nxt][:, hs, :], in_=vh(pm)), reads=[pmk], writes=[("D", u, nxt, hf)])
                P.op(DVE, lambda e, pm2=pm2, nxt=nxt, DT_c=DT_c: e.tensor_tensor(out=DTd[u][nxt][:, hs, :], in0=vh(pm2), in1=DT_c[:, hs, :], op=ALU.add),
                     reads=[pm2k, kDT], writes=[("DT", u, nxt, hf)])
                cur = nxt
            yield None
            TT_ = DTd[u][cur]
            pw, pwk = next_ps(C, 2, 6)

            def f_w(e):
                ins = None
                for c in range(ncn):
                    ins = e.matmul(pw[:, c * 128:(c + 1) * 128], lhsT=kbd[u][:, c0 + c, :], rhs=TT_[:, c0 + c, :], start=True, stop=True)
                return ins
            P.op(PE, f_w, reads=[K_("kbd"), ("DT", u, cur, hf)], writes=[pwk])
            yield None
            P.op(ACT, lambda e: e.activation(out=nwT[u][:, hs, :], in_=vh(pw), func=AF.Copy, scale=-1.0), reads=[pwk], writes=[K_("nwT")])
            yield (TT_, cur)

        def seq(h, n, u, TT_, cur, po, pok):
            c = n % 4
            hf = c // 2
            w = u
            v2 = n % 2
            cl = slice(c * 128, (c + 1) * 128)
            K_ = lambda nm: (nm, u, hf)
            pv, pvk = next_ps(C, 1, 2)
            mm_chain(P, pv[:, 0:128], [(TT_[:, c, :], vb[u][:, c, :]), (nwT[u][:, c, :], Sbf[:, :])], reads=[("DT", u, cur, hf), K_("vb"), K_("nwT"), ("Sbf",)], writes=[pvk])
            P.op(ACT, lambda e: e.copy(out=vnew[v2][:, :], in_=pv[:, 0:128]), reads=[pvk], writes=[("vnew", v2)])
            mm_chain(P, po[:, cl], [(Sbf[:, :], QdT[w][:, cl]), (vnew[v2][:, :], qkT[u][:, c, :])], reads=[("Sbf",), ("QdT", w), ("vnew", v2), K_("qkT")], writes=[pok])
            pS, pSk = next_ps(C, 1, 2)
            mm_chain(P, pS[:, 0:128], [(kdec[u][:, c, :], vnew[v2][:, :])], reads=[K_("kdec"), ("vnew", v2)], writes=[pSk])
            P.op(DVE, lambda e: e.scalar_tensor_tensor(out=Sst[:, :], in0=Sst[:, :], scalar=egl[:, n, h:h + 1], in1=pS[:, 0:128], op0=ALU.mult, op1=ALU.add),
                 reads=[pSk, ("S",)], writes=[("S",)])
            P.op(ACT, lambda e: e.copy(out=Sbf[:, :], in_=Sst[:, :]), reads=[("S",)], writes=[("Sbf",)])

        def finish_tile(h, jt, v, po, pok):
            hb = h % 2
            tsl = slice(jt * TT, (jt + 1) * TT)
            P.op(ACT, lambda e: e.copy(out=o_sb[v][:, :], in_=po[:, :]), reads=[pok], writes=[("osb", v)])
            P.op(ACT, lambda e: e.activation(out=osq[v][:, :], in_=po[:, :], func=AF.Square), reads=[pok], writes=[("osq", v)])
            pss, pssk = next_ps(C, 1, 2)
            mm_chain(P, pss[:, :], [(C.ones_bf[:, :], osq[v][:, :])], reads=[("osq", v)], writes=[pssk])
            P.op(ACT, lambda e: e.activation(out=rr[v][:, :], in_=pss[:, :], func=AF.Ln, scale=1.0 / 128, bias=C.eps_col[:, 0:1]), reads=[pssk], writes=[("rr", v)])
            P.op(ACT, lambda e: e.activation(out=rr[v][:, :], in_=rr[v][:, :], func=AF.Exp, scale=-0.5), reads=[("rr", v)], writes=[("rr", v)])
            P.op(POOL, lambda e: e.tensor_tensor(out=o_sb[v][:, :], in0=o_sb[v][:, :], in1=rr[v][:, :], op=ALU.mult), reads=[("osb", v), ("rr", v)], writes=[("osb", v)])
            P.op(DVE, lambda e: e.scalar_tensor_tensor(out=mo[v][:, :], in0=o_sb[v][:, :], scalar=onorm[:, 0:1], in1=ZT[hb][:, tsl], op0=ALU.mult, op1=ALU.mult),
                 reads=[("osb", v), ("gz", hb)], writes=[("mo", v)])
            P.op(POOL, lambda e: e.dma_start(out=C.scr["mT"][h * 128:(h + 1) * 128, tsl], in_=mo[v][:, :]), reads=[("mo", v)], dkey=("Gst", v))

        tiles = [(h, jt) for h in range(8) for jt in range(NT)]
        load_head(0)
        load_head(1)

        def run_pair(h, jt, u, hooks):
            g0 = prep_half(h, jt, u, 0, 2)
            g1 = prep_half(h, jt, u, 2, 2)
            res = [None, None]
            step = 0
            alive = [True, True]
            import os
            if os.environ.get("GDN_SEQ"):
                for gi, g in enumerate((g0, g1)):
                    for r in g:
                        if r is not None:
                            res[gi] = r
                alive = [False, False]
            while alive[0] or alive[1]:
                for gi, g in enumerate((g0, g1)):
                    if alive[gi]:
                        try:
                            r = next(g)
                            if r is not None:
                                res[gi] = r
                        except StopIteration:
                            alive[gi] = False
                if step in hooks:
                    hooks[step]()
                step += 1
            assert res[0][1] == res[1][1]
            return res[0]

        pend = run_pair(0, 0, 0, {})
        for k, (h, jt) in enumerate(tiles):
            u = k % 2
            if jt == 0:
                P.op(POOL, lambda e: e.memset(Sst[:, :], 0.0), writes=[("S",)])
                P.op(POOL, lambda e: e.memset(Sbf[:, :], 0.0), writes=[("Sbf",)])
            po, pok = next_ps(C, 0, 1)
            done = []

            def mk(c):
                def f():
                    seq(h, 4 * jt + c, u, pend[0], pend[1], po, pok)
                    done.append(c)
                return f
            nxt_p = None
            if k + 1 < len(tiles):
                h2, jt2 = tiles[k + 1]
                nxt_p = run_pair(h2, jt2, 1 - u, {4: mk(0), 10: mk(1), 16: mk(2), 22: mk(3)})
            for c in range(4):
                if c not in done:
                    seq(h, 4 * jt + c, u, pend[0], pend[1], po, pok)
            finish_tile(h, jt, u, po, pok)
            if jt == NT - 1 and h + 2 < 8:
                load_head(h + 2)
            pend = nxt_p
        P.barrier()


def stage_l1_out(P, C):
    nc = P.nc
    xres = C.xres
    with ExitStack() as es:
        sb = lambda name, shape, dt=F32: es.enter_context(nc.sbuf_tensor(name, list(shape), dt))
        wout = sb("O1_wout", [128, 8, 1024], BF16)
        wload(P, C, wout, C.w["l1_w_out"], 8, 1024, gain=None, name="O1_wout")
        P.barrier()
        xt = [sb(f"O1_xt{i}", [128, 8, TT]) for i in range(2)]
        mg = [sb(f"O1_mg{i}", [128, 8, TT], BF16) for i in range(2)]
        xv = xres.rearrange("(c p) t -> p c t", p=128)
        mv = C.scr["mT"].rearrange("(c p) t -> p c t", p=128)
        for j in range(NT):
            u = j % 2
            tsl = slice(j * TT, (j + 1) * TT)
            P.op(SP, lambda e, u=u, tsl=tsl: e.dma_start(out=xt[u][:, :, :], in_=xv[:, :, tsl]), writes=[("xt", u)], dkey=("O1ld", "x", u))
            P.op(SP, lambda e, u=u, tsl=tsl: e.dma_start(out=mg[u][:, :, :], in_=mv[:, :, tsl]), writes=[("mg", u)], dkey=("O1ld", "m", u))
            for fb in range(8):
                pa, pak = next_ps(C)
                mm_chain(P, pa[:, :], [(wout[:, c, fb * 128:(fb + 1) * 128], mg[u][:, c, :]) for c in range(8)], reads=[*wkeys(C, "O1_wout", fb * 128), ("mg", u)], writes=[pak])
                P.op(DVE, lambda e, pa=pa, u=u, fb=fb: e.tensor_tensor(out=xt[u][:, fb, :], in0=pa[:, :], in1=xt[u][:, fb, :], op=ALU.add),
                     reads=[pak, ("xt", u)], writes=[("xt", u)])
            P.op(POOL, lambda e, u=u, tsl=tsl: e.dma_start(out=xv[:, :, tsl], in_=xt[u][:, :, :]), reads=[("xt", u)], dkey=("O1st", u))
        P.barrier()


def stage_copy_in(P, C):
    P.op(SP, lambda e: e.dma_start(out=C.xres[:, :], in_=C.xT_in[:, :]), dkey=("cpin",))
    P.barrier()


def stage_copy_out(P, C):
    P.op(SP, lambda e: e.dma_start(out=C.outT[:, :], in_=C.xres[:, :]), dkey=("cpout",))
    P.barrier()


WNAMES = ["xa_wq", "xa_wkv", "xa_wo", "ffn_w_up", "ffn_conv", "ffn_w_down"]
GNAMES = ["xa_norm", "mem_norm", "ffn_norm"]


def build_program(plan):
    nc = bass.Bass("TRN2", target_bir_lowering=False)
    P = Prog(nc)
    C = Ctx()
    C.ps_cnt = {}
    C.wreg = {}
    dt_in = lambda name, shape, dt=F32: nc.dram_tensor(name, list(shape), dt, kind="ExternalInput").ap()
    C.xT_in = dt_in("xT", [D, S])
    C.memT = dt_in("memT", [D, NMEM])
    C.outT = nc.dram_tensor("outT", [D, S], F32, kind="ExternalOutput").ap()
    C.xres = nc.dram_tensor("xres", [D, S], F32, kind="Internal").ap()
    C.w = {}
    for lp in ["l0_", "l1_"]:
        C.w[lp + "xa_wq"] = dt_in(lp + "xa_wq", [D, D])
        C.w[lp + "xa_wkv"] = dt_in(lp + "xa_wkv", [D, 2 * D])
        C.w[lp + "xa_wo"] = dt_in(lp + "xa_wo", [D, D])
        C.w[lp + "ffn_w_up"] = dt_in(lp + "ffn_w_up", [D, 2 * FFN])
        C.w[lp + "ffn_conv"] = dt_in(lp + "ffn_conv", [128, 3, 2 * NPAIR])
        C.w[lp + "ffn_w_down"] = dt_in(lp + "ffn_w_down", [FFN, D])
    for n, shp in [("l0_w_in", [D, 2560]), ("l0_w_in_sw", [D, 1024]), ("l0_ret_norm", [128, 4]), ("l0_s5_w_glu", [512, 512]),
                   ("l0_s5_b_glu", [128, 4]), ("l0_w_out", [D, D]), ("s5_lam_s", [128, 2, 32]), ("s5_ldt_s", [128, 32]),
                   ("s5_lam_b", [16, 2, 2048]), ("s5_ldt_b", [16, 2048]), ("s5_b_b", [16, 2, 2048]), ("s5_c1", [128, 32, 16]),
                   ("s5_c2", [128, 32, 16]), ("s5_d_t", [16, 32])]:
        C.w[n] = dt_in(n, shp)
    for n, shp in [("l1_w_in", [D, 4112]), ("l1_conv", [128, 4, 24]), ("l1_hp", [8, 2]), ("l1_o_norm", [128, 1]), ("l1_w_out", [D, D])]:
        C.w[n] = dt_in(n, shp)
    C.cst = {}
    for n, shp in [("rot_tab", [128, 4, S]), ("gq_tab", [128, 4, TT]), ("dt_tab", [128, 4, 128]), ("kd_tab", [128, 4]),
                   ("id16", [16, 16]), ("tpos", [128, S]), ("t0s", [128, 32, 8]), ("t0b", [16, 8, 128]), ("cmask", [8, TT]), ("sel16", [16, 16, 128]), ("l2c", [16, 2]), ("gmasks", [128, 18, 128]), ("gsel", [8, 8, 128]),
                   ("gsellast", [128, 128])]:
        C.cst[n] = dt_in(n, shp)
    C.scr = {}
    for n, shp in [("qT", [512, S]), ("qdT", [512, S]), ("kT", [512, S]), ("gT", [512, S]), ("uT", [512, S]), ("vtok", [S, 512]),
                   ("mT", [D, S]), ("ygT", [512, S]), ("gq", [D, S]), ("gk", [D, S]), ("gv", [D, S]), ("gz", [D, S])]:
        C.scr[n] = nc.dram_tensor("scr_" + n, list(shp), BF16, kind="Internal").ap()
    C.scr32 = {n: nc.dram_tensor("scr_" + n, [8, S], F32, kind="Internal").ap() for n in ["gcT", "btT"]}
    gnames = [lp + g for lp in ["l0_", "l1_"] for g in GNAMES + ["mix_norm"]] + ["final_norm"]
    gains_d = dt_in("gains", [128, len(gnames), 8])
    consts_bf = dt_in("consts_bf", [128, 2, 128], BF16)
    C.cst_oh16 = dt_in("oh16", [128, 16, 16], BF16)

    gains_t = P.sb("gains_t", [128, len(gnames), 8])
    cbf = P.sb("cbf", [128, 2, 128], BF16)
    C.eps_col = P.sb("eps_col", [128, 1])
    C.one_col = P.sb("one_col", [128, 1])
    C.eps128_col = P.sb("eps128_col", [128, 1])
    C.psum = [P.ps(f"psum{i}", [128, 512]) for i in range(6)]
    C.psbs = [P.ps(f"psb{i}", [128, 1024], BF16) for i in range(2)]
    C.psb = C.psbs[0]
    C.psb_rr = 0
    P.op(SP, lambda e: e.dma_start(out=gains_t[:, :, :], in_=gains_d), writes=[gain_key(None)], dkey=("gains",))
    P.op(SP, lambda e: e.dma_start(out=cbf[:, :, :], in_=consts_bf), writes=[("cbf",)], dkey=("cbf",))
    P.op(POOL, lambda e: e.memset(C.eps_col[:, :], EPS), writes=[("eps",)])
    P.op(POOL, lambda e: e.memset(C.one_col[:, :], 1.0), writes=[("one",)])
    P.op(POOL, lambda e: e.memset(C.eps128_col[:, :], 128.0 * EPS), writes=[("eps128",)])
    P.barrier()
    C.gains = {n: gains_t[:, i, :] for i, n in enumerate(gnames)}
    C.ident_bf = cbf[:, 0, :]
    C.ones_bf = cbf[:, 1, :]

    for st in plan:
        if st == "copy_in":
            stage_copy_in(P, C)
        elif st == "copy_out":
            stage_copy_out(P, C)
        elif st == "l0_inproj":
            stage_l0_inproj(P, C)
        elif st == "l0_ret":
            stage_retention(P, C)
        elif st == "l0_s5":
            stage_s5(P, C)
        elif st == "l0_out":
            stage_l0_out(P, C)
        elif st == "l1_inproj":
            stage_l1_inproj(P, C)
        elif st == "l1_gdn":
            stage_gdn(P, C)
        elif st == "l1_out":
            stage_l1_out(P, C)
        elif st.endswith("_xa"):
            stage_xattn(P, C, st[:3])
        elif st.endswith("_ffn"):
            stage_ffn(P, C, st[:3], final=False)
        elif st.endswith("_ffnfinal"):
            stage_ffn(P, C, st[:3], final=True)
        else:
            raise ValueError(st)
        P.barrier(final=True)
    P.emit()
    return nc, P, gnames


def host_prep(inputs, gnames):
    shared = {}
    for lp in ["l0_", "l1_"]:
        for n in ["xa_wq", "xa_wkv", "xa_wo", "ffn_w_up", "ffn_w_down"]:
            shared[lp + n] = np.ascontiguousarray(inputs[lp + n], dtype=np.float32)
        cw = np.asarray(inputs[lp + "ffn_conv"], dtype=np.float32)
        shared[lp + "ffn_conv"] = np.ascontiguousarray(cw.reshape(3, 2 * NPAIR, 128).transpose(2, 0, 1))
    f32 = lambda a: np.ascontiguousarray(np.asarray(a, dtype=np.float32))
    w_in = f32(inputs["l0_w_in"])
    shared["l0_w_in"] = w_in
    qk = w_in[:, :1024].reshape(D, 8, 128)
    shared["l0_w_in_sw"] = f32(np.concatenate([qk[:, :, 64:], qk[:, :, :64]], axis=2).reshape(D, 1024))
    shared["l0_ret_norm"] = f32(np.asarray(inputs["l0_ret_norm"]).reshape(4, 128).T)
    shared["l0_s5_w_glu"] = f32(inputs["l0_s5_w_glu"])
    shared["l0_s5_b_glu"] = f32(np.asarray(inputs["l0_s5_b_glu"]).reshape(4, 128).T)
    shared["l0_w_out"] = f32(inputs["l0_w_out"])
    lre = np.asarray(inputs["l0_s5_lambda_re"], np.float32)
    lim = np.asarray(inputs["l0_s5_lambda_im"], np.float32)
    ldt = np.asarray(inputs["l0_s5_log_dt"], np.float32)
    lam_s = np.stack([np.concatenate([lre.T, lre.T], 0), np.concatenate([lim.T, lim.T], 0)], axis=1)
    shared["s5_lam_s"] = f32(lam_s)
    shared["s5_ldt_s"] = f32(np.broadcast_to(ldt[None, :], (128, 32)))
    shared["s5_lam_b"] = f32(np.broadcast_to(np.stack([lre.reshape(-1), lim.reshape(-1)], 0)[None], (16, 2, 2048)))
    shared["s5_ldt_b"] = f32(np.broadcast_to(np.repeat(ldt, 64)[None], (16, 2048)))
    bre = np.asarray(inputs["l0_s5_b_re"], np.float32).transpose(1, 0, 2).reshape(16, 2048)
    bim = np.asarray(inputs["l0_s5_b_im"], np.float32).transpose(1, 0, 2).reshape(16, 2048)
    shared["s5_b_b"] = f32(np.stack([bre, bim], axis=1))
    cre = np.asarray(inputs["l0_s5_c_re"], np.float32).transpose(1, 0, 2)
    cim = np.asarray(inputs["l0_s5_c_im"], np.float32).transpose(1, 0, 2)
    shared["s5_c1"] = f32(np.concatenate([cre, cim], 0))
    shared["s5_c2"] = f32(np.concatenate([cim, cre], 0))
    shared["s5_d_t"] = f32(np.asarray(inputs["l0_s5_d"], np.float32).T)
    shared["l1_w_in"] = f32(inputs["l1_w_in"])
    shared["l1_conv"] = f32(np.asarray(inputs["l1_conv"], np.float32).reshape(4, 24, 128).transpose(2, 0, 1))
    shared["l1_hp"] = f32(np.stack([np.asarray(inputs["l1_a_log"], np.float32), np.asarray(inputs["l1_dt_bias"], np.float32)], axis=1))
    shared["l1_o_norm"] = f32(np.asarray(inputs["l1_o_norm"], np.float32).reshape(128, 1))
    shared["l1_w_out"] = f32(inputs["l1_w_out"])
    cm = np.ones((8, TT), np.float32)
    cm[:, ::128] = 0.0
    shared["cmask"] = cm
    s16 = np.zeros((16, 16, 128), np.float32)
    oh = np.zeros((128, 16, 16), np.float32)
    for q_ in range(16):
        s16[q_, q_, :] = 1.0
        oh[:, q_, q_] = 1.0
    shared["sel16"] = s16
    shared["oh16"] = oh.astype(ml_dtypes.bfloat16)
    l2 = np.zeros((16, 2), np.float32)
    l2[:8, 0] = 128.0
    l2[:8, 1] = 128.0 * EPS
    l2[8:, 0] = 1.0
    l2[8:, 1] = EPS
    shared["l2c"] = l2
    ii_, jj_ = np.meshgrid(np.arange(128), np.arange(128), indexing="ij")
    gm = np.zeros((128, 18, 128), np.float32)
    gm[:, 0, :] = (ii_ > jj_)
    gm[:, 1, :] = (jj_ >= ii_)
    for li, s_ in enumerate([1, 2, 4, 8, 16, 32, 64]):
        m = ((ii_ // (2 * s_)) == (jj_ // (2 * s_))) & ((ii_ % (2 * s_)) >= s_) & ((jj_ % (2 * s_)) < s_)
        if li < 6:
            gm[:, 2 + li, :] = -m.astype(np.float32)
        gm[:, 8 + li, :] = -m.T.astype(np.float32)
    gm[:, 15, :] = np.eye(128, dtype=np.float32)
    BIG = 30000.0
    gm[:, 16, :] = BIG * (1.0 - gm[:, 0, :])
    gm[:, 17, :] = -BIG * gm[:, 0, :]
    shared["gmasks"] = gm
    gs = np.zeros((8, 8, 128), np.float32)
    for h_ in range(8):
        gs[h_, h_, :] = 1.0
    shared["gsel"] = gs
    sl = np.zeros((128, 128), np.float32)
    sl[127, :] = 1.0
    shared["gsellast"] = sl
    inv = np.exp(-math.log(10000.0) * np.arange(64, dtype=np.float32) / 64).astype(np.float32)
    ang = (np.arange(S, dtype=np.float32)[:, None] * inv[None, :]).astype(np.float32).astype(np.float64)
    cosT = np.cos(ang).T
    sinT = np.sin(ang).T
    cos128 = np.concatenate([cosT, cosT], 0)
    sin128 = np.concatenate([-sinT, sinT], 0)
    ksc = 128.0 ** -0.5
    shared["rot_tab"] = f32(np.stack([cos128, sin128, cos128 * ksc, sin128 * ksc], axis=1))
    gam = np.array(RET_GAMMA, np.float64)
    ii = np.arange(TT) % 128
    shared["gq_tab"] = f32(np.broadcast_to((gam[:, None] ** (ii[None, :] + 1))[None], (128, 4, TT)))
    jj = np.arange(128)
    diff = jj[None, :] - jj[:, None]
    dtab = np.where(diff[:, None, :] >= 0, gam[None, :, None] ** np.maximum(diff[:, None, :], 0), 0.0)
    shared["dt_tab"] = f32(dtab)
    shared["kd_tab"] = f32(gam[None, :] ** (127 - jj[:, None]))
    shared["id16"] = f32(np.eye(16))
    shared["tpos"] = f32(np.broadcast_to(np.arange(S, dtype=np.float32)[None], (128, S)))
    shared["t0s"] = f32(np.broadcast_to((np.arange(8, dtype=np.float32) * TT)[None, None, :], (128, 32, 8)))
    shared["t0b"] = f32(np.broadcast_to((np.arange(8, dtype=np.float32) * TT)[None, :, None], (16, 8, 128)))
    g = np.stack([np.asarray(inputs[n], np.float32).reshape(8, 128).T for n in gnames], axis=1)
    shared["gains"] = np.ascontiguousarray(g)
    cb = np.stack([np.eye(128, dtype=np.float32), np.ones((128, 128), np.float32)], axis=1)
    shared["consts_bf"] = np.ascontiguousarray(cb).astype(ml_dtypes.bfloat16)
    return shared


PLAN_FULL = ["copy_in", "l0_inproj", "l0_ret", "l0_s5", "l0_out", "l0_xa", "l0_ffn", "l1_inproj", "l1_gdn", "l1_out", "l1_xa", "l1_ffnfinal"]


def kernel(**inputs):
    nc, P, gnames = build_program(PLAN_FULL)
    shared = host_prep(inputs, gnames)
    x = np.asarray(inputs["x"], np.float32)
    mem = np.asarray(inputs["mem"], np.float32)
    in_maps = []
    for b in range(8):
        m = dict(shared)
        m["xT"] = np.ascontiguousarray(x[b].T)
        m["memT"] = np.ascontiguousarray(mem[b].T)
        in_maps.append(m)
    res = run_bass_kernel_spmd(nc, in_maps, core_ids=list(range(8)))
    out = np.stack([np.ascontiguousarray(res.results[b]["outT"].T) for b in range(8)], axis=0)
    return out.astype(np.float32)
```
